# Optimizing a Trainium2 kernel written in Bass

```python
import math
import jax, jax.numpy as jnp
from jax import lax
import numpy as np

D_MODEL = 1024
BATCH = 16
SEQ = 2048
DEPTH = 1
DEC_BATCH = 8
DEC_SEQ = 4096
PAST_LEN = 128

RWKV_HEAD = 64
RWKV_WIDTH = D_MODEL
RWKV_HEADS = RWKV_WIDTH // RWKV_HEAD
DECAY_RANK = 64
ICLR_RANK = 64
GATE_RANK = 128
GN_EPS = 64e-5
DA_HEAD = 64
DA_HEADS = D_MODEL // (2 * DA_HEAD)
DA_WIDTH = DA_HEADS * 2 * DA_HEAD
ROPE_THETA = 500000.0
ROPE_DIM = DA_HEAD // 4
Q_BLOCK = 128
SUBLN_EPS = 1e-5
D_FF = 4 * D_MODEL
NORM_EPS = 1e-6
RWKV_COLS = 3 * RWKV_WIDTH + 2 * DECAY_RANK + 2 * ICLR_RANK + GATE_RANK
ATTN_COLS = 3 * DA_WIDTH
GATE_COLS = 2 * D_MODEL
IN_COLS = RWKV_COLS + ATTN_COLS + GATE_COLS
RWKV_SPLITS = (RWKV_WIDTH, 2 * RWKV_WIDTH, 3 * RWKV_WIDTH, 3 * RWKV_WIDTH + 2 * DECAY_RANK, 3 * RWKV_WIDTH + 2 * DECAY_RANK + 2 * ICLR_RANK)
ATTN_SPLITS = (DA_WIDTH, 2 * DA_WIDTH, 3 * DA_WIDTH, 3 * DA_WIDTH + D_MODEL)

kernel_name = 'hybrid_rwkv7_diffattn_encoder'


def rmsnorm(x, g, eps=NORM_EPS):
    xf = x.astype(jnp.float32)
    y = xf * lax.rsqrt(jnp.mean(xf * xf, axis=-1, keepdims=True) + eps)
    return (y * g.astype(jnp.float32)).astype(x.dtype)


def centred_shift(p, mu):
    pad = jnp.pad(p, ((0, 0), (1, 1), (0, 0)))
    nb = 0.5 * (pad[:, :-2] + pad[:, 2:])
    return p + (nb - p) * mu


def wkv7_scan(r, w, k, v, a, b, reverse):
    B, S, H, N = r.shape

    def step(st, inp):
        r_t, w_t, k_t, v_t, a_t, b_t = inp
        sa = jnp.einsum('bhvk,bhk->bhv', st, a_t)
        st = st * w_t[:, :, None, :] + sa[..., None] * b_t[:, :, None, :] + v_t[..., None] * k_t[:, :, None, :]
        y = jnp.einsum('bhvk,bhk->bhv', st, r_t)
        return st, y

    xs = tuple(jnp.moveaxis(t, 1, 0) for t in (r, w, k, v, a, b))
    s0 = jnp.zeros((B, H, N, N), jnp.float32)
    _, ys = lax.scan(step, s0, xs, reverse=reverse)
    return jnp.moveaxis(ys, 0, 1)


def rwkv7_mixer(r, k, v, lw, la, lg, w0, w_up, a0, a_up, g_up, k_k, k_a, r_k, ln_w, ln_b):
    B, S, C = r.shape
    H, N = RWKV_HEADS, RWKV_HEAD
    f32 = jnp.float32
    r, k, v = r.astype(f32), k.astype(f32), v.astype(f32)
    lw = lw.astype(f32).reshape(B, S, 2, DECAY_RANK)
    la = la.astype(f32).reshape(B, S, 2, ICLR_RANK)
    w_raw = w0.astype(f32) + jnp.einsum('bsdr,drc->bsdc', jnp.tanh(lw), w_up.astype(f32))
    decay = jnp.exp(-jnp.exp(-jax.nn.softplus(-w_raw) - 0.5))
    a = jax.nn.sigmoid(a0.astype(f32) + jnp.einsum('bsdr,drc->bsdc', la, a_up.astype(f32)))
    g = jax.nn.sigmoid(lg.astype(f32)) @ g_up.astype(f32)
    kk = (k * k_k.astype(f32)).reshape(B, S, H, N)
    kk = kk / jnp.maximum(jnp.sqrt(jnp.sum(kk * kk, axis=-1, keepdims=True)), 1e-12)
    k_dir = k[:, :, None, :] * (1.0 + (a - 1.0) * k_a.astype(f32))
    heads = lambda t: t.reshape(B, S, H, N)
    rh, vh = heads(r), heads(v)
    rk = r_k.astype(f32)
    y = jnp.zeros((B, S, H, N), f32)
    bonus = jnp.zeros((B, S, H, N), f32)
    for d, rev in ((0, False), (1, True)):
        kd = heads(k_dir[:, :, d])
        ad = heads(a[:, :, d])
        y = y + wkv7_scan(rh, heads(decay[:, :, d]), kd, vh, -kk, kk * ad, rev)
        bonus = bonus + jnp.sum(rh * kd * rk, axis=-1, keepdims=True) * vh
    mean = jnp.mean(y, axis=-1, keepdims=True)
    var = jnp.mean(jnp.square(y - mean), axis=-1, keepdims=True)
    gn = ((y - mean) * lax.rsqrt(var + GN_EPS)).reshape(B, S, C) * ln_w.astype(f32) + ln_b.astype(f32)
    return (gn + bonus.reshape(B, S, C)) * g


def partial_rope(x, cos, sin):
    half = ROPE_DIM // 2
    xr, xp = x[..., :ROPE_DIM], x[..., ROPE_DIM:]
    x1, x2 = xr[..., :half], xr[..., half:]
    c = cos[None, :, None, None, :]
    s = sin[None, :, None, None, :]
    return jnp.concatenate([x1 * c - x2 * s, x2 * c + x1 * s, xp], axis=-1)


def diff_attention(q, k, v, lq1, lk1, lq2, lk2, subln_w, lam_init):
    B, S, _ = q.shape
    H, d = DA_HEADS, DA_HEAD
    f32 = jnp.float32
    q = q.astype(f32).reshape(B, S, H, 2, d)
    k = k.astype(f32).reshape(B, S, H, 2, d)
    v = v.astype(f32).reshape(B, S, H, 2 * d).transpose(0, 2, 1, 3)
    pos = jnp.arange(S, dtype=f32)
    inv_freq = ROPE_THETA ** (-jnp.arange(0, ROPE_DIM, 2, dtype=f32) / ROPE_DIM)
    ang = pos[:, None] * inv_freq[None, :]
    cos, sin = jnp.cos(ang), jnp.sin(ang)
    q = partial_rope(q, cos, sin).transpose(0, 2, 3, 1, 4) * (d ** -0.5)
    k = partial_rope(k, cos, sin).transpose(0, 2, 3, 1, 4)
    lam = (jnp.exp(jnp.sum(lq1.astype(f32) * lk1.astype(f32))) - jnp.exp(jnp.sum(lq2.astype(f32) * lk2.astype(f32))) + lam_init)
    nblk = S // Q_BLOCK
    qb = q.reshape(B, H, 2, nblk, Q_BLOCK, d).transpose(3, 0, 1, 2, 4, 5)

    def attend(qblk):
        s = jnp.einsum('bhcqd,bhckd->bhcqk', qblk, k)
        p = jax.nn.softmax(s, axis=-1)
        att = p[:, :, 0] - lam * p[:, :, 1]
        return jnp.einsum('bhqk,bhkv->bhqv', att, v)

    o = lax.map(attend, qb)
    o = o.transpose(1, 0, 3, 2, 4).reshape(B, S, H, 2 * d)
    o = o * lax.rsqrt(jnp.mean(o * o, axis=-1, keepdims=True) + SUBLN_EPS) * subln_w.astype(f32)
    return (o * (1.0 - lam_init)).reshape(B, S, DA_WIDTH)


def encoder_layer(x, l, w_in, mu_shift, w0, w_lora_up, a0, a_lora_up, g_lora_up, k_k, k_a, r_k, ln_x_w, ln_x_b, lambda_q1, lambda_k1, lambda_q2, lambda_k2, subln_w, proj_a, proj_b, w_out, norm_mix, norm_mlp, w_mlp_in, w_mlp_out):
    dt = x.dtype
    xn = rmsnorm(x, norm_mix[l])
    proj = xn @ w_in[l]
    p_rwkv = centred_shift(proj[..., :RWKV_COLS], mu_shift[l])
    r, k, v, lw, la, lg = jnp.split(p_rwkv, RWKV_SPLITS, axis=-1)
    q_b, k_b, v_b, gate_a, gate_b = jnp.split(proj[..., RWKV_COLS:], ATTN_SPLITS, axis=-1)
    y_a = rwkv7_mixer(r, k, v, lw, la, lg, w0[l], w_lora_up[l], a0[l], a_lora_up[l], g_lora_up[l], k_k[l], k_a[l], r_k[l], ln_x_w[l], ln_x_b[l]).astype(dt)
    lam_init = 0.8 - 0.6 * math.exp(-0.3 * l)
    y_b = diff_attention(q_b, k_b, v_b, lambda_q1[l], lambda_k1[l], lambda_q2[l], lambda_k2[l], subln_w[l], lam_init).astype(dt)
    merged = jax.nn.sigmoid(gate_a) * (y_a @ proj_a[l]) + jax.nn.sigmoid(gate_b) * (y_b @ proj_b[l])
    h = x + merged @ w_out[l]
    hid = jnp.square(jax.nn.relu(rmsnorm(h, norm_mlp[l]) @ w_mlp_in[l]))
    return h + hid @ w_mlp_out[l]


def setup_inputs(seed: int = 0) -> dict:
    key = jax.random.key(seed)
    ks = jax.random.split(key, 32)
    f32 = jnp.float32
    nrm = lambda kk, shape, s: jax.random.normal(kk, shape, f32) * s
    L, C, D = DEPTH, RWKV_WIDTH, D_MODEL
    return {
        'x_prompt': nrm(ks[0], (BATCH, SEQ, D), 1.0),
        'x_sample': nrm(ks[1], (DEC_BATCH, DEC_SEQ, D), 1.0),
        'w_in': nrm(ks[2], (L, D, IN_COLS), D ** -0.5),
        'mu_shift': jax.random.uniform(ks[3], (L, RWKV_COLS), f32),
        'w0': jax.random.uniform(ks[4], (L, 2, C), f32, -6.0, 1.0),
        'w_lora_up': nrm(ks[5], (L, 2, DECAY_RANK, C), 0.3 * DECAY_RANK ** -0.5),
        'a0': nrm(ks[6], (L, 2, C), 0.1),
        'a_lora_up': nrm(ks[7], (L, 2, ICLR_RANK, C), 0.3 * ICLR_RANK ** -0.5),
        'g_lora_up': nrm(ks[8], (L, GATE_RANK, C), GATE_RANK ** -0.5),
        'k_k': 0.85 + nrm(ks[9], (L, C), 0.05),
        'k_a': 1.0 + nrm(ks[10], (L, C), 0.05),
        'r_k': nrm(ks[11], (L, RWKV_HEADS, RWKV_HEAD), 0.1),
        'ln_x_w': 1.0 + nrm(ks[12], (L, C), 0.02),
        'ln_x_b': nrm(ks[13], (L, C), 0.02),
        'lambda_q1': nrm(ks[14], (L, DA_HEAD), 0.1),
        'lambda_k1': nrm(ks[15], (L, DA_HEAD), 0.1),
        'lambda_q2': nrm(ks[16], (L, DA_HEAD), 0.1),
        'lambda_k2': nrm(ks[17], (L, DA_HEAD), 0.1),
        'subln_w': 1.0 + nrm(ks[18], (L, 2 * DA_HEAD), 0.02),
        'proj_a': nrm(ks[19], (L, C, D), C ** -0.5),
        'proj_b': nrm(ks[20], (L, DA_WIDTH, D), DA_WIDTH ** -0.5),
        'w_out': nrm(ks[21], (L, D, D), D ** -0.5),
        'norm_mix': 1.0 + nrm(ks[22], (L, D), 0.02),
        'norm_mlp': 1.0 + nrm(ks[23], (L, D), 0.02),
        'w_mlp_in': nrm(ks[24], (L, D, D_FF), D ** -0.5),
        'w_mlp_out': nrm(ks[25], (L, D_FF, D), D_FF ** -0.5),
        'norm_final': 1.0 + nrm(ks[26], (D,), 0.02),
    }


def reference(x_prompt, x_sample, w_in, mu_shift, w0, w_lora_up, a0, a_lora_up, g_lora_up, k_k, k_a, r_k, ln_x_w, ln_x_b, lambda_q1, lambda_k1, lambda_q2, lambda_k2, subln_w, proj_a, proj_b, w_out, norm_mix, norm_mlp, w_mlp_in, w_mlp_out, norm_final):
    h_p = x_prompt
    h_s = x_sample
    for l in range(DEPTH):
        h_p = encoder_layer(h_p, l, w_in, mu_shift, w0, w_lora_up, a0, a_lora_up, g_lora_up, k_k, k_a, r_k, ln_x_w, ln_x_b, lambda_q1, lambda_k1, lambda_q2, lambda_k2, subln_w, proj_a, proj_b, w_out, norm_mix, norm_mlp, w_mlp_in, w_mlp_out)
        h_s = encoder_layer(h_s, l, w_in, mu_shift, w0, w_lora_up, a0, a_lora_up, g_lora_up, k_k, k_a, r_k, ln_x_w, ln_x_b, lambda_q1, lambda_k1, lambda_q2, lambda_k2, subln_w, proj_a, proj_b, w_out, norm_mix, norm_mlp, w_mlp_in, w_mlp_out)
    y_prompt = rmsnorm(h_p, norm_final)
    y_sample = rmsnorm(h_s, norm_final)
    return (y_prompt, y_sample)
```

```python
import math
from contextlib import ExitStack
import numpy as np
import concourse.bass as bass
import concourse.mybir as mybir
from concourse.bass_utils import run_bass_kernel_spmd

F32 = mybir.dt.float32
BF16 = mybir.dt.bfloat16
I32 = mybir.dt.int32
AF = mybir.ActivationFunctionType
ALU = mybir.AluOpType
AX = mybir.AxisListType

D = 1024
NRW = 3456
NIN = 8576
DFF = 4096
ENGS = ("pe", "act", "dve", "pool", "sp")
NDS = 24


GST = [None]


class Buf:
    __slots__ = ("w", "r", "name")

    def __init__(self, name=""):
        self.w = None
        self.r = []
        self.name = name


class Op:
    __slots__ = ("fn", "deps", "dma", "sig", "val", "sem")

    def __init__(self, fn, deps, dma):
        self.fn = fn
        self.deps = deps
        self.dma = dma
        self.sig = False
        self.val = 0
        self.sem = None


class _Rec:
    def __init__(self):
        self.calls = []

    def __getattr__(self, name):
        def f(*a, **k):
            self.calls.append((name, a, k))
            return None

        return f


class Phase:
    def __init__(self, nc, name, bar):
        self.nc = nc
        self.name = name
        self.bar = bar
        self.ops = {e: [] for e in ENGS}
        self.stack = ExitStack()
        self.ndma = 0
        self.dma_ids = []
        self.nps = 0

    def sb(self, name, shape, dt):
        return self.stack.enter_context(self.nc.sbuf_tensor(self.name + "_" + name, list(shape), dt))

    def ps(self, name, shape, dt=F32):
        return self.stack.enter_context(self.nc.psum_tensor(self.name + "_" + name, list(shape), dt))

    def add(self, eng, fn, rd=(), wr=(), dma=False):
        if eng != "pe" and not dma:
            rec = _Rec()
            fn(rec)
            me = None
            for name, a, k in rec.calls:
                me = self._add1(eng, (lambda e, name=name, a=a, k=k: getattr(e, name)(*a, **k)), rd, wr, False)
            return me
        return self._add1(eng, fn, rd, wr, dma)

    def _add1(self, eng, fn, rd=(), wr=(), dma=False):
        ops = self.ops[eng]
        me = (eng, len(ops))
        deps = set()
        for b in rd:
            if b.w is not None:
                deps.add(b.w)
        for b in wr:
            if b.w is not None:
                deps.add(b.w)
            deps.update(b.r)
        deps.discard(me)
        op = Op(fn, deps, dma)
        if dma:
            k = self.ndma
            self.ndma += 1
            op.sem = k % NDS
            op.val = 16 * (k // NDS + 1)
            if k >= NDS:
                deps.add(self.dma_ids[k - NDS])
            self.dma_ids.append(me)
        ops.append(op)
        for b in rd:
            b.r.append(me)
        for b in wr:
            b.w = me
            b.r = []
        return me

    def dma(self, out, in_, rd=(), wr=(), q="sp", **kw):
        return self.add(q, lambda e: e.dma_start(out=out, in_=in_, **kw), rd, wr, dma=True)

    def run(self):
        nc = self.nc
        ops = self.ops
        bar_sem, bar_val = self.bar
        fin = set(self.dma_ids)
        for e in ENGS:
            if e != "sp" and ops[e]:
                fin.add((e, len(ops[e]) - 1))
        sems, dsems = GST[0]

        def f_bar(e):
            for sm in list(sems.values()) + list(dsems):
                e.sem_clear(sm)
            return e.sem_inc(bar_sem, 1)

        ops["sp"].append(Op(f_bar, fin, False))
        for e in ENGS:
            for op in ops[e]:
                for d in op.deps:
                    ops[d[0]][d[1]].sig = True
        if True:
            for e in ENGS:
                c = 0
                for op in ops[e]:
                    if op.dma:
                        op.sem = dsems[op.sem]
                    elif op.sig:
                        c += 1
                        op.val = c
                        op.sem = sems[e]

            def emit(ename, eng):
                known = {}
                if bar_val > 0:
                    eng.wait_ge(bar_sem, bar_val)
                for op in ops[ename]:
                    need = {}
                    for d in op.deps:
                        dop = ops[d[0]][d[1]]
                        if ename == "pe" and d[0] == "pe" and not dop.dma:
                            continue
                        key = id(dop.sem)
                        if known.get(key, 0) < dop.val and need.get(key, (None, 0))[1] < dop.val:
                            need[key] = (dop.sem, dop.val)
                    for key, (sem, val) in need.items():
                        eng.wait_ge(sem, val)
                        known[key] = val
                    inst = op.fn(eng)
                    if op.dma:
                        inst.then_inc(op.sem, 16)
                    elif op.sig:
                        inst.then_inc(op.sem, 1)

            with nc.Block() as block:
                @block.sync
                def _(e):
                    emit("sp", e)

                @block.scalar
                def _(e):
                    emit("act", e)

                @block.vector
                def _(e):
                    emit("dve", e)

                @block.gpsimd
                def _(e):
                    emit("pool", e)

                @block.tensor
                def _(e):
                    emit("pe", e)
        self.stack.close()
        return (bar_sem, bar_val + 1)


def make_ident(ph, dt=BF16, n=128, name="ident"):
    f = ph.sb(name + "_f", [128, n], F32)
    o = ph.sb(name, [128, n], dt)
    b = Buf(name)

    def fn(g):
        g.memset(f[:], 0.0)
        g.affine_select(out=f[:], in_=f[:], compare_op=ALU.not_equal, fill=1.0, base=0, pattern=[[-1, n]], channel_multiplier=1)
        return g.tensor_copy(out=o[:], in_=f[:])

    ph.add("pool", fn, wr=[b])
    return o, b


class Cfg:
    def __init__(self, seqs):
        self.seqs = list(seqs)
        self.off = [0]
        for s in self.seqs:
            self.off.append(self.off[-1] + s)
        self.ntok = self.off[-1]
        self.smax = max(self.seqs)


def phase1(nc, cfg, T, bar):
    ph = Phase(nc, "p1", bar)
    NT = cfg.ntok
    NB = NT // 512
    ident, b_ident = make_ident(ph)
    xnT = ph.sb("xnT", [128, 8, NT], BF16)
    b_xnT = [Buf() for _ in range(NT // 128)]
    gcol = ph.sb("gcol", [128, 8], F32)
    b_g = Buf()
    ph.dma(gcol[:], T["norm_mix"].rearrange("(kc p) -> p kc", p=128), wr=[b_g], allow_slow_non_contiguous=True)
    banks = [ph.ps(f"bk{i}", [128, 512], F32) for i in range(6)]
    b_bank = [Buf() for _ in range(6)]
    pst = [ph.ps(f"pt{i}", [128, 1024], BF16) for i in range(2)]
    b_pst = [Buf(), Buf()]

    SM = cfg.smax
    Ct = ph.sb("Ct", [128, SM], F32)
    St = ph.sb("St", [128, SM], F32)
    b_rope = Buf()
    Pm = ph.sb("Pm", [128, 128], BF16)
    SEG = 512
    with_tmp = ph.sb("rtmp", [128, SEG], F32)
    posf = ph.sb("posf", [128, SEG], F32)
    cols = ph.sb("rcols", [128, 8], F32)
    pmf = ph.sb("pmf", [128, 128], F32)
    posi = ph.sb("posi", [128, SEG], I32)
    ki = ph.sb("rki", [128, SEG], I32)
    b_r0 = Buf()
    b_seg = Buf()
    ph.dma(cols[:], T["cst"][:, :], wr=[b_r0])
    ph.dma(pmf[:], T["pm"][:, :], wr=[b_r0])
    ph.add("pool", lambda g: g.tensor_copy(out=Pm[:], in_=pmf[:]), rd=[b_r0], wr=[b_r0])
    ph.add("act", lambda a: a.activation(out=cols[:, 7:8], in_=cols[:, 0:1], func=AF.Exp, scale=-math.log(500000.0) / 8.0), rd=[b_r0], wr=[b_r0])
    TWO_PI = 2.0 * math.pi
    for sg in range(SM // SEG):
        ss_ = slice(sg * SEG, (sg + 1) * SEG)

        def f_pos(g, sg=sg):
            g.iota(posi[:], pattern=[[1, SEG]], base=sg * SEG, channel_multiplier=0)
            return g.tensor_copy(out=posf[:], in_=posi[:])

        ph.add("pool", f_pos, wr=[b_seg])

        def reduce_fn(v, dst, shift):
            v.tensor_scalar(out=dst, in0=posf[:], scalar1=cols[:, 7:8], scalar2=shift, op0=ALU.mult, op1=ALU.add)
            v.tensor_scalar(out=ki[:], in0=dst, scalar1=1.0 / TWO_PI, scalar2=None, op0=ALU.mult)
            v.tensor_copy(out=with_tmp[:], in_=ki[:])
            v.scalar_tensor_tensor(out=dst, in0=with_tmp[:], scalar=-TWO_PI, in1=dst, op0=ALU.mult, op1=ALU.add)
            v.tensor_scalar(out=with_tmp[:], in0=dst, scalar1=math.pi, scalar2=TWO_PI, op0=ALU.is_gt, op1=ALU.mult)
            v.tensor_tensor(out=dst, in0=dst, in1=with_tmp[:], op=ALU.subtract)
            v.tensor_scalar(out=with_tmp[:], in0=dst, scalar1=-math.pi, scalar2=TWO_PI, op0=ALU.is_lt, op1=ALU.mult)
            return v.tensor_tensor(out=dst, in0=dst, in1=with_tmp[:], op=ALU.add)

        def rope_fn2(v, ss_=ss_):
            reduce_fn(v, St[:, ss_], 0.0)
            return reduce_fn(v, Ct[:, ss_], 0.5 * math.pi)

        ph.add("dve", rope_fn2, rd=[b_r0], wr=[b_seg, b_rope])

        def rope_fn3(a, ss_=ss_):
            a.activation(out=St[:, ss_], in_=St[:, ss_], func=AF.Sin)
            return a.activation(out=Ct[:, ss_], in_=Ct[:, ss_], func=AF.Sin)

        ph.add("act", rope_fn3, rd=[], wr=[b_rope])

        def rope_fn4(v, ss_=ss_):
            v.tensor_scalar(out=St[:, ss_], in0=St[:, ss_], scalar1=cols[:, 2:3], scalar2=None, op0=ALU.mult)
            v.tensor_scalar(out=Ct[:, ss_], in0=Ct[:, ss_], scalar1=-1.0, scalar2=None, op0=ALU.add)
            return v.tensor_scalar(out=Ct[:, ss_], in0=Ct[:, ss_], scalar1=cols[:, 1:2], scalar2=1.0, op0=ALU.mult, op1=ALU.add)

        ph.add("dve", rope_fn4, rd=[b_r0], wr=[b_rope])

    xt = [ph.sb(f"xt{i}", [128, D], F32) for i in range(2)]
    b_xt = [Buf(), Buf()]
    xnb = [ph.sb(f"xnb{i}", [128, D], BF16) for i in range(2)]
    b_xnb = [Buf(), Buf()]
    ss = ph.sb("ss", [128, 2, 2], F32)
    b_ss = [Buf(), Buf()]
    xin = T["x"]
    for t in range(NT // 128):
        s = t % 2
        ph.dma(xt[s][:], xin[t * 128:(t + 1) * 128, :], wr=[b_xt[s]])
        ph.add("act", lambda a, s=s: a.activation(out=xnb[s][:], in_=xt[s][:], func=AF.Square, accum_out=ss[:, s, 0:1]), rd=[b_xt[s]], wr=[b_xnb[s], b_ss[s]])

        ph.add("act", lambda a, s=s: a.activation(out=ss[:, s, 1:2], in_=ss[:, s, 0:1], func=AF.Sqrt, scale=1.0 / D, bias=cols[:, 4:5]), rd=[b_r0], wr=[b_ss[s]])
        ph.add("dve", lambda v, s=s: v.reciprocal(out=ss[:, s, 1:2], in_=ss[:, s, 1:2]), rd=[], wr=[b_ss[s]])
        ph.add("act", lambda a, s=s: a.activation(out=xnb[s][:], in_=xt[s][:], func=AF.Copy, scale=ss[:, s, 1:2]), rd=[b_xt[s], b_ss[s]], wr=[b_xnb[s]])

        def f_tr(pe, s=s):
            for kc in range(8):
                i = pe.transpose(out=pst[s][:, kc * 128:(kc + 1) * 128], in_=xnb[s][:, kc * 128:(kc + 1) * 128], identity=ident[:])
            return i

        ph.add("pe", f_tr, rd=[b_xnb[s], b_ident], wr=[b_pst[s]])
        ph.add("dve", lambda v, s=s, t=t: v.tensor_copy(out=xnT[:, :, t * 128:(t + 1) * 128], in_=pst[s][:, :].rearrange("p (k t) -> p k t", k=8)), rd=[], wr=[b_pst[s], b_xnT[t]])

    wf = [ph.sb(f"wf{i}", [128, 8, 128], F32) for i in range(2)]
    b_wf = [Buf(), Buf()]
    wb = [ph.sb(f"wb{i}", [128, 8, 128], BF16) for i in range(2)]
    b_wb = [Buf(), Buf()]
    stg = [ph.sb(f"stg{i}", [128, 512], F32) for i in range(3)]
    b_stg = [Buf() for _ in range(3)]
    stgb = [ph.sb(f"stgb{i}", [128, 512], BF16) for i in range(3)]
    b_stgb = [Buf() for _ in range(3)]
    qraw = [ph.sb("qraw0", [128, 512], BF16)] * 2
    b_qraw = [Buf()] * 2
    t1 = [ph.sb("t1_0", [128, 512], F32)] * 2
    b_t1 = [Buf()] * 2
    t2 = [ph.sb("t2_0", [128, 512], F32)] * 2
    b_t2 = [Buf()] * 2
    w_in = T["w_in"]
    chunks = []
    for c in range(27):
        chunks.append(("rw", c * 128, c * 128))
    for c in range(16):
        chunks.append(("qk", NRW + c * 128, c * 128))
    for c in range(16):
        chunks.append(("gt", NRW + 3072 + c * 128, c * 128))
    for c in range(8):
        chunks.append(("vv", NRW + 2048 + c * 128, c * 128))
    cnt = dict(bank=0, stg=0, stgb=0, q=0)
    blk_pos = []
    for si, S in enumerate(cfg.seqs):
        for j in range(S // 512):
            blk_pos.append(j * 512)
    for ci, (kind, c0, r0) in enumerate(chunks):
        s = ci % 2
        ph.dma(wf[s][:], w_in[:, c0:c0 + 128].rearrange("(kc p) c -> p kc c", p=128), wr=[b_wf[s]])
        ph.add("pool", lambda g, s=s: g.tensor_tensor(out=wb[s][:], in0=wf[s][:], in1=gcol[:].unsqueeze(2).to_broadcast([128, 8, 128]), op=ALU.mult), rd=[b_wf[s], b_g], wr=[b_wb[s]])
        for b in range(NB):
            bk = cnt["bank"] % 4
            cnt["bank"] += 1

            def f_mm(pe, s=s, b=b, bk=bk, kind=kind):
                if kind == "vv":
                    for j in range(4):
                        for kc in range(8):
                            i = pe.matmul(banks[bk][:, j * 128:(j + 1) * 128], lhsT=xnT[:, kc, b * 512 + j * 128:b * 512 + (j + 1) * 128], rhs=wb[s][:, kc, :], start=(kc == 0), stop=(kc == 7))
                    return i
                for kc in range(8):
                    i = pe.matmul(banks[bk][:, :], lhsT=wb[s][:, kc, :], rhs=xnT[:, kc, b * 512:(b + 1) * 512], start=(kc == 0), stop=(kc == 7))
                return i

            ph.add("pe", f_mm, rd=[b_wb[s]] + b_xnT[b * 4:(b + 1) * 4], wr=[b_bank[bk]])
            tok = slice(b * 512, (b + 1) * 512)
            if kind == "rw":
                g_ = cnt["stg"] % 3
                cnt["stg"] += 1
                ph.add("act", lambda a, bk=bk, g_=g_: a.activation(out=stg[g_][:], in_=banks[bk][:, :], func=AF.Copy), wr=[b_bank[bk], b_stg[g_]])
                ph.dma(T["pr"][r0:r0 + 128, tok], stg[g_][:], rd=[b_stg[g_]])
            elif kind == "vv":
                g_ = cnt["stgb"] % 3
                cnt["stgb"] += 1
                ph.add("act", lambda a, bk=bk, g_=g_: a.activation(out=stgb[g_][:], in_=banks[bk][:, :], func=AF.Copy), wr=[b_bank[bk], b_stgb[g_]])
                ph.dma(T["vv"][tok, r0:r0 + 128].rearrange("(j p) c -> p j c", p=128), stgb[g_][:].rearrange("p (j c) -> p j c", j=4), rd=[b_stgb[g_]])
            elif kind == "gt":
                g_ = cnt["stgb"] % 3
                cnt["stgb"] += 1
                ph.add("act", lambda a, bk=bk, g_=g_: a.activation(out=stgb[g_][:], in_=banks[bk][:, :], func=AF.Sigmoid), wr=[b_bank[bk], b_stgb[g_]])
                ph.dma(T["gt"][r0:r0 + 128, tok], stgb[g_][:], rd=[b_stgb[g_]])
            else:
                q_ = cnt["q"] % 2
                cnt["q"] += 1
                g_ = cnt["stgb"] % 3
                cnt["stgb"] += 1
                p0 = blk_pos[b]
                ph.add("act", lambda a, bk=bk, q_=q_: a.activation(out=qraw[q_][:], in_=banks[bk][:, :], func=AF.Copy), wr=[b_bank[bk], b_qraw[q_]])
                ph.add("dve", lambda v, bk=bk, q_=q_, p0=p0: v.tensor_tensor(out=t1[q_][:], in0=banks[bk][:, :], in1=Ct[:, p0:p0 + 512], op=ALU.mult), rd=[b_rope], wr=[b_bank[bk], b_t1[q_]])
                pb = 4 + (cnt["q"] % 2)
                ph.add("pe", lambda pe, q_=q_, pb=pb: pe.matmul(banks[pb][:, :], lhsT=Pm[:], rhs=qraw[q_][:], start=True, stop=True), rd=[b_qraw[q_], b_r0], wr=[b_bank[pb]])
                ph.add("dve", lambda v, pb=pb, q_=q_, p0=p0: v.tensor_tensor(out=t2[q_][:], in0=banks[pb][:, :], in1=St[:, p0:p0 + 512], op=ALU.mult), rd=[b_rope], wr=[b_bank[pb], b_t2[q_]])
                ph.add("pool", lambda g, q_=q_, g_=g_: g.tensor_tensor(out=stgb[g_][:], in0=t1[q_][:], in1=t2[q_][:], op=ALU.add), rd=[b_t1[q_], b_t2[q_]], wr=[b_stgb[g_]])
                ph.dma(T["qk"][r0:r0 + 128, tok], stgb[g_][:], rd=[b_stgb[g_]])

    return ph.run()


def bcast_rows(ap_1d, n):
    return ap_1d.partition_broadcast(128)


def phase3(nc, cfg, T, bar):
    ph = Phase(nc, "p3", bar)
    SM = cfg.smax
    lam_init = 0.8 - 0.6 * math.exp(-0.3 * 0)
    lv = ph.sb("lv", [128, 4, 64], F32)
    b_lv = Buf()
    for i, nm in enumerate(("lambda_q1", "lambda_k1", "lambda_q2", "lambda_k2")):
        ph.dma(lv[:, i, :], T[nm].partition_broadcast(128), wr=[b_lv])
    cc = ph.sb("cc", [128, 8], F32)
    b_cc = Buf()
    ph.dma(cc[:, 3:4], T["subln_w"].rearrange("(p o) -> p o", o=1), wr=[b_cc])
    lt = ph.sb("ltmp", [128, 2, 64], F32)
    ones = ph.sb("ones", [128, 128], BF16)

    def f_c(v):
        v.memset(ones[:], 1.0)
        v.memset(cc[:, 4:5], 1e-5)
        v.tensor_tensor(out=lt[:, 0, :], in0=lv[:, 0, :], in1=lv[:, 1, :], op=ALU.mult)
        v.tensor_tensor(out=lt[:, 1, :], in0=lv[:, 2, :], in1=lv[:, 3, :], op=ALU.mult)
        v.reduce_sum(out=cc[:, 0:2], in_=lt[:], axis=AX.X)
        return v.tensor_scalar(out=cc[:, 3:4], in0=cc[:, 3:4], scalar1=1.0 - lam_init, scalar2=None, op0=ALU.mult)

    ph.add("dve", f_c, rd=[b_lv], wr=[b_cc])
    ph.add("act", lambda a: a.activation(out=cc[:, 0:2], in_=cc[:, 0:2], func=AF.Exp), wr=[b_cc])

    def f_c2(v):
        v.tensor_tensor(out=cc[:, 2:3], in0=cc[:, 1:2], in1=cc[:, 0:1], op=ALU.subtract)
        return v.tensor_scalar(out=cc[:, 2:3], in0=cc[:, 2:3], scalar1=-lam_init, scalar2=None, op0=ALU.add)

    ph.add("dve", f_c2, wr=[b_cc])

    qT = [ph.sb(f"qT{i}", [128, SM], BF16) for i in range(2)]
    kT = [ph.sb(f"kT{i}", [128, SM], BF16) for i in range(2)]
    Vt = [ph.sb(f"Vt{i}", [128, SM // 128, 128], BF16) for i in range(2)]
    b_in = [Buf(), Buf()]
    bS = [ph.ps(f"bS{i}", [128, 512], F32) for i in range(4)]
    b_bS = [Buf() for _ in range(4)]
    bO = [ph.ps(f"bO{i}", [128, 512], F32) for i in range(4)]
    b_bO = [Buf() for _ in range(4)]
    Pt = [ph.sb(f"Pt{i}", [128, 512], BF16) for i in range(4)]
    b_Pt = [Buf() for _ in range(4)]
    rz = [ph.sb(f"rz{i}", [128, 512], F32) for i in range(2)]
    oo = [ph.sb(f"oo{i}", [128, 512], F32) for i in range(2)]
    sq = ph.sb("sq", [128, 512], BF16)
    rs = ph.sb("rs", [128, 512], F32)
    yb = [ph.sb(f"yb{i}", [128, 512], BF16) for i in range(2)]
    b_ep = Buf()
    b_yb = [Buf(), Buf()]
    it = 0
    ne = 0
    for si, S in enumerate(cfg.seqs):
        t0 = cfg.off[si]
        for h in range(8):
            u = it % 2
            it += 1
            ph.dma(qT[u][:, 0:S], T["qk"][h * 128:(h + 1) * 128, t0:t0 + S], wr=[b_in[u]])
            ph.dma(kT[u][:, 0:S], T["qk"][1024 + h * 128:1024 + (h + 1) * 128, t0:t0 + S], wr=[b_in[u]])
            ph.dma(Vt[u][:, 0:S // 128, :], T["vv"][t0:t0 + S, h * 128:(h + 1) * 128].rearrange("(kt p) v -> p kt v", p=128), wr=[b_in[u]])
            nkt = S // 128
            for qb in range(S // 512):
                qs = slice(qb * 512, (qb + 1) * 512)
                for kt in range(nkt):
                    w = (kt % 2) * 2
                    ks = slice(kt * 128, (kt + 1) * 128)

                    def f_s(pe, u=u, w=w, ks=ks, qs=qs):
                        pe.matmul(bS[w][:, :], lhsT=kT[u][0:64, ks], rhs=qT[u][0:64, qs], start=True, stop=True)
                        return pe.matmul(bS[w + 1][:, :], lhsT=kT[u][64:128, ks], rhs=qT[u][64:128, qs], start=True, stop=True)

                    ph.add("pe", f_s, rd=[b_in[u]], wr=[b_bS[w], b_bS[w + 1]])
                    ph.add("act", lambda a, w=w: a.activation(out=Pt[w][:], in_=bS[w][:, :], func=AF.Exp, scale=0.125), wr=[b_bS[w], b_Pt[w]])
                    ph.add("act", lambda a, w=w: a.activation(out=Pt[w + 1][:], in_=bS[w + 1][:, :], func=AF.Exp, scale=0.125), wr=[b_bS[w + 1], b_Pt[w + 1]])

                    def f_pv(pe, u=u, w=w, kt=kt, nkt=nkt):
                        st, sp_ = (kt == 0), (kt == nkt - 1)
                        pe.matmul(bO[0][:, :], lhsT=Vt[u][:, kt, :], rhs=Pt[w][:], start=st, stop=sp_)
                        pe.matmul(bO[2][:, :], lhsT=ones[:], rhs=Pt[w][:], start=st, stop=sp_)
                        pe.matmul(bO[1][:, :], lhsT=Vt[u][:, kt, :], rhs=Pt[w + 1][:], start=st, stop=sp_)
                        return pe.matmul(bO[3][:, :], lhsT=ones[:], rhs=Pt[w + 1][:], start=st, stop=sp_)

                    ph.add("pe", f_pv, rd=[b_in[u], b_Pt[w], b_Pt[w + 1], b_cc], wr=b_bO)
                e = ne % 2
                ne += 1

                def f_e1(v):
                    v.reciprocal(out=rz[0][:], in_=bO[2][:, :])
                    v.reciprocal(out=rz[1][:], in_=bO[3][:, :])
                    v.tensor_tensor(out=oo[0][:], in0=bO[0][:, :], in1=rz[0][:], op=ALU.mult)
                    v.tensor_tensor(out=oo[1][:], in0=bO[1][:, :], in1=rz[1][:], op=ALU.mult)
                    return v.scalar_tensor_tensor(out=oo[0][:], in0=oo[1][:], scalar=cc[:, 2:3], in1=oo[0][:], op0=ALU.mult, op1=ALU.add)

                ph.add("dve", f_e1, rd=[b_cc], wr=b_bO + [b_ep])
                ph.add("act", lambda a: a.activation(out=sq[:], in_=oo[0][:], func=AF.Square), wr=[b_ep])
                ph.add("pe", lambda pe: pe.matmul(bS[0][:, :], lhsT=ones[:], rhs=sq[:], start=True, stop=True), rd=[b_ep], wr=[b_bS[0]])
                ph.add("act", lambda a: a.activation(out=rs[:], in_=bS[0][:, :], func=AF.Sqrt, scale=1.0 / 128.0, bias=cc[:, 4:5]), rd=[b_cc], wr=[b_bS[0], b_ep])
                ph.add("dve", lambda v: v.reciprocal(out=rs[:], in_=rs[:]), wr=[b_ep])
                ph.add("dve", lambda g, e=e: g.scalar_tensor_tensor(out=yb[e][:], in0=oo[0][:], scalar=cc[:, 3:4], in1=rs[:], op0=ALU.mult, op1=ALU.mult), rd=[b_ep, b_cc], wr=[b_yb[e]])
                ph.dma(T["ybT"][h * 128:(h + 1) * 128, t0 + qb * 512:t0 + (qb + 1) * 512], yb[e][:], rd=[b_yb[e]])
    return ph.run()


def load_w_bf16(ph, dst, b_dst, w_ap, nk, ncols, stage, b_stage, scale_col=None, b_scale=None, eng="pool"):
    step = 512
    kst = stage.shape[1]
    for c0 in range(0, ncols, step):
        for k0 in range(0, nk, kst):
            k1 = min(nk, k0 + kst)
            ph.dma(stage[:, 0:k1 - k0, :], w_ap[k0 * 128:k1 * 128, c0:c0 + step].rearrange("(kc p) c -> p kc c", p=128), wr=[b_stage])
            if scale_col is None:
                ph.add(eng, lambda g, k0=k0, k1=k1, c0=c0: g.tensor_copy(out=dst[:, k0:k1, c0:c0 + step], in_=stage[:, 0:k1 - k0, :]), rd=[b_stage], wr=[b_dst])
            else:
                ph.add(eng, lambda g, k0=k0, k1=k1, c0=c0: g.tensor_tensor(out=dst[:, k0:k1, c0:c0 + step], in0=stage[:, 0:k1 - k0, :], in1=scale_col[:, k0:k1].unsqueeze(2).to_broadcast([128, k1 - k0, step]), op=ALU.mult), rd=[b_stage, b_scale], wr=[b_dst])


def phase4a(nc, cfg, T, bar):
    ph = Phase(nc, "p4a", bar)
    NT = cfg.ntok
    ident, b_ident = make_ident(ph)
    stage = ph.sb("stage", [128, 8, 512], F32)
    b_stage = Buf()
    W = {}
    bW = {}
    for nm in ("proj_a", "proj_b", "w_out"):
        W[nm] = ph.sb("w_" + nm, [128, 8, 1024], BF16)
        bW[nm] = Buf()
        load_w_bf16(ph, W[nm], bW[nm], T[nm], 8, 1024, stage, b_stage)
    cst = ph.sb("cst", [128, 1], F32)
    b_cst = Buf()
    ph.add("pool", lambda g: g.memset(cst[:], 1e-6), wr=[b_cst])
    banks = [ph.ps(f"bk{i}", [128, 512], F32) for i in range(6)]
    b_bank = [Buf() for _ in range(6)]
    pst = ph.ps("pt", [128, 1024], BF16)
    b_pst = Buf()
    ya = ph.sb("ya", [128, 8, 512], BF16)
    ybb = ph.sb("ybb", [128, 8, 512], BF16)
    ga = ph.sb("ga", [128, 8, 512], BF16)
    gb = ph.sb("gb", [128, 8, 512], BF16)
    b_ld = Buf()
    mg = ph.sb("mg", [128, 8, 512], BF16)
    b_mg = Buf()
    m1 = [ph.sb(f"m1_{i}", [128, 512], F32) for i in range(2)]
    m2 = [ph.sb(f"m2_{i}", [128, 512], F32) for i in range(2)]
    b_m = [Buf(), Buf()]
    xt = [ph.sb(f"xt{i}", [128, D], F32) for i in range(2)]
    b_xt = [Buf(), Buf()]
    hh = [ph.sb(f"hh{i}", [128, D], F32) for i in range(2)]
    b_hh = [Buf(), Buf()]
    hb = [ph.sb(f"hb{i}", [128, D], BF16) for i in range(2)]
    b_hb = [Buf(), Buf()]
    hT = [ph.sb(f"hT{i}", [128, 8, 128], BF16) for i in range(2)]
    b_hT = [Buf(), Buf()]
    junk = ph.sb("junk", [128, D], BF16)
    b_junk = Buf()
    ss = ph.sb("ss", [128, 2, 2], F32)
    b_ss = [Buf(), Buf()]
    nb = 0
    nm_ = 0
    nt_ = 0
    for b in range(NT // 512):
        tok = slice(b * 512, (b + 1) * 512)
        for dst, src, r0 in ((ya, "yaT", 0), (ybb, "ybT", 0), (ga, "gt", 0), (gb, "gt", 1024)):
            ph.dma(dst[:], T[src][r0:r0 + 1024, tok].rearrange("(kc p) t -> p kc t", p=128), wr=[b_ld])
        for oc in range(8):
            ba, bb = nb % 6, (nb + 1) % 6
            nb += 2

            def f_ab(pe, oc=oc, ba=ba, bb=bb):
                for kc in range(8):
                    pe.matmul(banks[ba][:, :], lhsT=W["proj_a"][:, kc, oc * 128:(oc + 1) * 128], rhs=ya[:, kc, :], start=(kc == 0), stop=(kc == 7))
                for kc in range(8):
                    i = pe.matmul(banks[bb][:, :], lhsT=W["proj_b"][:, kc, oc * 128:(oc + 1) * 128], rhs=ybb[:, kc, :], start=(kc == 0), stop=(kc == 7))
                return i

            ph.add("pe", f_ab, rd=[b_ld, bW["proj_a"], bW["proj_b"]], wr=[b_bank[ba], b_bank[bb]])
            m = nm_ % 2
            nm_ += 1

            def f_m(v, oc=oc, ba=ba, bb=bb, m=m):
                v.tensor_tensor(out=m1[m][:], in0=banks[ba][:, :], in1=ga[:, oc, :], op=ALU.mult)
                return v.tensor_tensor(out=m2[m][:], in0=banks[bb][:, :], in1=gb[:, oc, :], op=ALU.mult)

            ph.add("dve", f_m, rd=[b_ld], wr=[b_bank[ba], b_bank[bb], b_m[m]])
            ph.add("pool", lambda g, oc=oc, m=m: g.tensor_tensor(out=mg[:, oc, :], in0=m1[m][:], in1=m2[m][:], op=ALU.add), rd=[b_m[m]], wr=[b_mg])
        for ti in range(4):
            t = b * 4 + ti
            s_ = nt_ % 2
            nt_ += 1
            ph.dma(xt[s_][:], T["x"][t * 128:(t + 1) * 128, :], wr=[b_xt[s_]])
            ba, bb = nb % 6, (nb + 1) % 6
            nb += 2

            def f_o(pe, ti=ti, ba=ba, bb=bb):
                for hf, bk in ((0, ba), (1, bb)):
                    for kc in range(8):
                        i = pe.matmul(banks[bk][:, :], lhsT=mg[:, kc, ti * 128:(ti + 1) * 128], rhs=W["w_out"][:, kc, hf * 512:(hf + 1) * 512], start=(kc == 0), stop=(kc == 7))
                return i

            ph.add("pe", f_o, rd=[b_mg, bW["w_out"]], wr=[b_bank[ba], b_bank[bb]])

            def f_h(v, s_=s_, ba=ba, bb=bb):
                v.tensor_tensor(out=hh[s_][:, 0:512], in0=banks[ba][:, :], in1=xt[s_][:, 0:512], op=ALU.add)
                return v.tensor_tensor(out=hh[s_][:, 512:1024], in0=banks[bb][:, :], in1=xt[s_][:, 512:1024], op=ALU.add)

            ph.add("dve", f_h, rd=[b_xt[s_]], wr=[b_bank[ba], b_bank[bb], b_hh[s_]])
            ph.dma(T["hs"][t * 128:(t + 1) * 128, :], hh[s_][:], rd=[b_hh[s_]])
            ph.add("act", lambda a, s_=s_: a.activation(out=junk[:], in_=hh[s_][:], func=AF.Square, accum_out=ss[:, s_, 0:1]), rd=[b_hh[s_]], wr=[b_junk, b_ss[s_]])
            ph.add("act", lambda a, s_=s_: a.activation(out=ss[:, s_, 1:2], in_=ss[:, s_, 0:1], func=AF.Sqrt, scale=1.0 / D, bias=cst[:, 0:1]), rd=[b_cst], wr=[b_ss[s_]])
            ph.add("dve", lambda v, s_=s_: v.reciprocal(out=ss[:, s_, 1:2], in_=ss[:, s_, 1:2]), wr=[b_ss[s_]])
            ph.add("act", lambda a, s_=s_: a.activation(out=hb[s_][:], in_=hh[s_][:], func=AF.Copy, scale=ss[:, s_, 1:2]), rd=[b_hh[s_], b_ss[s_]], wr=[b_hb[s_]])

            def f_tr(pe, s_=s_):
                for kc in range(8):
                    i = pe.transpose(out=pst[:, kc * 128:(kc + 1) * 128], in_=hb[s_][:, kc * 128:(kc + 1) * 128], identity=ident[:])
                return i

            ph.add("pe", f_tr, rd=[b_hb[s_], b_ident], wr=[b_pst])
            ph.add("dve", lambda v, s_=s_: v.tensor_copy(out=hT[s_][:], in_=pst[:, :].rearrange("p (k t) -> p k t", k=8)), wr=[b_pst, b_hT[s_]])
            ph.dma(T["hnT"][:, t * 128:(t + 1) * 128].rearrange("(kc p) t -> p kc t", p=128), hT[s_][:], rd=[b_hT[s_]])
    return ph.run()


def phase4b(nc, cfg, T, bar):
    ph = Phase(nc, "p4b", bar)
    NT = cfg.ntok
    stage = ph.sb("stage", [128, 4, 512], F32)
    b_stage = Buf()
    gcol = ph.sb("gcol", [128, 8], F32)
    b_g = Buf()
    ph.dma(gcol[:], T["norm_mlp"].rearrange("(kc p) -> p kc", p=128), wr=[b_g], allow_slow_non_contiguous=True)
    gfin = ph.sb("gfin", [128, D], F32)
    b_gf = Buf()
    ph.dma(gfin[:], T["norm_final"].partition_broadcast(128), wr=[b_gf])
    w1 = ph.sb("w1", [128, 8, DFF], BF16)
    b_w1 = Buf()
    w2 = ph.sb("w2", [128, 32, D], BF16)
    b_w2 = Buf()
    load_w_bf16(ph, w1, b_w1, T["w_mlp_in"], 8, DFF, stage, b_stage, scale_col=gcol, b_scale=b_g)
    load_w_bf16(ph, w2, b_w2, T["w_mlp_out"], 32, D, stage, b_stage)
    cst = ph.sb("cst", [128, 1], F32)
    b_cst = Buf()
    ph.add("pool", lambda g: g.memset(cst[:], 1e-6), wr=[b_cst])
    banks = [ph.ps(f"bk{i}", [128, 512], F32) for i in range(8)]
    b_bank = [Buf() for _ in range(8)]
    hn = [ph.sb("hn0", [128, 8, 512], BF16)] * 2
    b_hn = [Buf()] * 2
    hid = ph.sb("hid", [128, 32, 512], BF16)
    b_hid = [Buf() for _ in range(32)]
    rl = [ph.sb(f"rl{i}", [128, 512], F32) for i in range(2)]
    b_rl = [Buf(), Buf()]
    ht = [ph.sb(f"ht{i}", [128, D], F32) for i in range(2)]
    b_ht = [Buf(), Buf()]
    oo = [ph.sb(f"oo{i}", [128, D], F32) for i in range(2)]
    b_oo = [Buf(), Buf()]
    junk = ph.sb("junk", [128, D], BF16)
    b_junk = Buf()
    ss = ph.sb("ss", [128, 2, 2], F32)
    b_ss = [Buf(), Buf()]
    nb = 0
    nr = 0
    nt_ = 0
    for b in range(NT // 512):
        u = b % 2
        tok = slice(b * 512, (b + 1) * 512)
        ph.dma(hn[u][:], T["hnT"][:, tok].rearrange("(kc p) t -> p kc t", p=128), wr=[b_hn[u]])
        for fc in range(32):
            bk = nb % 8
            nb += 1

            def f_h(pe, fc=fc, bk=bk, u=u):
                for kc in range(8):
                    i = pe.matmul(banks[bk][:, :], lhsT=w1[:, kc, fc * 128:(fc + 1) * 128], rhs=hn[u][:, kc, :], start=(kc == 0), stop=(kc == 7))
                return i

            ph.add("pe", f_h, rd=[b_w1, b_hn[u]], wr=[b_bank[bk]])
            r_ = nr % 2
            nr += 1
            ph.add("act", lambda a, bk=bk, r_=r_: a.activation(out=rl[r_][:], in_=banks[bk][:, :], func=AF.Relu), wr=[b_bank[bk], b_rl[r_]])
            ph.add("pool", lambda g, fc=fc, r_=r_: g.tensor_tensor(out=hid[:, fc, :], in0=rl[r_][:], in1=rl[r_][:], op=ALU.mult), rd=[b_rl[r_]], wr=[b_hid[fc]])
        for ti in range(4):
            t = b * 4 + ti
            s_ = nt_ % 2
            nt_ += 1
            ph.dma(ht[s_][:], T["hs"][t * 128:(t + 1) * 128, :], wr=[b_ht[s_]])
            ba, bb = nb % 8, (nb + 1) % 8
            nb += 2

            def f_o(pe, ti=ti, ba=ba, bb=bb):
                for hf, bk in ((0, ba), (1, bb)):
                    for fc in range(32):
                        i = pe.matmul(banks[bk][:, :], lhsT=hid[:, fc, ti * 128:(ti + 1) * 128], rhs=w2[:, fc, hf * 512:(hf + 1) * 512], start=(fc == 0), stop=(fc == 31))
                return i

            ph.add("pe", f_o, rd=[b_w2] + b_hid, wr=[b_bank[ba], b_bank[bb]])

            def f_r(v, s_=s_, ba=ba, bb=bb):
                v.tensor_tensor(out=oo[s_][:, 0:512], in0=banks[ba][:, :], in1=ht[s_][:, 0:512], op=ALU.add)
                return v.tensor_tensor(out=oo[s_][:, 512:1024], in0=banks[bb][:, :], in1=ht[s_][:, 512:1024], op=ALU.add)

            ph.add("dve", f_r, rd=[b_ht[s_]], wr=[b_bank[ba], b_bank[bb], b_oo[s_]])
            ph.add("act", lambda a, s_=s_: a.activation(out=junk[:], in_=oo[s_][:], func=AF.Square, accum_out=ss[:, s_, 0:1]), rd=[b_oo[s_]], wr=[b_junk, b_ss[s_]])
            ph.add("act", lambda a, s_=s_: a.activation(out=ss[:, s_, 1:2], in_=ss[:, s_, 0:1], func=AF.Sqrt, scale=1.0 / D, bias=cst[:, 0:1]), rd=[b_cst], wr=[b_ss[s_]])
            ph.add("dve", lambda v, s_=s_: v.reciprocal(out=ss[:, s_, 1:2], in_=ss[:, s_, 1:2]), wr=[b_ss[s_]])
            ph.add("dve", lambda g, s_=s_: g.scalar_tensor_tensor(out=oo[s_][:], in0=oo[s_][:], scalar=ss[:, s_, 1:2], in1=gfin[:], op0=ALU.mult, op1=ALU.mult), rd=[b_ss[s_], b_gf], wr=[b_oo[s_]])
            ph.dma(T["y"][t * 128:(t + 1) * 128, :], oo[s_][:], rd=[b_oo[s_]])
    return ph.run()


def phase2(nc, cfg, T, bar):
    ph = Phase(nc, "p2", bar)
    NT = cfg.ntok
    C0 = -math.exp(-0.5)
    sb, add = ph.sb, ph.add
    identb, b_identb = make_ident(ph, BF16, name="idb")
    identf = ph.sb("idf", [128, 128], F32)
    MK = sb("MK", [128, 4, 128], F32)
    TR = sb("TR", [128, 4, 128], BF16)
    BD = sb("BD", [128, 128], BF16)
    onesr = sb("onesr", [1, 128], BF16)
    mA = [sb(f"mA{d}", [128, 512], F32) for d in range(2)]
    mL = [sb(f"mL{d}", [128, 256], F32) for d in range(2)]
    trc = [sb(f"trc{d}", [128, 384], BF16) for d in range(2)]
    b_cst = Buf()

    def f_masks(g):
        g.memset(MK[:], 1.0)
        g.affine_select(out=MK[:, 0, :], in_=MK[:, 0, :], compare_op=ALU.is_gt, fill=0.0, base=0, pattern=[[1, 128]], channel_multiplier=-1)
        g.affine_select(out=MK[:, 1, :], in_=MK[:, 1, :], compare_op=ALU.is_ge, fill=0.0, base=0, pattern=[[1, 128]], channel_multiplier=-1)
        g.affine_select(out=MK[:, 2, :], in_=MK[:, 2, :], compare_op=ALU.is_gt, fill=0.0, base=0, pattern=[[-1, 128]], channel_multiplier=1)
        g.affine_select(out=MK[:, 3, :], in_=MK[:, 3, :], compare_op=ALU.is_ge, fill=0.0, base=0, pattern=[[-1, 128]], channel_multiplier=1)
        g.tensor_scalar(out=TR[:], in0=MK[:], scalar1=C0, scalar2=None, op0=ALU.mult)
        g.memset(identf[:], 0.0)
        g.affine_select(out=identf[:], in_=identf[:], compare_op=ALU.not_equal, fill=1.0, base=0, pattern=[[-1, 128]], channel_multiplier=1)
        g.memset(BD[:], 0.0)
        g.memset(BD[0:64, 0:64], 1.0)
        g.memset(BD[64:128, 64:128], 1.0)
        g.memset(onesr[:], 1.0)
        for d in range(2):
            st, inc, lm = (0, 1, 2) if d == 0 else (2, 3, 0)
            for q in range(2):
                g.tensor_copy(out=mA[d][:, q * 256:q * 256 + 128], in_=MK[:, st, :])
                g.tensor_copy(out=mA[d][:, q * 256 + 128:q * 256 + 256], in_=MK[:, inc, :])
                g.tensor_copy(out=mL[d][:, q * 128:(q + 1) * 128], in_=MK[:, lm, :])
            g.tensor_copy(out=trc[d][:, 0:128], in_=TR[:, inc, :])
            g.tensor_copy(out=trc[d][:, 128:256], in_=TR[:, st, :])
            i = g.tensor_copy(out=trc[d][:, 256:384], in_=TR[:, lm, :])
        return i

    add("pool", f_masks, wr=[b_cst])
    par = sb("par", [128, 8, 8], F32)
    b_par = Buf()
    for i, ap_ in enumerate((T["k_k"], T["k_a"], T["k_a"], T["r_k"], T["ln_x_w"], T["ln_x_b"], T["a0"][0:1024], T["a0"][1024:2048])):
        ph.dma(par[:, i, :], ap_.rearrange("(c p) -> p c", p=128), wr=[b_par], allow_slow_non_contiguous=True)
    add("pool", lambda g: g.tensor_scalar(out=par[:, 2, :], in0=par[:, 2, :], scalar1=-1.0, scalar2=1.0, op0=ALU.mult, op1=ALU.add), wr=[b_par])
    mu = sb("mu", [128, 27], F32)
    ph.dma(mu[:], T["mu_shift"].rearrange("(c p) -> p c", p=128), wr=[b_par], allow_slow_non_contiguous=True)
    wst = sb("wst", [128, 1024], F32)
    b_wst = Buf()
    wup = sb("wup", [128, 1024], BF16)
    aup = sb("aup", [128, 1024], BF16)
    gup = sb("gup", [128, 1024], BF16)
    b_w = Buf()
    for dst, src in ((wup, T["w_lora_up"]), (aup, T["a_lora_up"]), (gup, T["g_lora_up"])):
        ph.dma(wst[:], src[:, :], wr=[b_wst])
        add("pool", lambda g, dst=dst: g.tensor_copy(out=dst[:], in_=wst[:]), rd=[b_wst], wr=[b_w])
    w0h = sb("w0h", [1, 2, 1024], BF16)
    w0l = sb("w0l", [1, 2, 1024], BF16)
    w0t = sb("w0t", [1, 1024], F32)
    for d_ in range(2):
        ph.dma(wst[0:1, :], T["w0"][d_ * 1024:(d_ + 1) * 1024].rearrange("(o c) -> o c", o=1), wr=[b_wst])

        def f_w0(g, d_=d_):
            g.tensor_copy(out=w0h[0:1, d_, :], in_=wst[0:1, :])
            g.tensor_copy(out=w0t[0:1, :], in_=w0h[0:1, d_, :])
            g.tensor_tensor(out=w0t[0:1, :], in0=wst[0:1, :], in1=w0t[0:1, :], op=ALU.subtract)
            return g.tensor_copy(out=w0l[0:1, d_, :], in_=w0t[0:1, :])

        add("pool", f_w0, rd=[], wr=[b_wst, b_w])
    eps = sb("eps", [128, 2], F32)
    add("pool", lambda g: (g.memset(eps[:, 0:1], 64e-5), g.memset(eps[:, 1:2], 0.0))[1], wr=[b_cst])

    bk = [ph.ps(f"bk{i}", [128, 512], F32) for i in range(6)]
    b_bk = [Buf() for _ in range(6)]
    bt = [ph.ps(f"bt{i}", [128, 1024], BF16) for i in range(2)]
    b_bt = [Buf(), Buf()]
    cnt = dict(b=0, t=0)

    def nb():
        i = cnt["b"] % 6
        cnt["b"] += 1
        return i

    def ntb():
        i = cnt["t"] % 2
        cnt["t"] += 1
        return i

    P = sb("P", [128, 27, 130], F32)
    SH = sb("SH", [128, 27, 128], F32)
    b_P, b_SH = Buf(), Buf()
    f4 = lambda nm: sb(nm, [128, 8, 128], F32)
    h4 = lambda nm: sb(nm, [128, 8, 128], BF16)
    av, kkr, kk, bb_, kd, Ei, En, Ee, Es, tmp1 = [f4(n) for n in ("av", "kkr", "kk", "bb", "kd", "Ei", "En", "Ee", "Es", "tmp1")]
    tmp2 = tmp1
    B = {n: Buf(n) for n in ("av", "kkr", "kk", "bb", "kd", "E", "tmp1", "tmp2", "tl", "lab", "sig", "sq", "rn", "AR", "bt", "kt", "Bg", "Kg", "vb",
                             "Atok", "Bgtok", "Kgtok", "Vtok", "AM", "KM", "LL", "PL0", "PL1", "PA0", "PA1", "X0", "X1", "Zb", "AhT", "What", "Ub", "S", "Sbf",
                             "Y", "Yp", "gv", "sg", "bon", "bonp", "yn", "st", "fin", "yo", "Stmp")}
    tl = sb("tl", [128, 128], BF16)
    lab = sb("lab", [128, 128], BF16)
    sgb = sb("sgb", [128, 128], BF16)
    sig = sb("sig", [128, 1024], BF16)
    sq = h4("sq")
    AR = sb("AR", [128, 8, 256], BF16)
    btT, ktT, BgT, KgT, vb = [h4(n) for n in ("btT", "ktT", "BgT", "KgT", "vbb")]
    Atok, Bgtok, Kgtok, Vtok, Zb = [sb(n, [128, 1024], BF16) for n in ("Atok", "Bgtok", "Kgtok", "Vtok", "Zb")]
    AM = sb("AM", [128, 16, 256], BF16)
    KM = sb("KM", [128, 16, 256], BF16)
    LL = sb("LL", [128, 16, 128], BF16)
    PL = [LL, sb("PL1", [128, 16, 128], BF16), sb("PL2", [128, 16, 128], BF16)]
    PA = [sb("PA1", [128, 16, 128], BF16), sb("PA2", [128, 16, 128], BF16)]
    X = [sb("X0", [128, 16, 128], BF16), sb("X1", [128, 16, 128], BF16)]
    AhT = sb("AhT", [128, 8, 128], BF16)
    What = sb("What", [128, 1024], F32)
    Ub = sb("Ub", [128, 1024], BF16)
    Sf = sb("Sf", [128, 8, 64], F32)
    Stmp = sb("Stmp", [128, 8, 64], F32)
    Sbf = sb("Sbf", [128, 8, 64], BF16)
    Y = sb("Y", [128, 1024], F32)
    Yp = sb("Yp", [128, 1024], F32)
    bon = f4("bon")
    bonp = kkr
    rn = bon
    B["bonp"] = B["kkr"]
    B["rn"] = B["bon"]
    st = sb("st", [128, 16, 4], F32)
    yn2 = Yp
    B["yn"] = B["Yp"]
    B["tmp2"] = B["tmp1"]
    yn = yn2[:].rearrange("p (h v) -> p h v", v=64)
    fin = av
    B["fin"] = B["av"]
    yo = h4("yo")

    def chunk_step(si, d, ci):
        S = cfg.seqs[si]
        base = cfg.off[si]
        nch = S // 128
        t0 = base + ci * 128
        first = (ci == 0) if d == 0 else (ci == nch - 1)
        final = (d == 1)
        lo = 1 if ci == 0 else 0
        hi = 129 if ci == nch - 1 else 130
        if lo == 1:
            add("pool", lambda g: g.memset(P[:, :, 0:1], 0.0), wr=[b_P])
        if hi == 129:
            add("pool", lambda g: g.memset(P[:, :, 129:130], 0.0), wr=[b_P])
        for c0_, c1_ in ((0, 9), (9, 18), (18, 27)):
            ph.dma(P[:, c0_:c1_, lo:hi], T["pr"][c0_ * 128:c1_ * 128, t0 - 1 + lo:t0 - 1 + hi].rearrange("(c p) t -> p c t", p=128), wr=[b_P])

        def f_shift(g):
            g.tensor_tensor(out=SH[:], in0=P[:, :, 0:128], in1=P[:, :, 2:130], op=ALU.add)
            g.tensor_scalar(out=SH[:], in0=SH[:], scalar1=0.5, scalar2=None, op0=ALU.mult)
            g.tensor_tensor(out=SH[:], in0=SH[:], in1=P[:, :, 1:129], op=ALU.subtract)
            g.tensor_tensor(out=SH[:], in0=SH[:], in1=mu[:].unsqueeze(2).to_broadcast([128, 27, 128]), op=ALU.mult)
            return g.tensor_tensor(out=SH[:], in0=SH[:], in1=P[:, :, 1:129], op=ALU.add)

        add("pool", f_shift, rd=[b_P, b_par], wr=[b_SH])
        R, Kx, Vx = SH[:, 0:8, :], SH[:, 8:16, :], SH[:, 16:24, :]
        dsl = slice(d * 64, (d + 1) * 64)
        add("act", lambda a: a.activation(out=tl[:], in_=SH[:, 24, :], func=AF.Tanh), rd=[b_SH], wr=[B["tl"]])
        add("act", lambda a: a.activation(out=lab[:], in_=SH[:, 25, :], func=AF.Copy), rd=[b_SH], wr=[B["lab"]])
        w1_, w2_ = nb(), nb()

        def f_lw(pe):
            for hf, b_ in ((0, w1_), (1, w2_)):
                cs = slice(hf * 512, (hf + 1) * 512)
                pe.matmul(bk[b_][:, :], lhsT=tl[dsl, :], rhs=wup[dsl, cs], start=True, stop=False)
                pe.matmul(bk[b_][:, :], lhsT=onesr[0:1, :], rhs=w0h[0:1, d, cs], start=False, stop=False)
                i = pe.matmul(bk[b_][:, :], lhsT=onesr[0:1, :], rhs=w0l[0:1, d, cs], start=False, stop=True)
            return i

        add("pe", f_lw, rd=[B["tl"], b_w, b_cst], wr=[b_bk[w1_], b_bk[w2_]])
        add("act", lambda a: a.activation(out=sig[:, 0:512], in_=bk[w1_][:, :], func=AF.Sigmoid), wr=[b_bk[w1_], B["sig"]])
        add("act", lambda a: a.activation(out=sig[:, 512:1024], in_=bk[w2_][:, :], func=AF.Sigmoid), wr=[b_bk[w2_], B["sig"]])
        a1_, a2_ = nb(), nb()

        def f_la(pe):
            for c in range(8):
                b_ = a1_ if c < 4 else a2_
                i = pe.matmul(bk[b_][:, (c % 4) * 128:(c % 4 + 1) * 128], lhsT=aup[dsl, c * 128:(c + 1) * 128], rhs=lab[dsl, :], start=True, stop=True)
            return i

        add("pe", f_la, rd=[B["lab"], b_w], wr=[b_bk[a1_], b_bk[a2_]])
        for c in range(8):
            b_ = a1_ if c < 4 else a2_
            add("act", lambda a, c=c, b_=b_: a.activation(out=av[:, c, :], in_=bk[b_][:, (c % 4) * 128:(c % 4 + 1) * 128], func=AF.Sigmoid, bias=par[:, 6 + d, c:c + 1]), rd=[b_par], wr=[b_bk[b_], B["av"]])
        for hf in range(2):
            cb = [nb(), nb(), nb()]

            def f_cum(pe, hf=hf, cb=cb):
                for cc_ in range(4):
                    c = hf * 4 + cc_
                    for x in range(3):
                        i = pe.matmul(bk[cb[x]][:, cc_ * 128:(cc_ + 1) * 128], lhsT=sig[:, c * 128:(c + 1) * 128], rhs=trc[d][:, x * 128:(x + 1) * 128], start=True, stop=True)
                return i

            add("pe", f_cum, rd=[B["sig"], b_cst], wr=[b_bk[i] for i in cb])
            hs_ = slice(hf * 4, hf * 4 + 4)
            v3 = lambda b_: bk[b_][:, :].rearrange("p (c t) -> p c t", c=4)
            add("act", lambda a, cb=cb, hs_=hs_: (a.activation(out=Ei[:, hs_, :], in_=v3(cb[0]), func=AF.Exp), a.activation(out=En[:, hs_, :], in_=v3(cb[0]), func=AF.Exp, scale=-1.0))[1], wr=[b_bk[cb[0]], B["E"]])
            add("act", lambda a, cb=cb, hs_=hs_: a.activation(out=Ee[:, hs_, :], in_=v3(cb[1]), func=AF.Exp), wr=[b_bk[cb[1]], B["E"]])
            add("act", lambda a, cb=cb, hs_=hs_: a.activation(out=Es[:, hs_, :], in_=v3(cb[2]), func=AF.Exp), wr=[b_bk[cb[2]], B["E"]])
        bc = lambda i: par[:, i, :].unsqueeze(2).to_broadcast([128, 8, 128])
        add("dve", lambda v: v.tensor_tensor(out=kkr[:], in0=Kx, in1=bc(0), op=ALU.mult), rd=[b_SH, b_par], wr=[B["kkr"]])
        add("act", lambda a: a.activation(out=sq[:], in_=kkr[:], func=AF.Square), rd=[B["kkr"]], wr=[B["sq"]])
        n1_, n2_ = nb(), nb()

        def f_nrm(pe):
            sqf = sq[:].rearrange("p c t -> p (c t)")
            pe.matmul(bk[n1_][:, :], lhsT=BD[:], rhs=sqf[:, 0:512], start=True, stop=True)
            return pe.matmul(bk[n2_][:, :], lhsT=BD[:], rhs=sqf[:, 512:1024], start=True, stop=True)

        add("pe", f_nrm, rd=[B["sq"], b_cst], wr=[b_bk[n1_], b_bk[n2_]])
        add("act", lambda a: a.activation(out=rn[:, 0:4, :], in_=bk[n1_][:, :].rearrange("p (c t) -> p c t", c=4), func=AF.Sqrt), wr=[b_bk[n1_], B["rn"]])
        add("act", lambda a: a.activation(out=rn[:, 4:8, :], in_=bk[n2_][:, :].rearrange("p (c t) -> p c t", c=4), func=AF.Sqrt), wr=[b_bk[n2_], B["rn"]])

        def f_kk(v):
            v.tensor_scalar(out=rn[:], in0=rn[:], scalar1=1e-12, scalar2=None, op0=ALU.max)
            v.reciprocal(out=rn[:], in_=rn[:])
            return v.tensor_tensor(out=kk[:], in0=kkr[:], in1=rn[:], op=ALU.mult)

        add("dve", f_kk, rd=[B["kkr"]], wr=[B["rn"], B["kk"]])

        def f_kd(g):
            g.tensor_tensor(out=tmp1[:], in0=av[:], in1=bc(1), op=ALU.mult)
            g.tensor_tensor(out=tmp1[:], in0=tmp1[:], in1=bc(2), op=ALU.add)
            g.tensor_tensor(out=kd[:], in0=tmp1[:], in1=Kx, op=ALU.mult)
            return g.tensor_tensor(out=bb_[:], in0=kk[:], in1=av[:], op=ALU.mult)

        add("pool", f_kd, rd=[B["av"], b_par, b_SH, B["kk"]], wr=[B["tmp1"], B["kd"], B["bb"]])

        def f_sc1(v):
            v.scalar_tensor_tensor(out=AR[:, :, 0:128], in0=kk[:], scalar=-1.0, in1=Ee[:], op0=ALU.mult, op1=ALU.mult)
            v.tensor_tensor(out=AR[:, :, 128:256], in0=R, in1=Ei[:], op=ALU.mult)
            return v.tensor_tensor(out=btT[:], in0=bb_[:], in1=En[:], op=ALU.mult)

        add("dve", f_sc1, rd=[B["kk"], B["E"], b_SH, B["bb"]], wr=[B["AR"], B["bt"]])

        def f_sc2(g):
            g.tensor_tensor(out=ktT[:], in0=kd[:], in1=En[:], op=ALU.mult)
            g.tensor_tensor(out=BgT[:], in0=bb_[:], in1=Es[:], op=ALU.mult)
            return g.tensor_tensor(out=KgT[:], in0=kd[:], in1=Es[:], op=ALU.mult)

        add("pool", f_sc2, rd=[B["kd"], B["E"], B["bb"]], wr=[B["kt"], B["Bg"], B["Kg"]])
        add("act", lambda a: a.activation(out=vb[:], in_=Vx, func=AF.Copy), rd=[b_SH], wr=[B["vb"]])
        add("pool", lambda g: (g.tensor_tensor(out=tmp2[:], in0=R, in1=bc(3), op=ALU.mult), g.tensor_tensor(out=sq[:], in0=tmp2[:], in1=kd[:], op=ALU.mult))[1], rd=[b_SH, b_par, B["kd"]], wr=[B["tmp2"], B["sq"]])
        o1_, o2_ = nb(), nb()

        def f_bon(pe):
            sqf = sq[:].rearrange("p c t -> p (c t)")
            pe.matmul(bk[o1_][:, :], lhsT=BD[:], rhs=sqf[:, 0:512], start=True, stop=True)
            return pe.matmul(bk[o2_][:, :], lhsT=BD[:], rhs=sqf[:, 512:1024], start=True, stop=True)

        add("pe", f_bon, rd=[B["sq"], b_cst], wr=[b_bk[o1_], b_bk[o2_]])
        add("dve", lambda v: v.tensor_tensor(out=bon[:, 0:4, :], in0=bk[o1_][:, :].rearrange("p (c t) -> p c t", c=4), in1=SH[:, 16:20, :], op=ALU.mult), rd=[b_SH], wr=[b_bk[o1_], B["bon"]])
        add("dve", lambda v: v.tensor_tensor(out=bon[:, 4:8, :], in0=bk[o2_][:, :].rearrange("p (c t) -> p c t", c=4), in1=SH[:, 20:24, :], op=ALU.mult), rd=[b_SH], wr=[b_bk[o2_], B["bon"]])
        for src, srcb, dst, dstb in ((AR, "AR", Atok, "Atok"), (BgT, "Bg", Bgtok, "Bgtok"), (KgT, "Kg", Kgtok, "Kgtok"), (vb, "vb", Vtok, "Vtok")):
            tb = ntb()

            def f_tr(pe, src=src, tb=tb):
                for c in range(8):
                    i = pe.transpose(out=bt[tb][:, c * 128:(c + 1) * 128], in_=src[:, c, 0:128], identity=identb[:])
                return i

            add("pe", f_tr, rd=[B[srcb], b_identb], wr=[b_bt[tb]])
            add("act", lambda a, dst=dst, tb=tb: a.activation(out=dst[:], in_=bt[tb][:, :], func=AF.Copy), wr=[b_bt[tb], B[dstb]])
        for g4 in range(8):
            hp0 = (g4 // 2) * 2
            par_ = g4 % 2
            hs2 = [2 * hp0 + par_, 2 * (hp0 + 1) + par_]
            ba, bk_, bl = nb(), nb(), nb()
            ps_ = slice(par_ * 64, par_ * 64 + 64)

            def f_sc(pe, hs2=hs2, ba=ba, bk_=bk_, bl=bl, ps_=ps_):
                for q, h in enumerate(hs2):
                    c = h // 2
                    pe.matmul(bk[ba][:, q * 256:(q + 1) * 256], lhsT=btT[ps_, c, :], rhs=AR[ps_, c, :], start=True, stop=True)
                    pe.matmul(bk[bk_][:, q * 256:(q + 1) * 256], lhsT=ktT[ps_, c, :], rhs=AR[ps_, c, :], start=True, stop=True)
                    i = pe.matmul(bk[bl][:, q * 128:(q + 1) * 128], lhsT=AR[ps_, c, 0:128], rhs=btT[ps_, c, :], start=True, stop=True)
                return i

            add("pe", f_sc, rd=[B["AR"], B["bt"], B["kt"]], wr=[b_bk[ba], b_bk[bk_], b_bk[bl]])

            def f_ev(v, ba=ba, bk_=bk_, bl=bl, hs2=hs2):
                for q, h in enumerate(hs2):
                    v.tensor_tensor(out=AM[:, h, :], in0=bk[ba][:, q * 256:(q + 1) * 256], in1=mA[d][:, 0:256], op=ALU.mult)
                    v.tensor_tensor(out=KM[:, h, :], in0=bk[bk_][:, q * 256:(q + 1) * 256], in1=mA[d][:, 0:256], op=ALU.mult)
                    i = v.tensor_tensor(out=LL[:, h, :], in0=bk[bl][:, q * 128:(q + 1) * 128], in1=mL[d][:, 0:128], op=ALU.mult)
                return i

            add("dve", f_ev, rd=[b_cst], wr=[b_bk[ba], b_bk[bk_], b_bk[bl], B["AM"], B["KM"], B["LL"]])
        add("pool", lambda g: g.tensor_tensor(out=X[0][:], in0=AM[:, :, 0:128], in1=identb[:].unsqueeze(1).to_broadcast([128, 16, 128]), op=ALU.add), rd=[B["AM"], b_identb], wr=[B["X0"]])
        add("pool", lambda g: g.tensor_copy(out=PA[0][:], in_=AM[:, :, 0:128]), rd=[B["AM"]], wr=[B["PA0"]])
        Lcur, Acur, Xc = 0, 0, 0
        Lb = ["LL", "PL0", "PL1"]
        for lvl in range(6):
            Ln = 1 + (lvl % 2)
            An = 1 - Acur
            Xn = 1 - Xc
            for g4 in range(4):
                hsl = slice(g4 * 4, g4 * 4 + 4)
                b1, b2, b3 = nb(), nb(), nb()

                def f_sq(pe, g4=g4, b1=b1, b2=b2, Lc=Lcur, Ac=Acur, lvl=lvl):
                    for q in range(4):
                        h = g4 * 4 + q
                        i = pe.matmul(bk[b1][:, q * 128:(q + 1) * 128], lhsT=PA[Ac][:, h, :], rhs=PL[Lc][:, h, :], start=True, stop=True)
                    if lvl < 5:
                        for q in range(4):
                            h = g4 * 4 + q
                            i = pe.matmul(bk[b2][:, q * 128:(q + 1) * 128], lhsT=PL[Lc][:, h, :], rhs=PA[Ac][:, h, :], start=True, stop=True)
                    return i

                add("pe", f_sq, rd=[B[Lb[Lcur]], B["PA%d" % Acur]], wr=[b_bk[b1], b_bk[b2]])
                add("act", lambda a, b1=b1, hsl=hsl, Ln=Ln: a.activation(out=PL[Ln][:, hsl, :], in_=bk[b1][:, :].rearrange("p (h t) -> p h t", h=4), func=AF.Copy), wr=[b_bk[b1], B[Lb[Ln]]])
                if lvl < 5:
                    add("act", lambda a, b2=b2, hsl=hsl, An=An: a.activation(out=PA[An][:, hsl, :], in_=bk[b2][:, :].rearrange("p (h t) -> p h t", h=4), func=AF.Copy), wr=[b_bk[b2], B["PA%d" % An]])

                def f_x(pe, g4=g4, b3=b3, Ln=Ln, Xc=Xc):
                    for q in range(4):
                        h = g4 * 4 + q
                        i = pe.matmul(bk[b3][:, q * 128:(q + 1) * 128], lhsT=PL[Ln][:, h, :], rhs=X[Xc][:, h, :], start=True, stop=True)
                    return i

                add("pe", f_x, rd=[B[Lb[Ln]], B["X%d" % Xc]], wr=[b_bk[b3]])
                add("dve", lambda v, b3=b3, hsl=hsl, Xc=Xc, Xn=Xn: v.tensor_tensor(out=X[Xn][:, hsl, :], in0=bk[b3][:, :].rearrange("p (h t) -> p h t", h=4), in1=X[Xc][:, hsl, :], op=ALU.add), rd=[B["X%d" % Xc]], wr=[b_bk[b3], B["X%d" % Xn]])
            Lcur, Acur, Xc = Ln, An, Xn
        XT = X[Xc]
        bX = B["X%d" % Xc]
        z1, z2 = nb(), nb()

        def f_z(pe):
            for h in range(16):
                b_ = z1 if h < 8 else z2
                i = pe.matmul(bk[b_][:, (h % 8) * 64:(h % 8 + 1) * 64], lhsT=KM[:, h, 0:128], rhs=Vtok[:, h * 64:(h + 1) * 64], start=True, stop=True)
            return i

        add("pe", f_z, rd=[B["KM"], B["Vtok"]], wr=[b_bk[z1], b_bk[z2]])
        add("act", lambda a: a.activation(out=Zb[:, 0:512], in_=bk[z1][:, :], func=AF.Copy), wr=[b_bk[z1], B["Zb"]])
        add("act", lambda a: a.activation(out=Zb[:, 512:1024], in_=bk[z2][:, :], func=AF.Copy), wr=[b_bk[z2], B["Zb"]])
        q1, q2 = nb(), nb()

        def f_w(pe):
            for h in range(16):
                b_ = q1 if h < 8 else q2
                i = pe.matmul(bk[b_][:, (h % 8) * 64:(h % 8 + 1) * 64], lhsT=XT[:, h, :], rhs=Zb[:, h * 64:(h + 1) * 64], start=True, stop=True)
            return i

        add("pe", f_w, rd=[bX, B["Zb"]], wr=[b_bk[q1], b_bk[q2]])
        add("act", lambda a: a.activation(out=What[:, 0:512], in_=bk[q1][:, :], func=AF.Copy), wr=[b_bk[q1], B["What"]])
        add("act", lambda a: a.activation(out=What[:, 512:1024], in_=bk[q2][:, :], func=AF.Copy), wr=[b_bk[q2], B["What"]])
        for pg in range(4):
            b_ = nb()

            def f_ah(pe, pg=pg, b_=b_):
                for q in range(2):
                    hp = pg * 2 + q
                    i = pe.matmul(bk[b_][:, q * 256:(q + 1) * 256], lhsT=Atok[:, hp * 128:(hp + 1) * 128], rhs=XT[:].rearrange("p h t -> p (h t)")[:, hp * 256:(hp + 1) * 256], start=True, stop=True)
                return i

            add("pe", f_ah, rd=[B["Atok"], bX], wr=[b_bk[b_]])

            def f_ahe(v, pg=pg, b_=b_):
                v4 = bk[b_][:, :].rearrange("p (q s t) -> p q s t", q=2, s=2)
                v.tensor_copy(out=AhT[0:64, pg * 2:pg * 2 + 2, :], in_=v4[0:64, :, 0, :])
                return v.tensor_copy(out=AhT[64:128, pg * 2:pg * 2 + 2, :], in_=v4[64:128, :, 1, :])

            add("dve", f_ahe, wr=[b_bk[b_], B["AhT"]])
        if first:
            add("pool", lambda g: (g.memset(Sf[:], 0.0), g.memset(Sbf[:], 0.0))[1], wr=[B["S"], B["Sbf"]])
        ue, uo = nb(), nb()

        def f_u(pe):
            for h in range(16):
                hp, p_ = h // 2, h % 2
                i = pe.matmul(bk[(ue, uo)[p_]][:, hp * 64:(hp + 1) * 64], lhsT=AhT[p_ * 64:(p_ + 1) * 64, hp, :], rhs=Sbf[p_ * 64:(p_ + 1) * 64, hp, :], start=True, stop=True)
            return i

        add("pe", f_u, rd=[B["AhT"], B["Sbf"]], wr=[b_bk[ue], b_bk[uo]])
        Ub4 = Ub[:, :].rearrange("p (hp s v) -> p s hp v", s=2, v=64)
        Wh4 = What[:, :].rearrange("p (hp s v) -> p s hp v", s=2, v=64)
        add("dve", lambda v: (v.tensor_tensor(out=Ub4[:, 0], in0=bk[ue][:, :].rearrange("p (hp v) -> p hp v", v=64), in1=Wh4[:, 0], op=ALU.add),
                              v.tensor_tensor(out=Ub4[:, 1], in0=bk[uo][:, :].rearrange("p (hp v) -> p hp v", v=64), in1=Wh4[:, 1], op=ALU.add))[1], rd=[B["What"]], wr=[b_bk[ue], b_bk[uo], B["Ub"]])
        ye, yo_ = nb(), nb()

        def f_y(pe):
            for h in range(16):
                hp, p_ = h // 2, h % 2
                o = bk[(ye, yo_)[p_]][:, hp * 64:(hp + 1) * 64]
                pe.matmul(o, lhsT=AR[p_ * 64:(p_ + 1) * 64, hp, 128:256], rhs=Sbf[p_ * 64:(p_ + 1) * 64, hp, :], start=True, stop=False)
                pe.matmul(o, lhsT=AM[:, h, 128:256], rhs=Ub[:, h * 64:(h + 1) * 64], start=False, stop=False)
                i = pe.matmul(o, lhsT=KM[:, h, 128:256], rhs=Vtok[:, h * 64:(h + 1) * 64], start=False, stop=True)
            return i

        add("pe", f_y, rd=[B["AR"], B["Sbf"], B["AM"], B["Ub"], B["KM"], B["Vtok"]], wr=[b_bk[ye], b_bk[yo_]])
        Y4 = Y[:, :].rearrange("p (hp s v) -> p s hp v", s=2, v=64)
        if final:
            ph.dma(Yp[:], T["ytmp"][t0:t0 + 128, :], wr=[B["Yp"]])
            Yp4 = Yp[:, :].rearrange("p (hp s v) -> p s hp v", s=2, v=64)
            add("dve", lambda v: (v.tensor_tensor(out=Y4[:, 0], in0=bk[ye][:, :].rearrange("p (hp v) -> p hp v", v=64), in1=Yp4[:, 0], op=ALU.add),
                                  v.tensor_tensor(out=Y4[:, 1], in0=bk[yo_][:, :].rearrange("p (hp v) -> p hp v", v=64), in1=Yp4[:, 1], op=ALU.add))[1], rd=[B["Yp"]], wr=[b_bk[ye], b_bk[yo_], B["Y"]])
        else:
            add("act", lambda a: (a.activation(out=Y4[:, 0], in_=bk[ye][:, :].rearrange("p (hp v) -> p hp v", v=64), func=AF.Copy),
                                  a.activation(out=Y4[:, 1], in_=bk[yo_][:, :].rearrange("p (hp v) -> p hp v", v=64), func=AF.Copy))[1], wr=[b_bk[ye], b_bk[yo_], B["Y"]])
            ph.dma(T["ytmp"][t0:t0 + 128, :], Y[:], rd=[B["Y"]])
            ph.dma(T["bon"][:, t0:t0 + 128].rearrange("(c p) t -> p c t", p=128), bon[:], rd=[B["bon"]])
        s1, s2 = nb(), nb()

        def f_s(pe):
            for hp in range(8):
                o = bk[s1 if hp < 4 else s2][:, (hp % 4) * 128:(hp % 4 + 1) * 128]
                pe.matmul(o, lhsT=Bgtok[:, hp * 128:(hp + 1) * 128], rhs=Ub[:, hp * 128:(hp + 1) * 128], start=True, stop=False)
                i = pe.matmul(o, lhsT=Kgtok[:, hp * 128:(hp + 1) * 128], rhs=Vtok[:, hp * 128:(hp + 1) * 128], start=False, stop=True)
            return i

        add("pe", f_s, rd=[B["Bgtok"], B["Ub"], B["Kgtok"], B["Vtok"]], wr=[b_bk[s1], b_bk[s2]])
        ec = 127 if d == 0 else 0
        add("pool", lambda g: g.tensor_tensor(out=Stmp[:], in0=Sf[:], in1=Ei[:, :, ec:ec + 1].to_broadcast([128, 8, 64]), op=ALU.mult), rd=[B["E"]], wr=[B["S"], B["Stmp"]])

        def f_su(v):
            for q, b_ in ((0, s1), (1, s2)):
                v4 = bk[b_][:, :].rearrange("p (hp s v) -> p hp s v", hp=4, s=2)
                v.tensor_tensor(out=Sf[0:64, q * 4:q * 4 + 4, :], in0=v4[0:64, :, 0, :], in1=Stmp[0:64, q * 4:q * 4 + 4, :], op=ALU.add)
                i = v.tensor_tensor(out=Sf[64:128, q * 4:q * 4 + 4, :], in0=v4[64:128, :, 1, :], in1=Stmp[64:128, q * 4:q * 4 + 4, :], op=ALU.add)
            return i

        add("dve", f_su, rd=[B["Stmp"]], wr=[b_bk[s1], b_bk[s2], B["S"]])
        add("act", lambda a: a.activation(out=Sbf[:], in_=Sf[:], func=AF.Copy), rd=[B["S"]], wr=[B["Sbf"]])
        if not final:
            return
        ph.dma(bonp[:], T["bon"][:, t0:t0 + 128].rearrange("(c p) t -> p c t", p=128), wr=[B["bonp"]])
        Y3 = Y[:, :].rearrange("p (h v) -> p h v", v=64)

        def f_gn(v):
            v.reduce_sum(out=st[:, :, 0], in_=Y3, axis=AX.X)
            v.tensor_tensor(out=yn, in0=Y3, in1=Y3, op=ALU.mult)
            v.reduce_sum(out=st[:, :, 1], in_=yn, axis=AX.X)
            v.tensor_scalar(out=st[:, :, 0], in0=st[:, :, 0], scalar1=1.0 / 64, scalar2=None, op0=ALU.mult)
            v.tensor_tensor(out=st[:, :, 2], in0=st[:, :, 0], in1=st[:, :, 0], op=ALU.mult)
            return v.scalar_tensor_tensor(out=st[:, :, 1], in0=st[:, :, 1], scalar=1.0 / 64, in1=st[:, :, 2], op0=ALU.mult, op1=ALU.subtract)

        add("dve", f_gn, rd=[B["Y"]], wr=[B["st"], B["yn"]])
        add("act", lambda a: a.activation(out=st[:, :, 3], in_=st[:, :, 1], func=AF.Sqrt, bias=eps[:, 0:1]), rd=[b_cst], wr=[B["st"]])

        def f_gn2(v):
            v.reciprocal(out=st[:, :, 3], in_=st[:, :, 3])
            v.tensor_tensor(out=yn, in0=Y3, in1=st[:, :, 0:1].to_broadcast([128, 16, 64]), op=ALU.subtract)
            return v.tensor_tensor(out=yn, in0=yn, in1=st[:, :, 3:4].to_broadcast([128, 16, 64]), op=ALU.mult)

        add("dve", f_gn2, rd=[B["Y"]], wr=[B["st"], B["yn"]])
        f1, f2 = nb(), nb()

        def f_trf(pe):
            for c in range(8):
                i = pe.transpose(out=bk[f1 if c < 4 else f2][:, (c % 4) * 128:(c % 4 + 1) * 128], in_=yn2[:, c * 128:(c + 1) * 128], identity=identf[:])
            return i

        add("pe", f_trf, rd=[B["yn"], b_cst], wr=[b_bk[f1], b_bk[f2]])
        for c in range(8):
            b_ = f1 if c < 4 else f2
            add("act", lambda a, c=c, b_=b_: a.activation(out=fin[:, c, :], in_=bk[b_][:, (c % 4) * 128:(c % 4 + 1) * 128], func=AF.Identity, scale=par[:, 4, c:c + 1], bias=par[:, 5, c:c + 1]), rd=[b_par], wr=[b_bk[b_], B["fin"]])
        add("act", lambda a: a.activation(out=sgb[:], in_=SH[:, 26, :], func=AF.Sigmoid), rd=[b_SH], wr=[B["sg"]])
        g1, g2 = nb(), nb()

        def f_g(pe):
            for c in range(8):
                i = pe.matmul(bk[g1 if c < 4 else g2][:, (c % 4) * 128:(c % 4 + 1) * 128], lhsT=gup[:, c * 128:(c + 1) * 128], rhs=sgb[:], start=True, stop=True)
            return i

        add("pe", f_g, rd=[B["sg"], b_w], wr=[b_bk[g1], b_bk[g2]])

        def f_fin(g):
            g.tensor_tensor(out=fin[:], in0=fin[:], in1=bon[:], op=ALU.add)
            return g.tensor_tensor(out=fin[:], in0=fin[:], in1=bonp[:], op=ALU.add)

        add("pool", f_fin, rd=[B["bon"], B["bonp"]], wr=[B["fin"]])
        add("dve", lambda v: (v.tensor_tensor(out=yo[:, 0:4, :], in0=bk[g1][:, :].rearrange("p (c t) -> p c t", c=4), in1=fin[:, 0:4, :], op=ALU.mult),
                              v.tensor_tensor(out=yo[:, 4:8, :], in0=bk[g2][:, :].rearrange("p (c t) -> p c t", c=4), in1=fin[:, 4:8, :], op=ALU.mult))[1], rd=[B["fin"]], wr=[b_bk[g1], b_bk[g2], B["yo"]])
        ph.dma(T["yaT"][:, t0:t0 + 128].rearrange("(c p) t -> p c t", p=128), yo[:], rd=[B["yo"]])

    for si, S in enumerate(cfg.seqs):
        nch = S // 128
        for ci in range(nch):
            chunk_step(si, 0, ci)
        for ci in reversed(range(nch)):
            chunk_step(si, 1, ci)
    return ph.run()

def build(cfg, debug=False, phases="1234"):
    nc = bass.Bass("TRN2", target_bir_lowering=False)
    NT = cfg.ntok
    T = {}

    def inp(name, shape):
        T[name] = nc.dram_tensor(name, list(shape), F32, kind="ExternalInput").ap()

    inp("x", [NT, D])
    inp("w_in", [D, NIN])
    for nm, n in (("norm_mix", D), ("norm_mlp", D), ("norm_final", D), ("mu_shift", NRW), ("w0", 2048), ("a0", 2048), ("k_k", D), ("k_a", D),
                  ("r_k", D), ("ln_x_w", D), ("ln_x_b", D), ("lambda_q1", 64), ("lambda_k1", 64), ("lambda_q2", 64), ("lambda_k2", 64), ("subln_w", 128)):
        inp(nm, [n])
    for nm in ("w_lora_up", "a_lora_up", "g_lora_up"):
        inp(nm, [128, 1024])
    for nm in ("proj_a", "proj_b", "w_out"):
        inp(nm, [D, D])
    inp("w_mlp_in", [D, DFF])
    inp("w_mlp_out", [DFF, D])
    inp("cst", [128, 8])
    inp("pm", [128, 128])
    kind = "ExternalOutput" if debug else "Internal"
    for nm, shp, dt in (("pr", [NRW, NT], F32), ("qk", [2048, NT], BF16), ("gt", [2048, NT], BF16), ("vv", [NT, 1024], BF16),
                        ("ytmp", [NT, D], F32), ("bon", [D, NT], F32), ("yaT", [D, NT], BF16), ("ybT", [D, NT], BF16),
                        ("hs", [NT, D], F32), ("hnT", [D, NT], BF16)):
        T[nm] = nc.dram_tensor(nm, shp, dt, kind=kind).ap()
    T["y"] = nc.dram_tensor("y", [NT, D], F32, kind="ExternalOutput").ap()
    with ExitStack() as gst, nc.semaphore("bar") as bar_sem:
        GST[0] = ({e: gst.enter_context(nc.semaphore("s_" + e)) for e in ENGS}, [gst.enter_context(nc.semaphore(f"d{i}")) for i in range(NDS)])
        bar = (bar_sem, 0)
        if "1" in phases:
            bar = phase1(nc, cfg, T, bar)
        if "2" in phases:
            bar = phase2(nc, cfg, T, bar)
        if "3" in phases:
            bar = phase3(nc, cfg, T, bar)
        if "4" in phases or "a" in phases:
            bar = phase4a(nc, cfg, T, bar)
        if "4" in phases or "b" in phases:
            bar = phase4b(nc, cfg, T, bar)
    return nc


def core_inputs(inputs, xs):
    cst, pm = host_consts()
    f = lambda k: np.ascontiguousarray(np.asarray(inputs[k], np.float32))
    m = {"x": np.ascontiguousarray(np.concatenate(xs, axis=0)), "w_in": f("w_in")[0], "cst": cst, "pm": pm}
    for nm in ("norm_mix", "norm_mlp", "mu_shift", "w0", "a0", "k_k", "k_a", "r_k", "ln_x_w", "ln_x_b", "lambda_q1", "lambda_k1", "lambda_q2", "lambda_k2", "subln_w"):
        m[nm] = f(nm)[0].reshape(-1)
    m["norm_final"] = f("norm_final").reshape(-1)
    m["w_lora_up"] = f("w_lora_up")[0].reshape(128, 1024)
    m["a_lora_up"] = f("a_lora_up")[0].reshape(128, 1024)
    m["g_lora_up"] = f("g_lora_up")[0].reshape(128, 1024)
    for nm in ("proj_a", "proj_b", "w_out", "w_mlp_in", "w_mlp_out"):
        m[nm] = f(nm)[0]
    return m


_NC_CACHE = {}


def kernel(**inputs):
    xp = np.asarray(inputs["x_prompt"], np.float32)
    xs = np.asarray(inputs["x_sample"], np.float32)
    n = 8
    cfg = Cfg([xp.shape[1], xp.shape[1], xs.shape[1]])
    key = tuple(cfg.seqs)
    if key not in _NC_CACHE:
        _NC_CACHE[key] = build(cfg)
    nc = _NC_CACHE[key]
    in_maps = [core_inputs(inputs, [xp[2 * c], xp[2 * c + 1], xs[c]]) for c in range(n)]
    res = run_bass_kernel_spmd(nc, in_maps, core_ids=list(range(n)))
    yp = np.empty_like(xp)
    ys = np.empty_like(xs)
    S1 = xp.shape[1]
    for c in range(n):
        y = res.results[c]["y"]
        yp[2 * c] = y[0:S1]
        yp[2 * c + 1] = y[S1:2 * S1]
        ys[c] = y[2 * S1:]
    return (yp, ys)


def host_consts():
    p = np.arange(128)
    cst = np.zeros((128, 8), np.float32)
    cst[:, 0] = p % 8
    m = ((p % 64) < 16).astype(np.float32)
    cst[:, 1] = m
    cst[:, 2] = np.where((p % 16) >= 8, 1.0, -1.0) * m
    cst[:, 3] = -math.pi
    cst[:, 4] = 1e-6
    pm = np.zeros((128, 128), np.float32)
    for mm in range(128):
        if (mm % 64) < 16:
            k = mm + 8 if (mm % 16) < 8 else mm - 8
            pm[k, mm] = 1.0
    return cst, pm
```

```python
import math
from contextlib import ExitStack
import numpy as np
import concourse.bass as bass
import concourse.mybir as mybir
from concourse.bass_utils import run_bass_kernel_spmd

F32 = mybir.dt.float32
BF16 = mybir.dt.bfloat16
I32 = mybir.dt.int32
AF = mybir.ActivationFunctionType
ALU = mybir.AluOpType
AX = mybir.AxisListType

D = 1024
NRW = 3456
NIN = 8576
DFF = 4096
ENGS = ("pe", "act", "dve", "pool", "sp")
NDS = 24


GST = [None]
NOINTER = False
SPLITBANKS = False
DBG_PICK = None
LAT = 180.0


class Buf:
    __slots__ = ("w", "r", "name")

    def __init__(self, name=""):
        self.w = None
        self.r = []
        self.name = name


class Op:
    __slots__ = ("fn", "deps", "dma", "sig", "val", "sem")

    def __init__(self, fn, deps, dma):
        self.fn = fn
        self.deps = deps
        self.dma = dma
        self.sig = False
        self.val = 0
        self.sem = None


class _Rec:
    def __init__(self):
        self.calls = []

    def __getattr__(self, name):
        def f(*a, **k):
            self.calls.append((name, a, k))
            return None

        return f


class Phase:
    def __init__(self, nc, name, bar):
        self.nc = nc
        self.name = name
        self.bar = bar
        self.ops = {e: [] for e in ENGS}
        self.stack = ExitStack()
        self.ndma = 0
        self.dma_ids = []
        self.nps = 0
        self.sched = False
        self.est_end = {}
        self.eng_free = {}

    def sb(self, name, shape, dt):
        return self.stack.enter_context(self.nc.sbuf_tensor(self.name + "_" + name, list(shape), dt))

    def ps(self, name, shape, dt=F32):
        return self.stack.enter_context(self.nc.psum_tensor(self.name + "_" + name, list(shape), dt))

    cap = None

    def _add_cap(self, eng, fn, rd, wr, dma):
        return self.add(eng, fn, rd, wr, dma)

    def add(self, eng, fn, rd=(), wr=(), dma=False):
        if self.cap is not None:
            self.cap.append((eng, fn, tuple(rd), tuple(wr), dma))
            return None
        if eng != "pe" and not dma:
            rec = _Rec()
            fn(rec)
            me = None
            for name, a, k in rec.calls:
                me = self._add1(eng, (lambda e, name=name, a=a, k=k: getattr(e, name)(*a, **k)), rd, wr, False)
            return me
        return self._add1(eng, fn, rd, wr, dma)

    def est_dur(self, eng, fn, dma):
        rec = _Rec()
        try:
            fn(rec)
        except Exception:
            return 500.0
        tot = 0.0
        for name, a, k in rec.calls:
            o = k.get("out", a[0] if a else None)
            try:
                n = 1
                for d_ in o.shape[1:]:
                    n *= d_
            except Exception:
                n = 128
            if dma:
                tot += 2000.0 + n * 0.5
            elif eng == "pe":
                tot += (100.0 if name == "transpose" else max(64, n) / 2.4 + 8.0)
            elif eng == "act":
                tot += 230.0 + 0.83 * n
            elif eng == "dve":
                tot += 130.0 + 0.95 * n
            else:
                tot += 250.0 + 6.5 * n
        return tot

    def peek_ready(self, eng, rd, wr):
        t = self.eng_free.get(eng, 0.0)
        for b in list(rd) + list(wr):
            if b.w is not None:
                t = max(t, self.est_end.get(b.w, 0.0) + (0.0 if b.w[0] == eng == "pe" else LAT))
        for b in wr:
            for r_ in b.r:
                t = max(t, self.est_end.get(r_, 0.0) + LAT)
        return t

    def _add1(self, eng, fn, rd=(), wr=(), dma=False):
        ops = self.ops[eng]
        me = (eng, len(ops))
        if self.sched:
            st_ = self.peek_ready(eng, rd, wr)
            en_ = st_ + self.est_dur(eng, fn, dma)
            self.est_end[me] = en_
            if not dma:
                self.eng_free[eng] = en_
            else:
                self.eng_free[eng] = st_ + 60.0
        deps = set()
        for b in rd:
            if b.w is not None:
                deps.add(b.w)
        for b in wr:
            if b.w is not None:
                deps.add(b.w)
            deps.update(b.r)
        deps.discard(me)
        op = Op(fn, deps, dma)
        if dma:
            k = self.ndma
            self.ndma += 1
            op.sem = k % NDS
            op.val = 16 * (k // NDS + 1)
            if k >= NDS:
                deps.add(self.dma_ids[k - NDS])
            self.dma_ids.append(me)
        ops.append(op)
        for b in rd:
            b.r.append(me)
        for b in wr:
            b.w = me
            b.r = []
        return me

    def dma(self, out, in_, rd=(), wr=(), q="sp", **kw):
        return self.add(q, lambda e: e.dma_start(out=out, in_=in_, **kw), rd, wr, dma=True)

    def run(self):
        nc = self.nc
        ops = self.ops
        bar_sem, bar_val = self.bar
        fin = set(self.dma_ids)
        for e in ENGS:
            if e != "sp" and ops[e]:
                fin.add((e, len(ops[e]) - 1))
        sems, dsems = GST[0]

        def f_bar(e):
            for sm in list(sems.values()) + list(dsems):
                e.sem_clear(sm)
            return e.sem_inc(bar_sem, 1)

        ops["sp"].append(Op(f_bar, fin, False))
        for e in ENGS:
            for op in ops[e]:
                for d in op.deps:
                    ops[d[0]][d[1]].sig = True
        if True:
            for e in ENGS:
                c = 0
                for op in ops[e]:
                    if op.dma:
                        op.sem = dsems[op.sem]
                    elif op.sig:
                        c += 1
                        op.val = c
                        op.sem = sems[e]

            def emit(ename, eng):
                known = {}
                if bar_val > 0:
                    eng.wait_ge(bar_sem, bar_val)
                for op in ops[ename]:
                    need = {}
                    for d in op.deps:
                        dop = ops[d[0]][d[1]]
                        if ename == "pe" and d[0] == "pe" and not dop.dma:
                            continue
                        key = id(dop.sem)
                        if known.get(key, 0) < dop.val and need.get(key, (None, 0))[1] < dop.val:
                            need[key] = (dop.sem, dop.val)
                    for key, (sem, val) in need.items():
                        eng.wait_ge(sem, val)
                        known[key] = val
                    inst = op.fn(eng)
                    if op.dma:
                        inst.then_inc(op.sem, 16)
                    elif op.sig:
                        inst.then_inc(op.sem, 1)

            with nc.Block() as block:
                @block.sync
                def _(e):
                    emit("sp", e)

                @block.scalar
                def _(e):
                    emit("act", e)

                @block.vector
                def _(e):
                    emit("dve", e)

                @block.gpsimd
                def _(e):
                    emit("pool", e)

                @block.tensor
                def _(e):
                    emit("pe", e)
        self.stack.close()
        return (bar_sem, bar_val + 1)


def make_ident(ph, dt=BF16, n=128, name="ident"):
    f = ph.sb(name + "_f", [128, n], F32)
    o = ph.sb(name, [128, n], dt)
    b = Buf(name)

    def fn(g):
        g.memset(f[:], 0.0)
        g.affine_select(out=f[:], in_=f[:], compare_op=ALU.not_equal, fill=1.0, base=0, pattern=[[-1, n]], channel_multiplier=1)
        return g.tensor_copy(out=o[:], in_=f[:])

    ph.add("pool", fn, wr=[b])
    return o, b


class Cfg:
    def __init__(self, seqs):
        self.seqs = list(seqs)
        self.off = [0]
        for s in self.seqs:
            self.off.append(self.off[-1] + s)
        self.ntok = self.off[-1]
        self.smax = max(self.seqs)


def phase1(nc, cfg, T, bar):
    ph = Phase(nc, "p1", bar)
    NT = cfg.ntok
    NB = NT // 512
    ident, b_ident = make_ident(ph)
    xnT = ph.sb("xnT", [128, 8, NT], BF16)
    b_xnT = [Buf() for _ in range(NT // 128)]
    gcol = ph.sb("gcol", [128, 8], F32)
    b_g = Buf()
    ph.dma(gcol[:], T["norm_mix"].rearrange("(kc p) -> p kc", p=128), wr=[b_g], allow_slow_non_contiguous=True)
    banks = [ph.ps(f"bk{i}", [128, 512], F32) for i in range(6)]
    b_bank = [Buf() for _ in range(6)]
    pst = [ph.ps(f"pt{i}", [128, 1024], BF16) for i in range(2)]
    b_pst = [Buf(), Buf()]

    SM = cfg.smax
    Ct = ph.sb("Ct", [128, SM], F32)
    St = ph.sb("St", [128, SM], F32)
    b_rope = Buf()
    Pm = ph.sb("Pm", [128, 128], BF16)
    SEG = 512
    with_tmp = ph.sb("rtmp", [128, SEG], F32)
    posf = ph.sb("posf", [128, SEG], F32)
    cols = ph.sb("rcols", [128, 8], F32)
    pmf = ph.sb("pmf", [128, 128], F32)
    posi = ph.sb("posi", [128, SEG], I32)
    ki = ph.sb("rki", [128, SEG], I32)
    b_r0 = Buf()
    b_seg = Buf()
    ph.dma(cols[:], T["cst"][:, :], wr=[b_r0])
    ph.dma(pmf[:], T["pm"][:, :], wr=[b_r0])
    ph.add("pool", lambda g: g.tensor_copy(out=Pm[:], in_=pmf[:]), rd=[b_r0], wr=[b_r0])
    ph.add("act", lambda a: a.activation(out=cols[:, 7:8], in_=cols[:, 0:1], func=AF.Exp, scale=-math.log(500000.0) / 8.0), rd=[b_r0], wr=[b_r0])
    TWO_PI = 2.0 * math.pi
    for sg in range(SM // SEG):
        ss_ = slice(sg * SEG, (sg + 1) * SEG)

        def f_pos(g, sg=sg):
            g.iota(posi[:], pattern=[[1, SEG]], base=sg * SEG, channel_multiplier=0)
            return g.tensor_copy(out=posf[:], in_=posi[:])

        ph.add("pool", f_pos, wr=[b_seg])

        def reduce_fn(v, dst, shift):
            v.tensor_scalar(out=dst, in0=posf[:], scalar1=cols[:, 7:8], scalar2=shift, op0=ALU.mult, op1=ALU.add)
            v.tensor_scalar(out=ki[:], in0=dst, scalar1=1.0 / TWO_PI, scalar2=None, op0=ALU.mult)
            v.tensor_copy(out=with_tmp[:], in_=ki[:])
            v.scalar_tensor_tensor(out=dst, in0=with_tmp[:], scalar=-TWO_PI, in1=dst, op0=ALU.mult, op1=ALU.add)
            v.tensor_scalar(out=with_tmp[:], in0=dst, scalar1=math.pi, scalar2=TWO_PI, op0=ALU.is_gt, op1=ALU.mult)
            v.tensor_tensor(out=dst, in0=dst, in1=with_tmp[:], op=ALU.subtract)
            v.tensor_scalar(out=with_tmp[:], in0=dst, scalar1=-math.pi, scalar2=TWO_PI, op0=ALU.is_lt, op1=ALU.mult)
            return v.tensor_tensor(out=dst, in0=dst, in1=with_tmp[:], op=ALU.add)

        def rope_fn2(v, ss_=ss_):
            reduce_fn(v, St[:, ss_], 0.0)
            return reduce_fn(v, Ct[:, ss_], 0.5 * math.pi)

        ph.add("dve", rope_fn2, rd=[b_r0], wr=[b_seg, b_rope])

        def rope_fn3(a, ss_=ss_):
            a.activation(out=St[:, ss_], in_=St[:, ss_], func=AF.Sin)
            return a.activation(out=Ct[:, ss_], in_=Ct[:, ss_], func=AF.Sin)

        ph.add("act", rope_fn3, rd=[], wr=[b_rope])

        def rope_fn4(v, ss_=ss_):
            v.tensor_scalar(out=St[:, ss_], in0=St[:, ss_], scalar1=cols[:, 2:3], scalar2=None, op0=ALU.mult)
            v.tensor_scalar(out=Ct[:, ss_], in0=Ct[:, ss_], scalar1=-1.0, scalar2=None, op0=ALU.add)
            return v.tensor_scalar(out=Ct[:, ss_], in0=Ct[:, ss_], scalar1=cols[:, 1:2], scalar2=1.0, op0=ALU.mult, op1=ALU.add)

        ph.add("dve", rope_fn4, rd=[b_r0], wr=[b_rope])

    xt = [ph.sb(f"xt{i}", [128, D], F32) for i in range(2)]
    b_xt = [Buf(), Buf()]
    xnb = [ph.sb(f"xnb{i}", [128, D], BF16) for i in range(2)]
    b_xnb = [Buf(), Buf()]
    ss = ph.sb("ss", [128, 2, 2], F32)
    b_ss = [Buf(), Buf()]
    xin = T["x"]
    for t in range(NT // 128):
        s = t % 2
        ph.dma(xt[s][:], xin[t * 128:(t + 1) * 128, :], wr=[b_xt[s]])
        ph.add("act", lambda a, s=s: a.activation(out=xnb[s][:], in_=xt[s][:], func=AF.Square, accum_out=ss[:, s, 0:1]), rd=[b_xt[s]], wr=[b_xnb[s], b_ss[s]])

        ph.add("act", lambda a, s=s: a.activation(out=ss[:, s, 1:2], in_=ss[:, s, 0:1], func=AF.Sqrt, scale=1.0 / D, bias=cols[:, 4:5]), rd=[b_r0], wr=[b_ss[s]])
        ph.add("dve", lambda v, s=s: v.reciprocal(out=ss[:, s, 1:2], in_=ss[:, s, 1:2]), rd=[], wr=[b_ss[s]])
        ph.add("act", lambda a, s=s: a.activation(out=xnb[s][:], in_=xt[s][:], func=AF.Copy, scale=ss[:, s, 1:2]), rd=[b_xt[s], b_ss[s]], wr=[b_xnb[s]])

        def f_tr(pe, s=s):
            for kc in range(8):
                i = pe.transpose(out=pst[s][:, kc * 128:(kc + 1) * 128], in_=xnb[s][:, kc * 128:(kc + 1) * 128], identity=ident[:])
            return i

        ph.add("pe", f_tr, rd=[b_xnb[s], b_ident], wr=[b_pst[s]])
        ph.add("dve", lambda v, s=s, t=t: v.tensor_copy(out=xnT[:, :, t * 128:(t + 1) * 128], in_=pst[s][:, :].rearrange("p (k t) -> p k t", k=8)), rd=[], wr=[b_pst[s], b_xnT[t]])

    wf = [ph.sb(f"wf{i}", [128, 8, 128], F32) for i in range(2)]
    b_wf = [Buf(), Buf()]
    wb = [ph.sb(f"wb{i}", [128, 8, 128], BF16) for i in range(2)]
    b_wb = [Buf(), Buf()]
    stg = [ph.sb(f"stg{i}", [128, 512], F32) for i in range(3)]
    b_stg = [Buf() for _ in range(3)]
    stgb = [ph.sb(f"stgb{i}", [128, 512], BF16) for i in range(3)]
    b_stgb = [Buf() for _ in range(3)]
    qraw = [ph.sb("qraw0", [128, 512], BF16)] * 2
    b_qraw = [Buf()] * 2
    t1 = [ph.sb("t1_0", [128, 512], F32)] * 2
    b_t1 = [Buf()] * 2
    t2 = [ph.sb("t2_0", [128, 512], F32)] * 2
    b_t2 = [Buf()] * 2
    w_in = T["w_in"]
    chunks = []
    for c in range(27):
        chunks.append(("rw", c * 128, c * 128))
    for c in range(16):
        chunks.append(("qk", NRW + c * 128, c * 128))
    for c in range(16):
        chunks.append(("gt", NRW + 3072 + c * 128, c * 128))
    for c in range(8):
        chunks.append(("vv", NRW + 2048 + c * 128, c * 128))
    cnt = dict(bank=0, stg=0, stgb=0, q=0)
    blk_pos = []
    for si, S in enumerate(cfg.seqs):
        for j in range(S // 512):
            blk_pos.append(j * 512)
    for ci, (kind, c0, r0) in enumerate(chunks):
        s = ci % 2
        ph.dma(wf[s][:], w_in[:, c0:c0 + 128].rearrange("(kc p) c -> p kc c", p=128), wr=[b_wf[s]])
        ph.add("pool", lambda g, s=s: g.tensor_tensor(out=wb[s][:], in0=wf[s][:], in1=gcol[:].unsqueeze(2).to_broadcast([128, 8, 128]), op=ALU.mult), rd=[b_wf[s], b_g], wr=[b_wb[s]])
        for b in range(NB):
            bk = cnt["bank"] % 4
            cnt["bank"] += 1

            def f_mm(pe, s=s, b=b, bk=bk, kind=kind):
                if kind == "vv":
                    for j in range(4):
                        for kc in range(8):
                            i = pe.matmul(banks[bk][:, j * 128:(j + 1) * 128], lhsT=xnT[:, kc, b * 512 + j * 128:b * 512 + (j + 1) * 128], rhs=wb[s][:, kc, :], start=(kc == 0), stop=(kc == 7))
                    return i
                for kc in range(8):
                    i = pe.matmul(banks[bk][:, :], lhsT=wb[s][:, kc, :], rhs=xnT[:, kc, b * 512:(b + 1) * 512], start=(kc == 0), stop=(kc == 7))
                return i

            ph.add("pe", f_mm, rd=[b_wb[s]] + b_xnT[b * 4:(b + 1) * 4], wr=[b_bank[bk]])
            tok = slice(b * 512, (b + 1) * 512)
            if kind == "rw":
                g_ = cnt["stg"] % 3
                cnt["stg"] += 1
                ph.add("act", lambda a, bk=bk, g_=g_: a.activation(out=stg[g_][:], in_=banks[bk][:, :], func=AF.Copy), wr=[b_bank[bk], b_stg[g_]])
                ph.dma(T["pr"][r0:r0 + 128, tok], stg[g_][:], rd=[b_stg[g_]])
            elif kind == "vv":
                g_ = cnt["stgb"] % 3
                cnt["stgb"] += 1
                ph.add("act", lambda a, bk=bk, g_=g_: a.activation(out=stgb[g_][:], in_=banks[bk][:, :], func=AF.Copy), wr=[b_bank[bk], b_stgb[g_]])
                ph.dma(T["vv"][tok, r0:r0 + 128].rearrange("(j p) c -> p j c", p=128), stgb[g_][:].rearrange("p (j c) -> p j c", j=4), rd=[b_stgb[g_]])
            elif kind == "gt":
                g_ = cnt["stgb"] % 3
                cnt["stgb"] += 1
                ph.add("act", lambda a, bk=bk, g_=g_: a.activation(out=stgb[g_][:], in_=banks[bk][:, :], func=AF.Sigmoid), wr=[b_bank[bk], b_stgb[g_]])
                ph.dma(T["gt"][r0:r0 + 128, tok], stgb[g_][:], rd=[b_stgb[g_]])
            else:
                q_ = cnt["q"] % 2
                cnt["q"] += 1
                g_ = cnt["stgb"] % 3
                cnt["stgb"] += 1
                p0 = blk_pos[b]
                ph.add("act", lambda a, bk=bk, q_=q_: a.activation(out=qraw[q_][:], in_=banks[bk][:, :], func=AF.Copy), wr=[b_bank[bk], b_qraw[q_]])
                ph.add("dve", lambda v, bk=bk, q_=q_, p0=p0: v.tensor_tensor(out=t1[q_][:], in0=banks[bk][:, :], in1=Ct[:, p0:p0 + 512], op=ALU.mult), rd=[b_rope], wr=[b_bank[bk], b_t1[q_]])
                pb = 4 + (cnt["q"] % 2)
                ph.add("pe", lambda pe, q_=q_, pb=pb: pe.matmul(banks[pb][:, :], lhsT=Pm[:], rhs=qraw[q_][:], start=True, stop=True), rd=[b_qraw[q_], b_r0], wr=[b_bank[pb]])
                ph.add("dve", lambda v, pb=pb, q_=q_, p0=p0: v.tensor_tensor(out=t2[q_][:], in0=banks[pb][:, :], in1=St[:, p0:p0 + 512], op=ALU.mult), rd=[b_rope], wr=[b_bank[pb], b_t2[q_]])
                ph.add("dve", lambda g, q_=q_, g_=g_: g.tensor_tensor(out=stgb[g_][:], in0=t1[q_][:], in1=t2[q_][:], op=ALU.add), rd=[b_t1[q_], b_t2[q_]], wr=[b_stgb[g_]])
                ph.dma(T["qk"][r0:r0 + 128, tok], stgb[g_][:], rd=[b_stgb[g_]])

    return ph.run()


def bcast_rows(ap_1d, n):
    return ap_1d.partition_broadcast(128)


def phase3(nc, cfg, T, bar):
    ph = Phase(nc, "p3", bar)
    SM = cfg.smax
    lam_init = 0.8 - 0.6 * math.exp(-0.3 * 0)
    lv = ph.sb("lv", [128, 4, 64], F32)
    b_lv = Buf()
    for i, nm in enumerate(("lambda_q1", "lambda_k1", "lambda_q2", "lambda_k2")):
        ph.dma(lv[:, i, :], T[nm].partition_broadcast(128), wr=[b_lv])
    cc = ph.sb("cc", [128, 8], F32)
    b_cc = Buf()
    ph.dma(cc[:, 3:4], T["subln_w"].rearrange("(p o) -> p o", o=1), wr=[b_cc])
    lt = ph.sb("ltmp", [128, 2, 64], F32)
    ones = ph.sb("ones", [128, 128], BF16)

    def f_c(v):
        v.memset(ones[:], 1.0)
        v.memset(cc[:, 4:5], 1e-5)
        v.tensor_tensor(out=lt[:, 0, :], in0=lv[:, 0, :], in1=lv[:, 1, :], op=ALU.mult)
        v.tensor_tensor(out=lt[:, 1, :], in0=lv[:, 2, :], in1=lv[:, 3, :], op=ALU.mult)
        v.reduce_sum(out=cc[:, 0:2], in_=lt[:], axis=AX.X)
        return v.tensor_scalar(out=cc[:, 3:4], in0=cc[:, 3:4], scalar1=1.0 - lam_init, scalar2=None, op0=ALU.mult)

    ph.add("dve", f_c, rd=[b_lv], wr=[b_cc])
    ph.add("act", lambda a: a.activation(out=cc[:, 0:2], in_=cc[:, 0:2], func=AF.Exp), wr=[b_cc])

    def f_c2(v):
        v.tensor_tensor(out=cc[:, 2:3], in0=cc[:, 1:2], in1=cc[:, 0:1], op=ALU.subtract)
        return v.tensor_scalar(out=cc[:, 2:3], in0=cc[:, 2:3], scalar1=-lam_init, scalar2=None, op0=ALU.add)

    ph.add("dve", f_c2, wr=[b_cc])

    qT = [ph.sb(f"qT{i}", [128, SM], BF16) for i in range(2)]
    kT = [ph.sb(f"kT{i}", [128, SM], BF16) for i in range(2)]
    Vt = [ph.sb(f"Vt{i}", [128, SM // 128, 128], BF16) for i in range(2)]
    b_in = [Buf(), Buf()]
    bS = [ph.ps(f"bS{i}", [128, 512], F32) for i in range(4)]
    b_bS = [Buf() for _ in range(4)]
    bO = [ph.ps(f"bO{i}", [128, 512], F32) for i in range(4)]
    b_bO = [Buf() for _ in range(4)]
    Pt = [ph.sb(f"Pt{i}", [128, 512], BF16) for i in range(4)]
    b_Pt = [Buf() for _ in range(4)]
    rz = [ph.sb(f"rz{i}", [128, 512], F32) for i in range(2)]
    oo = [ph.sb(f"oo{i}", [128, 512], F32) for i in range(2)]
    sq = ph.sb("sq", [128, 512], BF16)
    rs = ph.sb("rs", [128, 512], F32)
    yb = [ph.sb(f"yb{i}", [128, 512], BF16) for i in range(2)]
    b_ep = Buf()
    b_yb = [Buf(), Buf()]
    blocks = []
    it = 0
    for si, S in enumerate(cfg.seqs):
        for h in range(8):
            u = it % 2
            it += 1
            for qb in range(S // 512):
                for kt in range(S // 128):
                    blocks.append((si, h, u, qb, kt))
    state = dict(ne=0)

    def emit_load(si, h, u):
        S = cfg.seqs[si]
        t0 = cfg.off[si]
        ph.dma(qT[u][:, 0:S], T["qk"][h * 128:(h + 1) * 128, t0:t0 + S], wr=[b_in[u]])
        ph.dma(kT[u][:, 0:S], T["qk"][1024 + h * 128:1024 + (h + 1) * 128, t0:t0 + S], wr=[b_in[u]])
        ph.dma(Vt[u][:, 0:S // 128, :], T["vv"][t0:t0 + S, h * 128:(h + 1) * 128].rearrange("(kt p) v -> p kt v", p=128), wr=[b_in[u]])

    def emit_scores(n):
        si, h, u, qb, kt = blocks[n]
        w = (n % 2) * 2
        ks = slice(kt * 128, (kt + 1) * 128)
        qs = slice(qb * 512, (qb + 1) * 512)

        def f_s(pe):
            pe.matmul(bS[w][:, :], lhsT=kT[u][0:64, ks], rhs=qT[u][0:64, qs], start=True, stop=True)
            return pe.matmul(bS[w + 1][:, :], lhsT=kT[u][64:128, ks], rhs=qT[u][64:128, qs], start=True, stop=True)

        ph.add("pe", f_s, rd=[b_in[u]], wr=[b_bS[w], b_bS[w + 1]])
        ph.add("act", lambda a: a.activation(out=Pt[w][:], in_=bS[w][:, :], func=AF.Exp, scale=0.125), wr=[b_bS[w], b_Pt[w]])
        ph.add("act", lambda a: a.activation(out=Pt[w + 1][:], in_=bS[w + 1][:, :], func=AF.Exp, scale=0.125), wr=[b_bS[w + 1], b_Pt[w + 1]])

    def emit_pv(n):
        si, h, u, qb, kt = blocks[n]
        S = cfg.seqs[si]
        t0 = cfg.off[si]
        nkt = S // 128
        w = (n % 2) * 2

        def f_pv(pe):
            st, sp_ = (kt == 0), (kt == nkt - 1)
            pe.matmul(bO[0][:, :], lhsT=Vt[u][:, kt, :], rhs=Pt[w][:], start=st, stop=sp_)
            pe.matmul(bO[2][:, :], lhsT=ones[:], rhs=Pt[w][:], start=st, stop=sp_)
            pe.matmul(bO[1][:, :], lhsT=Vt[u][:, kt, :], rhs=Pt[w + 1][:], start=st, stop=sp_)
            return pe.matmul(bO[3][:, :], lhsT=ones[:], rhs=Pt[w + 1][:], start=st, stop=sp_)

        ph.add("pe", f_pv, rd=[b_in[u], b_Pt[w], b_Pt[w + 1], b_cc], wr=b_bO)
        if kt != nkt - 1:
            return
        e = state["ne"] % 2
        state["ne"] += 1

        def f_e1(v):
            v.reciprocal(out=rz[0][:], in_=bO[2][:, :])
            v.reciprocal(out=rz[1][:], in_=bO[3][:, :])
            v.tensor_tensor(out=oo[0][:], in0=bO[0][:, :], in1=rz[0][:], op=ALU.mult)
            v.tensor_tensor(out=oo[1][:], in0=bO[1][:, :], in1=rz[1][:], op=ALU.mult)
            return v.scalar_tensor_tensor(out=oo[0][:], in0=oo[1][:], scalar=cc[:, 2:3], in1=oo[0][:], op0=ALU.mult, op1=ALU.add)

        ph.add("dve", f_e1, rd=[b_cc], wr=b_bO + [b_ep])
        ph.add("act", lambda a: a.activation(out=sq[:], in_=oo[0][:], func=AF.Square), wr=[b_ep])
        ph.add("pe", lambda pe: pe.matmul(bS[w][:, :], lhsT=ones[:], rhs=sq[:], start=True, stop=True), rd=[b_ep], wr=[b_bS[w]])
        ph.add("act", lambda a: a.activation(out=rs[:], in_=bS[w][:, :], func=AF.Sqrt, scale=1.0 / 128.0, bias=cc[:, 4:5]), rd=[b_cc], wr=[b_bS[w], b_ep])
        ph.add("dve", lambda v: v.reciprocal(out=rs[:], in_=rs[:]), wr=[b_ep])
        ph.add("dve", lambda g: g.scalar_tensor_tensor(out=yb[e][:], in0=oo[0][:], scalar=cc[:, 3:4], in1=rs[:], op0=ALU.mult, op1=ALU.mult), rd=[b_ep, b_cc], wr=[b_yb[e]])
        ph.dma(T["ybT"][h * 128:(h + 1) * 128, t0 + qb * 512:t0 + (qb + 1) * 512], yb[e][:], rd=[b_yb[e]])

    heads = []
    for bl in blocks:
        if not heads or heads[-1] != bl[0:3]:
            heads.append(bl[0:3])
    for hd in heads[0:2]:
        emit_load(*hd)
    emit_scores(0)
    hj = 0
    for n in range(len(blocks)):
        if n + 1 < len(blocks):
            emit_scores(n + 1)
        emit_pv(n)
        if n + 1 == len(blocks) or blocks[n + 1][0:3] != blocks[n][0:3]:
            if hj + 2 < len(heads):
                emit_load(*heads[hj + 2])
            hj += 1
    return ph.run()


def load_w_bf16(ph, dst, b_dst, w_ap, nk, ncols, stage, b_stage, scale_col=None, b_scale=None, eng="pool"):
    step = 512
    kst = stage.shape[1]
    for c0 in range(0, ncols, step):
        for k0 in range(0, nk, kst):
            k1 = min(nk, k0 + kst)
            ph.dma(stage[:, 0:k1 - k0, :], w_ap[k0 * 128:k1 * 128, c0:c0 + step].rearrange("(kc p) c -> p kc c", p=128), wr=[b_stage])
            if scale_col is None:
                ph.add(eng, lambda g, k0=k0, k1=k1, c0=c0: g.tensor_copy(out=dst[:, k0:k1, c0:c0 + step], in_=stage[:, 0:k1 - k0, :]), rd=[b_stage], wr=[b_dst])
            else:
                ph.add(eng, lambda g, k0=k0, k1=k1, c0=c0: g.tensor_tensor(out=dst[:, k0:k1, c0:c0 + step], in0=stage[:, 0:k1 - k0, :], in1=scale_col[:, k0:k1].unsqueeze(2).to_broadcast([128, k1 - k0, step]), op=ALU.mult), rd=[b_stage, b_scale], wr=[b_dst])


def phase4a(nc, cfg, T, bar):
    ph = Phase(nc, "p4a", bar)
    NT = cfg.ntok
    ident, b_ident = make_ident(ph)
    stage = ph.sb("stage", [128, 8, 512], F32)
    b_stage = Buf()
    W = {}
    bW = {}
    for nm in ("proj_a", "proj_b", "w_out"):
        W[nm] = ph.sb("w_" + nm, [128, 8, 1024], BF16)
        bW[nm] = Buf()
        load_w_bf16(ph, W[nm], bW[nm], T[nm], 8, 1024, stage, b_stage)
    cst = ph.sb("cst", [128, 1], F32)
    b_cst = Buf()
    ph.add("pool", lambda g: g.memset(cst[:], 1e-6), wr=[b_cst])
    banks = [ph.ps(f"bk{i}", [128, 512], F32) for i in range(6)]
    b_bank = [Buf() for _ in range(6)]
    pst = ph.ps("pt", [128, 1024], BF16)
    b_pst = Buf()
    ya = ph.sb("ya", [128, 8, 512], BF16)
    ybb = ph.sb("ybb", [128, 8, 512], BF16)
    ga = ph.sb("ga", [128, 8, 512], BF16)
    gb = ph.sb("gb", [128, 8, 512], BF16)
    b_ld = Buf()
    mg = ph.sb("mg", [128, 8, 512], BF16)
    b_mg = Buf()
    m1 = [ph.sb(f"m1_{i}", [128, 512], F32) for i in range(2)]
    m2 = [ph.sb(f"m2_{i}", [128, 512], F32) for i in range(2)]
    b_m = [Buf(), Buf()]
    xt = [ph.sb(f"xt{i}", [128, D], F32) for i in range(2)]
    b_xt = [Buf(), Buf()]
    hh = [ph.sb(f"hh{i}", [128, D], F32) for i in range(2)]
    b_hh = [Buf(), Buf()]
    hb = [ph.sb(f"hb{i}", [128, D], BF16) for i in range(2)]
    b_hb = [Buf(), Buf()]
    hT = [ph.sb(f"hT{i}", [128, 8, 128], BF16) for i in range(2)]
    b_hT = [Buf(), Buf()]
    junk = ph.sb("junk", [128, D], BF16)
    b_junk = Buf()
    ss = ph.sb("ss", [128, 2, 2], F32)
    b_ss = [Buf(), Buf()]
    nb = 0
    nm_ = 0
    nt_ = 0
    for b in range(NT // 512):
        tok = slice(b * 512, (b + 1) * 512)
        for dst, src, r0 in ((ya, "yaT", 0), (ybb, "ybT", 0), (ga, "gt", 0), (gb, "gt", 1024)):
            ph.dma(dst[:], T[src][r0:r0 + 1024, tok].rearrange("(kc p) t -> p kc t", p=128), wr=[b_ld])
        for oc in range(8):
            ba, bb = nb % 6, (nb + 1) % 6
            nb += 2

            def f_ab(pe, oc=oc, ba=ba, bb=bb):
                for kc in range(8):
                    pe.matmul(banks[ba][:, :], lhsT=W["proj_a"][:, kc, oc * 128:(oc + 1) * 128], rhs=ya[:, kc, :], start=(kc == 0), stop=(kc == 7))
                for kc in range(8):
                    i = pe.matmul(banks[bb][:, :], lhsT=W["proj_b"][:, kc, oc * 128:(oc + 1) * 128], rhs=ybb[:, kc, :], start=(kc == 0), stop=(kc == 7))
                return i

            ph.add("pe", f_ab, rd=[b_ld, bW["proj_a"], bW["proj_b"]], wr=[b_bank[ba], b_bank[bb]])
            m = nm_ % 2
            nm_ += 1

            def f_m(v, oc=oc, ba=ba, bb=bb, m=m):
                v.tensor_tensor(out=m1[m][:], in0=banks[ba][:, :], in1=ga[:, oc, :], op=ALU.mult)
                return v.tensor_tensor(out=m2[m][:], in0=banks[bb][:, :], in1=gb[:, oc, :], op=ALU.mult)

            ph.add("dve", f_m, rd=[b_ld], wr=[b_bank[ba], b_bank[bb], b_m[m]])
            ph.add("dve", lambda g, oc=oc, m=m: g.tensor_tensor(out=mg[:, oc, :], in0=m1[m][:], in1=m2[m][:], op=ALU.add), rd=[b_m[m]], wr=[b_mg])
        for ti in range(4):
            t = b * 4 + ti
            s_ = nt_ % 2
            nt_ += 1
            ph.dma(xt[s_][:], T["x"][t * 128:(t + 1) * 128, :], wr=[b_xt[s_]])
            ba, bb = nb % 6, (nb + 1) % 6
            nb += 2

            def f_o(pe, ti=ti, ba=ba, bb=bb):
                for hf, bk in ((0, ba), (1, bb)):
                    for kc in range(8):
                        i = pe.matmul(banks[bk][:, :], lhsT=mg[:, kc, ti * 128:(ti + 1) * 128], rhs=W["w_out"][:, kc, hf * 512:(hf + 1) * 512], start=(kc == 0), stop=(kc == 7))
                return i

            ph.add("pe", f_o, rd=[b_mg, bW["w_out"]], wr=[b_bank[ba], b_bank[bb]])

            def f_h(v, s_=s_, ba=ba, bb=bb):
                v.tensor_tensor(out=hh[s_][:, 0:512], in0=banks[ba][:, :], in1=xt[s_][:, 0:512], op=ALU.add)
                return v.tensor_tensor(out=hh[s_][:, 512:1024], in0=banks[bb][:, :], in1=xt[s_][:, 512:1024], op=ALU.add)

            ph.add("dve", f_h, rd=[b_xt[s_]], wr=[b_bank[ba], b_bank[bb], b_hh[s_]])
            ph.dma(T["hs"][t * 128:(t + 1) * 128, :], hh[s_][:], rd=[b_hh[s_]])
            ph.add("act", lambda a, s_=s_: a.activation(out=junk[:], in_=hh[s_][:], func=AF.Square, accum_out=ss[:, s_, 0:1]), rd=[b_hh[s_]], wr=[b_junk, b_ss[s_]])
            ph.add("act", lambda a, s_=s_: a.activation(out=ss[:, s_, 1:2], in_=ss[:, s_, 0:1], func=AF.Sqrt, scale=1.0 / D, bias=cst[:, 0:1]), rd=[b_cst], wr=[b_ss[s_]])
            ph.add("dve", lambda v, s_=s_: v.reciprocal(out=ss[:, s_, 1:2], in_=ss[:, s_, 1:2]), wr=[b_ss[s_]])
            ph.add("act", lambda a, s_=s_: a.activation(out=hb[s_][:], in_=hh[s_][:], func=AF.Copy, scale=ss[:, s_, 1:2]), rd=[b_hh[s_], b_ss[s_]], wr=[b_hb[s_]])

            def f_tr(pe, s_=s_):
                for kc in range(8):
                    i = pe.transpose(out=pst[:, kc * 128:(kc + 1) * 128], in_=hb[s_][:, kc * 128:(kc + 1) * 128], identity=ident[:])
                return i

            ph.add("pe", f_tr, rd=[b_hb[s_], b_ident], wr=[b_pst])
            ph.add("dve", lambda v, s_=s_: v.tensor_copy(out=hT[s_][:], in_=pst[:, :].rearrange("p (k t) -> p k t", k=8)), wr=[b_pst, b_hT[s_]])
            ph.dma(T["hnT"][:, t * 128:(t + 1) * 128].rearrange("(kc p) t -> p kc t", p=128), hT[s_][:], rd=[b_hT[s_]])
    return ph.run()


def phase4b(nc, cfg, T, bar):
    ph = Phase(nc, "p4b", bar)
    NT = cfg.ntok
    stage = ph.sb("stage", [128, 4, 512], F32)
    b_stage = Buf()
    gcol = ph.sb("gcol", [128, 8], F32)
    b_g = Buf()
    ph.dma(gcol[:], T["norm_mlp"].rearrange("(kc p) -> p kc", p=128), wr=[b_g], allow_slow_non_contiguous=True)
    gfin = ph.sb("gfin", [128, D], F32)
    b_gf = Buf()
    ph.dma(gfin[:], T["norm_final"].partition_broadcast(128), wr=[b_gf])
    w1 = ph.sb("w1", [128, 8, DFF], BF16)
    b_w1 = Buf()
    w2 = ph.sb("w2", [128, 32, D], BF16)
    b_w2 = Buf()
    load_w_bf16(ph, w1, b_w1, T["w_mlp_in"], 8, DFF, stage, b_stage, scale_col=gcol, b_scale=b_g)
    load_w_bf16(ph, w2, b_w2, T["w_mlp_out"], 32, D, stage, b_stage)
    cst = ph.sb("cst", [128, 1], F32)
    b_cst = Buf()
    ph.add("pool", lambda g: g.memset(cst[:], 1e-6), wr=[b_cst])
    banks = [ph.ps(f"bk{i}", [128, 512], F32) for i in range(8)]
    b_bank = [Buf() for _ in range(8)]
    hn = [ph.sb("hn0", [128, 8, 512], BF16)] * 2
    b_hn = [Buf()] * 2
    hid = ph.sb("hid", [128, 32, 512], BF16)
    b_hid = [Buf() for _ in range(32)]
    rl = [ph.sb(f"rl{i}", [128, 512], F32) for i in range(2)]
    b_rl = [Buf(), Buf()]
    ht = [ph.sb(f"ht{i}", [128, D], F32) for i in range(2)]
    b_ht = [Buf(), Buf()]
    oo = [ph.sb(f"oo{i}", [128, D], F32) for i in range(2)]
    b_oo = [Buf(), Buf()]
    junk = ph.sb("junk", [128, D], BF16)
    b_junk = Buf()
    ss = ph.sb("ss", [128, 2, 2], F32)
    b_ss = [Buf(), Buf()]
    nb = 0
    nr = 0
    nt_ = 0
    for b in range(NT // 512):
        u = b % 2
        tok = slice(b * 512, (b + 1) * 512)
        ph.dma(hn[u][:], T["hnT"][:, tok].rearrange("(kc p) t -> p kc t", p=128), wr=[b_hn[u]])
        for fc in range(32):
            bk = nb % 8
            nb += 1

            def f_h(pe, fc=fc, bk=bk, u=u):
                for kc in range(8):
                    i = pe.matmul(banks[bk][:, :], lhsT=w1[:, kc, fc * 128:(fc + 1) * 128], rhs=hn[u][:, kc, :], start=(kc == 0), stop=(kc == 7))
                return i

            ph.add("pe", f_h, rd=[b_w1, b_hn[u]], wr=[b_bank[bk]])
            r_ = nr % 2
            nr += 1
            ph.add("act", lambda a, bk=bk, r_=r_: a.activation(out=rl[r_][:], in_=banks[bk][:, :], func=AF.Relu), wr=[b_bank[bk], b_rl[r_]])
            ph.add("dve", lambda g, fc=fc, r_=r_: g.tensor_tensor(out=hid[:, fc, :], in0=rl[r_][:], in1=rl[r_][:], op=ALU.mult), rd=[b_rl[r_]], wr=[b_hid[fc]])
        for ti in range(4):
            t = b * 4 + ti
            s_ = nt_ % 2
            nt_ += 1
            ph.dma(ht[s_][:], T["hs"][t * 128:(t + 1) * 128, :], wr=[b_ht[s_]])
            ba, bb = nb % 8, (nb + 1) % 8
            nb += 2

            def f_o(pe, ti=ti, ba=ba, bb=bb):
                for hf, bk in ((0, ba), (1, bb)):
                    for fc in range(32):
                        i = pe.matmul(banks[bk][:, :], lhsT=hid[:, fc, ti * 128:(ti + 1) * 128], rhs=w2[:, fc, hf * 512:(hf + 1) * 512], start=(fc == 0), stop=(fc == 31))
                return i

            ph.add("pe", f_o, rd=[b_w2] + b_hid, wr=[b_bank[ba], b_bank[bb]])

            def f_r(v, s_=s_, ba=ba, bb=bb):
                v.tensor_tensor(out=oo[s_][:, 0:512], in0=banks[ba][:, :], in1=ht[s_][:, 0:512], op=ALU.add)
                return v.tensor_tensor(out=oo[s_][:, 512:1024], in0=banks[bb][:, :], in1=ht[s_][:, 512:1024], op=ALU.add)

            ph.add("dve", f_r, rd=[b_ht[s_]], wr=[b_bank[ba], b_bank[bb], b_oo[s_]])
            ph.add("act", lambda a, s_=s_: a.activation(out=junk[:], in_=oo[s_][:], func=AF.Square, accum_out=ss[:, s_, 0:1]), rd=[b_oo[s_]], wr=[b_junk, b_ss[s_]])
            ph.add("act", lambda a, s_=s_: a.activation(out=ss[:, s_, 1:2], in_=ss[:, s_, 0:1], func=AF.Sqrt, scale=1.0 / D, bias=cst[:, 0:1]), rd=[b_cst], wr=[b_ss[s_]])
            ph.add("dve", lambda v, s_=s_: v.reciprocal(out=ss[:, s_, 1:2], in_=ss[:, s_, 1:2]), wr=[b_ss[s_]])
            ph.add("dve", lambda g, s_=s_: g.scalar_tensor_tensor(out=oo[s_][:], in0=oo[s_][:], scalar=ss[:, s_, 1:2], in1=gfin[:], op0=ALU.mult, op1=ALU.mult), rd=[b_ss[s_], b_gf], wr=[b_oo[s_]])
            ph.dma(T["y"][t * 128:(t + 1) * 128, :], oo[s_][:], rd=[b_oo[s_]])
    return ph.run()


def phase15(nc, cfg, T, bar):
    ph = Phase(nc, "p15", bar)
    TB = 256
    mu = ph.sb("mu", [128, 27], F32)
    omu = ph.sb("omu", [128, 27], F32)
    hmu = ph.sb("hmu", [128, 27], F32)
    b_par = Buf()
    ph.dma(mu[:], T["mu_shift"].rearrange("(c p) -> p c", p=128), wr=[b_par], allow_slow_non_contiguous=True)
    ph.add("dve", lambda v: v.tensor_scalar(out=omu[:], in0=mu[:], scalar1=-1.0, scalar2=1.0, op0=ALU.mult, op1=ALU.add), wr=[b_par])
    ph.add("dve", lambda v: v.tensor_scalar(out=hmu[:], in0=mu[:], scalar1=0.5, scalar2=None, op0=ALU.mult), wr=[b_par])
    P = [ph.sb(f"P{i}", [128, 27, TB + 2], F32) for i in range(2)]
    b_P = [Buf(), Buf()]
    t1 = [ph.sb(f"t1_{i}", [128, 27, TB], F32) for i in range(2)]
    b_t1 = [Buf(), Buf()]
    t2 = [ph.sb(f"t2_{i}", [128, 27, TB], F32) for i in range(2)]
    b_t2 = [Buf(), Buf()]
    n = 0
    for si, S in enumerate(cfg.seqs):
        base = cfg.off[si]
        nb_ = S // TB
        for bi in range(nb_):
            u = n % 2
            n += 1
            t0 = base + bi * TB
            lo = 1 if bi == 0 else 0
            hi = TB + 1 if bi == nb_ - 1 else TB + 2
            if lo == 1:
                ph.add("pool", lambda g, u=u: g.memset(P[u][:, :, 0:1], 0.0), wr=[b_P[u]])
            if hi == TB + 1:
                ph.add("pool", lambda g, u=u: g.memset(P[u][:, :, TB + 1:TB + 2], 0.0), wr=[b_P[u]])
            for c0_, c1_ in ((0, 9), (9, 18), (18, 27)):
                ph.dma(P[u][:, c0_:c1_, lo:hi], T["pr"][c0_ * 128:c1_ * 128, t0 - 1 + lo:t0 - 1 + hi].rearrange("(c p) t -> p c t", p=128), wr=[b_P[u]])
            for c in range(27):
                ph.add("act", lambda a, u=u, c=c: a.activation(out=t1[u][:, c, :], in_=P[u][:, c, 1:TB + 1], func=AF.Copy, scale=omu[:, c:c + 1]), rd=[b_P[u], b_par], wr=[b_t1[u]])

            def f_sh(v, u=u):
                v.tensor_tensor(out=t2[u][:], in0=P[u][:, :, 0:TB], in1=P[u][:, :, 2:TB + 2], op=ALU.add)
                v.tensor_tensor(out=t2[u][:], in0=t2[u][:], in1=hmu[:].unsqueeze(2).to_broadcast([128, 27, TB]), op=ALU.mult)
                return v.tensor_tensor(out=t2[u][:], in0=t2[u][:], in1=t1[u][:], op=ALU.add)

            ph.add("dve", f_sh, rd=[b_P[u], b_par, b_t1[u]], wr=[b_t2[u]])
            for c0_, c1_ in ((0, 9), (9, 18), (18, 27)):
                ph.dma(T["prs"][c0_ * 128:c1_ * 128, t0:t0 + TB].rearrange("(c p) t -> p c t", p=128), t2[u][:, c0_:c1_, :], rd=[b_t2[u]])
    return ph.run()


def phase2(nc, cfg, T, bar):
    ph = Phase(nc, "p2", bar)
    NT = cfg.ntok
    C0 = -math.exp(-0.5)
    sb, add = ph.sb, ph.add
    identb, b_identb = make_ident(ph, BF16, name="idb")
    Y = sb("Y", [128, 1024], F32)
    Yp = sb("Yp", [128, 1024], F32)
    wst = Yp
    MK = Y[:, 0:512].rearrange("p (m c) -> p m c", m=4)
    TRt = sb("TRt", [128, 4, 128], BF16)
    TR = TRt
    identf = ph.sb("idf", [128, 128], F32)
    BD = sb("BD", [128, 128], BF16)
    onesr = sb("onesr", [1, 128], BF16)
    mA = [sb(f"mA{d}", [128, 256], F32) for d in range(2)]
    mL = [sb(f"mL{d}", [128, 256], F32) for d in range(2)]
    trc = [sb(f"trc{d}", [128, 384], BF16) for d in range(2)]
    b_cst = Buf()

    def f_masks(g):
        g.memset(MK, 1.0)
        g.affine_select(out=MK[:, 0, :], in_=MK[:, 0, :], compare_op=ALU.is_gt, fill=0.0, base=0, pattern=[[1, 128]], channel_multiplier=-1)
        g.affine_select(out=MK[:, 1, :], in_=MK[:, 1, :], compare_op=ALU.is_ge, fill=0.0, base=0, pattern=[[1, 128]], channel_multiplier=-1)
        g.affine_select(out=MK[:, 2, :], in_=MK[:, 2, :], compare_op=ALU.is_gt, fill=0.0, base=0, pattern=[[-1, 128]], channel_multiplier=1)
        g.affine_select(out=MK[:, 3, :], in_=MK[:, 3, :], compare_op=ALU.is_ge, fill=0.0, base=0, pattern=[[-1, 128]], channel_multiplier=1)
        g.tensor_scalar(out=TR[:], in0=MK, scalar1=C0, scalar2=None, op0=ALU.mult)
        g.memset(identf[:], 0.0)
        g.affine_select(out=identf[:], in_=identf[:], compare_op=ALU.not_equal, fill=1.0, base=0, pattern=[[-1, 128]], channel_multiplier=1)
        g.memset(BD[:], 0.0)
        g.memset(BD[0:64, 0:64], 1.0)
        g.memset(BD[64:128, 64:128], 1.0)
        g.memset(onesr[:], 1.0)
        for d in range(2):
            st, inc, lm = (0, 1, 2) if d == 0 else (2, 3, 0)
            g.tensor_copy(out=mA[d][:, 0:128], in_=MK[:, st, :])
            g.tensor_copy(out=mA[d][:, 128:256], in_=MK[:, inc, :])
            for q in range(2):
                g.tensor_copy(out=mL[d][:, q * 128:(q + 1) * 128], in_=MK[:, lm, :])
            g.tensor_copy(out=trc[d][:, 0:128], in_=TR[:, inc, :])
            g.tensor_copy(out=trc[d][:, 128:256], in_=TR[:, st, :])
            i = g.tensor_copy(out=trc[d][:, 256:384], in_=TR[:, lm, :])
        return i

    add("pool", f_masks, wr=[b_cst])
    par = sb("par", [128, 8, 8], F32)
    b_par = Buf()
    for i, ap_ in enumerate((T["k_k"], T["k_a"], T["k_a"], T["r_k"], T["ln_x_w"], T["ln_x_b"], T["a0"][0:1024], T["a0"][1024:2048])):
        ph.dma(par[:, i, :], ap_.rearrange("(c p) -> p c", p=128), wr=[b_par], allow_slow_non_contiguous=True)
    add("pool", lambda g: g.tensor_scalar(out=par[:, 2, :], in0=par[:, 2, :], scalar1=-1.0, scalar2=1.0, op0=ALU.mult, op1=ALU.add), wr=[b_par])
    b_wst = Buf()
    wup = sb("wup", [128, 1024], BF16)
    aup = sb("aup", [128, 1024], BF16)
    gup = sb("gup", [128, 1024], BF16)
    b_w = Buf()
    for dst, src in ((wup, T["w_lora_up"]), (aup, T["a_lora_up"]), (gup, T["g_lora_up"])):
        ph.dma(wst[:], src[:, :], wr=[b_wst])
        add("pool", lambda g, dst=dst: g.tensor_copy(out=dst[:], in_=wst[:]), rd=[b_wst], wr=[b_w])
    w0h = sb("w0h", [1, 2, 1024], BF16)
    w0l = sb("w0l", [1, 2, 1024], BF16)
    w0t = Y
    for d_ in range(2):
        ph.dma(wst[0:1, :], T["w0"][d_ * 1024:(d_ + 1) * 1024].rearrange("(o c) -> o c", o=1), wr=[b_wst])

        def f_w0(g, d_=d_):
            g.tensor_copy(out=w0h[0:1, d_, :], in_=wst[0:1, :])
            g.tensor_copy(out=w0t[0:1, :], in_=w0h[0:1, d_, :])
            g.tensor_tensor(out=w0t[0:1, :], in0=wst[0:1, :], in1=w0t[0:1, :], op=ALU.subtract)
            return g.tensor_copy(out=w0l[0:1, d_, :], in_=w0t[0:1, :])

        add("pool", f_w0, rd=[], wr=[b_wst, b_w, b_cst])
    eps = sb("eps", [128, 2], F32)
    add("pool", lambda g: (g.memset(eps[:, 0:1], 64e-5), g.memset(eps[:, 1:2], 0.0))[1], wr=[b_cst])

    bk = [ph.ps(f"bk{i}", [128, 512], F32) for i in range(6)]
    b_bk = [Buf() for _ in range(6)]
    bt = [ph.ps(f"bt{i}", [128, 1024], BF16) for i in range(2)]
    b_bt = [Buf(), Buf()]
    cnt = dict(b=0, t=0)

    def nb():
        pool_ = cnt.get("pool")
        if pool_ is None:
            i = cnt["b"] % 6
            cnt["b"] += 1
            return i
        k_ = "b" + str(pool_[0])
        i = pool_[cnt.get(k_, 0) % len(pool_)]
        cnt[k_] = cnt.get(k_, 0) + 1
        return i

    def ntb():
        i = cnt["t"] % 2
        cnt["t"] += 1
        return i

    SH = sb("SH", [128, 27, 128], F32)
    b_SH = Buf()
    f4 = lambda nm: sb(nm, [128, 8, 128], F32)
    h4 = lambda nm: sb(nm, [128, 8, 128], BF16)
    av, kk, kd, Ei, tmp1 = [f4(n) for n in ("av", "kk", "kd", "Ei", "tmp1")]
    bb_ = av
    En, Ee, Es = [h4(n) for n in ("En", "Ee", "Es")]
    tmp2 = tmp1
    kkr = tmp1
    rn = kd
    names = ("av", "kk", "bb", "kd", "E", "tmp1", "tl", "lab", "sig", "sq", "AR", "bt", "kt", "Bg", "Kg", "vb",
             "Atok", "Bgtok", "Kgtok", "Vtok", "AM", "KM", "LL", "PL0", "PL1", "PA0", "PA1", "X0", "X1", "Zb", "AhT", "What", "Ub", "S", "Sbf",
             "Y", "Yp", "sg", "bon", "bonp", "st", "yo", "Stmp", "Gc")
    dbl = ("AR", "Atok", "Bgtok", "Kgtok", "Vtok", "AM", "KM", "LL", "sg", "bon", "Gc")
    B0 = {n: Buf(n) for n in names}
    Bp = [dict(B0), dict(B0)]
    for n_ in dbl:
        Bp[1][n_] = Buf(n_ + "1")
    grp = ("AM", "KM", "LL", "PL0", "PL1", "PA0", "PA1", "X0", "X1")
    for n_ in grp:
        l0 = [Buf(n_ + str(g_)) for g_ in range(4)]
        Bp[0][n_] = l0
        Bp[1][n_] = [Buf(n_ + "b" + str(g_)) for g_ in range(4)] if n_ in dbl else l0
    for Bx in Bp:
        Bx["bb"] = Bx["av"]
        Bx["kkr"] = Bx["tmp1"]
        Bx["tmp2"] = Bx["tmp1"]
        Bx["rn"] = Bx["kd"]
        Bx["fin"] = Bx["What"]
        Bx["yn"] = Bx["Yp"]
    tl = sb("tl", [128, 128], BF16)
    lab = sb("lab", [128, 128], BF16)
    sig = sb("sig", [128, 1024], BF16)
    sq = h4("sq")
    btT, ktT, BgT, KgT, vb = [h4(n) for n in ("btT", "ktT", "BgT", "KgT", "vbb")]
    Zb = sb("Zb", [128, 1024], BF16)
    two = lambda nm, shp, dt: [sb(nm + "0", shp, dt), sb(nm + "1", shp, dt)]
    sgbs = two("sgb", [128, 128], BF16)
    ARs = two("AR", [128, 8, 256], BF16)
    Atoks, Bgtoks, Kgtoks, Vtoks = [two(n, [128, 1024], BF16) for n in ("Atok", "Bgtok", "Kgtok", "Vtok")]
    AMs = two("AM", [128, 16, 256], BF16)
    KMs = two("KM", [128, 16, 256], BF16)
    LLs = two("LL", [128, 16, 128], BF16)
    bons = two("bon", [128, 8, 128], F32)
    Gcs = two("Gc", [128, 8, 1], F32)
    PL12 = [sb("PL1", [128, 16, 128], BF16), sb("PL2", [128, 16, 128], BF16)]
    PA = [sb("PA1", [128, 16, 128], BF16), sb("PA2", [128, 16, 128], BF16)]
    X = [sb("X0", [128, 16, 128], BF16), sb("X1", [128, 16, 128], BF16)]
    AhT = sb("AhT", [128, 8, 128], BF16)
    What = sb("What", [128, 1024], F32)
    Ub = sb("Ub", [128, 1024], BF16)
    Sf = sb("Sf", [128, 8, 64], F32)
    Stmp = sb("Stmp", [128, 8, 64], F32)
    Sbf = sb("Sbf", [128, 8, 64], BF16)
    bonp = f4("bonp")
    st = sb("st", [128, 16, 4], F32)
    yn2 = Yp
    yn = yn2[:].rearrange("p (h v) -> p h v", v=64)
    fin = What[:].rearrange("p (c t) -> p c t", c=8)
    yo = Zb[:].rearrange("p (c t) -> p c t", c=8)
    for Bx in Bp:
        Bx["yo"] = Bx["Zb"]
    dramB = {}

    def dB(kind, t0):
        return dramB.setdefault((kind, t0), Buf())

    def stageA(si, d, ci, p):
        S = cfg.seqs[si]
        base = cfg.off[si]
        nch = S // 128
        t0 = base + ci * 128
        first = (ci == 0) if d == 0 else (ci == nch - 1)
        final = (d == 1)
        B = Bp[p]
        AR, Atok, Bgtok, Kgtok, Vtok, AM, KM, LL, sgb, bon, Gc = ARs[p], Atoks[p], Bgtoks[p], Kgtoks[p], Vtoks[p], AMs[p], KMs[p], LLs[p], sgbs[p], bons[p], Gcs[p]
        PL = [LL, PL12[0], PL12[1]]
        R, Kx, Vx = SH[:, 0:8, :], SH[:, 8:16, :], SH[:, 16:24, :]
        dsl = slice(d * 64, (d + 1) * 64)
        bc = lambda i: par[:, i, :].unsqueeze(2).to_broadcast([128, 8, 128])
        ec = 127 if d == 0 else 0
        for c0_, c1_ in ((0, 9), (9, 18), (18, 27)):
            ph.dma(SH[:, c0_:c1_, :], T["prs"][c0_ * 128:c1_ * 128, t0:t0 + 128].rearrange("(c p) t -> p c t", p=128), wr=[b_SH])
        add("act", lambda a: a.activation(out=tl[:], in_=SH[:, 24, :], func=AF.Tanh), rd=[b_SH], wr=[B["tl"]])
        add("act", lambda a: a.activation(out=lab[:], in_=SH[:, 25, :], func=AF.Copy), rd=[b_SH], wr=[B["lab"]])
        w1_, w2_ = nb(), nb()

        def f_lw(pe):
            for hf, b_ in ((0, w1_), (1, w2_)):
                cs = slice(hf * 512, (hf + 1) * 512)
                pe.matmul(bk[b_][:, :], lhsT=tl[dsl, :], rhs=wup[dsl, cs], start=True, stop=False)
                pe.matmul(bk[b_][:, :], lhsT=onesr[0:1, :], rhs=w0h[0:1, d, cs], start=False, stop=False)
                i = pe.matmul(bk[b_][:, :], lhsT=onesr[0:1, :], rhs=w0l[0:1, d, cs], start=False, stop=True)
            return i

        add("pe", f_lw, rd=[B["tl"], b_w, b_cst], wr=[b_bk[w1_], b_bk[w2_]])
        add("act", lambda a: a.activation(out=sig[:, 0:512], in_=bk[w1_][:, :], func=AF.Sigmoid), wr=[b_bk[w1_], B["sig"]])
        add("act", lambda a: a.activation(out=sig[:, 512:1024], in_=bk[w2_][:, :], func=AF.Sigmoid), wr=[b_bk[w2_], B["sig"]])
        a1_, a2_ = nb(), nb()

        def f_la(pe):
            for c in range(8):
                b_ = a1_ if c < 4 else a2_
                i = pe.matmul(bk[b_][:, (c % 4) * 128:(c % 4 + 1) * 128], lhsT=aup[dsl, c * 128:(c + 1) * 128], rhs=lab[dsl, :], start=True, stop=True)
            return i

        add("pe", f_la, rd=[B["lab"], b_w], wr=[b_bk[a1_], b_bk[a2_]])
        for c in range(8):
            b_ = a1_ if c < 4 else a2_
            add("act", lambda a, c=c, b_=b_: a.activation(out=av[:, c, :], in_=bk[b_][:, (c % 4) * 128:(c % 4 + 1) * 128], func=AF.Sigmoid, bias=par[:, 6 + d, c:c + 1]), rd=[b_par], wr=[b_bk[b_], B["av"]])
        for hf in range(2):
            cb = [nb(), nb(), nb()]

            def f_cum(pe, hf=hf, cb=cb):
                for cc_ in range(4):
                    c = hf * 4 + cc_
                    for x in range(3):
                        i = pe.matmul(bk[cb[x]][:, cc_ * 128:(cc_ + 1) * 128], lhsT=sig[:, c * 128:(c + 1) * 128], rhs=trc[d][:, x * 128:(x + 1) * 128], start=True, stop=True)
                return i

            add("pe", f_cum, rd=[B["sig"], b_cst], wr=[b_bk[i] for i in cb])
            hs_ = slice(hf * 4, hf * 4 + 4)
            v3 = lambda b_: bk[b_][:, :].rearrange("p (c t) -> p c t", c=4)
            add("act", lambda a, cb=cb, hs_=hs_: (a.activation(out=Ei[:, hs_, :], in_=v3(cb[0]), func=AF.Exp), a.activation(out=En[:, hs_, :], in_=v3(cb[0]), func=AF.Exp, scale=-1.0))[1], wr=[b_bk[cb[0]], B["E"]])
            add("act", lambda a, cb=cb, hs_=hs_: a.activation(out=Ee[:, hs_, :], in_=v3(cb[1]), func=AF.Exp), wr=[b_bk[cb[1]], B["E"]])
            add("act", lambda a, cb=cb, hs_=hs_: a.activation(out=Es[:, hs_, :], in_=v3(cb[2]), func=AF.Exp), wr=[b_bk[cb[2]], B["E"]])
        add("dve", lambda v: v.tensor_tensor(out=kkr[:], in0=Kx, in1=bc(0), op=ALU.mult), rd=[b_SH, b_par], wr=[B["kkr"]])
        add("act", lambda a: a.activation(out=sq[:], in_=kkr[:], func=AF.Square), rd=[B["kkr"]], wr=[B["sq"]])
        n1_, n2_ = nb(), nb()

        def f_nrm(pe):
            sqf = sq[:].rearrange("p c t -> p (c t)")
            pe.matmul(bk[n1_][:, :], lhsT=BD[:], rhs=sqf[:, 0:512], start=True, stop=True)
            return pe.matmul(bk[n2_][:, :], lhsT=BD[:], rhs=sqf[:, 512:1024], start=True, stop=True)

        add("pe", f_nrm, rd=[B["sq"], b_cst], wr=[b_bk[n1_], b_bk[n2_]])
        add("act", lambda a: a.activation(out=rn[:, 0:4, :], in_=bk[n1_][:, :].rearrange("p (c t) -> p c t", c=4), func=AF.Sqrt), wr=[b_bk[n1_], B["rn"]])
        add("act", lambda a: a.activation(out=rn[:, 4:8, :], in_=bk[n2_][:, :].rearrange("p (c t) -> p c t", c=4), func=AF.Sqrt), wr=[b_bk[n2_], B["rn"]])

        def f_kk(v):
            v.tensor_scalar(out=rn[:], in0=rn[:], scalar1=1e-12, scalar2=None, op0=ALU.max)
            v.reciprocal(out=rn[:], in_=rn[:])
            return v.tensor_tensor(out=kk[:], in0=kkr[:], in1=rn[:], op=ALU.mult)

        add("dve", f_kk, rd=[B["kkr"]], wr=[B["rn"], B["kk"]])

        def f_kd(g):
            g.tensor_tensor(out=tmp1[:], in0=av[:], in1=bc(1), op=ALU.mult)
            g.tensor_tensor(out=tmp1[:], in0=tmp1[:], in1=bc(2), op=ALU.add)
            g.tensor_tensor(out=kd[:], in0=tmp1[:], in1=Kx, op=ALU.mult)
            return g.tensor_tensor(out=bb_[:], in0=kk[:], in1=av[:], op=ALU.mult)

        add("dve", f_kd, rd=[B["av"], b_par, b_SH, B["kk"]], wr=[B["tmp1"], B["kd"], B["bb"]])

        def f_sc1(v):
            v.scalar_tensor_tensor(out=AR[:, :, 0:128], in0=kk[:], scalar=-1.0, in1=Ee[:], op0=ALU.mult, op1=ALU.mult)
            v.tensor_tensor(out=AR[:, :, 128:256], in0=R, in1=Ei[:], op=ALU.mult)
            return v.tensor_tensor(out=btT[:], in0=bb_[:], in1=En[:], op=ALU.mult)

        add("dve", f_sc1, rd=[B["kk"], B["E"], b_SH, B["bb"]], wr=[B["AR"], B["bt"]])

        add("dve", lambda g: g.tensor_tensor(out=ktT[:], in0=kd[:], in1=En[:], op=ALU.mult), rd=[B["kd"], B["E"]], wr=[B["kt"]])
        add("pool", lambda g: g.tensor_tensor(out=BgT[:], in0=bb_[:], in1=Es[:], op=ALU.mult), rd=[B["E"], B["bb"]], wr=[B["Bg"]])
        add("pool", lambda g: g.tensor_tensor(out=KgT[:], in0=kd[:], in1=Es[:], op=ALU.mult), rd=[B["kd"], B["E"]], wr=[B["Kg"]])
        add("act", lambda a: a.activation(out=vb[:], in_=Vx, func=AF.Copy), rd=[b_SH], wr=[B["vb"]])
        add("dve", lambda g: (g.tensor_tensor(out=tmp2[:], in0=R, in1=bc(3), op=ALU.mult), g.tensor_tensor(out=sq[:], in0=tmp2[:], in1=kd[:], op=ALU.mult))[1], rd=[b_SH, b_par, B["kd"]], wr=[B["tmp2"], B["sq"]])
        o1_, o2_ = nb(), nb()

        def f_bon(pe):
            sqf = sq[:].rearrange("p c t -> p (c t)")
            pe.matmul(bk[o1_][:, :], lhsT=BD[:], rhs=sqf[:, 0:512], start=True, stop=True)
            return pe.matmul(bk[o2_][:, :], lhsT=BD[:], rhs=sqf[:, 512:1024], start=True, stop=True)

        add("pe", f_bon, rd=[B["sq"], b_cst], wr=[b_bk[o1_], b_bk[o2_]])
        add("dve", lambda v: v.tensor_tensor(out=bon[:, 0:4, :], in0=bk[o1_][:, :].rearrange("p (c t) -> p c t", c=4), in1=SH[:, 16:20, :], op=ALU.mult), rd=[b_SH], wr=[b_bk[o1_], B["bon"]])
        add("dve", lambda v: v.tensor_tensor(out=bon[:, 4:8, :], in0=bk[o2_][:, :].rearrange("p (c t) -> p c t", c=4), in1=SH[:, 20:24, :], op=ALU.mult), rd=[b_SH], wr=[b_bk[o2_], B["bon"]])
        for src, srcb, dst, dstb in ((AR, "AR", Atok, "Atok"), (BgT, "Bg", Bgtok, "Bgtok"), (KgT, "Kg", Kgtok, "Kgtok"), (vb, "vb", Vtok, "Vtok")):
            tb = ntb()

            def f_tr(pe, src=src, tb=tb):
                for c in range(8):
                    i = pe.transpose(out=bt[tb][:, c * 128:(c + 1) * 128], in_=src[:, c, 0:128], identity=identb[:])
                return i

            add("pe", f_tr, rd=[B[srcb], b_identb], wr=[b_bt[tb]])
            add("act", lambda a, dst=dst, tb=tb: a.activation(out=dst[:], in_=bt[tb][:, :], func=AF.Copy), wr=[b_bt[tb], B[dstb]])
        for g4 in range(8):
            hp0 = (g4 // 2) * 2
            par_ = g4 % 2
            hs2 = [2 * hp0 + par_, 2 * (hp0 + 1) + par_]
            ba, bk_, bl = nb(), nb(), nb()
            ps_ = slice(par_ * 64, par_ * 64 + 64)

            def f_sc(pe, hs2=hs2, ba=ba, bk_=bk_, bl=bl, ps_=ps_):
                for q, h in enumerate(hs2):
                    c = h // 2
                    pe.matmul(bk[ba][:, q * 256:(q + 1) * 256], lhsT=btT[ps_, c, :], rhs=AR[ps_, c, :], start=True, stop=True)
                    pe.matmul(bk[bk_][:, q * 256:(q + 1) * 256], lhsT=ktT[ps_, c, :], rhs=AR[ps_, c, :], start=True, stop=True)
                    i = pe.matmul(bk[bl][:, q * 128:(q + 1) * 128], lhsT=AR[ps_, c, 0:128], rhs=btT[ps_, c, :], start=True, stop=True)
                return i

            add("pe", f_sc, rd=[B["AR"], B["bt"], B["kt"]], wr=[b_bk[ba], b_bk[bk_], b_bk[bl]])

            def f_ev(v, ba=ba, bk_=bk_, bl=bl, hs2=hs2):
                for q, h in enumerate(hs2):
                    v.tensor_tensor(out=AM[:, h, :], in0=bk[ba][:, q * 256:(q + 1) * 256], in1=mA[d][:, 0:256], op=ALU.mult)
                    v.tensor_tensor(out=KM[:, h, :], in0=bk[bk_][:, q * 256:(q + 1) * 256], in1=mA[d][:, 0:256], op=ALU.mult)
                    i = v.tensor_tensor(out=LL[:, h, :], in0=bk[bl][:, q * 128:(q + 1) * 128], in1=mL[d][:, 0:128], op=ALU.mult)
                return i

            gq = hs2[0] // 4
            add("dve", f_ev, rd=[b_cst], wr=[b_bk[ba], b_bk[bk_], b_bk[bl], B["AM"][gq], B["KM"][gq], B["LL"][gq]])
        add("act", lambda a: a.activation(out=Gc[:], in_=Ei[:, :, ec:ec + 1], func=AF.Copy), rd=[B["E"]], wr=[B["Gc"]])
        add("act", lambda a: a.activation(out=sgb[:], in_=SH[:, 26, :], func=AF.Sigmoid), rd=[b_SH], wr=[B["sg"]])

    def stageBC(si, d, ci, p):
        S = cfg.seqs[si]
        base = cfg.off[si]
        nch = S // 128
        t0 = base + ci * 128
        first = (ci == 0) if d == 0 else (ci == nch - 1)
        final = (d == 1)
        B = Bp[p]
        AR, Atok, Bgtok, Kgtok, Vtok, AM, KM, LL, sgb, bon, Gc = ARs[p], Atoks[p], Bgtoks[p], Kgtoks[p], Vtoks[p], AMs[p], KMs[p], LLs[p], sgbs[p], bons[p], Gcs[p]
        PL = [LL, PL12[0], PL12[1]]
        R, Kx, Vx = SH[:, 0:8, :], SH[:, 8:16, :], SH[:, 16:24, :]
        dsl = slice(d * 64, (d + 1) * 64)
        bc = lambda i: par[:, i, :].unsqueeze(2).to_broadcast([128, 8, 128])
        ec = 127 if d == 0 else 0
        for gq in range(4):
            add("dve", lambda g, gq=gq: g.tensor_tensor(out=X[0][:, gq * 4:gq * 4 + 4, :], in0=AM[:, gq * 4:gq * 4 + 4, 0:128], in1=identb[:].unsqueeze(1).to_broadcast([128, 4, 128]), op=ALU.add), rd=[B["AM"][gq], b_identb], wr=[B["X0"][gq]])
        Lcur, Acur, Xc = 0, 0, 0
        Lb = ["LL", "PL0", "PL1"]
        Ab = ["AM", "PA0", "PA1"]
        PAv = [AM[:, :, 0:128], PA[0][:], PA[1][:]]
        for lvl in range(6):
            Ln = 1 + (lvl % 2)
            An = 1 + (lvl % 2)
            Xn = 1 - Xc
            for g4 in range(4):
                hsl = slice(g4 * 4, g4 * 4 + 4)
                b1, b2 = nb(), nb()

                def f_sq(pe, g4=g4, b1=b1, b2=b2, Lc=Lcur, Ac=Acur, lvl=lvl):
                    for q in range(4):
                        h = g4 * 4 + q
                        i = pe.matmul(bk[b1][:, q * 128:(q + 1) * 128], lhsT=PAv[Ac][:, h, :], rhs=PL[Lc][:, h, :], start=True, stop=True)
                    if lvl < 5:
                        for q in range(4):
                            h = g4 * 4 + q
                            i = pe.matmul(bk[b2][:, q * 128:(q + 1) * 128], lhsT=PL[Lc][:, h, :], rhs=PAv[Ac][:, h, :], start=True, stop=True)
                    return i

                add("pe", f_sq, rd=[B[Lb[Lcur]][g4], B[Ab[Acur]][g4]], wr=[b_bk[b1], b_bk[b2]])
                add("act", lambda a, b1=b1, hsl=hsl, Ln=Ln: a.activation(out=PL[Ln][:, hsl, :], in_=bk[b1][:, :].rearrange("p (h t) -> p h t", h=4), func=AF.Copy), wr=[b_bk[b1], B[Lb[Ln]][g4]])
                if lvl < 5:
                    if g4 < 2:
                        add("act", lambda a, b2=b2, hsl=hsl, An=An: a.activation(out=PAv[An][:, hsl, :], in_=bk[b2][:, :].rearrange("p (h t) -> p h t", h=4), func=AF.Copy), wr=[b_bk[b2], B[Ab[An]][g4]])
                    else:
                        add("dve", lambda v, b2=b2, hsl=hsl, An=An: v.tensor_copy(out=PAv[An][:, hsl, :], in_=bk[b2][:, :].rearrange("p (h t) -> p h t", h=4)), wr=[b_bk[b2], B[Ab[An]][g4]])

            for g4 in range(4):
                hsl = slice(g4 * 4, g4 * 4 + 4)
                b3 = nb()

                def f_x(pe, g4=g4, b3=b3, Ln=Ln, Xc=Xc):
                    for q in range(4):
                        h = g4 * 4 + q
                        i = pe.matmul(bk[b3][:, q * 128:(q + 1) * 128], lhsT=PL[Ln][:, h, :], rhs=X[Xc][:, h, :], start=True, stop=True)
                    return i

                add("pe", f_x, rd=[B[Lb[Ln]][g4], B["X%d" % Xc][g4]], wr=[b_bk[b3]])
                add("dve", lambda v, b3=b3, hsl=hsl, Xc=Xc, Xn=Xn: v.tensor_tensor(out=X[Xn][:, hsl, :], in0=bk[b3][:, :].rearrange("p (h t) -> p h t", h=4), in1=X[Xc][:, hsl, :], op=ALU.add), rd=[B["X%d" % Xc][g4]], wr=[b_bk[b3], B["X%d" % Xn][g4]])
            Lcur, Acur, Xc = Ln, An, Xn
        XT = X[Xc]
        bX = B["X%d" % Xc]
        z1, z2 = nb(), nb()

        def f_z(pe):
            for h in range(16):
                b_ = z1 if h < 8 else z2
                i = pe.matmul(bk[b_][:, (h % 8) * 64:(h % 8 + 1) * 64], lhsT=KM[:, h, 0:128], rhs=Vtok[:, h * 64:(h + 1) * 64], start=True, stop=True)
            return i

        add("pe", f_z, rd=B["KM"] + [B["Vtok"]], wr=[b_bk[z1], b_bk[z2]])
        add("act", lambda a: a.activation(out=Zb[:, 0:512], in_=bk[z1][:, :], func=AF.Copy), wr=[b_bk[z1], B["Zb"]])
        add("act", lambda a: a.activation(out=Zb[:, 512:1024], in_=bk[z2][:, :], func=AF.Copy), wr=[b_bk[z2], B["Zb"]])
        q1, q2 = nb(), nb()

        def f_w(pe):
            for h in range(16):
                b_ = q1 if h < 8 else q2
                i = pe.matmul(bk[b_][:, (h % 8) * 64:(h % 8 + 1) * 64], lhsT=XT[:, h, :], rhs=Zb[:, h * 64:(h + 1) * 64], start=True, stop=True)
            return i

        add("pe", f_w, rd=bX + [B["Zb"]], wr=[b_bk[q1], b_bk[q2]])
        add("act", lambda a: a.activation(out=What[:, 0:512], in_=bk[q1][:, :], func=AF.Copy), wr=[b_bk[q1], B["What"]])
        add("act", lambda a: a.activation(out=What[:, 512:1024], in_=bk[q2][:, :], func=AF.Copy), wr=[b_bk[q2], B["What"]])
        for pg in range(4):
            b_ = nb()

            def f_ah(pe, pg=pg, b_=b_):
                for q in range(2):
                    hp = pg * 2 + q
                    i = pe.matmul(bk[b_][:, q * 256:(q + 1) * 256], lhsT=Atok[:, hp * 128:(hp + 1) * 128], rhs=XT[:].rearrange("p h t -> p (h t)")[:, hp * 256:(hp + 1) * 256], start=True, stop=True)
                return i

            add("pe", f_ah, rd=[B["Atok"], bX[pg]], wr=[b_bk[b_]])

            def f_ahe(v, pg=pg, b_=b_):
                v4 = bk[b_][:, :].rearrange("p (q s t) -> p q s t", q=2, s=2)
                v.tensor_copy(out=AhT[0:64, pg * 2:pg * 2 + 2, :], in_=v4[0:64, :, 0, :])
                return v.tensor_copy(out=AhT[64:128, pg * 2:pg * 2 + 2, :], in_=v4[64:128, :, 1, :])

            add("dve", f_ahe, wr=[b_bk[b_], B["AhT"]])
        if first:
            add("pool", lambda g: (g.memset(Sf[:], 0.0), g.memset(Sbf[:], 0.0))[1], wr=[B["S"], B["Sbf"]])
        ue, uo = nb(), nb()

        def f_u(pe):
            for h in range(16):
                hp, p_ = h // 2, h % 2
                i = pe.matmul(bk[(ue, uo)[p_]][:, hp * 64:(hp + 1) * 64], lhsT=AhT[p_ * 64:(p_ + 1) * 64, hp, :], rhs=Sbf[p_ * 64:(p_ + 1) * 64, hp, :], start=True, stop=True)
            return i

        add("pe", f_u, rd=[B["AhT"], B["Sbf"]], wr=[b_bk[ue], b_bk[uo]])
        Ub4 = Ub[:, :].rearrange("p (hp s v) -> p s hp v", s=2, v=64)
        Wh4 = What[:, :].rearrange("p (hp s v) -> p s hp v", s=2, v=64)
        add("dve", lambda v: (v.tensor_tensor(out=Ub4[:, 0], in0=bk[ue][:, :].rearrange("p (hp v) -> p hp v", v=64), in1=Wh4[:, 0], op=ALU.add),
                              v.tensor_tensor(out=Ub4[:, 1], in0=bk[uo][:, :].rearrange("p (hp v) -> p hp v", v=64), in1=Wh4[:, 1], op=ALU.add))[1], rd=[B["What"]], wr=[b_bk[ue], b_bk[uo], B["Ub"]])
        ye, yo_ = nb(), nb()

        def f_y(pe):
            for h in range(16):
                hp, p_ = h // 2, h % 2
                o = bk[(ye, yo_)[p_]][:, hp * 64:(hp + 1) * 64]
                pe.matmul(o, lhsT=AR[p_ * 64:(p_ + 1) * 64, hp, 128:256], rhs=Sbf[p_ * 64:(p_ + 1) * 64, hp, :], start=True, stop=False)
                pe.matmul(o, lhsT=AM[:, h, 128:256], rhs=Ub[:, h * 64:(h + 1) * 64], start=False, stop=False)
                i = pe.matmul(o, lhsT=KM[:, h, 128:256], rhs=Vtok[:, h * 64:(h + 1) * 64], start=False, stop=True)
            return i

        add("pe", f_y, rd=[B["AR"], B["Sbf"], B["Ub"], B["Vtok"]] + B["AM"] + B["KM"], wr=[b_bk[ye], b_bk[yo_]])
        Y4 = Y[:, :].rearrange("p (hp s v) -> p s hp v", s=2, v=64)
        if final:
            ph.dma(Yp[:], T["ytmp"][t0:t0 + 128, :], rd=[dB("y", t0)], wr=[B["Yp"]])
            Yp4 = Yp[:, :].rearrange("p (hp s v) -> p s hp v", s=2, v=64)
            add("dve", lambda v: (v.tensor_tensor(out=Y4[:, 0], in0=bk[ye][:, :].rearrange("p (hp v) -> p hp v", v=64), in1=Yp4[:, 0], op=ALU.add),
                                  v.tensor_tensor(out=Y4[:, 1], in0=bk[yo_][:, :].rearrange("p (hp v) -> p hp v", v=64), in1=Yp4[:, 1], op=ALU.add))[1], rd=[B["Yp"]], wr=[b_bk[ye], b_bk[yo_], B["Y"]])
        else:
            add("act", lambda a: (a.activation(out=Y4[:, 0], in_=bk[ye][:, :].rearrange("p (hp v) -> p hp v", v=64), func=AF.Copy),
                                  a.activation(out=Y4[:, 1], in_=bk[yo_][:, :].rearrange("p (hp v) -> p hp v", v=64), func=AF.Copy))[1], wr=[b_bk[ye], b_bk[yo_], B["Y"]])
            ph.dma(T["ytmp"][t0:t0 + 128, :], Y[:], rd=[B["Y"]], wr=[dB("y", t0)])
            ph.dma(T["bon"][:, t0:t0 + 128].rearrange("(c p) t -> p c t", p=128), bon[:], rd=[B["bon"]], wr=[dB("b", t0)])
        s1, s2 = nb(), nb()

        def f_s(pe):
            for hp in range(8):
                o = bk[s1 if hp < 4 else s2][:, (hp % 4) * 128:(hp % 4 + 1) * 128]
                pe.matmul(o, lhsT=Bgtok[:, hp * 128:(hp + 1) * 128], rhs=Ub[:, hp * 128:(hp + 1) * 128], start=True, stop=False)
                i = pe.matmul(o, lhsT=Kgtok[:, hp * 128:(hp + 1) * 128], rhs=Vtok[:, hp * 128:(hp + 1) * 128], start=False, stop=True)
            return i

        add("pe", f_s, rd=[B["Bgtok"], B["Ub"], B["Kgtok"], B["Vtok"]], wr=[b_bk[s1], b_bk[s2]])
        add("dve", lambda g: g.tensor_tensor(out=Stmp[:], in0=Sf[:], in1=Gc[:].to_broadcast([128, 8, 64]), op=ALU.mult), rd=[B["Gc"]], wr=[B["S"], B["Stmp"]])

        def f_su(v):
            for q, b_ in ((0, s1), (1, s2)):
                v4 = bk[b_][:, :].rearrange("p (hp s v) -> p hp s v", hp=4, s=2)
                v.tensor_tensor(out=Sf[0:64, q * 4:q * 4 + 4, :], in0=v4[0:64, :, 0, :], in1=Stmp[0:64, q * 4:q * 4 + 4, :], op=ALU.add)
                i = v.tensor_tensor(out=Sf[64:128, q * 4:q * 4 + 4, :], in0=v4[64:128, :, 1, :], in1=Stmp[64:128, q * 4:q * 4 + 4, :], op=ALU.add)
            return i

        add("dve", f_su, rd=[B["Stmp"]], wr=[b_bk[s1], b_bk[s2], B["S"]])
        add("act", lambda a: a.activation(out=Sbf[:], in_=Sf[:], func=AF.Copy), rd=[B["S"]], wr=[B["Sbf"]])
        if not final:
            return
        ph.dma(bonp[:], T["bon"][:, t0:t0 + 128].rearrange("(c p) t -> p c t", p=128), rd=[dB("b", t0)], wr=[B["bonp"]])
        Y3 = Y[:, :].rearrange("p (h v) -> p h v", v=64)

        def f_gn(v):
            v.reduce_sum(out=st[:, :, 0], in_=Y3, axis=AX.X)
            v.tensor_tensor(out=yn, in0=Y3, in1=Y3, op=ALU.mult)
            v.reduce_sum(out=st[:, :, 1], in_=yn, axis=AX.X)
            v.tensor_scalar(out=st[:, :, 0], in0=st[:, :, 0], scalar1=1.0 / 64, scalar2=None, op0=ALU.mult)
            v.tensor_tensor(out=st[:, :, 2], in0=st[:, :, 0], in1=st[:, :, 0], op=ALU.mult)
            return v.scalar_tensor_tensor(out=st[:, :, 1], in0=st[:, :, 1], scalar=1.0 / 64, in1=st[:, :, 2], op0=ALU.mult, op1=ALU.subtract)

        add("dve", f_gn, rd=[B["Y"]], wr=[B["st"], B["yn"]])
        add("act", lambda a: a.activation(out=st[:, :, 3], in_=st[:, :, 1], func=AF.Sqrt, bias=eps[:, 0:1]), rd=[b_cst], wr=[B["st"]])

        def f_gn2(v):
            v.reciprocal(out=st[:, :, 3], in_=st[:, :, 3])
            v.tensor_tensor(out=yn, in0=Y3, in1=st[:, :, 0:1].to_broadcast([128, 16, 64]), op=ALU.subtract)
            return v.tensor_tensor(out=yn, in0=yn, in1=st[:, :, 3:4].to_broadcast([128, 16, 64]), op=ALU.mult)

        add("dve", f_gn2, rd=[B["Y"]], wr=[B["st"], B["yn"]])
        f1, f2 = nb(), nb()

        def f_trf(pe):
            for c in range(8):
                i = pe.transpose(out=bk[f1 if c < 4 else f2][:, (c % 4) * 128:(c % 4 + 1) * 128], in_=yn2[:, c * 128:(c + 1) * 128], identity=identf[:])
            return i

        add("pe", f_trf, rd=[B["yn"], b_cst], wr=[b_bk[f1], b_bk[f2]])
        for c in range(8):
            b_ = f1 if c < 4 else f2
            add("act", lambda a, c=c, b_=b_: a.activation(out=fin[:, c, :], in_=bk[b_][:, (c % 4) * 128:(c % 4 + 1) * 128], func=AF.Identity, scale=par[:, 4, c:c + 1], bias=par[:, 5, c:c + 1]), rd=[b_par], wr=[b_bk[b_], B["fin"]])
        g1, g2 = nb(), nb()

        def f_g(pe):
            for c in range(8):
                i = pe.matmul(bk[g1 if c < 4 else g2][:, (c % 4) * 128:(c % 4 + 1) * 128], lhsT=gup[:, c * 128:(c + 1) * 128], rhs=sgb[:], start=True, stop=True)
            return i

        add("pe", f_g, rd=[B["sg"], b_w], wr=[b_bk[g1], b_bk[g2]])

        def f_fin(g):
            g.tensor_tensor(out=fin, in0=fin, in1=bon[:], op=ALU.add)
            return g.tensor_tensor(out=fin, in0=fin, in1=bonp[:], op=ALU.add)

        add("dve", f_fin, rd=[B["bon"], B["bonp"]], wr=[B["fin"]])
        add("dve", lambda v: (v.tensor_tensor(out=yo[:, 0:4, :], in0=bk[g1][:, :].rearrange("p (c t) -> p c t", c=4), in1=fin[:, 0:4, :], op=ALU.mult),
                              v.tensor_tensor(out=yo[:, 4:8, :], in0=bk[g2][:, :].rearrange("p (c t) -> p c t", c=4), in1=fin[:, 4:8, :], op=ALU.mult))[1], rd=[B["fin"]], wr=[b_bk[g1], b_bk[g2], B["yo"]])
        ph.dma(T["yaT"][:, t0:t0 + 128].rearrange("(c p) t -> p c t", p=128), yo, rd=[B["yo"]])

    def capture(fn, *args):
        ph.cap = []
        cnt["pool"] = ((0, 1, 2) if fn is stageA else (3, 4, 5)) if SPLITBANKS else None
        fn(*args)
        ops_, ph.cap = ph.cap, None
        return ops_

    ph.sched = True
    steps = []
    for si, S in enumerate(cfg.seqs):
        nch = S // 128
        steps += [(si, 0, ci) for ci in range(nch)]
        steps += [(si, 1, ci) for ci in reversed(range(nch))]
    curA = capture(stageA, *steps[0], 0)
    for op_ in curA:
        ph._add_cap(*op_)
    for i, stp in enumerate(steps):
        bc_ops = capture(stageBC, *stp, i % 2)
        a_ops = capture(stageA, *steps[i + 1], (i + 1) % 2) if i + 1 < len(steps) else []
        def groups(ops_):
            gs = []
            for o_ in ops_:
                if o_[0] == "pe" or not gs:
                    gs.append([])
                gs[-1].append(o_)
            return gs

        ga, gb = groups(a_ops), groups(bc_ops)
        na, nbc = len(a_ops), len(bc_ops)
        ia = ib = 0
        ja = jb = 0
        while ja < len(ga) or jb < len(gb):
            if ja < len(ga) and jb < len(gb) and not NOINTER:
                ta = ph.peek_ready(ga[ja][0][0], ga[ja][0][2], ga[ja][0][3])
                tb_ = ph.peek_ready(gb[jb][0][0], gb[jb][0][2], gb[jb][0][3])
                if abs(ta - tb_) < 300.0:
                    pick_a = ia * nbc <= ib * na
                else:
                    pick_a = ta < tb_
                if DBG_PICK is not None:
                    DBG_PICK.append(("A" if pick_a else "B", round(ta), round(tb_)))
            else:
                pick_a = jb >= len(gb) or (ja < len(ga) and NOINTER)
            if pick_a:
                for o_ in ga[ja]:
                    ph._add_cap(*o_)
                ia += len(ga[ja])
                ja += 1
            else:
                for o_ in gb[jb]:
                    ph._add_cap(*o_)
                ib += len(gb[jb])
                jb += 1
    return ph.run()

def build(cfg, debug=False, phases="1234"):
    nc = bass.Bass("TRN2", target_bir_lowering=False)
    NT = cfg.ntok
    T = {}

    def inp(name, shape):
        T[name] = nc.dram_tensor(name, list(shape), F32, kind="ExternalInput").ap()

    inp("x", [NT, D])
    inp("w_in", [D, NIN])
    for nm, n in (("norm_mix", D), ("norm_mlp", D), ("norm_final", D), ("mu_shift", NRW), ("w0", 2048), ("a0", 2048), ("k_k", D), ("k_a", D),
                  ("r_k", D), ("ln_x_w", D), ("ln_x_b", D), ("lambda_q1", 64), ("lambda_k1", 64), ("lambda_q2", 64), ("lambda_k2", 64), ("subln_w", 128)):
        inp(nm, [n])
    for nm in ("w_lora_up", "a_lora_up", "g_lora_up"):
        inp(nm, [128, 1024])
    for nm in ("proj_a", "proj_b", "w_out"):
        inp(nm, [D, D])
    inp("w_mlp_in", [D, DFF])
    inp("w_mlp_out", [DFF, D])
    inp("cst", [128, 8])
    inp("pm", [128, 128])
    kind = "ExternalOutput" if debug else "Internal"
    for nm, shp, dt in (("pr", [NRW, NT], F32), ("prs", [NRW, NT], F32), ("qk", [2048, NT], BF16), ("gt", [2048, NT], BF16), ("vv", [NT, 1024], BF16),
                        ("ytmp", [NT, D], F32), ("bon", [D, NT], F32), ("yaT", [D, NT], BF16), ("ybT", [D, NT], BF16),
                        ("hs", [NT, D], F32), ("hnT", [D, NT], BF16)):
        T[nm] = nc.dram_tensor(nm, shp, dt, kind=kind).ap()
    T["y"] = nc.dram_tensor("y", [NT, D], F32, kind="ExternalOutput").ap()
    with ExitStack() as gst, nc.semaphore("bar") as bar_sem:
        GST[0] = ({e: gst.enter_context(nc.semaphore("s_" + e)) for e in ENGS}, [gst.enter_context(nc.semaphore(f"d{i}")) for i in range(NDS)])
        bar = (bar_sem, 0)
        if "1" in phases:
            bar = phase1(nc, cfg, T, bar)
        if "2" in phases or "s" in phases:
            bar = phase15(nc, cfg, T, bar)
        if "2" in phases:
            bar = phase2(nc, cfg, T, bar)
        if "3" in phases:
            bar = phase3(nc, cfg, T, bar)
        if "4" in phases or "a" in phases:
            bar = phase4a(nc, cfg, T, bar)
        if "4" in phases or "b" in phases:
            bar = phase4b(nc, cfg, T, bar)
    return nc


def core_inputs(inputs, xs):
    cst, pm = host_consts()
    f = lambda k: np.ascontiguousarray(np.asarray(inputs[k], np.float32))
    m = {"x": np.ascontiguousarray(np.concatenate(xs, axis=0)), "w_in": f("w_in")[0], "cst": cst, "pm": pm}
    for nm in ("norm_mix", "norm_mlp", "mu_shift", "w0", "a0", "k_k", "k_a", "r_k", "ln_x_w", "ln_x_b", "lambda_q1", "lambda_k1", "lambda_q2", "lambda_k2", "subln_w"):
        m[nm] = f(nm)[0].reshape(-1)
    m["norm_final"] = f("norm_final").reshape(-1)
    m["w_lora_up"] = f("w_lora_up")[0].reshape(128, 1024)
    m["a_lora_up"] = f("a_lora_up")[0].reshape(128, 1024)
    m["g_lora_up"] = f("g_lora_up")[0].reshape(128, 1024)
    for nm in ("proj_a", "proj_b", "w_out", "w_mlp_in", "w_mlp_out"):
        m[nm] = f(nm)[0]
    return m


_NC_CACHE = {}


def kernel(**inputs):
    xp = np.asarray(inputs["x_prompt"], np.float32)
    xs = np.asarray(inputs["x_sample"], np.float32)
    n = 8
    cfg = Cfg([xp.shape[1], xp.shape[1], xs.shape[1]])
    key = tuple(cfg.seqs)
    if key not in _NC_CACHE:
        _NC_CACHE[key] = build(cfg)
    nc = _NC_CACHE[key]
    in_maps = [core_inputs(inputs, [xp[2 * c], xp[2 * c + 1], xs[c]]) for c in range(n)]
    res = run_bass_kernel_spmd(nc, in_maps, core_ids=list(range(n)))
    yp = np.empty_like(xp)
    ys = np.empty_like(xs)
    S1 = xp.shape[1]
    for c in range(n):
        y = res.results[c]["y"]
        yp[2 * c] = y[0:S1]
        yp[2 * c + 1] = y[S1:2 * S1]
        ys[c] = y[2 * S1:]
    return (yp, ys)


def host_consts():
    p = np.arange(128)
    cst = np.zeros((128, 8), np.float32)
    cst[:, 0] = p % 8
    m = ((p % 64) < 16).astype(np.float32)
    cst[:, 1] = m
    cst[:, 2] = np.where((p % 16) >= 8, 1.0, -1.0) * m
    cst[:, 3] = -math.pi
    cst[:, 4] = 1e-6
    pm = np.zeros((128, 128), np.float32)
    for mm in range(128):
        if (mm % 64) < 16:
            k = mm + 8 if (mm % 16) < 8 else mm - 8
            pm[k, mm] = 1.0
    return cst, pm
```

```python
import math
from contextlib import ExitStack
import numpy as np
import concourse.bass as bass
import concourse.mybir as mybir
from concourse.bass_utils import run_bass_kernel_spmd

F32 = mybir.dt.float32
BF16 = mybir.dt.bfloat16
I32 = mybir.dt.int32
AF = mybir.ActivationFunctionType
ALU = mybir.AluOpType
AX = mybir.AxisListType

D = 1024
NRW = 3456
NIN = 8576
DFF = 4096
ENGS = ("pe", "act", "dve", "pool", "sp")
NDS = 24


GST = [None]
NOINTER = False
SPLITBANKS = False
DBG_PICK = None
LAT = 180.0


class Buf:
    __slots__ = ("w", "r", "name")

    def __init__(self, name=""):
        self.w = None
        self.r = []
        self.name = name


class Op:
    __slots__ = ("fn", "deps", "dma", "sig", "val", "sem")

    def __init__(self, fn, deps, dma):
        self.fn = fn
        self.deps = deps
        self.dma = dma
        self.sig = False
        self.val = 0
        self.sem = None


class _Rec:
    def __init__(self):
        self.calls = []

    def __getattr__(self, name):
        def f(*a, **k):
            self.calls.append((name, a, k))
            return None

        return f


class Phase:
    def __init__(self, nc, name, bar):
        self.nc = nc
        self.name = name
        self.bar = bar
        self.ops = {e: [] for e in ENGS}
        self.stack = ExitStack()
        self.ndma = 0
        self.dma_ids = []
        self.nps = 0
        self.sched = False
        self.est_end = {}
        self.eng_free = {}

    def sb(self, name, shape, dt):
        return self.stack.enter_context(self.nc.sbuf_tensor(self.name + "_" + name, list(shape), dt))

    def ps(self, name, shape, dt=F32):
        return self.stack.enter_context(self.nc.psum_tensor(self.name + "_" + name, list(shape), dt))

    cap = None

    def _add_cap(self, eng, fn, rd, wr, dma):
        return self.add(eng, fn, rd, wr, dma)

    def add(self, eng, fn, rd=(), wr=(), dma=False):
        if self.cap is not None:
            self.cap.append((eng, fn, tuple(rd), tuple(wr), dma))
            return None
        if eng != "pe" and not dma:
            rec = _Rec()
            fn(rec)
            me = None
            for name, a, k in rec.calls:
                me = self._add1(eng, (lambda e, name=name, a=a, k=k: getattr(e, name)(*a, **k)), rd, wr, False)
            return me
        return self._add1(eng, fn, rd, wr, dma)

    def est_dur(self, eng, fn, dma):
        rec = _Rec()
        try:
            fn(rec)
        except Exception:
            return 500.0
        tot = 0.0
        for name, a, k in rec.calls:
            o = k.get("out", a[0] if a else None)
            try:
                n = 1
                for d_ in o.shape[1:]:
                    n *= d_
            except Exception:
                n = 128
            if dma:
                tot += 2000.0 + n * 0.5
            elif eng == "pe":
                tot += (100.0 if name == "transpose" else max(64, n) / 2.4 + 8.0)
            elif eng == "act":
                tot += 230.0 + 0.83 * n
            elif eng == "dve":
                tot += 130.0 + 0.95 * n
            else:
                tot += 250.0 + 6.5 * n
        return tot

    def peek_ready(self, eng, rd, wr):
        t = self.eng_free.get(eng, 0.0)
        for b in list(rd) + list(wr):
            if b.w is not None:
                t = max(t, self.est_end.get(b.w, 0.0) + (0.0 if b.w[0] == eng == "pe" else LAT))
        for b in wr:
            for r_ in b.r:
                t = max(t, self.est_end.get(r_, 0.0) + LAT)
        return t

    def _add1(self, eng, fn, rd=(), wr=(), dma=False):
        ops = self.ops[eng]
        me = (eng, len(ops))
        if self.sched:
            st_ = self.peek_ready(eng, rd, wr)
            en_ = st_ + self.est_dur(eng, fn, dma)
            self.est_end[me] = en_
            if not dma:
                self.eng_free[eng] = en_
            else:
                self.eng_free[eng] = st_ + 60.0
        deps = set()
        for b in rd:
            if b.w is not None:
                deps.add(b.w)
        for b in wr:
            if b.w is not None:
                deps.add(b.w)
            deps.update(b.r)
        deps.discard(me)
        op = Op(fn, deps, dma)
        if dma:
            k = self.ndma
            self.ndma += 1
            op.sem = k % NDS
            op.val = 16 * (k // NDS + 1)
            if k >= NDS:
                deps.add(self.dma_ids[k - NDS])
            self.dma_ids.append(me)
        ops.append(op)
        for b in rd:
            b.r.append(me)
        for b in wr:
            b.w = me
            b.r = []
        return me

    def dma(self, out, in_, rd=(), wr=(), q="sp", **kw):
        return self.add(q, lambda e: e.dma_start(out=out, in_=in_, **kw), rd, wr, dma=True)

    def run(self):
        nc = self.nc
        ops = self.ops
        bar_sem, bar_val = self.bar
        fin = set(self.dma_ids)
        for e in ENGS:
            if e != "sp" and ops[e]:
                fin.add((e, len(ops[e]) - 1))
        sems, dsems = GST[0]

        def f_bar(e):
            for sm in list(sems.values()) + list(dsems):
                e.sem_clear(sm)
            return e.sem_inc(bar_sem, 1)

        ops["sp"].append(Op(f_bar, fin, False))
        for e in ENGS:
            for op in ops[e]:
                for d in op.deps:
                    ops[d[0]][d[1]].sig = True
        if True:
            for e in ENGS:
                c = 0
                for op in ops[e]:
                    if op.dma:
                        op.sem = dsems[op.sem]
                    elif op.sig:
                        c += 1
                        op.val = c
                        op.sem = sems[e]

            def emit(ename, eng):
                known = {}
                if bar_val > 0:
                    eng.wait_ge(bar_sem, bar_val)
                for op in ops[ename]:
                    need = {}
                    for d in op.deps:
                        dop = ops[d[0]][d[1]]
                        if ename == "pe" and d[0] == "pe" and not dop.dma:
                            continue
                        key = id(dop.sem)
                        if known.get(key, 0) < dop.val and need.get(key, (None, 0))[1] < dop.val:
                            need[key] = (dop.sem, dop.val)
                    for key, (sem, val) in need.items():
                        eng.wait_ge(sem, val)
                        known[key] = val
                    inst = op.fn(eng)
                    if op.dma:
                        inst.then_inc(op.sem, 16)
                    elif op.sig:
                        inst.then_inc(op.sem, 1)

            with nc.Block() as block:
                @block.sync
                def _(e):
                    emit("sp", e)

                @block.scalar
                def _(e):
                    emit("act", e)

                @block.vector
                def _(e):
                    emit("dve", e)

                @block.gpsimd
                def _(e):
                    emit("pool", e)

                @block.tensor
                def _(e):
                    emit("pe", e)
        self.stack.close()
        return (bar_sem, bar_val + 1)


def make_ident(ph, dt=BF16, n=128, name="ident"):
    f = ph.sb(name + "_f", [128, n], F32)
    o = ph.sb(name, [128, n], dt)
    b = Buf(name)

    def fn(g):
        g.memset(f[:], 0.0)
        g.affine_select(out=f[:], in_=f[:], compare_op=ALU.not_equal, fill=1.0, base=0, pattern=[[-1, n]], channel_multiplier=1)
        return g.tensor_copy(out=o[:], in_=f[:])

    ph.add("pool", fn, wr=[b])
    return o, b


class Cfg:
    def __init__(self, seqs):
        self.seqs = list(seqs)
        self.off = [0]
        for s in self.seqs:
            self.off.append(self.off[-1] + s)
        self.ntok = self.off[-1]
        self.smax = max(self.seqs)


def phase1(nc, cfg, T, bar):
    ph = Phase(nc, "p1", bar)
    NT = cfg.ntok
    NB = NT // 512
    ident, b_ident = make_ident(ph)
    xnT = ph.sb("xnT", [128, 8, NT], BF16)
    b_xnT = [Buf() for _ in range(NT // 128)]
    gcol = ph.sb("gcol", [128, 8], F32)
    b_g = Buf()
    ph.dma(gcol[:], T["norm_mix"].rearrange("(kc p) -> p kc", p=128), wr=[b_g], allow_slow_non_contiguous=True)
    banks = [ph.ps(f"bk{i}", [128, 512], F32) for i in range(6)]
    b_bank = [Buf() for _ in range(6)]
    pst = [ph.ps(f"pt{i}", [128, 1024], BF16) for i in range(2)]
    b_pst = [Buf(), Buf()]

    SM = cfg.smax
    Ct = ph.sb("Ct", [128, SM], F32)
    St = ph.sb("St", [128, SM], F32)
    b_rope = Buf()
    Pm = ph.sb("Pm", [128, 128], BF16)
    SEG = 512
    with_tmp = ph.sb("rtmp", [128, SEG], F32)
    posf = ph.sb("posf", [128, SEG], F32)
    cols = ph.sb("rcols", [128, 8], F32)
    pmf = ph.sb("pmf", [128, 128], F32)
    posi = ph.sb("posi", [128, SEG], I32)
    ki = ph.sb("rki", [128, SEG], I32)
    b_r0 = Buf()
    b_seg = Buf()
    ph.dma(cols[:], T["cst"][:, :], wr=[b_r0])
    ph.dma(pmf[:], T["pm"][:, :], wr=[b_r0])
    ph.add("pool", lambda g: g.tensor_copy(out=Pm[:], in_=pmf[:]), rd=[b_r0], wr=[b_r0])
    ph.add("act", lambda a: a.activation(out=cols[:, 7:8], in_=cols[:, 0:1], func=AF.Exp, scale=-math.log(500000.0) / 8.0), rd=[b_r0], wr=[b_r0])
    TWO_PI = 2.0 * math.pi
    for sg in range(SM // SEG):
        ss_ = slice(sg * SEG, (sg + 1) * SEG)

        def f_pos(g, sg=sg):
            g.iota(posi[:], pattern=[[1, SEG]], base=sg * SEG, channel_multiplier=0)
            return g.tensor_copy(out=posf[:], in_=posi[:])

        ph.add("pool", f_pos, wr=[b_seg])

        def reduce_fn(v, dst, shift):
            v.tensor_scalar(out=dst, in0=posf[:], scalar1=cols[:, 7:8], scalar2=shift, op0=ALU.mult, op1=ALU.add)
            v.tensor_scalar(out=ki[:], in0=dst, scalar1=1.0 / TWO_PI, scalar2=None, op0=ALU.mult)
            v.tensor_copy(out=with_tmp[:], in_=ki[:])
            v.scalar_tensor_tensor(out=dst, in0=with_tmp[:], scalar=-TWO_PI, in1=dst, op0=ALU.mult, op1=ALU.add)
            v.tensor_scalar(out=with_tmp[:], in0=dst, scalar1=math.pi, scalar2=TWO_PI, op0=ALU.is_gt, op1=ALU.mult)
            v.tensor_tensor(out=dst, in0=dst, in1=with_tmp[:], op=ALU.subtract)
            v.tensor_scalar(out=with_tmp[:], in0=dst, scalar1=-math.pi, scalar2=TWO_PI, op0=ALU.is_lt, op1=ALU.mult)
            return v.tensor_tensor(out=dst, in0=dst, in1=with_tmp[:], op=ALU.add)

        def rope_fn2(v, ss_=ss_):
            reduce_fn(v, St[:, ss_], 0.0)
            return reduce_fn(v, Ct[:, ss_], 0.5 * math.pi)

        ph.add("dve", rope_fn2, rd=[b_r0], wr=[b_seg, b_rope])

        def rope_fn3(a, ss_=ss_):
            a.activation(out=St[:, ss_], in_=St[:, ss_], func=AF.Sin)
            return a.activation(out=Ct[:, ss_], in_=Ct[:, ss_], func=AF.Sin)

        ph.add("act", rope_fn3, rd=[], wr=[b_rope])

        def rope_fn4(v, ss_=ss_):
            v.tensor_scalar(out=St[:, ss_], in0=St[:, ss_], scalar1=cols[:, 2:3], scalar2=None, op0=ALU.mult)
            v.tensor_scalar(out=Ct[:, ss_], in0=Ct[:, ss_], scalar1=-1.0, scalar2=None, op0=ALU.add)
            return v.tensor_scalar(out=Ct[:, ss_], in0=Ct[:, ss_], scalar1=cols[:, 1:2], scalar2=1.0, op0=ALU.mult, op1=ALU.add)

        ph.add("dve", rope_fn4, rd=[b_r0], wr=[b_rope])

    xt = [ph.sb(f"xt{i}", [128, D], F32) for i in range(2)]
    b_xt = [Buf(), Buf()]
    xnb = [ph.sb(f"xnb{i}", [128, D], BF16) for i in range(2)]
    b_xnb = [Buf(), Buf()]
    ss = ph.sb("ss", [128, 2, 2], F32)
    b_ss = [Buf(), Buf()]
    xin = T["x"]
    for t in range(NT // 128):
        s = t % 2
        ph.dma(xt[s][:], xin[t * 128:(t + 1) * 128, :], wr=[b_xt[s]])
        ph.add("act", lambda a, s=s: a.activation(out=xnb[s][:], in_=xt[s][:], func=AF.Square, accum_out=ss[:, s, 0:1]), rd=[b_xt[s]], wr=[b_xnb[s], b_ss[s]])

        ph.add("act", lambda a, s=s: a.activation(out=ss[:, s, 1:2], in_=ss[:, s, 0:1], func=AF.Sqrt, scale=1.0 / D, bias=cols[:, 4:5]), rd=[b_r0], wr=[b_ss[s]])
        ph.add("dve", lambda v, s=s: v.reciprocal(out=ss[:, s, 1:2], in_=ss[:, s, 1:2]), rd=[], wr=[b_ss[s]])
        ph.add("act", lambda a, s=s: a.activation(out=xnb[s][:], in_=xt[s][:], func=AF.Copy, scale=ss[:, s, 1:2]), rd=[b_xt[s], b_ss[s]], wr=[b_xnb[s]])

        def f_tr(pe, s=s):
            for kc in range(8):
                i = pe.transpose(out=pst[s][:, kc * 128:(kc + 1) * 128], in_=xnb[s][:, kc * 128:(kc + 1) * 128], identity=ident[:])
            return i

        ph.add("pe", f_tr, rd=[b_xnb[s], b_ident], wr=[b_pst[s]])
        ph.add("dve", lambda v, s=s, t=t: v.tensor_copy(out=xnT[:, :, t * 128:(t + 1) * 128], in_=pst[s][:, :].rearrange("p (k t) -> p k t", k=8)), rd=[], wr=[b_pst[s], b_xnT[t]])

    wf = [ph.sb(f"wf{i}", [128, 8, 128], F32) for i in range(2)]
    b_wf = [Buf(), Buf()]
    wb = [ph.sb(f"wb{i}", [128, 8, 128], BF16) for i in range(2)]
    b_wb = [Buf(), Buf()]
    stg = [ph.sb(f"stg{i}", [128, 512], F32) for i in range(3)]
    b_stg = [Buf() for _ in range(3)]
    stgb = [ph.sb(f"stgb{i}", [128, 512], BF16) for i in range(3)]
    b_stgb = [Buf() for _ in range(3)]
    qraw = [ph.sb("qraw0", [128, 512], BF16)] * 2
    b_qraw = [Buf()] * 2
    t1 = [ph.sb("t1_0", [128, 512], F32)] * 2
    b_t1 = [Buf()] * 2
    t2 = [ph.sb("t2_0", [128, 512], F32)] * 2
    b_t2 = [Buf()] * 2
    w_in = T["w_in"]
    chunks = []
    for c in range(27):
        chunks.append(("rw", c * 128, c * 128))
    for c in range(16):
        chunks.append(("qk", NRW + c * 128, c * 128))
    for c in range(16):
        chunks.append(("gt", NRW + 3072 + c * 128, c * 128))
    for c in range(8):
        chunks.append(("vv", NRW + 2048 + c * 128, c * 128))
    cnt = dict(bank=0, stg=0, stgb=0, q=0)
    blk_pos = []
    for si, S in enumerate(cfg.seqs):
        for j in range(S // 512):
            blk_pos.append(j * 512)
    for ci, (kind, c0, r0) in enumerate(chunks):
        s = ci % 2
        ph.dma(wf[s][:], w_in[:, c0:c0 + 128].rearrange("(kc p) c -> p kc c", p=128), wr=[b_wf[s]])
        ph.add("pool", lambda g, s=s: g.tensor_tensor(out=wb[s][:], in0=wf[s][:], in1=gcol[:].unsqueeze(2).to_broadcast([128, 8, 128]), op=ALU.mult), rd=[b_wf[s], b_g], wr=[b_wb[s]])
        for b in range(NB):
            bk = cnt["bank"] % 4
            cnt["bank"] += 1

            def f_mm(pe, s=s, b=b, bk=bk, kind=kind):
                if kind == "vv":
                    for j in range(4):
                        for kc in range(8):
                            i = pe.matmul(banks[bk][:, j * 128:(j + 1) * 128], lhsT=xnT[:, kc, b * 512 + j * 128:b * 512 + (j + 1) * 128], rhs=wb[s][:, kc, :], start=(kc == 0), stop=(kc == 7))
                    return i
                for kc in range(8):
                    i = pe.matmul(banks[bk][:, :], lhsT=wb[s][:, kc, :], rhs=xnT[:, kc, b * 512:(b + 1) * 512], start=(kc == 0), stop=(kc == 7))
                return i

            ph.add("pe", f_mm, rd=[b_wb[s]] + b_xnT[b * 4:(b + 1) * 4], wr=[b_bank[bk]])
            tok = slice(b * 512, (b + 1) * 512)
            if kind == "rw":
                g_ = cnt["stg"] % 3
                cnt["stg"] += 1
                ph.add("act", lambda a, bk=bk, g_=g_: a.activation(out=stg[g_][:], in_=banks[bk][:, :], func=AF.Copy), wr=[b_bank[bk], b_stg[g_]])
                ph.dma(T["pr"][r0:r0 + 128, tok], stg[g_][:], rd=[b_stg[g_]])
            elif kind == "vv":
                g_ = cnt["stgb"] % 3
                cnt["stgb"] += 1
                ph.add("act", lambda a, bk=bk, g_=g_: a.activation(out=stgb[g_][:], in_=banks[bk][:, :], func=AF.Copy), wr=[b_bank[bk], b_stgb[g_]])
                ph.dma(T["vv"][tok, r0:r0 + 128].rearrange("(j p) c -> p j c", p=128), stgb[g_][:].rearrange("p (j c) -> p j c", j=4), rd=[b_stgb[g_]])
            elif kind == "gt":
                g_ = cnt["stgb"] % 3
                cnt["stgb"] += 1
                ph.add("act", lambda a, bk=bk, g_=g_: a.activation(out=stgb[g_][:], in_=banks[bk][:, :], func=AF.Sigmoid), wr=[b_bank[bk], b_stgb[g_]])
                ph.dma(T["gt"][r0:r0 + 128, tok], stgb[g_][:], rd=[b_stgb[g_]])
            else:
                q_ = cnt["q"] % 2
                cnt["q"] += 1
                g_ = cnt["stgb"] % 3
                cnt["stgb"] += 1
                p0 = blk_pos[b]
                ph.add("act", lambda a, bk=bk, q_=q_: a.activation(out=qraw[q_][:], in_=banks[bk][:, :], func=AF.Copy), wr=[b_bank[bk], b_qraw[q_]])
                ph.add("dve", lambda v, bk=bk, q_=q_, p0=p0: v.tensor_tensor(out=t1[q_][:], in0=banks[bk][:, :], in1=Ct[:, p0:p0 + 512], op=ALU.mult), rd=[b_rope], wr=[b_bank[bk], b_t1[q_]])
                pb = 4 + (cnt["q"] % 2)
                ph.add("pe", lambda pe, q_=q_, pb=pb: pe.matmul(banks[pb][:, :], lhsT=Pm[:], rhs=qraw[q_][:], start=True, stop=True), rd=[b_qraw[q_], b_r0], wr=[b_bank[pb]])
                ph.add("dve", lambda v, pb=pb, q_=q_, p0=p0: v.tensor_tensor(out=t2[q_][:], in0=banks[pb][:, :], in1=St[:, p0:p0 + 512], op=ALU.mult), rd=[b_rope], wr=[b_bank[pb], b_t2[q_]])
                ph.add("dve", lambda g, q_=q_, g_=g_: g.tensor_tensor(out=stgb[g_][:], in0=t1[q_][:], in1=t2[q_][:], op=ALU.add), rd=[b_t1[q_], b_t2[q_]], wr=[b_stgb[g_]])
                ph.dma(T["qk"][r0:r0 + 128, tok], stgb[g_][:], rd=[b_stgb[g_]])

    return ph.run()


def bcast_rows(ap_1d, n):
    return ap_1d.partition_broadcast(128)


def phase3(nc, cfg, T, bar):
    ph = Phase(nc, "p3", bar)
    SM = cfg.smax
    lam_init = 0.8 - 0.6 * math.exp(-0.3 * 0)
    lv = ph.sb("lv", [128, 4, 64], F32)
    b_lv = Buf()
    for i, nm in enumerate(("lambda_q1", "lambda_k1", "lambda_q2", "lambda_k2")):
        ph.dma(lv[:, i, :], T[nm].partition_broadcast(128), wr=[b_lv])
    cc = ph.sb("cc", [128, 8], F32)
    b_cc = Buf()
    ph.dma(cc[:, 3:4], T["subln_w"].rearrange("(p o) -> p o", o=1), wr=[b_cc])
    lt = ph.sb("ltmp", [128, 2, 64], F32)
    ones = ph.sb("ones", [128, 128], BF16)

    def f_c(v):
        v.memset(ones[:], 1.0)
        v.memset(cc[:, 4:5], 1e-5)
        v.tensor_tensor(out=lt[:, 0, :], in0=lv[:, 0, :], in1=lv[:, 1, :], op=ALU.mult)
        v.tensor_tensor(out=lt[:, 1, :], in0=lv[:, 2, :], in1=lv[:, 3, :], op=ALU.mult)
        v.reduce_sum(out=cc[:, 0:2], in_=lt[:], axis=AX.X)
        return v.tensor_scalar(out=cc[:, 3:4], in0=cc[:, 3:4], scalar1=1.0 - lam_init, scalar2=None, op0=ALU.mult)

    ph.add("dve", f_c, rd=[b_lv], wr=[b_cc])
    ph.add("act", lambda a: a.activation(out=cc[:, 0:2], in_=cc[:, 0:2], func=AF.Exp), wr=[b_cc])

    def f_c2(v):
        v.tensor_tensor(out=cc[:, 2:3], in0=cc[:, 1:2], in1=cc[:, 0:1], op=ALU.subtract)
        return v.tensor_scalar(out=cc[:, 2:3], in0=cc[:, 2:3], scalar1=-lam_init, scalar2=None, op0=ALU.add)

    ph.add("dve", f_c2, wr=[b_cc])

    qT = [ph.sb(f"qT{i}", [128, SM], BF16) for i in range(2)]
    kT = [ph.sb(f"kT{i}", [128, SM], BF16) for i in range(2)]
    Vt = [ph.sb(f"Vt{i}", [128, SM // 128, 128], BF16) for i in range(2)]
    b_in = [Buf(), Buf()]
    bS = [ph.ps(f"bS{i}", [128, 512], F32) for i in range(4)]
    b_bS = [Buf() for _ in range(4)]
    bO = [ph.ps(f"bO{i}", [128, 512], F32) for i in range(4)]
    b_bO = [Buf() for _ in range(4)]
    Pt = [ph.sb(f"Pt{i}", [128, 512], BF16) for i in range(4)]
    b_Pt = [Buf() for _ in range(4)]
    rz = [ph.sb(f"rz{i}", [128, 512], F32) for i in range(2)]
    oo = [ph.sb(f"oo{i}", [128, 512], F32) for i in range(2)]
    sq = ph.sb("sq", [128, 512], BF16)
    rs = ph.sb("rs", [128, 512], F32)
    yb = [ph.sb(f"yb{i}", [128, 512], BF16) for i in range(2)]
    b_ep = Buf()
    b_yb = [Buf(), Buf()]
    blocks = []
    it = 0
    for si, S in enumerate(cfg.seqs):
        for h in range(8):
            u = it % 2
            it += 1
            for qb in range(S // 512):
                for kt in range(S // 128):
                    blocks.append((si, h, u, qb, kt))
    state = dict(ne=0)

    def emit_load(si, h, u):
        S = cfg.seqs[si]
        t0 = cfg.off[si]
        ph.dma(qT[u][:, 0:S], T["qk"][h * 128:(h + 1) * 128, t0:t0 + S], wr=[b_in[u]])
        ph.dma(kT[u][:, 0:S], T["qk"][1024 + h * 128:1024 + (h + 1) * 128, t0:t0 + S], wr=[b_in[u]])
        ph.dma(Vt[u][:, 0:S // 128, :], T["vv"][t0:t0 + S, h * 128:(h + 1) * 128].rearrange("(kt p) v -> p kt v", p=128), wr=[b_in[u]])

    def emit_scores(n):
        si, h, u, qb, kt = blocks[n]
        w = (n % 2) * 2
        ks = slice(kt * 128, (kt + 1) * 128)
        qs = slice(qb * 512, (qb + 1) * 512)

        def f_s(pe):
            pe.matmul(bS[w][:, :], lhsT=kT[u][0:64, ks], rhs=qT[u][0:64, qs], start=True, stop=True)
            return pe.matmul(bS[w + 1][:, :], lhsT=kT[u][64:128, ks], rhs=qT[u][64:128, qs], start=True, stop=True)

        ph.add("pe", f_s, rd=[b_in[u]], wr=[b_bS[w], b_bS[w + 1]])
        ph.add("act", lambda a: a.activation(out=Pt[w][:], in_=bS[w][:, :], func=AF.Exp, scale=0.125), wr=[b_bS[w], b_Pt[w]])
        ph.add("act", lambda a: a.activation(out=Pt[w + 1][:], in_=bS[w + 1][:, :], func=AF.Exp, scale=0.125), wr=[b_bS[w + 1], b_Pt[w + 1]])

    def emit_pv(n):
        si, h, u, qb, kt = blocks[n]
        S = cfg.seqs[si]
        t0 = cfg.off[si]
        nkt = S // 128
        w = (n % 2) * 2

        def f_pv(pe):
            st, sp_ = (kt == 0), (kt == nkt - 1)
            pe.matmul(bO[0][:, :], lhsT=Vt[u][:, kt, :], rhs=Pt[w][:], start=st, stop=sp_)
            pe.matmul(bO[2][:, :], lhsT=ones[:], rhs=Pt[w][:], start=st, stop=sp_)
            pe.matmul(bO[1][:, :], lhsT=Vt[u][:, kt, :], rhs=Pt[w + 1][:], start=st, stop=sp_)
            return pe.matmul(bO[3][:, :], lhsT=ones[:], rhs=Pt[w + 1][:], start=st, stop=sp_)

        ph.add("pe", f_pv, rd=[b_in[u], b_Pt[w], b_Pt[w + 1], b_cc], wr=b_bO)
        if kt != nkt - 1:
            return
        e = state["ne"] % 2
        state["ne"] += 1

        def f_e0(a):
            a.activation(out=rz[0][:], in_=bO[2][:, :], func=AF.Ln)
            a.activation(out=rz[1][:], in_=bO[3][:, :], func=AF.Ln)
            a.activation(out=rz[0][:], in_=rz[0][:], func=AF.Exp, scale=-1.0)
            return a.activation(out=rz[1][:], in_=rz[1][:], func=AF.Exp, scale=-1.0)

        ph.add("act", f_e0, wr=[b_bO[2], b_bO[3], b_ep])

        def f_e1(v):
            v.tensor_tensor(out=oo[0][:], in0=bO[0][:, :], in1=rz[0][:], op=ALU.mult)
            v.tensor_tensor(out=oo[1][:], in0=bO[1][:, :], in1=rz[1][:], op=ALU.mult)
            return v.scalar_tensor_tensor(out=oo[0][:], in0=oo[1][:], scalar=cc[:, 2:3], in1=oo[0][:], op0=ALU.mult, op1=ALU.add)

        ph.add("dve", f_e1, rd=[b_cc], wr=b_bO + [b_ep])
        ph.add("act", lambda a: a.activation(out=sq[:], in_=oo[0][:], func=AF.Square), wr=[b_ep])
        ph.add("pe", lambda pe: pe.matmul(bS[w][:, :], lhsT=ones[:], rhs=sq[:], start=True, stop=True), rd=[b_ep], wr=[b_bS[w]])
        ph.add("act", lambda a: (a.activation(out=rs[:], in_=bS[w][:, :], func=AF.Ln, scale=1.0 / 128.0, bias=cc[:, 4:5]),
                                 a.activation(out=rs[:], in_=rs[:], func=AF.Exp, scale=-0.5))[1], rd=[b_cc], wr=[b_bS[w], b_ep])
        ph.add("dve", lambda g: g.scalar_tensor_tensor(out=yb[e][:], in0=oo[0][:], scalar=cc[:, 3:4], in1=rs[:], op0=ALU.mult, op1=ALU.mult), rd=[b_ep, b_cc], wr=[b_yb[e]])
        ph.dma(T["ybT"][h * 128:(h + 1) * 128, t0 + qb * 512:t0 + (qb + 1) * 512], yb[e][:], rd=[b_yb[e]])

    heads = []
    for bl in blocks:
        if not heads or heads[-1] != bl[0:3]:
            heads.append(bl[0:3])
    for hd in heads[0:2]:
        emit_load(*hd)
    emit_scores(0)
    hj = 0
    for n in range(len(blocks)):
        if n + 1 < len(blocks):
            emit_scores(n + 1)
        emit_pv(n)
        if n + 1 == len(blocks) or blocks[n + 1][0:3] != blocks[n][0:3]:
            if hj + 2 < len(heads):
                emit_load(*heads[hj + 2])
            hj += 1
    return ph.run()


def load_w_bf16(ph, dst, b_dst, w_ap, nk, ncols, stage, b_stage, scale_col=None, b_scale=None, eng="pool"):
    step = 512
    kst = stage.shape[1]
    for c0 in range(0, ncols, step):
        for k0 in range(0, nk, kst):
            k1 = min(nk, k0 + kst)
            ph.dma(stage[:, 0:k1 - k0, :], w_ap[k0 * 128:k1 * 128, c0:c0 + step].rearrange("(kc p) c -> p kc c", p=128), wr=[b_stage])
            if scale_col is None:
                ph.add(eng, lambda g, k0=k0, k1=k1, c0=c0: g.tensor_copy(out=dst[:, k0:k1, c0:c0 + step], in_=stage[:, 0:k1 - k0, :]), rd=[b_stage], wr=[b_dst])
            else:
                ph.add(eng, lambda g, k0=k0, k1=k1, c0=c0: g.tensor_tensor(out=dst[:, k0:k1, c0:c0 + step], in0=stage[:, 0:k1 - k0, :], in1=scale_col[:, k0:k1].unsqueeze(2).to_broadcast([128, k1 - k0, step]), op=ALU.mult), rd=[b_stage, b_scale], wr=[b_dst])


def phase4a(nc, cfg, T, bar):
    ph = Phase(nc, "p4a", bar)
    NT = cfg.ntok
    ident, b_ident = make_ident(ph)
    stage = ph.sb("stage", [128, 8, 512], F32)
    b_stage = Buf()
    W = {}
    bW = {}
    for nm in ("proj_a", "proj_b", "w_out"):
        W[nm] = ph.sb("w_" + nm, [128, 8, 1024], BF16)
        bW[nm] = Buf()
        load_w_bf16(ph, W[nm], bW[nm], T[nm], 8, 1024, stage, b_stage)
    cst = ph.sb("cst", [128, 1], F32)
    b_cst = Buf()
    ph.add("pool", lambda g: g.memset(cst[:], 1e-6), wr=[b_cst])
    banks = [ph.ps(f"bk{i}", [128, 512], F32) for i in range(6)]
    b_bank = [Buf() for _ in range(6)]
    pst = ph.ps("pt", [128, 1024], BF16)
    b_pst = Buf()
    ya = ph.sb("ya", [128, 8, 512], BF16)
    ybb = ph.sb("ybb", [128, 8, 512], BF16)
    ga = ph.sb("ga", [128, 8, 512], BF16)
    gb = ph.sb("gb", [128, 8, 512], BF16)
    b_ld = Buf()
    mg = ph.sb("mg", [128, 8, 512], BF16)
    b_mg = Buf()
    m1 = [ph.sb(f"m1_{i}", [128, 512], F32) for i in range(2)]
    m2 = [ph.sb(f"m2_{i}", [128, 512], F32) for i in range(2)]
    b_m = [Buf(), Buf()]
    xt = [ph.sb(f"xt{i}", [128, D], F32) for i in range(2)]
    b_xt = [Buf(), Buf()]
    hh = [ph.sb(f"hh{i}", [128, D], F32) for i in range(2)]
    b_hh = [Buf(), Buf()]
    hb = [ph.sb(f"hb{i}", [128, D], BF16) for i in range(2)]
    b_hb = [Buf(), Buf()]
    hT = [ph.sb(f"hT{i}", [128, 8, 128], BF16) for i in range(2)]
    b_hT = [Buf(), Buf()]
    junk = ph.sb("junk", [128, D], BF16)
    b_junk = Buf()
    ss = ph.sb("ss", [128, 2, 2], F32)
    b_ss = [Buf(), Buf()]
    nb = 0
    nm_ = 0
    nt_ = 0
    for b in range(NT // 512):
        tok = slice(b * 512, (b + 1) * 512)
        for dst, src, r0 in ((ya, "yaT", 0), (ybb, "ybT", 0), (ga, "gt", 0), (gb, "gt", 1024)):
            ph.dma(dst[:], T[src][r0:r0 + 1024, tok].rearrange("(kc p) t -> p kc t", p=128), wr=[b_ld])
        for oc in range(8):
            ba, bb = nb % 6, (nb + 1) % 6
            nb += 2

            def f_ab(pe, oc=oc, ba=ba, bb=bb):
                for kc in range(8):
                    pe.matmul(banks[ba][:, :], lhsT=W["proj_a"][:, kc, oc * 128:(oc + 1) * 128], rhs=ya[:, kc, :], start=(kc == 0), stop=(kc == 7))
                for kc in range(8):
                    i = pe.matmul(banks[bb][:, :], lhsT=W["proj_b"][:, kc, oc * 128:(oc + 1) * 128], rhs=ybb[:, kc, :], start=(kc == 0), stop=(kc == 7))
                return i

            ph.add("pe", f_ab, rd=[b_ld, bW["proj_a"], bW["proj_b"]], wr=[b_bank[ba], b_bank[bb]])
            m = nm_ % 2
            nm_ += 1

            def f_m(v, oc=oc, ba=ba, bb=bb, m=m):
                v.tensor_tensor(out=m1[m][:], in0=banks[ba][:, :], in1=ga[:, oc, :], op=ALU.mult)
                return v.tensor_tensor(out=m2[m][:], in0=banks[bb][:, :], in1=gb[:, oc, :], op=ALU.mult)

            ph.add("dve", f_m, rd=[b_ld], wr=[b_bank[ba], b_bank[bb], b_m[m]])
            ph.add("dve", lambda g, oc=oc, m=m: g.tensor_tensor(out=mg[:, oc, :], in0=m1[m][:], in1=m2[m][:], op=ALU.add), rd=[b_m[m]], wr=[b_mg])
        for ti in range(4):
            t = b * 4 + ti
            s_ = nt_ % 2
            nt_ += 1
            ph.dma(xt[s_][:], T["x"][t * 128:(t + 1) * 128, :], wr=[b_xt[s_]])
            ba, bb = nb % 6, (nb + 1) % 6
            nb += 2

            def f_o(pe, ti=ti, ba=ba, bb=bb):
                for hf, bk in ((0, ba), (1, bb)):
                    for kc in range(8):
                        i = pe.matmul(banks[bk][:, :], lhsT=mg[:, kc, ti * 128:(ti + 1) * 128], rhs=W["w_out"][:, kc, hf * 512:(hf + 1) * 512], start=(kc == 0), stop=(kc == 7))
                return i

            ph.add("pe", f_o, rd=[b_mg, bW["w_out"]], wr=[b_bank[ba], b_bank[bb]])

            def f_h(v, s_=s_, ba=ba, bb=bb):
                v.tensor_tensor(out=hh[s_][:, 0:512], in0=banks[ba][:, :], in1=xt[s_][:, 0:512], op=ALU.add)
                return v.tensor_tensor(out=hh[s_][:, 512:1024], in0=banks[bb][:, :], in1=xt[s_][:, 512:1024], op=ALU.add)

            ph.add("dve", f_h, rd=[b_xt[s_]], wr=[b_bank[ba], b_bank[bb], b_hh[s_]])
            ph.dma(T["hs"][t * 128:(t + 1) * 128, :], hh[s_][:], rd=[b_hh[s_]])
            ph.add("act", lambda a, s_=s_: a.activation(out=junk[:], in_=hh[s_][:], func=AF.Square, accum_out=ss[:, s_, 0:1]), rd=[b_hh[s_]], wr=[b_junk, b_ss[s_]])
            ph.add("act", lambda a, s_=s_: a.activation(out=ss[:, s_, 1:2], in_=ss[:, s_, 0:1], func=AF.Sqrt, scale=1.0 / D, bias=cst[:, 0:1]), rd=[b_cst], wr=[b_ss[s_]])
            ph.add("dve", lambda v, s_=s_: v.reciprocal(out=ss[:, s_, 1:2], in_=ss[:, s_, 1:2]), wr=[b_ss[s_]])
            ph.add("act", lambda a, s_=s_: a.activation(out=hb[s_][:], in_=hh[s_][:], func=AF.Copy, scale=ss[:, s_, 1:2]), rd=[b_hh[s_], b_ss[s_]], wr=[b_hb[s_]])

            def f_tr(pe, s_=s_):
                for kc in range(8):
                    i = pe.transpose(out=pst[:, kc * 128:(kc + 1) * 128], in_=hb[s_][:, kc * 128:(kc + 1) * 128], identity=ident[:])
                return i

            ph.add("pe", f_tr, rd=[b_hb[s_], b_ident], wr=[b_pst])
            ph.add("dve", lambda v, s_=s_: v.tensor_copy(out=hT[s_][:], in_=pst[:, :].rearrange("p (k t) -> p k t", k=8)), wr=[b_pst, b_hT[s_]])
            ph.dma(T["hnT"][:, t * 128:(t + 1) * 128].rearrange("(kc p) t -> p kc t", p=128), hT[s_][:], rd=[b_hT[s_]])
    return ph.run()


def phase4b(nc, cfg, T, bar):
    ph = Phase(nc, "p4b", bar)
    NT = cfg.ntok
    stage = ph.sb("stage", [128, 4, 512], F32)
    b_stage = Buf()
    gcol = ph.sb("gcol", [128, 8], F32)
    b_g = Buf()
    ph.dma(gcol[:], T["norm_mlp"].rearrange("(kc p) -> p kc", p=128), wr=[b_g], allow_slow_non_contiguous=True)
    gfin = ph.sb("gfin", [128, D], F32)
    b_gf = Buf()
    ph.dma(gfin[:], T["norm_final"].partition_broadcast(128), wr=[b_gf])
    w1 = ph.sb("w1", [128, 8, DFF], BF16)
    b_w1 = Buf()
    w2 = ph.sb("w2", [128, 32, D], BF16)
    b_w2 = Buf()
    load_w_bf16(ph, w1, b_w1, T["w_mlp_in"], 8, DFF, stage, b_stage, scale_col=gcol, b_scale=b_g)
    load_w_bf16(ph, w2, b_w2, T["w_mlp_out"], 32, D, stage, b_stage)
    cst = ph.sb("cst", [128, 1], F32)
    b_cst = Buf()
    ph.add("pool", lambda g: g.memset(cst[:], 1e-6), wr=[b_cst])
    banks = [ph.ps(f"bk{i}", [128, 512], F32) for i in range(8)]
    b_bank = [Buf() for _ in range(8)]
    hn = [ph.sb("hn0", [128, 8, 512], BF16)] * 2
    b_hn = [Buf()] * 2
    hid = ph.sb("hid", [128, 32, 512], BF16)
    b_hid = [Buf() for _ in range(32)]
    rl = [ph.sb(f"rl{i}", [128, 512], F32) for i in range(2)]
    b_rl = [Buf(), Buf()]
    ht = [ph.sb(f"ht{i}", [128, D], F32) for i in range(2)]
    b_ht = [Buf(), Buf()]
    oo = [ph.sb(f"oo{i}", [128, D], F32) for i in range(2)]
    b_oo = [Buf(), Buf()]
    junk = ph.sb("junk", [128, D], BF16)
    b_junk = Buf()
    ss = ph.sb("ss", [128, 2, 2], F32)
    b_ss = [Buf(), Buf()]
    nb = 0
    nr = 0
    nt_ = 0
    for b in range(NT // 512):
        u = b % 2
        tok = slice(b * 512, (b + 1) * 512)
        ph.dma(hn[u][:], T["hnT"][:, tok].rearrange("(kc p) t -> p kc t", p=128), wr=[b_hn[u]])
        for fc in range(32):
            bk = nb % 8
            nb += 1

            def f_h(pe, fc=fc, bk=bk, u=u):
                for kc in range(8):
                    i = pe.matmul(banks[bk][:, :], lhsT=w1[:, kc, fc * 128:(fc + 1) * 128], rhs=hn[u][:, kc, :], start=(kc == 0), stop=(kc == 7))
                return i

            ph.add("pe", f_h, rd=[b_w1, b_hn[u]], wr=[b_bank[bk]])
            r_ = nr % 2
            nr += 1
            ph.add("act", lambda a, bk=bk, r_=r_: a.activation(out=rl[r_][:], in_=banks[bk][:, :], func=AF.Relu), wr=[b_bank[bk], b_rl[r_]])
            ph.add("dve", lambda g, fc=fc, r_=r_: g.tensor_tensor(out=hid[:, fc, :], in0=rl[r_][:], in1=rl[r_][:], op=ALU.mult), rd=[b_rl[r_]], wr=[b_hid[fc]])
        for ti in range(4):
            t = b * 4 + ti
            s_ = nt_ % 2
            nt_ += 1
            ph.dma(ht[s_][:], T["hs"][t * 128:(t + 1) * 128, :], wr=[b_ht[s_]])
            ba, bb = nb % 8, (nb + 1) % 8
            nb += 2

            def f_o(pe, ti=ti, ba=ba, bb=bb):
                for hf, bk in ((0, ba), (1, bb)):
                    for fc in range(32):
                        i = pe.matmul(banks[bk][:, :], lhsT=hid[:, fc, ti * 128:(ti + 1) * 128], rhs=w2[:, fc, hf * 512:(hf + 1) * 512], start=(fc == 0), stop=(fc == 31))
                return i

            ph.add("pe", f_o, rd=[b_w2] + b_hid, wr=[b_bank[ba], b_bank[bb]])

            def f_r(v, s_=s_, ba=ba, bb=bb):
                v.tensor_tensor(out=oo[s_][:, 0:512], in0=banks[ba][:, :], in1=ht[s_][:, 0:512], op=ALU.add)
                return v.tensor_tensor(out=oo[s_][:, 512:1024], in0=banks[bb][:, :], in1=ht[s_][:, 512:1024], op=ALU.add)

            ph.add("dve", f_r, rd=[b_ht[s_]], wr=[b_bank[ba], b_bank[bb], b_oo[s_]])
            ph.add("act", lambda a, s_=s_: a.activation(out=junk[:], in_=oo[s_][:], func=AF.Square, accum_out=ss[:, s_, 0:1]), rd=[b_oo[s_]], wr=[b_junk, b_ss[s_]])
            ph.add("act", lambda a, s_=s_: a.activation(out=ss[:, s_, 1:2], in_=ss[:, s_, 0:1], func=AF.Sqrt, scale=1.0 / D, bias=cst[:, 0:1]), rd=[b_cst], wr=[b_ss[s_]])
            ph.add("dve", lambda v, s_=s_: v.reciprocal(out=ss[:, s_, 1:2], in_=ss[:, s_, 1:2]), wr=[b_ss[s_]])
            ph.add("dve", lambda g, s_=s_: g.scalar_tensor_tensor(out=oo[s_][:], in0=oo[s_][:], scalar=ss[:, s_, 1:2], in1=gfin[:], op0=ALU.mult, op1=ALU.mult), rd=[b_ss[s_], b_gf], wr=[b_oo[s_]])
            ph.dma(T["y"][t * 128:(t + 1) * 128, :], oo[s_][:], rd=[b_oo[s_]])
    return ph.run()


def phase15(nc, cfg, T, bar):
    ph = Phase(nc, "p15", bar)
    TB = 256
    mu = ph.sb("mu", [128, 27], F32)
    omu = ph.sb("omu", [128, 27], F32)
    hmu = ph.sb("hmu", [128, 27], F32)
    b_par = Buf()
    ph.dma(mu[:], T["mu_shift"].rearrange("(c p) -> p c", p=128), wr=[b_par], allow_slow_non_contiguous=True)
    ph.add("dve", lambda v: v.tensor_scalar(out=omu[:], in0=mu[:], scalar1=-1.0, scalar2=1.0, op0=ALU.mult, op1=ALU.add), wr=[b_par])
    ph.add("dve", lambda v: v.tensor_scalar(out=hmu[:], in0=mu[:], scalar1=0.5, scalar2=None, op0=ALU.mult), wr=[b_par])
    P = [ph.sb(f"P{i}", [128, 27, TB + 2], F32) for i in range(2)]
    b_P = [Buf(), Buf()]
    t1 = [ph.sb(f"t1_{i}", [128, 27, TB], F32) for i in range(2)]
    b_t1 = [Buf(), Buf()]
    t2 = [ph.sb(f"t2_{i}", [128, 27, TB], F32) for i in range(2)]
    b_t2 = [Buf(), Buf()]
    n = 0
    for si, S in enumerate(cfg.seqs):
        base = cfg.off[si]
        nb_ = S // TB
        for bi in range(nb_):
            u = n % 2
            n += 1
            t0 = base + bi * TB
            lo = 1 if bi == 0 else 0
            hi = TB + 1 if bi == nb_ - 1 else TB + 2
            if lo == 1:
                ph.add("pool", lambda g, u=u: g.memset(P[u][:, :, 0:1], 0.0), wr=[b_P[u]])
            if hi == TB + 1:
                ph.add("pool", lambda g, u=u: g.memset(P[u][:, :, TB + 1:TB + 2], 0.0), wr=[b_P[u]])
            for c0_, c1_ in ((0, 9), (9, 18), (18, 27)):
                ph.dma(P[u][:, c0_:c1_, lo:hi], T["pr"][c0_ * 128:c1_ * 128, t0 - 1 + lo:t0 - 1 + hi].rearrange("(c p) t -> p c t", p=128), wr=[b_P[u]])
            for c in range(27):
                ph.add("act", lambda a, u=u, c=c: a.activation(out=t1[u][:, c, :], in_=P[u][:, c, 1:TB + 1], func=AF.Copy, scale=omu[:, c:c + 1]), rd=[b_P[u], b_par], wr=[b_t1[u]])

            def f_sh(v, u=u):
                v.tensor_tensor(out=t2[u][:], in0=P[u][:, :, 0:TB], in1=P[u][:, :, 2:TB + 2], op=ALU.add)
                v.tensor_tensor(out=t2[u][:], in0=t2[u][:], in1=hmu[:].unsqueeze(2).to_broadcast([128, 27, TB]), op=ALU.mult)
                return v.tensor_tensor(out=t2[u][:], in0=t2[u][:], in1=t1[u][:], op=ALU.add)

            ph.add("dve", f_sh, rd=[b_P[u], b_par, b_t1[u]], wr=[b_t2[u]])
            for c0_, c1_ in ((0, 9), (9, 18), (18, 27)):
                ph.dma(T["prs"][c0_ * 128:c1_ * 128, t0:t0 + TB].rearrange("(c p) t -> p c t", p=128), t2[u][:, c0_:c1_, :], rd=[b_t2[u]])
    return ph.run()


def phase2(nc, cfg, T, bar):
    ph = Phase(nc, "p2", bar)
    NT = cfg.ntok
    C0 = -math.exp(-0.5)
    sb, add = ph.sb, ph.add
    identb, b_identb = make_ident(ph, BF16, name="idb")
    Y = sb("Y", [128, 1024], F32)
    Yp = sb("Yp", [128, 1024], F32)
    wst = Yp
    MK = Y[:, 0:512].rearrange("p (m c) -> p m c", m=4)
    TRt = sb("TRt", [128, 4, 128], BF16)
    TR = TRt
    identf = ph.sb("idf", [128, 128], F32)
    BD = sb("BD", [128, 128], BF16)
    onesr = sb("onesr", [1, 128], BF16)
    mA = [sb(f"mA{d}", [128, 256], F32) for d in range(2)]
    mL = [sb(f"mL{d}", [128, 256], F32) for d in range(2)]
    trc = [sb(f"trc{d}", [128, 384], BF16) for d in range(2)]
    b_cst = Buf()

    def f_masks(g):
        g.memset(MK, 1.0)
        g.affine_select(out=MK[:, 0, :], in_=MK[:, 0, :], compare_op=ALU.is_gt, fill=0.0, base=0, pattern=[[1, 128]], channel_multiplier=-1)
        g.affine_select(out=MK[:, 1, :], in_=MK[:, 1, :], compare_op=ALU.is_ge, fill=0.0, base=0, pattern=[[1, 128]], channel_multiplier=-1)
        g.affine_select(out=MK[:, 2, :], in_=MK[:, 2, :], compare_op=ALU.is_gt, fill=0.0, base=0, pattern=[[-1, 128]], channel_multiplier=1)
        g.affine_select(out=MK[:, 3, :], in_=MK[:, 3, :], compare_op=ALU.is_ge, fill=0.0, base=0, pattern=[[-1, 128]], channel_multiplier=1)
        g.tensor_scalar(out=TR[:], in0=MK, scalar1=C0, scalar2=None, op0=ALU.mult)
        g.memset(identf[:], 0.0)
        g.affine_select(out=identf[:], in_=identf[:], compare_op=ALU.not_equal, fill=1.0, base=0, pattern=[[-1, 128]], channel_multiplier=1)
        g.memset(BD[:], 0.0)
        g.memset(BD[0:64, 0:64], 1.0)
        g.memset(BD[64:128, 64:128], 1.0)
        g.memset(onesr[:], 1.0)
        for d in range(2):
            st, inc, lm = (0, 1, 2) if d == 0 else (2, 3, 0)
            g.tensor_copy(out=mA[d][:, 0:128], in_=MK[:, st, :])
            g.tensor_copy(out=mA[d][:, 128:256], in_=MK[:, inc, :])
            for q in range(2):
                g.tensor_copy(out=mL[d][:, q * 128:(q + 1) * 128], in_=MK[:, lm, :])
            g.tensor_copy(out=trc[d][:, 0:128], in_=TR[:, inc, :])
            g.tensor_copy(out=trc[d][:, 128:256], in_=TR[:, st, :])
            i = g.tensor_copy(out=trc[d][:, 256:384], in_=TR[:, lm, :])
        return i

    add("pool", f_masks, wr=[b_cst])
    par = sb("par", [128, 8, 8], F32)
    b_par = Buf()
    for i, ap_ in enumerate((T["k_k"], T["k_a"], T["k_a"], T["r_k"], T["ln_x_w"], T["ln_x_b"], T["a0"][0:1024], T["a0"][1024:2048])):
        ph.dma(par[:, i, :], ap_.rearrange("(c p) -> p c", p=128), wr=[b_par], allow_slow_non_contiguous=True)
    add("pool", lambda g: g.tensor_scalar(out=par[:, 2, :], in0=par[:, 2, :], scalar1=-1.0, scalar2=1.0, op0=ALU.mult, op1=ALU.add), wr=[b_par])
    b_wst = Buf()
    wup = sb("wup", [128, 1024], BF16)
    aup = sb("aup", [128, 1024], BF16)
    gup = sb("gup", [128, 1024], BF16)
    b_w = Buf()
    for dst, src in ((wup, T["w_lora_up"]), (aup, T["a_lora_up"]), (gup, T["g_lora_up"])):
        ph.dma(wst[:], src[:, :], wr=[b_wst])
        add("pool", lambda g, dst=dst: g.tensor_copy(out=dst[:], in_=wst[:]), rd=[b_wst], wr=[b_w])
    w0h = sb("w0h", [1, 2, 1024], BF16)
    w0l = sb("w0l", [1, 2, 1024], BF16)
    w0t = Y
    for d_ in range(2):
        ph.dma(wst[0:1, :], T["w0"][d_ * 1024:(d_ + 1) * 1024].rearrange("(o c) -> o c", o=1), wr=[b_wst])

        def f_w0(g, d_=d_):
            g.tensor_copy(out=w0h[0:1, d_, :], in_=wst[0:1, :])
            g.tensor_copy(out=w0t[0:1, :], in_=w0h[0:1, d_, :])
            g.tensor_tensor(out=w0t[0:1, :], in0=wst[0:1, :], in1=w0t[0:1, :], op=ALU.subtract)
            return g.tensor_copy(out=w0l[0:1, d_, :], in_=w0t[0:1, :])

        add("pool", f_w0, rd=[], wr=[b_wst, b_w, b_cst])
    eps = sb("eps", [128, 2], F32)
    add("pool", lambda g: (g.memset(eps[:, 0:1], 64e-5), g.memset(eps[:, 1:2], 1e-24))[1], wr=[b_cst])

    bk = [ph.ps(f"bk{i}", [128, 512], F32) for i in range(6)]
    b_bk = [Buf() for _ in range(6)]
    bt = [ph.ps(f"bt{i}", [128, 1024], BF16) for i in range(2)]
    b_bt = [Buf(), Buf()]
    cnt = dict(b=0, t=0)

    def nb():
        pool_ = cnt.get("pool")
        if pool_ is None:
            i = cnt["b"] % 6
            cnt["b"] += 1
            return i
        k_ = "b" + str(pool_[0])
        i = pool_[cnt.get(k_, 0) % len(pool_)]
        cnt[k_] = cnt.get(k_, 0) + 1
        return i

    def ntb():
        i = cnt["t"] % 2
        cnt["t"] += 1
        return i

    SH = sb("SH", [128, 27, 128], F32)
    b_SH = Buf()
    f4 = lambda nm: sb(nm, [128, 8, 128], F32)
    h4 = lambda nm: sb(nm, [128, 8, 128], BF16)
    av, kk, kd, Ei, tmp1 = [f4(n) for n in ("av", "kk", "kd", "Ei", "tmp1")]
    bb_ = av
    En, Ee, Es = [h4(n) for n in ("En", "Ee", "Es")]
    tmp2 = tmp1
    kkr = tmp1
    rn = kd
    names = ("av", "kk", "bb", "kd", "E", "tmp1", "tl", "lab", "sig", "sq", "AR", "bt", "kt", "Bg", "Kg", "vb",
             "Atok", "Bgtok", "Kgtok", "Vtok", "AM", "KM", "LL", "PL0", "PL1", "PA0", "PA1", "X0", "X1", "Zb", "AhT", "What", "Ub", "S", "Sbf",
             "Y", "Yp", "sg", "bon", "bonp", "st", "yo", "Stmp", "Gc")
    dbl = ("AR", "Atok", "Bgtok", "Kgtok", "Vtok", "AM", "KM", "LL", "sg", "bon", "Gc")
    B0 = {n: Buf(n) for n in names}
    Bp = [dict(B0), dict(B0)]
    for n_ in dbl:
        Bp[1][n_] = Buf(n_ + "1")
    grp = ("AM", "KM", "LL", "PL0", "PL1", "PA0", "PA1", "X0", "X1")
    for n_ in grp:
        l0 = [Buf(n_ + str(g_)) for g_ in range(4)]
        Bp[0][n_] = l0
        Bp[1][n_] = [Buf(n_ + "b" + str(g_)) for g_ in range(4)] if n_ in dbl else l0
    for Bx in Bp:
        Bx["bb"] = Bx["av"]
        Bx["kkr"] = Bx["tmp1"]
        Bx["tmp2"] = Bx["tmp1"]
        Bx["rn"] = Bx["kd"]
        Bx["fin"] = Bx["What"]
        Bx["yn"] = Bx["Yp"]
    tl = sb("tl", [128, 128], BF16)
    lab = sb("lab", [128, 128], BF16)
    sig = sb("sig", [128, 1024], BF16)
    sq = h4("sq")
    btT, ktT, BgT, KgT, vb = [h4(n) for n in ("btT", "ktT", "BgT", "KgT", "vbb")]
    Zb = sb("Zb", [128, 1024], BF16)
    two = lambda nm, shp, dt: [sb(nm + "0", shp, dt), sb(nm + "1", shp, dt)]
    sgbs = two("sgb", [128, 128], BF16)
    ARs = two("AR", [128, 8, 256], BF16)
    Atoks, Bgtoks, Kgtoks, Vtoks = [two(n, [128, 1024], BF16) for n in ("Atok", "Bgtok", "Kgtok", "Vtok")]
    AMs = two("AM", [128, 16, 256], BF16)
    KMs = two("KM", [128, 16, 256], BF16)
    LLs = two("LL", [128, 16, 128], BF16)
    bons = two("bon", [128, 8, 128], F32)
    Gcs = two("Gc", [128, 8, 1], F32)
    PL12 = [sb("PL1", [128, 16, 128], BF16), sb("PL2", [128, 16, 128], BF16)]
    PA = [sb("PA1", [128, 16, 128], BF16), sb("PA2", [128, 16, 128], BF16)]
    X = [sb("X0", [128, 16, 128], BF16), sb("X1", [128, 16, 128], BF16)]
    AhT = sb("AhT", [128, 8, 128], BF16)
    What = sb("What", [128, 1024], F32)
    Ub = sb("Ub", [128, 1024], BF16)
    Sf = sb("Sf", [128, 8, 64], F32)
    Stmp = sb("Stmp", [128, 8, 64], F32)
    Sbf = sb("Sbf", [128, 8, 64], BF16)
    bonp = f4("bonp")
    st = sb("st", [128, 16, 4], F32)
    yn2 = Yp
    yn = yn2[:].rearrange("p (h v) -> p h v", v=64)
    fin = What[:].rearrange("p (c t) -> p c t", c=8)
    yo = Zb[:].rearrange("p (c t) -> p c t", c=8)
    for Bx in Bp:
        Bx["yo"] = Bx["Zb"]
    dramB = {}

    def dB(kind, t0):
        return dramB.setdefault((kind, t0), Buf())

    def stageA(si, d, ci, p):
        S = cfg.seqs[si]
        base = cfg.off[si]
        nch = S // 128
        t0 = base + ci * 128
        first = (ci == 0) if d == 0 else (ci == nch - 1)
        final = (d == 1)
        B = Bp[p]
        AR, Atok, Bgtok, Kgtok, Vtok, AM, KM, LL, sgb, bon, Gc = ARs[p], Atoks[p], Bgtoks[p], Kgtoks[p], Vtoks[p], AMs[p], KMs[p], LLs[p], sgbs[p], bons[p], Gcs[p]
        PL = [LL, PL12[0], PL12[1]]
        R, Kx, Vx = SH[:, 0:8, :], SH[:, 8:16, :], SH[:, 16:24, :]
        dsl = slice(d * 64, (d + 1) * 64)
        bc = lambda i: par[:, i, :].unsqueeze(2).to_broadcast([128, 8, 128])
        ec = 127 if d == 0 else 0
        for c0_, c1_ in ((0, 9), (9, 18), (18, 27)):
            ph.dma(SH[:, c0_:c1_, :], T["prs"][c0_ * 128:c1_ * 128, t0:t0 + 128].rearrange("(c p) t -> p c t", p=128), wr=[b_SH])
        add("act", lambda a: a.activation(out=tl[:], in_=SH[:, 24, :], func=AF.Tanh), rd=[b_SH], wr=[B["tl"]])
        add("act", lambda a: a.activation(out=lab[:], in_=SH[:, 25, :], func=AF.Copy), rd=[b_SH], wr=[B["lab"]])
        w1_, w2_ = nb(), nb()

        def f_lw(pe):
            for hf, b_ in ((0, w1_), (1, w2_)):
                cs = slice(hf * 512, (hf + 1) * 512)
                pe.matmul(bk[b_][:, :], lhsT=tl[dsl, :], rhs=wup[dsl, cs], start=True, stop=False)
                pe.matmul(bk[b_][:, :], lhsT=onesr[0:1, :], rhs=w0h[0:1, d, cs], start=False, stop=False)
                i = pe.matmul(bk[b_][:, :], lhsT=onesr[0:1, :], rhs=w0l[0:1, d, cs], start=False, stop=True)
            return i

        add("pe", f_lw, rd=[B["tl"], b_w, b_cst], wr=[b_bk[w1_], b_bk[w2_]])
        add("act", lambda a: a.activation(out=sig[:, 0:512], in_=bk[w1_][:, :], func=AF.Sigmoid), wr=[b_bk[w1_], B["sig"]])
        add("act", lambda a: a.activation(out=sig[:, 512:1024], in_=bk[w2_][:, :], func=AF.Sigmoid), wr=[b_bk[w2_], B["sig"]])
        a1_, a2_ = nb(), nb()

        def f_la(pe):
            for c in range(8):
                b_ = a1_ if c < 4 else a2_
                i = pe.matmul(bk[b_][:, (c % 4) * 128:(c % 4 + 1) * 128], lhsT=aup[dsl, c * 128:(c + 1) * 128], rhs=lab[dsl, :], start=True, stop=True)
            return i

        add("pe", f_la, rd=[B["lab"], b_w], wr=[b_bk[a1_], b_bk[a2_]])
        for c in range(8):
            b_ = a1_ if c < 4 else a2_
            add("act", lambda a, c=c, b_=b_: a.activation(out=av[:, c, :], in_=bk[b_][:, (c % 4) * 128:(c % 4 + 1) * 128], func=AF.Sigmoid, bias=par[:, 6 + d, c:c + 1]), rd=[b_par], wr=[b_bk[b_], B["av"]])
        for hf in range(2):
            cb = [nb(), nb(), nb()]

            def f_cum(pe, hf=hf, cb=cb):
                for cc_ in range(4):
                    c = hf * 4 + cc_
                    for x in range(3):
                        i = pe.matmul(bk[cb[x]][:, cc_ * 128:(cc_ + 1) * 128], lhsT=sig[:, c * 128:(c + 1) * 128], rhs=trc[d][:, x * 128:(x + 1) * 128], start=True, stop=True)
                return i

            add("pe", f_cum, rd=[B["sig"], b_cst], wr=[b_bk[i] for i in cb])
            hs_ = slice(hf * 4, hf * 4 + 4)
            v3 = lambda b_: bk[b_][:, :].rearrange("p (c t) -> p c t", c=4)
            add("act", lambda a, cb=cb, hs_=hs_: (a.activation(out=Ei[:, hs_, :], in_=v3(cb[0]), func=AF.Exp), a.activation(out=En[:, hs_, :], in_=v3(cb[0]), func=AF.Exp, scale=-1.0))[1], wr=[b_bk[cb[0]], B["E"]])
            add("act", lambda a, cb=cb, hs_=hs_: a.activation(out=Ee[:, hs_, :], in_=v3(cb[1]), func=AF.Exp), wr=[b_bk[cb[1]], B["E"]])
            add("act", lambda a, cb=cb, hs_=hs_: a.activation(out=Es[:, hs_, :], in_=v3(cb[2]), func=AF.Exp), wr=[b_bk[cb[2]], B["E"]])
        add("dve", lambda v: v.tensor_tensor(out=kkr[:], in0=Kx, in1=bc(0), op=ALU.mult), rd=[b_SH, b_par], wr=[B["kkr"]])
        add("act", lambda a: a.activation(out=sq[:], in_=kkr[:], func=AF.Square), rd=[B["kkr"]], wr=[B["sq"]])
        n1_, n2_ = nb(), nb()

        def f_nrm(pe):
            sqf = sq[:].rearrange("p c t -> p (c t)")
            pe.matmul(bk[n1_][:, :], lhsT=BD[:], rhs=sqf[:, 0:512], start=True, stop=True)
            return pe.matmul(bk[n2_][:, :], lhsT=BD[:], rhs=sqf[:, 512:1024], start=True, stop=True)

        add("pe", f_nrm, rd=[B["sq"], b_cst], wr=[b_bk[n1_], b_bk[n2_]])
        add("act", lambda a: a.activation(out=rn[:, 0:4, :], in_=bk[n1_][:, :].rearrange("p (c t) -> p c t", c=4), func=AF.Ln, bias=eps[:, 1:2]), rd=[b_cst], wr=[b_bk[n1_], B["rn"]])
        add("act", lambda a: a.activation(out=rn[:, 4:8, :], in_=bk[n2_][:, :].rearrange("p (c t) -> p c t", c=4), func=AF.Ln, bias=eps[:, 1:2]), rd=[b_cst], wr=[b_bk[n2_], B["rn"]])
        add("act", lambda a: a.activation(out=rn[:], in_=rn[:], func=AF.Exp, scale=-0.5), wr=[B["rn"]])
        add("dve", lambda v: v.tensor_tensor(out=kk[:], in0=kkr[:], in1=rn[:], op=ALU.mult), rd=[B["kkr"], B["rn"]], wr=[B["kk"]])

        def f_kd(g):
            g.tensor_tensor(out=tmp1[:], in0=av[:], in1=bc(1), op=ALU.mult)
            g.tensor_tensor(out=tmp1[:], in0=tmp1[:], in1=bc(2), op=ALU.add)
            g.tensor_tensor(out=kd[:], in0=tmp1[:], in1=Kx, op=ALU.mult)
            return g.tensor_tensor(out=bb_[:], in0=kk[:], in1=av[:], op=ALU.mult)

        add("dve", f_kd, rd=[B["av"], b_par, b_SH, B["kk"]], wr=[B["tmp1"], B["kd"], B["bb"]])

        def f_sc1(v):
            v.scalar_tensor_tensor(out=AR[:, :, 0:128], in0=kk[:], scalar=-1.0, in1=Ee[:], op0=ALU.mult, op1=ALU.mult)
            v.tensor_tensor(out=AR[:, :, 128:256], in0=R, in1=Ei[:], op=ALU.mult)
            return v.tensor_tensor(out=btT[:], in0=bb_[:], in1=En[:], op=ALU.mult)

        add("dve", f_sc1, rd=[B["kk"], B["E"], b_SH, B["bb"]], wr=[B["AR"], B["bt"]])

        add("dve", lambda g: g.tensor_tensor(out=ktT[:], in0=kd[:], in1=En[:], op=ALU.mult), rd=[B["kd"], B["E"]], wr=[B["kt"]])
        add("pool", lambda g: g.tensor_tensor(out=BgT[:], in0=bb_[:], in1=Es[:], op=ALU.mult), rd=[B["E"], B["bb"]], wr=[B["Bg"]])
        add("pool", lambda g: g.tensor_tensor(out=KgT[:], in0=kd[:], in1=Es[:], op=ALU.mult), rd=[B["kd"], B["E"]], wr=[B["Kg"]])
        add("act", lambda a: a.activation(out=vb[:], in_=Vx, func=AF.Copy), rd=[b_SH], wr=[B["vb"]])
        add("dve", lambda g: (g.tensor_tensor(out=tmp2[:], in0=R, in1=bc(3), op=ALU.mult), g.tensor_tensor(out=sq[:], in0=tmp2[:], in1=kd[:], op=ALU.mult))[1], rd=[b_SH, b_par, B["kd"]], wr=[B["tmp2"], B["sq"]])
        o1_, o2_ = nb(), nb()

        def f_bon(pe):
            sqf = sq[:].rearrange("p c t -> p (c t)")
            pe.matmul(bk[o1_][:, :], lhsT=BD[:], rhs=sqf[:, 0:512], start=True, stop=True)
            return pe.matmul(bk[o2_][:, :], lhsT=BD[:], rhs=sqf[:, 512:1024], start=True, stop=True)

        add("pe", f_bon, rd=[B["sq"], b_cst], wr=[b_bk[o1_], b_bk[o2_]])
        add("dve", lambda v: v.tensor_tensor(out=bon[:, 0:4, :], in0=bk[o1_][:, :].rearrange("p (c t) -> p c t", c=4), in1=SH[:, 16:20, :], op=ALU.mult), rd=[b_SH], wr=[b_bk[o1_], B["bon"]])
        add("dve", lambda v: v.tensor_tensor(out=bon[:, 4:8, :], in0=bk[o2_][:, :].rearrange("p (c t) -> p c t", c=4), in1=SH[:, 20:24, :], op=ALU.mult), rd=[b_SH], wr=[b_bk[o2_], B["bon"]])
        for src, srcb, dst, dstb in ((AR, "AR", Atok, "Atok"), (BgT, "Bg", Bgtok, "Bgtok"), (KgT, "Kg", Kgtok, "Kgtok"), (vb, "vb", Vtok, "Vtok")):
            tb = ntb()

            def f_tr(pe, src=src, tb=tb):
                for c in range(8):
                    i = pe.transpose(out=bt[tb][:, c * 128:(c + 1) * 128], in_=src[:, c, 0:128], identity=identb[:])
                return i

            add("pe", f_tr, rd=[B[srcb], b_identb], wr=[b_bt[tb]])
            add("act", lambda a, dst=dst, tb=tb: a.activation(out=dst[:], in_=bt[tb][:, :], func=AF.Copy), wr=[b_bt[tb], B[dstb]])
        for g4 in range(8):
            hp0 = (g4 // 2) * 2
            par_ = g4 % 2
            hs2 = [2 * hp0 + par_, 2 * (hp0 + 1) + par_]
            ba, bk_, bl = nb(), nb(), nb()
            ps_ = slice(par_ * 64, par_ * 64 + 64)

            def f_sc(pe, hs2=hs2, ba=ba, bk_=bk_, bl=bl, ps_=ps_):
                for q, h in enumerate(hs2):
                    c = h // 2
                    pe.matmul(bk[ba][:, q * 256:(q + 1) * 256], lhsT=btT[ps_, c, :], rhs=AR[ps_, c, :], start=True, stop=True)
                    pe.matmul(bk[bk_][:, q * 256:(q + 1) * 256], lhsT=ktT[ps_, c, :], rhs=AR[ps_, c, :], start=True, stop=True)
                    i = pe.matmul(bk[bl][:, q * 128:(q + 1) * 128], lhsT=AR[ps_, c, 0:128], rhs=btT[ps_, c, :], start=True, stop=True)
                return i

            add("pe", f_sc, rd=[B["AR"], B["bt"], B["kt"]], wr=[b_bk[ba], b_bk[bk_], b_bk[bl]])

            def f_ev(v, ba=ba, bk_=bk_, bl=bl, hs2=hs2):
                for q, h in enumerate(hs2):
                    v.tensor_tensor(out=AM[:, h, :], in0=bk[ba][:, q * 256:(q + 1) * 256], in1=mA[d][:, 0:256], op=ALU.mult)
                    v.tensor_tensor(out=KM[:, h, :], in0=bk[bk_][:, q * 256:(q + 1) * 256], in1=mA[d][:, 0:256], op=ALU.mult)
                    i = v.tensor_tensor(out=LL[:, h, :], in0=bk[bl][:, q * 128:(q + 1) * 128], in1=mL[d][:, 0:128], op=ALU.mult)
                return i

            gq = hs2[0] // 4
            add("dve", f_ev, rd=[b_cst], wr=[b_bk[ba], b_bk[bk_], b_bk[bl], B["AM"][gq], B["KM"][gq], B["LL"][gq]])
        add("act", lambda a: a.activation(out=Gc[:], in_=Ei[:, :, ec:ec + 1], func=AF.Copy), rd=[B["E"]], wr=[B["Gc"]])
        add("act", lambda a: a.activation(out=sgb[:], in_=SH[:, 26, :], func=AF.Sigmoid), rd=[b_SH], wr=[B["sg"]])

    def stageBC(si, d, ci, p):
        S = cfg.seqs[si]
        base = cfg.off[si]
        nch = S // 128
        t0 = base + ci * 128
        first = (ci == 0) if d == 0 else (ci == nch - 1)
        final = (d == 1)
        B = Bp[p]
        AR, Atok, Bgtok, Kgtok, Vtok, AM, KM, LL, sgb, bon, Gc = ARs[p], Atoks[p], Bgtoks[p], Kgtoks[p], Vtoks[p], AMs[p], KMs[p], LLs[p], sgbs[p], bons[p], Gcs[p]
        PL = [LL, PL12[0], PL12[1]]
        R, Kx, Vx = SH[:, 0:8, :], SH[:, 8:16, :], SH[:, 16:24, :]
        dsl = slice(d * 64, (d + 1) * 64)
        bc = lambda i: par[:, i, :].unsqueeze(2).to_broadcast([128, 8, 128])
        ec = 127 if d == 0 else 0
        for gq in range(4):
            add("dve", lambda g, gq=gq: g.tensor_tensor(out=X[0][:, gq * 4:gq * 4 + 4, :], in0=AM[:, gq * 4:gq * 4 + 4, 0:128], in1=identb[:].unsqueeze(1).to_broadcast([128, 4, 128]), op=ALU.add), rd=[B["AM"][gq], b_identb], wr=[B["X0"][gq]])
        Lcur, Acur, Xc = 0, 0, 0
        Lb = ["LL", "PL0", "PL1"]
        Ab = ["AM", "PA0", "PA1"]
        PAv = [AM[:, :, 0:128], PA[0][:], PA[1][:]]
        for lvl in range(6):
            Ln = 1 + (lvl % 2)
            An = 1 + (lvl % 2)
            Xn = 1 - Xc
            for g4 in range(4):
                hsl = slice(g4 * 4, g4 * 4 + 4)
                b1, b2 = nb(), nb()

                def f_sq(pe, g4=g4, b1=b1, b2=b2, Lc=Lcur, Ac=Acur, lvl=lvl):
                    for q in range(4):
                        h = g4 * 4 + q
                        i = pe.matmul(bk[b1][:, q * 128:(q + 1) * 128], lhsT=PAv[Ac][:, h, :], rhs=PL[Lc][:, h, :], start=True, stop=True)
                    if lvl < 5:
                        for q in range(4):
                            h = g4 * 4 + q
                            i = pe.matmul(bk[b2][:, q * 128:(q + 1) * 128], lhsT=PL[Lc][:, h, :], rhs=PAv[Ac][:, h, :], start=True, stop=True)
                    return i

                add("pe", f_sq, rd=[B[Lb[Lcur]][g4], B[Ab[Acur]][g4]], wr=[b_bk[b1], b_bk[b2]])
                add("act", lambda a, b1=b1, hsl=hsl, Ln=Ln: a.activation(out=PL[Ln][:, hsl, :], in_=bk[b1][:, :].rearrange("p (h t) -> p h t", h=4), func=AF.Copy), wr=[b_bk[b1], B[Lb[Ln]][g4]])
                if lvl < 5:
                    if g4 < 2:
                        add("act", lambda a, b2=b2, hsl=hsl, An=An: a.activation(out=PAv[An][:, hsl, :], in_=bk[b2][:, :].rearrange("p (h t) -> p h t", h=4), func=AF.Copy), wr=[b_bk[b2], B[Ab[An]][g4]])
                    else:
                        add("dve", lambda v, b2=b2, hsl=hsl, An=An: v.tensor_copy(out=PAv[An][:, hsl, :], in_=bk[b2][:, :].rearrange("p (h t) -> p h t", h=4)), wr=[b_bk[b2], B[Ab[An]][g4]])

            for g4 in range(4):
                hsl = slice(g4 * 4, g4 * 4 + 4)
                b3 = nb()

                def f_x(pe, g4=g4, b3=b3, Ln=Ln, Xc=Xc):
                    for q in range(4):
                        h = g4 * 4 + q
                        i = pe.matmul(bk[b3][:, q * 128:(q + 1) * 128], lhsT=PL[Ln][:, h, :], rhs=X[Xc][:, h, :], start=True, stop=True)
                    return i

                add("pe", f_x, rd=[B[Lb[Ln]][g4], B["X%d" % Xc][g4]], wr=[b_bk[b3]])
                add("dve", lambda v, b3=b3, hsl=hsl, Xc=Xc, Xn=Xn: v.tensor_tensor(out=X[Xn][:, hsl, :], in0=bk[b3][:, :].rearrange("p (h t) -> p h t", h=4), in1=X[Xc][:, hsl, :], op=ALU.add), rd=[B["X%d" % Xc][g4]], wr=[b_bk[b3], B["X%d" % Xn][g4]])
            Lcur, Acur, Xc = Ln, An, Xn
        XT = X[Xc]
        bX = B["X%d" % Xc]
        z1, z2 = nb(), nb()

        def f_z(pe):
            for h in range(16):
                b_ = z1 if h < 8 else z2
                i = pe.matmul(bk[b_][:, (h % 8) * 64:(h % 8 + 1) * 64], lhsT=KM[:, h, 0:128], rhs=Vtok[:, h * 64:(h + 1) * 64], start=True, stop=True)
            return i

        add("pe", f_z, rd=B["KM"] + [B["Vtok"]], wr=[b_bk[z1], b_bk[z2]])
        add("act", lambda a: a.activation(out=Zb[:, 0:512], in_=bk[z1][:, :], func=AF.Copy), wr=[b_bk[z1], B["Zb"]])
        add("act", lambda a: a.activation(out=Zb[:, 512:1024], in_=bk[z2][:, :], func=AF.Copy), wr=[b_bk[z2], B["Zb"]])
        q1, q2 = nb(), nb()

        def f_w(pe):
            for h in range(16):
                b_ = q1 if h < 8 else q2
                i = pe.matmul(bk[b_][:, (h % 8) * 64:(h % 8 + 1) * 64], lhsT=XT[:, h, :], rhs=Zb[:, h * 64:(h + 1) * 64], start=True, stop=True)
            return i

        add("pe", f_w, rd=bX + [B["Zb"]], wr=[b_bk[q1], b_bk[q2]])
        add("act", lambda a: a.activation(out=What[:, 0:512], in_=bk[q1][:, :], func=AF.Copy), wr=[b_bk[q1], B["What"]])
        add("act", lambda a: a.activation(out=What[:, 512:1024], in_=bk[q2][:, :], func=AF.Copy), wr=[b_bk[q2], B["What"]])
        for pg in range(4):
            b_ = nb()

            def f_ah(pe, pg=pg, b_=b_):
                for q in range(2):
                    hp = pg * 2 + q
                    i = pe.matmul(bk[b_][:, q * 256:(q + 1) * 256], lhsT=Atok[:, hp * 128:(hp + 1) * 128], rhs=XT[:].rearrange("p h t -> p (h t)")[:, hp * 256:(hp + 1) * 256], start=True, stop=True)
                return i

            add("pe", f_ah, rd=[B["Atok"], bX[pg]], wr=[b_bk[b_]])

            def f_ahe(v, pg=pg, b_=b_):
                v4 = bk[b_][:, :].rearrange("p (q s t) -> p q s t", q=2, s=2)
                v.tensor_copy(out=AhT[0:64, pg * 2:pg * 2 + 2, :], in_=v4[0:64, :, 0, :])
                return v.tensor_copy(out=AhT[64:128, pg * 2:pg * 2 + 2, :], in_=v4[64:128, :, 1, :])

            add("dve", f_ahe, wr=[b_bk[b_], B["AhT"]])
        if first:
            add("pool", lambda g: (g.memset(Sf[:], 0.0), g.memset(Sbf[:], 0.0))[1], wr=[B["S"], B["Sbf"]])
        ue, uo = nb(), nb()

        def f_u(pe):
            for h in range(16):
                hp, p_ = h // 2, h % 2
                i = pe.matmul(bk[(ue, uo)[p_]][:, hp * 64:(hp + 1) * 64], lhsT=AhT[p_ * 64:(p_ + 1) * 64, hp, :], rhs=Sbf[p_ * 64:(p_ + 1) * 64, hp, :], start=True, stop=True)
            return i

        add("pe", f_u, rd=[B["AhT"], B["Sbf"]], wr=[b_bk[ue], b_bk[uo]])
        Ub4 = Ub[:, :].rearrange("p (hp s v) -> p s hp v", s=2, v=64)
        Wh4 = What[:, :].rearrange("p (hp s v) -> p s hp v", s=2, v=64)
        add("dve", lambda v: (v.tensor_tensor(out=Ub4[:, 0], in0=bk[ue][:, :].rearrange("p (hp v) -> p hp v", v=64), in1=Wh4[:, 0], op=ALU.add),
                              v.tensor_tensor(out=Ub4[:, 1], in0=bk[uo][:, :].rearrange("p (hp v) -> p hp v", v=64), in1=Wh4[:, 1], op=ALU.add))[1], rd=[B["What"]], wr=[b_bk[ue], b_bk[uo], B["Ub"]])
        ye, yo_ = nb(), nb()

        def f_y(pe):
            for h in range(16):
                hp, p_ = h // 2, h % 2
                o = bk[(ye, yo_)[p_]][:, hp * 64:(hp + 1) * 64]
                pe.matmul(o, lhsT=AR[p_ * 64:(p_ + 1) * 64, hp, 128:256], rhs=Sbf[p_ * 64:(p_ + 1) * 64, hp, :], start=True, stop=False)
                pe.matmul(o, lhsT=AM[:, h, 128:256], rhs=Ub[:, h * 64:(h + 1) * 64], start=False, stop=False)
                i = pe.matmul(o, lhsT=KM[:, h, 128:256], rhs=Vtok[:, h * 64:(h + 1) * 64], start=False, stop=True)
            return i

        add("pe", f_y, rd=[B["AR"], B["Sbf"], B["Ub"], B["Vtok"]] + B["AM"] + B["KM"], wr=[b_bk[ye], b_bk[yo_]])
        Y4 = Y[:, :].rearrange("p (hp s v) -> p s hp v", s=2, v=64)
        if final:
            ph.dma(Yp[:], T["ytmp"][t0:t0 + 128, :], rd=[dB("y", t0)], wr=[B["Yp"]])
            Yp4 = Yp[:, :].rearrange("p (hp s v) -> p s hp v", s=2, v=64)
            add("dve", lambda v: (v.tensor_tensor(out=Y4[:, 0], in0=bk[ye][:, :].rearrange("p (hp v) -> p hp v", v=64), in1=Yp4[:, 0], op=ALU.add),
                                  v.tensor_tensor(out=Y4[:, 1], in0=bk[yo_][:, :].rearrange("p (hp v) -> p hp v", v=64), in1=Yp4[:, 1], op=ALU.add))[1], rd=[B["Yp"]], wr=[b_bk[ye], b_bk[yo_], B["Y"]])
        else:
            add("act", lambda a: (a.activation(out=Y4[:, 0], in_=bk[ye][:, :].rearrange("p (hp v) -> p hp v", v=64), func=AF.Copy),
                                  a.activation(out=Y4[:, 1], in_=bk[yo_][:, :].rearrange("p (hp v) -> p hp v", v=64), func=AF.Copy))[1], wr=[b_bk[ye], b_bk[yo_], B["Y"]])
            ph.dma(T["ytmp"][t0:t0 + 128, :], Y[:], rd=[B["Y"]], wr=[dB("y", t0)])
            ph.dma(T["bon"][:, t0:t0 + 128].rearrange("(c p) t -> p c t", p=128), bon[:], rd=[B["bon"]], wr=[dB("b", t0)])
        s1, s2 = nb(), nb()

        def f_s(pe):
            for hp in range(8):
                o = bk[s1 if hp < 4 else s2][:, (hp % 4) * 128:(hp % 4 + 1) * 128]
                pe.matmul(o, lhsT=Bgtok[:, hp * 128:(hp + 1) * 128], rhs=Ub[:, hp * 128:(hp + 1) * 128], start=True, stop=False)
                i = pe.matmul(o, lhsT=Kgtok[:, hp * 128:(hp + 1) * 128], rhs=Vtok[:, hp * 128:(hp + 1) * 128], start=False, stop=True)
            return i

        add("pe", f_s, rd=[B["Bgtok"], B["Ub"], B["Kgtok"], B["Vtok"]], wr=[b_bk[s1], b_bk[s2]])
        add("dve", lambda g: g.tensor_tensor(out=Stmp[:], in0=Sf[:], in1=Gc[:].to_broadcast([128, 8, 64]), op=ALU.mult), rd=[B["Gc"]], wr=[B["S"], B["Stmp"]])

        def f_su(v):
            for q, b_ in ((0, s1), (1, s2)):
                v4 = bk[b_][:, :].rearrange("p (hp s v) -> p hp s v", hp=4, s=2)
                v.tensor_tensor(out=Sf[0:64, q * 4:q * 4 + 4, :], in0=v4[0:64, :, 0, :], in1=Stmp[0:64, q * 4:q * 4 + 4, :], op=ALU.add)
                i = v.tensor_tensor(out=Sf[64:128, q * 4:q * 4 + 4, :], in0=v4[64:128, :, 1, :], in1=Stmp[64:128, q * 4:q * 4 + 4, :], op=ALU.add)
            return i

        add("dve", f_su, rd=[B["Stmp"]], wr=[b_bk[s1], b_bk[s2], B["S"]])
        add("act", lambda a: a.activation(out=Sbf[:], in_=Sf[:], func=AF.Copy), rd=[B["S"]], wr=[B["Sbf"]])
        if not final:
            return
        ph.dma(bonp[:], T["bon"][:, t0:t0 + 128].rearrange("(c p) t -> p c t", p=128), rd=[dB("b", t0)], wr=[B["bonp"]])
        Y3 = Y[:, :].rearrange("p (h v) -> p h v", v=64)

        def f_gn(v):
            v.reduce_sum(out=st[:, :, 0], in_=Y3, axis=AX.X)
            v.tensor_tensor(out=yn, in0=Y3, in1=Y3, op=ALU.mult)
            v.reduce_sum(out=st[:, :, 1], in_=yn, axis=AX.X)
            v.tensor_scalar(out=st[:, :, 0], in0=st[:, :, 0], scalar1=1.0 / 64, scalar2=None, op0=ALU.mult)
            v.tensor_tensor(out=st[:, :, 2], in0=st[:, :, 0], in1=st[:, :, 0], op=ALU.mult)
            return v.scalar_tensor_tensor(out=st[:, :, 1], in0=st[:, :, 1], scalar=1.0 / 64, in1=st[:, :, 2], op0=ALU.mult, op1=ALU.subtract)

        add("dve", f_gn, rd=[B["Y"]], wr=[B["st"], B["yn"]])
        add("act", lambda a: (a.activation(out=st[:, :, 3], in_=st[:, :, 1], func=AF.Ln, bias=eps[:, 0:1]),
                              a.activation(out=st[:, :, 3], in_=st[:, :, 3], func=AF.Exp, scale=-0.5))[1], rd=[b_cst], wr=[B["st"]])

        def f_gn2(v):
            v.tensor_tensor(out=yn, in0=Y3, in1=st[:, :, 0:1].to_broadcast([128, 16, 64]), op=ALU.subtract)
            return v.tensor_tensor(out=yn, in0=yn, in1=st[:, :, 3:4].to_broadcast([128, 16, 64]), op=ALU.mult)

        add("dve", f_gn2, rd=[B["Y"]], wr=[B["st"], B["yn"]])
        f1, f2 = nb(), nb()

        def f_trf(pe):
            for c in range(8):
                i = pe.transpose(out=bk[f1 if c < 4 else f2][:, (c % 4) * 128:(c % 4 + 1) * 128], in_=yn2[:, c * 128:(c + 1) * 128], identity=identf[:])
            return i

        add("pe", f_trf, rd=[B["yn"], b_cst], wr=[b_bk[f1], b_bk[f2]])
        for c in range(8):
            b_ = f1 if c < 4 else f2
            add("act", lambda a, c=c, b_=b_: a.activation(out=fin[:, c, :], in_=bk[b_][:, (c % 4) * 128:(c % 4 + 1) * 128], func=AF.Identity, scale=par[:, 4, c:c + 1], bias=par[:, 5, c:c + 1]), rd=[b_par], wr=[b_bk[b_], B["fin"]])
        g1, g2 = nb(), nb()

        def f_g(pe):
            for c in range(8):
                i = pe.matmul(bk[g1 if c < 4 else g2][:, (c % 4) * 128:(c % 4 + 1) * 128], lhsT=gup[:, c * 128:(c + 1) * 128], rhs=sgb[:], start=True, stop=True)
            return i

        add("pe", f_g, rd=[B["sg"], b_w], wr=[b_bk[g1], b_bk[g2]])

        def f_fin(g):
            g.tensor_tensor(out=fin, in0=fin, in1=bon[:], op=ALU.add)
            return g.tensor_tensor(out=fin, in0=fin, in1=bonp[:], op=ALU.add)

        add("dve", f_fin, rd=[B["bon"], B["bonp"]], wr=[B["fin"]])
        add("dve", lambda v: (v.tensor_tensor(out=yo[:, 0:4, :], in0=bk[g1][:, :].rearrange("p (c t) -> p c t", c=4), in1=fin[:, 0:4, :], op=ALU.mult),
                              v.tensor_tensor(out=yo[:, 4:8, :], in0=bk[g2][:, :].rearrange("p (c t) -> p c t", c=4), in1=fin[:, 4:8, :], op=ALU.mult))[1], rd=[B["fin"]], wr=[b_bk[g1], b_bk[g2], B["yo"]])
        ph.dma(T["yaT"][:, t0:t0 + 128].rearrange("(c p) t -> p c t", p=128), yo, rd=[B["yo"]])

    def capture(fn, *args):
        ph.cap = []
        cnt["pool"] = ((0, 1, 2) if fn is stageA else (3, 4, 5)) if SPLITBANKS else None
        fn(*args)
        ops_, ph.cap = ph.cap, None
        return ops_

    ph.sched = True
    steps = []
    for si, S in enumerate(cfg.seqs):
        nch = S // 128
        steps += [(si, 0, ci) for ci in range(nch)]
        steps += [(si, 1, ci) for ci in reversed(range(nch))]
    curA = capture(stageA, *steps[0], 0)
    for op_ in curA:
        ph._add_cap(*op_)
    for i, stp in enumerate(steps):
        bc_ops = capture(stageBC, *stp, i % 2)
        a_ops = capture(stageA, *steps[i + 1], (i + 1) % 2) if i + 1 < len(steps) else []
        def groups(ops_):
            gs = []
            for o_ in ops_:
                if o_[0] == "pe" or not gs:
                    gs.append([])
                gs[-1].append(o_)
            return gs

        ga, gb = groups(a_ops), groups(bc_ops)
        na, nbc = len(a_ops), len(bc_ops)
        ia = ib = 0
        ja = jb = 0
        while ja < len(ga) or jb < len(gb):
            if ja < len(ga) and jb < len(gb) and not NOINTER:
                ta = ph.peek_ready(ga[ja][0][0], ga[ja][0][2], ga[ja][0][3])
                tb_ = ph.peek_ready(gb[jb][0][0], gb[jb][0][2], gb[jb][0][3])
                if abs(ta - tb_) < 300.0:
                    pick_a = ia * nbc <= ib * na
                else:
                    pick_a = ta < tb_
                if DBG_PICK is not None:
                    DBG_PICK.append(("A" if pick_a else "B", round(ta), round(tb_)))
            else:
                pick_a = jb >= len(gb) or (ja < len(ga) and NOINTER)
            if pick_a:
                for o_ in ga[ja]:
                    ph._add_cap(*o_)
                ia += len(ga[ja])
                ja += 1
            else:
                for o_ in gb[jb]:
                    ph._add_cap(*o_)
                ib += len(gb[jb])
                jb += 1
    return ph.run()

def build(cfg, debug=False, phases="1234"):
    nc = bass.Bass("TRN2", target_bir_lowering=False)
    NT = cfg.ntok
    T = {}

    def inp(name, shape):
        T[name] = nc.dram_tensor(name, list(shape), F32, kind="ExternalInput").ap()

    inp("x", [NT, D])
    inp("w_in", [D, NIN])
    for nm, n in (("norm_mix", D), ("norm_mlp", D), ("norm_final", D), ("mu_shift", NRW), ("w0", 2048), ("a0", 2048), ("k_k", D), ("k_a", D),
                  ("r_k", D), ("ln_x_w", D), ("ln_x_b", D), ("lambda_q1", 64), ("lambda_k1", 64), ("lambda_q2", 64), ("lambda_k2", 64), ("subln_w", 128)):
        inp(nm, [n])
    for nm in ("w_lora_up", "a_lora_up", "g_lora_up"):
        inp(nm, [128, 1024])
    for nm in ("proj_a", "proj_b", "w_out"):
        inp(nm, [D, D])
    inp("w_mlp_in", [D, DFF])
    inp("w_mlp_out", [DFF, D])
    inp("cst", [128, 8])
    inp("pm", [128, 128])
    kind = "ExternalOutput" if debug else "Internal"
    for nm, shp, dt in (("pr", [NRW, NT], F32), ("prs", [NRW, NT], F32), ("qk", [2048, NT], BF16), ("gt", [2048, NT], BF16), ("vv", [NT, 1024], BF16),
                        ("ytmp", [NT, D], F32), ("bon", [D, NT], F32), ("yaT", [D, NT], BF16), ("ybT", [D, NT], BF16),
                        ("hs", [NT, D], F32), ("hnT", [D, NT], BF16)):
        T[nm] = nc.dram_tensor(nm, shp, dt, kind=kind).ap()
    T["y"] = nc.dram_tensor("y", [NT, D], F32, kind="ExternalOutput").ap()
    with ExitStack() as gst, nc.semaphore("bar") as bar_sem:
        GST[0] = ({e: gst.enter_context(nc.semaphore("s_" + e)) for e in ENGS}, [gst.enter_context(nc.semaphore(f"d{i}")) for i in range(NDS)])
        bar = (bar_sem, 0)
        if "1" in phases:
            bar = phase1(nc, cfg, T, bar)
        if "2" in phases or "s" in phases:
            bar = phase15(nc, cfg, T, bar)
        if "2" in phases:
            bar = phase2(nc, cfg, T, bar)
        if "3" in phases:
            bar = phase3(nc, cfg, T, bar)
        if "4" in phases or "a" in phases:
            bar = phase4a(nc, cfg, T, bar)
        if "4" in phases or "b" in phases:
            bar = phase4b(nc, cfg, T, bar)
    return nc


def core_inputs(inputs, xs):
    cst, pm = host_consts()
    f = lambda k: np.ascontiguousarray(np.asarray(inputs[k], np.float32))
    m = {"x": np.ascontiguousarray(np.concatenate(xs, axis=0)), "w_in": f("w_in")[0], "cst": cst, "pm": pm}
    for nm in ("norm_mix", "norm_mlp", "mu_shift", "w0", "a0", "k_k", "k_a", "r_k", "ln_x_w", "ln_x_b", "lambda_q1", "lambda_k1", "lambda_q2", "lambda_k2", "subln_w"):
        m[nm] = f(nm)[0].reshape(-1)
    m["norm_final"] = f("norm_final").reshape(-1)
    m["w_lora_up"] = f("w_lora_up")[0].reshape(128, 1024)
    m["a_lora_up"] = f("a_lora_up")[0].reshape(128, 1024)
    m["g_lora_up"] = f("g_lora_up")[0].reshape(128, 1024)
    for nm in ("proj_a", "proj_b", "w_out", "w_mlp_in", "w_mlp_out"):
        m[nm] = f(nm)[0]
    return m


_NC_CACHE = {}


def kernel(**inputs):
    xp = np.asarray(inputs["x_prompt"], np.float32)
    xs = np.asarray(inputs["x_sample"], np.float32)
    n = 8
    cfg = Cfg([xp.shape[1], xp.shape[1], xs.shape[1]])
    key = tuple(cfg.seqs)
    if key not in _NC_CACHE:
        _NC_CACHE[key] = build(cfg)
    nc = _NC_CACHE[key]
    in_maps = [core_inputs(inputs, [xp[2 * c], xp[2 * c + 1], xs[c]]) for c in range(n)]
    res = run_bass_kernel_spmd(nc, in_maps, core_ids=list(range(n)))
    yp = np.empty_like(xp)
    ys = np.empty_like(xs)
    S1 = xp.shape[1]
    for c in range(n):
        y = res.results[c]["y"]
        yp[2 * c] = y[0:S1]
        yp[2 * c + 1] = y[S1:2 * S1]
        ys[c] = y[2 * S1:]
    return (yp, ys)


def host_consts():
    p = np.arange(128)
    cst = np.zeros((128, 8), np.float32)
    cst[:, 0] = p % 8
    m = ((p % 64) < 16).astype(np.float32)
    cst[:, 1] = m
    cst[:, 2] = np.where((p % 16) >= 8, 1.0, -1.0) * m
    cst[:, 3] = -math.pi
    cst[:, 4] = 1e-6
    pm = np.zeros((128, 128), np.float32)
    for mm in range(128):
        if (mm % 64) < 16:
            k = mm + 8 if (mm % 16) < 8 else mm - 8
            pm[k, mm] = 1.0
    return cst, pm
```

```python
import math
from contextlib import ExitStack
import numpy as np
import concourse.bass as bass
import concourse.mybir as mybir
from concourse.bass_utils import run_bass_kernel_spmd

F32 = mybir.dt.float32
BF16 = mybir.dt.bfloat16
I32 = mybir.dt.int32
AF = mybir.ActivationFunctionType
ALU = mybir.AluOpType
AX = mybir.AxisListType

D = 1024
NRW = 3456
NIN = 8576
DFF = 4096
ENGS = ("pe", "act", "dve", "pool", "sp")
NDS = 24


GST = [None]
NOINTER = False
SPLITBANKS = False
DBG_PICK = None
LAT = 180.0


class Buf:
    __slots__ = ("w", "r", "name")

    def __init__(self, name=""):
        self.w = None
        self.r = []
        self.name = name


class Op:
    __slots__ = ("fn", "deps", "dma", "sig", "val", "sem")

    def __init__(self, fn, deps, dma):
        self.fn = fn
        self.deps = deps
        self.dma = dma
        self.sig = False
        self.val = 0
        self.sem = None


class _Rec:
    def __init__(self):
        self.calls = []

    def __getattr__(self, name):
        def f(*a, **k):
            self.calls.append((name, a, k))
            return None

        return f


class Phase:
    def __init__(self, nc, name, bar):
        self.nc = nc
        self.name = name
        self.bar = bar
        self.ops = {e: [] for e in ENGS}
        self.stack = ExitStack()
        self.ndma = 0
        self.dma_ids = []
        self.nps = 0
        self.sched = False
        self.est_end = {}
        self.eng_free = {}

    def sb(self, name, shape, dt):
        return self.stack.enter_context(self.nc.sbuf_tensor(self.name + "_" + name, list(shape), dt))

    def ps(self, name, shape, dt=F32):
        return self.stack.enter_context(self.nc.psum_tensor(self.name + "_" + name, list(shape), dt))

    cap = None

    def _add_cap(self, eng, fn, rd, wr, dma):
        return self.add(eng, fn, rd, wr, dma)

    def add(self, eng, fn, rd=(), wr=(), dma=False):
        if self.cap is not None:
            self.cap.append((eng, fn, tuple(rd), tuple(wr), dma))
            return None
        if eng != "pe" and not dma:
            rec = _Rec()
            fn(rec)
            me = None
            for name, a, k in rec.calls:
                me = self._add1(eng, (lambda e, name=name, a=a, k=k: getattr(e, name)(*a, **k)), rd, wr, False)
            return me
        return self._add1(eng, fn, rd, wr, dma)

    def est_dur(self, eng, fn, dma):
        rec = _Rec()
        try:
            fn(rec)
        except Exception:
            return 500.0
        tot = 0.0
        for name, a, k in rec.calls:
            o = k.get("out", a[0] if a else None)
            try:
                n = 1
                for d_ in o.shape[1:]:
                    n *= d_
            except Exception:
                n = 128
            if dma:
                tot += 2000.0 + n * 0.5
            elif eng == "pe":
                tot += (100.0 if name == "transpose" else max(64, n) / 2.4 + 8.0)
            elif eng == "act":
                tot += 230.0 + 0.83 * n
            elif eng == "dve":
                tot += 130.0 + 0.95 * n
            else:
                tot += 250.0 + 6.5 * n
        return tot

    def peek_ready(self, eng, rd, wr):
        t = self.eng_free.get(eng, 0.0)
        for b in list(rd) + list(wr):
            if b.w is not None:
                t = max(t, self.est_end.get(b.w, 0.0) + (0.0 if b.w[0] == eng == "pe" else LAT))
        for b in wr:
            for r_ in b.r:
                t = max(t, self.est_end.get(r_, 0.0) + LAT)
        return t

    def _add1(self, eng, fn, rd=(), wr=(), dma=False):
        ops = self.ops[eng]
        me = (eng, len(ops))
        if self.sched:
            st_ = self.peek_ready(eng, rd, wr)
            en_ = st_ + self.est_dur(eng, fn, dma)
            self.est_end[me] = en_
            if not dma:
                self.eng_free[eng] = en_
            else:
                self.eng_free[eng] = st_ + 60.0
        deps = set()
        for b in rd:
            if b.w is not None:
                deps.add(b.w)
        for b in wr:
            if b.w is not None:
                deps.add(b.w)
            deps.update(b.r)
        deps.discard(me)
        op = Op(fn, deps, dma)
        if dma:
            k = self.ndma
            self.ndma += 1
            op.sem = k % NDS
            op.val = 16 * (k // NDS + 1)
            if k >= NDS:
                deps.add(self.dma_ids[k - NDS])
            self.dma_ids.append(me)
        ops.append(op)
        for b in rd:
            b.r.append(me)
        for b in wr:
            b.w = me
            b.r = []
        return me

    def dma(self, out, in_, rd=(), wr=(), q="sp", **kw):
        return self.add(q, lambda e: e.dma_start(out=out, in_=in_, **kw), rd, wr, dma=True)

    def run(self):
        nc = self.nc
        ops = self.ops
        bar_sem, bar_val = self.bar
        fin = set(self.dma_ids)
        for e in ENGS:
            if e != "sp" and ops[e]:
                fin.add((e, len(ops[e]) - 1))
        sems, dsems = GST[0]

        def f_bar(e):
            for sm in list(sems.values()) + list(dsems):
                e.sem_clear(sm)
            return e.sem_inc(bar_sem, 1)

        ops["sp"].append(Op(f_bar, fin, False))
        for e in ENGS:
            for op in ops[e]:
                for d in op.deps:
                    ops[d[0]][d[1]].sig = True
        if True:
            for e in ENGS:
                c = 0
                for op in ops[e]:
                    if op.dma:
                        op.sem = dsems[op.sem]
                    elif op.sig:
                        c += 1
                        op.val = c
                        op.sem = sems[e]

            def emit(ename, eng):
                known = {}
                if bar_val > 0:
                    eng.wait_ge(bar_sem, bar_val)
                for op in ops[ename]:
                    need = {}
                    for d in op.deps:
                        dop = ops[d[0]][d[1]]
                        if ename == "pe" and d[0] == "pe" and not dop.dma:
                            continue
                        key = id(dop.sem)
                        if known.get(key, 0) < dop.val and need.get(key, (None, 0))[1] < dop.val:
                            need[key] = (dop.sem, dop.val)
                    for key, (sem, val) in need.items():
                        eng.wait_ge(sem, val)
                        known[key] = val
                    inst = op.fn(eng)
                    if op.dma:
                        inst.then_inc(op.sem, 16)
                    elif op.sig:
                        inst.then_inc(op.sem, 1)

            with nc.Block() as block:
                @block.sync
                def _(e):
                    emit("sp", e)

                @block.scalar
                def _(e):
                    emit("act", e)

                @block.vector
                def _(e):
                    emit("dve", e)

                @block.gpsimd
                def _(e):
                    emit("pool", e)

                @block.tensor
                def _(e):
                    emit("pe", e)
        self.stack.close()
        return (bar_sem, bar_val + 1)


def make_ident(ph, dt=BF16, n=128, name="ident"):
    f = ph.sb(name + "_f", [128, n], F32)
    o = ph.sb(name, [128, n], dt)
    b = Buf(name)

    def fn(g):
        g.memset(f[:], 0.0)
        g.affine_select(out=f[:], in_=f[:], compare_op=ALU.not_equal, fill=1.0, base=0, pattern=[[-1, n]], channel_multiplier=1)
        return g.tensor_copy(out=o[:], in_=f[:])

    ph.add("pool", fn, wr=[b])
    return o, b


class Cfg:
    def __init__(self, seqs):
        self.seqs = list(seqs)
        self.off = [0]
        for s in self.seqs:
            self.off.append(self.off[-1] + s)
        self.ntok = self.off[-1]
        self.smax = max(self.seqs)


def phase1(nc, cfg, T, bar):
    ph = Phase(nc, "p1", bar)
    NT = cfg.ntok
    NB = NT // 512
    ident, b_ident = make_ident(ph)
    xnT = ph.sb("xnT", [128, 8, NT], BF16)
    b_xnT = [Buf() for _ in range(NT // 128)]
    gcol = ph.sb("gcol", [128, 8], F32)
    b_g = Buf()
    ph.dma(gcol[:], T["norm_mix"].rearrange("(kc p) -> p kc", p=128), wr=[b_g], allow_slow_non_contiguous=True)
    banks = [ph.ps(f"bk{i}", [128, 512], F32) for i in range(6)]
    b_bank = [Buf() for _ in range(6)]
    pst = [ph.ps(f"pt{i}", [128, 1024], BF16) for i in range(2)]
    b_pst = [Buf(), Buf()]

    SM = cfg.smax
    Ct = ph.sb("Ct", [128, SM], F32)
    St = ph.sb("St", [128, SM], F32)
    b_rope = Buf()
    Pm = ph.sb("Pm", [128, 128], BF16)
    SEG = 512
    with_tmp = ph.sb("rtmp", [128, SEG], F32)
    posf = ph.sb("posf", [128, SEG], F32)
    cols = ph.sb("rcols", [128, 8], F32)
    pmf = ph.sb("pmf", [128, 128], F32)
    posi = ph.sb("posi", [128, SEG], I32)
    ki = ph.sb("rki", [128, SEG], I32)
    b_r0 = Buf()
    b_seg = Buf()
    ph.dma(cols[:], T["cst"][:, :], wr=[b_r0])
    ph.dma(pmf[:], T["pm"][:, :], wr=[b_r0])
    ph.add("pool", lambda g: g.tensor_copy(out=Pm[:], in_=pmf[:]), rd=[b_r0], wr=[b_r0])
    ph.add("act", lambda a: a.activation(out=cols[:, 7:8], in_=cols[:, 0:1], func=AF.Exp, scale=-math.log(500000.0) / 8.0), rd=[b_r0], wr=[b_r0])
    TWO_PI = 2.0 * math.pi
    for sg in range(SM // SEG):
        ss_ = slice(sg * SEG, (sg + 1) * SEG)

        def f_pos(g, sg=sg):
            g.iota(posi[:], pattern=[[1, SEG]], base=sg * SEG, channel_multiplier=0)
            return g.tensor_copy(out=posf[:], in_=posi[:])

        ph.add("pool", f_pos, wr=[b_seg])

        def reduce_fn(v, dst, shift):
            v.tensor_scalar(out=dst, in0=posf[:], scalar1=cols[:, 7:8], scalar2=shift, op0=ALU.mult, op1=ALU.add)
            v.tensor_scalar(out=ki[:], in0=dst, scalar1=1.0 / TWO_PI, scalar2=None, op0=ALU.mult)
            v.tensor_copy(out=with_tmp[:], in_=ki[:])
            v.scalar_tensor_tensor(out=dst, in0=with_tmp[:], scalar=-TWO_PI, in1=dst, op0=ALU.mult, op1=ALU.add)
            v.tensor_scalar(out=with_tmp[:], in0=dst, scalar1=math.pi, scalar2=TWO_PI, op0=ALU.is_gt, op1=ALU.mult)
            v.tensor_tensor(out=dst, in0=dst, in1=with_tmp[:], op=ALU.subtract)
            v.tensor_scalar(out=with_tmp[:], in0=dst, scalar1=-math.pi, scalar2=TWO_PI, op0=ALU.is_lt, op1=ALU.mult)
            return v.tensor_tensor(out=dst, in0=dst, in1=with_tmp[:], op=ALU.add)

        def rope_fn2(v, ss_=ss_):
            reduce_fn(v, St[:, ss_], 0.0)
            return reduce_fn(v, Ct[:, ss_], 0.5 * math.pi)

        ph.add("dve", rope_fn2, rd=[b_r0], wr=[b_seg, b_rope])

        def rope_fn3(a, ss_=ss_):
            a.activation(out=St[:, ss_], in_=St[:, ss_], func=AF.Sin)
            return a.activation(out=Ct[:, ss_], in_=Ct[:, ss_], func=AF.Sin)

        ph.add("act", rope_fn3, rd=[], wr=[b_rope])

        def rope_fn4(v, ss_=ss_):
            v.tensor_scalar(out=St[:, ss_], in0=St[:, ss_], scalar1=cols[:, 2:3], scalar2=None, op0=ALU.mult)
            v.tensor_scalar(out=Ct[:, ss_], in0=Ct[:, ss_], scalar1=-1.0, scalar2=None, op0=ALU.add)
            return v.tensor_scalar(out=Ct[:, ss_], in0=Ct[:, ss_], scalar1=cols[:, 1:2], scalar2=1.0, op0=ALU.mult, op1=ALU.add)

        ph.add("dve", rope_fn4, rd=[b_r0], wr=[b_rope])

    xt = [ph.sb(f"xt{i}", [128, D], F32) for i in range(2)]
    b_xt = [Buf(), Buf()]
    xnb = [ph.sb(f"xnb{i}", [128, D], BF16) for i in range(2)]
    b_xnb = [Buf(), Buf()]
    ss = ph.sb("ss", [128, 2, 2], F32)
    b_ss = [Buf(), Buf()]
    xin = T["x"]
    for t in range(NT // 128):
        s = t % 2
        ph.dma(xt[s][:], xin[t * 128:(t + 1) * 128, :], wr=[b_xt[s]])
        ph.add("act", lambda a, s=s: a.activation(out=xnb[s][:], in_=xt[s][:], func=AF.Square, accum_out=ss[:, s, 0:1]), rd=[b_xt[s]], wr=[b_xnb[s], b_ss[s]])

        ph.add("act", lambda a, s=s: a.activation(out=ss[:, s, 1:2], in_=ss[:, s, 0:1], func=AF.Sqrt, scale=1.0 / D, bias=cols[:, 4:5]), rd=[b_r0], wr=[b_ss[s]])
        ph.add("dve", lambda v, s=s: v.reciprocal(out=ss[:, s, 1:2], in_=ss[:, s, 1:2]), rd=[], wr=[b_ss[s]])
        ph.add("act", lambda a, s=s: a.activation(out=xnb[s][:], in_=xt[s][:], func=AF.Copy, scale=ss[:, s, 1:2]), rd=[b_xt[s], b_ss[s]], wr=[b_xnb[s]])

        def f_tr(pe, s=s):
            for kc in range(8):
                i = pe.transpose(out=pst[s][:, kc * 128:(kc + 1) * 128], in_=xnb[s][:, kc * 128:(kc + 1) * 128], identity=ident[:])
            return i

        ph.add("pe", f_tr, rd=[b_xnb[s], b_ident], wr=[b_pst[s]])
        ph.add("dve", lambda v, s=s, t=t: v.tensor_copy(out=xnT[:, :, t * 128:(t + 1) * 128], in_=pst[s][:, :].rearrange("p (k t) -> p k t", k=8)), rd=[], wr=[b_pst[s], b_xnT[t]])

    wf = [ph.sb(f"wf{i}", [128, 8, 128], F32) for i in range(2)]
    b_wf = [Buf(), Buf()]
    wb = [ph.sb(f"wb{i}", [128, 8, 128], BF16) for i in range(2)]
    b_wb = [Buf(), Buf()]
    stg = [ph.sb(f"stg{i}", [128, 512], F32) for i in range(3)]
    b_stg = [Buf() for _ in range(3)]
    stgb = [ph.sb(f"stgb{i}", [128, 512], BF16) for i in range(3)]
    b_stgb = [Buf() for _ in range(3)]
    qraw = [ph.sb("qraw0", [128, 512], BF16)] * 2
    b_qraw = [Buf()] * 2
    t1 = [ph.sb("t1_0", [128, 512], F32)] * 2
    b_t1 = [Buf()] * 2
    t2 = [ph.sb("t2_0", [128, 512], F32)] * 2
    b_t2 = [Buf()] * 2
    w_in = T["w_in"]
    chunks = []
    for c in range(27):
        chunks.append(("rw", c * 128, c * 128))
    for c in range(16):
        chunks.append(("qk", NRW + c * 128, c * 128))
    for c in range(16):
        chunks.append(("gt", NRW + 3072 + c * 128, c * 128))
    for c in range(8):
        chunks.append(("vv", NRW + 2048 + c * 128, c * 128))
    cnt = dict(bank=0, stg=0, stgb=0, q=0)
    blk_pos = []
    for si, S in enumerate(cfg.seqs):
        for j in range(S // 512):
            blk_pos.append(j * 512)
    for ci, (kind, c0, r0) in enumerate(chunks):
        s = ci % 2
        ph.dma(wf[s][:], w_in[:, c0:c0 + 128].rearrange("(kc p) c -> p kc c", p=128), wr=[b_wf[s]])
        ph.add("pool", lambda g, s=s: g.tensor_tensor(out=wb[s][:], in0=wf[s][:], in1=gcol[:].unsqueeze(2).to_broadcast([128, 8, 128]), op=ALU.mult), rd=[b_wf[s], b_g], wr=[b_wb[s]])
        for b in range(NB):
            bk = cnt["bank"] % 4
            cnt["bank"] += 1

            def f_mm(pe, s=s, b=b, bk=bk, kind=kind):
                if kind == "vv":
                    for j in range(4):
                        for kc in range(8):
                            i = pe.matmul(banks[bk][:, j * 128:(j + 1) * 128], lhsT=xnT[:, kc, b * 512 + j * 128:b * 512 + (j + 1) * 128], rhs=wb[s][:, kc, :], start=(kc == 0), stop=(kc == 7))
                    return i
                for kc in range(8):
                    i = pe.matmul(banks[bk][:, :], lhsT=wb[s][:, kc, :], rhs=xnT[:, kc, b * 512:(b + 1) * 512], start=(kc == 0), stop=(kc == 7))
                return i

            ph.add("pe", f_mm, rd=[b_wb[s]] + b_xnT[b * 4:(b + 1) * 4], wr=[b_bank[bk]])
            tok = slice(b * 512, (b + 1) * 512)
            if kind == "rw":
                g_ = cnt["stg"] % 3
                cnt["stg"] += 1
                ph.add("act", lambda a, bk=bk, g_=g_: a.activation(out=stg[g_][:], in_=banks[bk][:, :], func=AF.Copy), wr=[b_bank[bk], b_stg[g_]])
                ph.dma(T["pr"][r0:r0 + 128, tok], stg[g_][:], rd=[b_stg[g_]])
            elif kind == "vv":
                g_ = cnt["stgb"] % 3
                cnt["stgb"] += 1
                ph.add("act", lambda a, bk=bk, g_=g_: a.activation(out=stgb[g_][:], in_=banks[bk][:, :], func=AF.Copy), wr=[b_bank[bk], b_stgb[g_]])
                ph.dma(T["vv"][tok, r0:r0 + 128].rearrange("(j p) c -> p j c", p=128), stgb[g_][:].rearrange("p (j c) -> p j c", j=4), rd=[b_stgb[g_]])
            elif kind == "gt":
                g_ = cnt["stgb"] % 3
                cnt["stgb"] += 1
                ph.add("act", lambda a, bk=bk, g_=g_: a.activation(out=stgb[g_][:], in_=banks[bk][:, :], func=AF.Sigmoid), wr=[b_bank[bk], b_stgb[g_]])
                ph.dma(T["gt"][r0:r0 + 128, tok], stgb[g_][:], rd=[b_stgb[g_]])
            else:
                q_ = cnt["q"] % 2
                cnt["q"] += 1
                g_ = cnt["stgb"] % 3
                cnt["stgb"] += 1
                p0 = blk_pos[b]
                ph.add("act", lambda a, bk=bk, q_=q_: a.activation(out=qraw[q_][:], in_=banks[bk][:, :], func=AF.Copy), wr=[b_bank[bk], b_qraw[q_]])
                ph.add("dve", lambda v, bk=bk, q_=q_, p0=p0: v.tensor_tensor(out=t1[q_][:], in0=banks[bk][:, :], in1=Ct[:, p0:p0 + 512], op=ALU.mult), rd=[b_rope], wr=[b_bank[bk], b_t1[q_]])
                pb = 4 + (cnt["q"] % 2)
                ph.add("pe", lambda pe, q_=q_, pb=pb: pe.matmul(banks[pb][:, :], lhsT=Pm[:], rhs=qraw[q_][:], start=True, stop=True), rd=[b_qraw[q_], b_r0], wr=[b_bank[pb]])
                ph.add("dve", lambda v, pb=pb, q_=q_, p0=p0: v.tensor_tensor(out=t2[q_][:], in0=banks[pb][:, :], in1=St[:, p0:p0 + 512], op=ALU.mult), rd=[b_rope], wr=[b_bank[pb], b_t2[q_]])
                ph.add("dve", lambda g, q_=q_, g_=g_: g.tensor_tensor(out=stgb[g_][:], in0=t1[q_][:], in1=t2[q_][:], op=ALU.add), rd=[b_t1[q_], b_t2[q_]], wr=[b_stgb[g_]])
                ph.dma(T["qk"][r0:r0 + 128, tok], stgb[g_][:], rd=[b_stgb[g_]])

    return ph.run()


def bcast_rows(ap_1d, n):
    return ap_1d.partition_broadcast(128)


def phase3(nc, cfg, T, bar):
    ph = Phase(nc, "p3", bar)
    SM = cfg.smax
    lam_init = 0.8 - 0.6 * math.exp(-0.3 * 0)
    lv = ph.sb("lv", [128, 4, 64], F32)
    b_lv = Buf()
    for i, nm in enumerate(("lambda_q1", "lambda_k1", "lambda_q2", "lambda_k2")):
        ph.dma(lv[:, i, :], T[nm].partition_broadcast(128), wr=[b_lv])
    cc = ph.sb("cc", [128, 8], F32)
    b_cc = Buf()
    ph.dma(cc[:, 3:4], T["subln_w"].rearrange("(p o) -> p o", o=1), wr=[b_cc])
    lt = ph.sb("ltmp", [128, 2, 64], F32)
    ones = ph.sb("ones", [128, 128], BF16)

    def f_c(v):
        v.memset(ones[:], 1.0)
        v.memset(cc[:, 4:5], 1e-5)
        v.tensor_tensor(out=lt[:, 0, :], in0=lv[:, 0, :], in1=lv[:, 1, :], op=ALU.mult)
        v.tensor_tensor(out=lt[:, 1, :], in0=lv[:, 2, :], in1=lv[:, 3, :], op=ALU.mult)
        v.reduce_sum(out=cc[:, 0:2], in_=lt[:], axis=AX.X)
        return v.tensor_scalar(out=cc[:, 3:4], in0=cc[:, 3:4], scalar1=1.0 - lam_init, scalar2=None, op0=ALU.mult)

    ph.add("dve", f_c, rd=[b_lv], wr=[b_cc])
    ph.add("act", lambda a: a.activation(out=cc[:, 0:2], in_=cc[:, 0:2], func=AF.Exp), wr=[b_cc])

    def f_c2(v):
        v.tensor_tensor(out=cc[:, 2:3], in0=cc[:, 1:2], in1=cc[:, 0:1], op=ALU.subtract)
        return v.tensor_scalar(out=cc[:, 2:3], in0=cc[:, 2:3], scalar1=-lam_init, scalar2=None, op0=ALU.add)

    ph.add("dve", f_c2, wr=[b_cc])

    qT = [ph.sb(f"qT{i}", [128, SM], BF16) for i in range(2)]
    kT = [ph.sb(f"kT{i}", [128, SM], BF16) for i in range(2)]
    Vt = [ph.sb(f"Vt{i}", [128, SM // 128, 128], BF16) for i in range(2)]
    b_in = [Buf(), Buf()]
    bS = [ph.ps(f"bS{i}", [128, 512], F32) for i in range(4)]
    b_bS = [Buf() for _ in range(4)]
    bO = [ph.ps(f"bO{i}", [128, 512], F32) for i in range(4)]
    b_bO = [Buf() for _ in range(4)]
    Pt = [ph.sb(f"Pt{i}", [128, 512], BF16) for i in range(4)]
    b_Pt = [Buf() for _ in range(4)]
    rz = [ph.sb(f"rz{i}", [128, 512], F32) for i in range(2)]
    oo = [ph.sb(f"oo{i}", [128, 512], F32) for i in range(2)]
    sq = ph.sb("sq", [128, 512], BF16)
    rs = ph.sb("rs", [128, 512], F32)
    yb = [ph.sb(f"yb{i}", [128, 512], BF16) for i in range(2)]
    b_ep = Buf()
    b_yb = [Buf(), Buf()]
    blocks = []
    it = 0
    for si, S in enumerate(cfg.seqs):
        for h in range(8):
            u = it % 2
            it += 1
            for qb in range(S // 512):
                for kt in range(S // 128):
                    blocks.append((si, h, u, qb, kt))
    state = dict(ne=0)

    def emit_load(si, h, u):
        S = cfg.seqs[si]
        t0 = cfg.off[si]
        ph.dma(qT[u][:, 0:S], T["qk"][h * 128:(h + 1) * 128, t0:t0 + S], wr=[b_in[u]])
        ph.dma(kT[u][:, 0:S], T["qk"][1024 + h * 128:1024 + (h + 1) * 128, t0:t0 + S], wr=[b_in[u]])
        ph.dma(Vt[u][:, 0:S // 128, :], T["vv"][t0:t0 + S, h * 128:(h + 1) * 128].rearrange("(kt p) v -> p kt v", p=128), wr=[b_in[u]])

    def emit_scores(n):
        si, h, u, qb, kt = blocks[n]
        w = (n % 2) * 2
        ks = slice(kt * 128, (kt + 1) * 128)
        qs = slice(qb * 512, (qb + 1) * 512)

        def f_s(pe):
            pe.matmul(bS[w][:, :], lhsT=kT[u][0:64, ks], rhs=qT[u][0:64, qs], start=True, stop=True)
            return pe.matmul(bS[w + 1][:, :], lhsT=kT[u][64:128, ks], rhs=qT[u][64:128, qs], start=True, stop=True)

        ph.add("pe", f_s, rd=[b_in[u]], wr=[b_bS[w], b_bS[w + 1]])
        ph.add("act", lambda a: a.activation(out=Pt[w][:], in_=bS[w][:, :], func=AF.Exp, scale=0.125), wr=[b_bS[w], b_Pt[w]])
        ph.add("act", lambda a: a.activation(out=Pt[w + 1][:], in_=bS[w + 1][:, :], func=AF.Exp, scale=0.125), wr=[b_bS[w + 1], b_Pt[w + 1]])

    def emit_pv(n):
        si, h, u, qb, kt = blocks[n]
        S = cfg.seqs[si]
        t0 = cfg.off[si]
        nkt = S // 128
        w = (n % 2) * 2

        def f_pv(pe):
            st, sp_ = (kt == 0), (kt == nkt - 1)
            pe.matmul(bO[0][:, :], lhsT=Vt[u][:, kt, :], rhs=Pt[w][:], start=st, stop=sp_)
            pe.matmul(bO[2][:, :], lhsT=ones[:], rhs=Pt[w][:], start=st, stop=sp_)
            pe.matmul(bO[1][:, :], lhsT=Vt[u][:, kt, :], rhs=Pt[w + 1][:], start=st, stop=sp_)
            return pe.matmul(bO[3][:, :], lhsT=ones[:], rhs=Pt[w + 1][:], start=st, stop=sp_)

        ph.add("pe", f_pv, rd=[b_in[u], b_Pt[w], b_Pt[w + 1], b_cc], wr=b_bO)
        if kt != nkt - 1:
            return
        e = state["ne"] % 2
        state["ne"] += 1

        def f_e0(a):
            a.activation(out=rz[0][:], in_=bO[2][:, :], func=AF.Ln)
            a.activation(out=rz[1][:], in_=bO[3][:, :], func=AF.Ln)
            a.activation(out=rz[0][:], in_=rz[0][:], func=AF.Exp, scale=-1.0)
            return a.activation(out=rz[1][:], in_=rz[1][:], func=AF.Exp, scale=-1.0)

        ph.add("act", f_e0, wr=[b_bO[2], b_bO[3], b_ep])

        def f_e1(v):
            v.tensor_tensor(out=oo[0][:], in0=bO[0][:, :], in1=rz[0][:], op=ALU.mult)
            v.tensor_tensor(out=oo[1][:], in0=bO[1][:, :], in1=rz[1][:], op=ALU.mult)
            return v.scalar_tensor_tensor(out=oo[0][:], in0=oo[1][:], scalar=cc[:, 2:3], in1=oo[0][:], op0=ALU.mult, op1=ALU.add)

        ph.add("dve", f_e1, rd=[b_cc], wr=b_bO + [b_ep])
        ph.add("act", lambda a: a.activation(out=sq[:], in_=oo[0][:], func=AF.Square), wr=[b_ep])
        ph.add("pe", lambda pe: pe.matmul(bS[w][:, :], lhsT=ones[:], rhs=sq[:], start=True, stop=True), rd=[b_ep], wr=[b_bS[w]])
        ph.add("act", lambda a: (a.activation(out=rs[:], in_=bS[w][:, :], func=AF.Ln, scale=1.0 / 128.0, bias=cc[:, 4:5]),
                                 a.activation(out=rs[:], in_=rs[:], func=AF.Exp, scale=-0.5))[1], rd=[b_cc], wr=[b_bS[w], b_ep])
        ph.add("dve", lambda g: g.scalar_tensor_tensor(out=yb[e][:], in0=oo[0][:], scalar=cc[:, 3:4], in1=rs[:], op0=ALU.mult, op1=ALU.mult), rd=[b_ep, b_cc], wr=[b_yb[e]])
        ph.dma(T["ybT"][h * 128:(h + 1) * 128, t0 + qb * 512:t0 + (qb + 1) * 512], yb[e][:], rd=[b_yb[e]])

    heads = []
    for bl in blocks:
        if not heads or heads[-1] != bl[0:3]:
            heads.append(bl[0:3])
    for hd in heads[0:2]:
        emit_load(*hd)
    emit_scores(0)
    hj = 0
    for n in range(len(blocks)):
        if n + 1 < len(blocks):
            emit_scores(n + 1)
        emit_pv(n)
        if n + 1 == len(blocks) or blocks[n + 1][0:3] != blocks[n][0:3]:
            if hj + 2 < len(heads):
                emit_load(*heads[hj + 2])
            hj += 1
    return ph.run()


def load_w_bf16(ph, dst, b_dst, w_ap, nk, ncols, stage, b_stage, scale_col=None, b_scale=None, eng="dve"):
    step = 512
    kst = stage.shape[1]
    for c0 in range(0, ncols, step):
        for k0 in range(0, nk, kst):
            k1 = min(nk, k0 + kst)
            ph.dma(stage[:, 0:k1 - k0, :], w_ap[k0 * 128:k1 * 128, c0:c0 + step].rearrange("(kc p) c -> p kc c", p=128), wr=[b_stage])
            if scale_col is None:
                ph.add(eng, lambda g, k0=k0, k1=k1, c0=c0: g.tensor_copy(out=dst[:, k0:k1, c0:c0 + step], in_=stage[:, 0:k1 - k0, :]), rd=[b_stage], wr=[b_dst])
            else:
                ph.add(eng, lambda g, k0=k0, k1=k1, c0=c0: g.tensor_tensor(out=dst[:, k0:k1, c0:c0 + step], in0=stage[:, 0:k1 - k0, :], in1=scale_col[:, k0:k1].unsqueeze(2).to_broadcast([128, k1 - k0, step]), op=ALU.mult), rd=[b_stage, b_scale], wr=[b_dst])


def phase4a(nc, cfg, T, bar):
    ph = Phase(nc, "p4a", bar)
    NT = cfg.ntok
    ident, b_ident = make_ident(ph)
    stage = ph.sb("stage", [128, 8, 512], F32)
    b_stage = Buf()
    W = {}
    bW = {}
    for nm in ("proj_a", "proj_b", "w_out"):
        W[nm] = ph.sb("w_" + nm, [128, 8, 1024], BF16)
        bW[nm] = Buf()
        load_w_bf16(ph, W[nm], bW[nm], T[nm], 8, 1024, stage, b_stage)
    cst = ph.sb("cst", [128, 1], F32)
    b_cst = Buf()
    ph.add("pool", lambda g: g.memset(cst[:], 1e-6), wr=[b_cst])
    banks = [ph.ps(f"bk{i}", [128, 512], F32) for i in range(6)]
    b_bank = [Buf() for _ in range(6)]
    pst = ph.ps("pt", [128, 1024], BF16)
    b_pst = Buf()
    ya = ph.sb("ya", [128, 8, 512], BF16)
    ybb = ph.sb("ybb", [128, 8, 512], BF16)
    ga = ph.sb("ga", [128, 8, 512], BF16)
    gb = ph.sb("gb", [128, 8, 512], BF16)
    b_ld = Buf()
    mg = ph.sb("mg", [128, 8, 512], BF16)
    b_mg = Buf()
    m1 = [ph.sb(f"m1_{i}", [128, 512], F32) for i in range(2)]
    m2 = [ph.sb(f"m2_{i}", [128, 512], F32) for i in range(2)]
    b_m = [Buf(), Buf()]
    xt = [ph.sb(f"xt{i}", [128, D], F32) for i in range(2)]
    b_xt = [Buf(), Buf()]
    hh = [ph.sb(f"hh{i}", [128, D], F32) for i in range(2)]
    b_hh = [Buf(), Buf()]
    hb = [ph.sb(f"hb{i}", [128, D], BF16) for i in range(2)]
    b_hb = [Buf(), Buf()]
    hT = [ph.sb(f"hT{i}", [128, 8, 128], BF16) for i in range(2)]
    b_hT = [Buf(), Buf()]
    junk = ph.sb("junk", [128, D], BF16)
    b_junk = Buf()
    ss = ph.sb("ss", [128, 2, 2], F32)
    b_ss = [Buf(), Buf()]
    nb = 0
    nm_ = 0
    nt_ = 0
    for b in range(NT // 512):
        tok = slice(b * 512, (b + 1) * 512)
        for dst, src, r0 in ((ya, "yaT", 0), (ybb, "ybT", 0), (ga, "gt", 0), (gb, "gt", 1024)):
            ph.dma(dst[:], T[src][r0:r0 + 1024, tok].rearrange("(kc p) t -> p kc t", p=128), wr=[b_ld])
        for oc in range(8):
            ba, bb = nb % 6, (nb + 1) % 6
            nb += 2

            def f_ab(pe, oc=oc, ba=ba, bb=bb):
                for kc in range(8):
                    pe.matmul(banks[ba][:, :], lhsT=W["proj_a"][:, kc, oc * 128:(oc + 1) * 128], rhs=ya[:, kc, :], start=(kc == 0), stop=(kc == 7))
                for kc in range(8):
                    i = pe.matmul(banks[bb][:, :], lhsT=W["proj_b"][:, kc, oc * 128:(oc + 1) * 128], rhs=ybb[:, kc, :], start=(kc == 0), stop=(kc == 7))
                return i

            ph.add("pe", f_ab, rd=[b_ld, bW["proj_a"], bW["proj_b"]], wr=[b_bank[ba], b_bank[bb]])
            m = nm_ % 2
            nm_ += 1

            def f_m(v, oc=oc, ba=ba, bb=bb, m=m):
                v.tensor_tensor(out=m1[m][:], in0=banks[ba][:, :], in1=ga[:, oc, :], op=ALU.mult)
                return v.tensor_tensor(out=m2[m][:], in0=banks[bb][:, :], in1=gb[:, oc, :], op=ALU.mult)

            ph.add("dve", f_m, rd=[b_ld], wr=[b_bank[ba], b_bank[bb], b_m[m]])
            ph.add("dve", lambda g, oc=oc, m=m: g.tensor_tensor(out=mg[:, oc, :], in0=m1[m][:], in1=m2[m][:], op=ALU.add), rd=[b_m[m]], wr=[b_mg])
        for ti in range(4):
            t = b * 4 + ti
            s_ = nt_ % 2
            nt_ += 1
            ph.dma(xt[s_][:], T["x"][t * 128:(t + 1) * 128, :], wr=[b_xt[s_]])
            ba, bb = nb % 6, (nb + 1) % 6
            nb += 2

            def f_o(pe, ti=ti, ba=ba, bb=bb):
                for hf, bk in ((0, ba), (1, bb)):
                    for kc in range(8):
                        i = pe.matmul(banks[bk][:, :], lhsT=mg[:, kc, ti * 128:(ti + 1) * 128], rhs=W["w_out"][:, kc, hf * 512:(hf + 1) * 512], start=(kc == 0), stop=(kc == 7))
                return i

            ph.add("pe", f_o, rd=[b_mg, bW["w_out"]], wr=[b_bank[ba], b_bank[bb]])

            def f_h(v, s_=s_, ba=ba, bb=bb):
                v.tensor_tensor(out=hh[s_][:, 0:512], in0=banks[ba][:, :], in1=xt[s_][:, 0:512], op=ALU.add)
                return v.tensor_tensor(out=hh[s_][:, 512:1024], in0=banks[bb][:, :], in1=xt[s_][:, 512:1024], op=ALU.add)

            ph.add("dve", f_h, rd=[b_xt[s_]], wr=[b_bank[ba], b_bank[bb], b_hh[s_]])
            ph.dma(T["hs"][t * 128:(t + 1) * 128, :], hh[s_][:], rd=[b_hh[s_]])
            ph.add("act", lambda a, s_=s_: a.activation(out=junk[:], in_=hh[s_][:], func=AF.Square, accum_out=ss[:, s_, 0:1]), rd=[b_hh[s_]], wr=[b_junk, b_ss[s_]])
            ph.add("act", lambda a, s_=s_: a.activation(out=ss[:, s_, 1:2], in_=ss[:, s_, 0:1], func=AF.Sqrt, scale=1.0 / D, bias=cst[:, 0:1]), rd=[b_cst], wr=[b_ss[s_]])
            ph.add("dve", lambda v, s_=s_: v.reciprocal(out=ss[:, s_, 1:2], in_=ss[:, s_, 1:2]), wr=[b_ss[s_]])
            ph.add("act", lambda a, s_=s_: a.activation(out=hb[s_][:], in_=hh[s_][:], func=AF.Copy, scale=ss[:, s_, 1:2]), rd=[b_hh[s_], b_ss[s_]], wr=[b_hb[s_]])

            def f_tr(pe, s_=s_):
                for kc in range(8):
                    i = pe.transpose(out=pst[:, kc * 128:(kc + 1) * 128], in_=hb[s_][:, kc * 128:(kc + 1) * 128], identity=ident[:])
                return i

            ph.add("pe", f_tr, rd=[b_hb[s_], b_ident], wr=[b_pst])
            ph.add("dve", lambda v, s_=s_: v.tensor_copy(out=hT[s_][:], in_=pst[:, :].rearrange("p (k t) -> p k t", k=8)), wr=[b_pst, b_hT[s_]])
            ph.dma(T["hnT"][:, t * 128:(t + 1) * 128].rearrange("(kc p) t -> p kc t", p=128), hT[s_][:], rd=[b_hT[s_]])
    return ph.run()


def phase4b(nc, cfg, T, bar):
    ph = Phase(nc, "p4b", bar)
    NT = cfg.ntok
    stage = ph.sb("stage", [128, 4, 512], F32)
    b_stage = Buf()
    gcol = ph.sb("gcol", [128, 8], F32)
    b_g = Buf()
    ph.dma(gcol[:], T["norm_mlp"].rearrange("(kc p) -> p kc", p=128), wr=[b_g], allow_slow_non_contiguous=True)
    gfin = ph.sb("gfin", [128, D], F32)
    b_gf = Buf()
    ph.dma(gfin[:], T["norm_final"].partition_broadcast(128), wr=[b_gf])
    w1 = ph.sb("w1", [128, 8, DFF], BF16)
    b_w1 = Buf()
    w2 = ph.sb("w2", [128, 32, D], BF16)
    b_w2 = Buf()
    load_w_bf16(ph, w1, b_w1, T["w_mlp_in"], 8, DFF, stage, b_stage, scale_col=gcol, b_scale=b_g)
    load_w_bf16(ph, w2, b_w2, T["w_mlp_out"], 32, D, stage, b_stage)
    cst = ph.sb("cst", [128, 1], F32)
    b_cst = Buf()
    ph.add("pool", lambda g: g.memset(cst[:], 1e-6), wr=[b_cst])
    banks = [ph.ps(f"bk{i}", [128, 512], F32) for i in range(8)]
    b_bank = [Buf() for _ in range(8)]
    hn = [ph.sb("hn0", [128, 8, 512], BF16)] * 2
    b_hn = [Buf()] * 2
    hid = ph.sb("hid", [128, 32, 512], BF16)
    b_hid = [Buf() for _ in range(32)]
    rl = [ph.sb(f"rl{i}", [128, 512], F32) for i in range(2)]
    b_rl = [Buf(), Buf()]
    ht = [ph.sb(f"ht{i}", [128, D], F32) for i in range(2)]
    b_ht = [Buf(), Buf()]
    oo = [ph.sb(f"oo{i}", [128, D], F32) for i in range(2)]
    b_oo = [Buf(), Buf()]
    junk = ph.sb("junk", [128, D], BF16)
    b_junk = Buf()
    ss = ph.sb("ss", [128, 2, 2], F32)
    b_ss = [Buf(), Buf()]
    nb = 0
    nr = 0
    nt_ = 0
    for b in range(NT // 512):
        u = b % 2
        tok = slice(b * 512, (b + 1) * 512)
        ph.dma(hn[u][:], T["hnT"][:, tok].rearrange("(kc p) t -> p kc t", p=128), wr=[b_hn[u]])
        for fc in range(32):
            bk = nb % 8
            nb += 1

            def f_h(pe, fc=fc, bk=bk, u=u):
                for kc in range(8):
                    i = pe.matmul(banks[bk][:, :], lhsT=w1[:, kc, fc * 128:(fc + 1) * 128], rhs=hn[u][:, kc, :], start=(kc == 0), stop=(kc == 7))
                return i

            ph.add("pe", f_h, rd=[b_w1, b_hn[u]], wr=[b_bank[bk]])
            r_ = nr % 2
            nr += 1
            ph.add("act", lambda a, bk=bk, r_=r_: a.activation(out=rl[r_][:], in_=banks[bk][:, :], func=AF.Relu), wr=[b_bank[bk], b_rl[r_]])
            ph.add("dve", lambda g, fc=fc, r_=r_: g.tensor_tensor(out=hid[:, fc, :], in0=rl[r_][:], in1=rl[r_][:], op=ALU.mult), rd=[b_rl[r_]], wr=[b_hid[fc]])
        for ti in range(4):
            t = b * 4 + ti
            s_ = nt_ % 2
            nt_ += 1
            ph.dma(ht[s_][:], T["hs"][t * 128:(t + 1) * 128, :], wr=[b_ht[s_]])
            ba, bb = nb % 8, (nb + 1) % 8
            nb += 2

            def f_o(pe, ti=ti, ba=ba, bb=bb):
                for hf, bk in ((0, ba), (1, bb)):
                    for fc in range(32):
                        i = pe.matmul(banks[bk][:, :], lhsT=hid[:, fc, ti * 128:(ti + 1) * 128], rhs=w2[:, fc, hf * 512:(hf + 1) * 512], start=(fc == 0), stop=(fc == 31))
                return i

            ph.add("pe", f_o, rd=[b_w2] + b_hid, wr=[b_bank[ba], b_bank[bb]])

            def f_r(v, s_=s_, ba=ba, bb=bb):
                v.tensor_tensor(out=oo[s_][:, 0:512], in0=banks[ba][:, :], in1=ht[s_][:, 0:512], op=ALU.add)
                return v.tensor_tensor(out=oo[s_][:, 512:1024], in0=banks[bb][:, :], in1=ht[s_][:, 512:1024], op=ALU.add)

            ph.add("dve", f_r, rd=[b_ht[s_]], wr=[b_bank[ba], b_bank[bb], b_oo[s_]])
            ph.add("act", lambda a, s_=s_: a.activation(out=junk[:], in_=oo[s_][:], func=AF.Square, accum_out=ss[:, s_, 0:1]), rd=[b_oo[s_]], wr=[b_junk, b_ss[s_]])
            ph.add("act", lambda a, s_=s_: a.activation(out=ss[:, s_, 1:2], in_=ss[:, s_, 0:1], func=AF.Sqrt, scale=1.0 / D, bias=cst[:, 0:1]), rd=[b_cst], wr=[b_ss[s_]])
            ph.add("dve", lambda v, s_=s_: v.reciprocal(out=ss[:, s_, 1:2], in_=ss[:, s_, 1:2]), wr=[b_ss[s_]])
            ph.add("dve", lambda g, s_=s_: g.scalar_tensor_tensor(out=oo[s_][:], in0=oo[s_][:], scalar=ss[:, s_, 1:2], in1=gfin[:], op0=ALU.mult, op1=ALU.mult), rd=[b_ss[s_], b_gf], wr=[b_oo[s_]])
            ph.dma(T["y"][t * 128:(t + 1) * 128, :], oo[s_][:], rd=[b_oo[s_]])
    return ph.run()


def phase15(nc, cfg, T, bar):
    ph = Phase(nc, "p15", bar)
    TB = 256
    mu = ph.sb("mu", [128, 27], F32)
    omu = ph.sb("omu", [128, 27], F32)
    hmu = ph.sb("hmu", [128, 27], F32)
    b_par = Buf()
    ph.dma(mu[:], T["mu_shift"].rearrange("(c p) -> p c", p=128), wr=[b_par], allow_slow_non_contiguous=True)
    ph.add("dve", lambda v: v.tensor_scalar(out=omu[:], in0=mu[:], scalar1=-1.0, scalar2=1.0, op0=ALU.mult, op1=ALU.add), wr=[b_par])
    ph.add("dve", lambda v: v.tensor_scalar(out=hmu[:], in0=mu[:], scalar1=0.5, scalar2=None, op0=ALU.mult), wr=[b_par])
    P = [ph.sb(f"P{i}", [128, 27, TB + 2], F32) for i in range(2)]
    b_P = [Buf(), Buf()]
    t1 = [ph.sb(f"t1_{i}", [128, 27, TB], F32) for i in range(2)]
    b_t1 = [Buf(), Buf()]
    t2 = [ph.sb(f"t2_{i}", [128, 27, TB], F32) for i in range(2)]
    b_t2 = [Buf(), Buf()]
    n = 0
    for si, S in enumerate(cfg.seqs):
        base = cfg.off[si]
        nb_ = S // TB
        for bi in range(nb_):
            u = n % 2
            n += 1
            t0 = base + bi * TB
            lo = 1 if bi == 0 else 0
            hi = TB + 1 if bi == nb_ - 1 else TB + 2
            if lo == 1:
                ph.add("pool", lambda g, u=u: g.memset(P[u][:, :, 0:1], 0.0), wr=[b_P[u]])
            if hi == TB + 1:
                ph.add("pool", lambda g, u=u: g.memset(P[u][:, :, TB + 1:TB + 2], 0.0), wr=[b_P[u]])
            for c0_, c1_ in ((0, 9), (9, 18), (18, 27)):
                ph.dma(P[u][:, c0_:c1_, lo:hi], T["pr"][c0_ * 128:c1_ * 128, t0 - 1 + lo:t0 - 1 + hi].rearrange("(c p) t -> p c t", p=128), wr=[b_P[u]])
            for c in range(27):
                ph.add("act", lambda a, u=u, c=c: a.activation(out=t1[u][:, c, :], in_=P[u][:, c, 1:TB + 1], func=AF.Copy, scale=omu[:, c:c + 1]), rd=[b_P[u], b_par], wr=[b_t1[u]])

            def f_sh(v, u=u):
                v.tensor_tensor(out=t2[u][:], in0=P[u][:, :, 0:TB], in1=P[u][:, :, 2:TB + 2], op=ALU.add)
                v.tensor_tensor(out=t2[u][:], in0=t2[u][:], in1=hmu[:].unsqueeze(2).to_broadcast([128, 27, TB]), op=ALU.mult)
                return v.tensor_tensor(out=t2[u][:], in0=t2[u][:], in1=t1[u][:], op=ALU.add)

            ph.add("dve", f_sh, rd=[b_P[u], b_par, b_t1[u]], wr=[b_t2[u]])
            for c0_, c1_ in ((0, 9), (9, 18), (18, 27)):
                ph.dma(T["prs"][c0_ * 128:c1_ * 128, t0:t0 + TB].rearrange("(c p) t -> p c t", p=128), t2[u][:, c0_:c1_, :], rd=[b_t2[u]])
    return ph.run()


def phase2(nc, cfg, T, bar):
    ph = Phase(nc, "p2", bar)
    NT = cfg.ntok
    C0 = -math.exp(-0.5)
    sb, add = ph.sb, ph.add
    identb, b_identb = make_ident(ph, BF16, name="idb")
    Y = sb("Y", [128, 1024], F32)
    Yp = sb("Yp", [128, 1024], F32)
    wst = Yp
    MK = Y[:, 0:512].rearrange("p (m c) -> p m c", m=4)
    TRt = sb("TRt", [128, 4, 128], BF16)
    TR = TRt
    identf = ph.sb("idf", [128, 128], F32)
    BD = sb("BD", [128, 128], BF16)
    onesr = sb("onesr", [1, 128], BF16)
    mA = [sb(f"mA{d}", [128, 256], F32) for d in range(2)]
    mL = [sb(f"mL{d}", [128, 256], F32) for d in range(2)]
    trc = [sb(f"trc{d}", [128, 384], BF16) for d in range(2)]
    b_cst = Buf()

    def f_masks(g):
        g.memset(MK, 1.0)
        g.affine_select(out=MK[:, 0, :], in_=MK[:, 0, :], compare_op=ALU.is_gt, fill=0.0, base=0, pattern=[[1, 128]], channel_multiplier=-1)
        g.affine_select(out=MK[:, 1, :], in_=MK[:, 1, :], compare_op=ALU.is_ge, fill=0.0, base=0, pattern=[[1, 128]], channel_multiplier=-1)
        g.affine_select(out=MK[:, 2, :], in_=MK[:, 2, :], compare_op=ALU.is_gt, fill=0.0, base=0, pattern=[[-1, 128]], channel_multiplier=1)
        g.affine_select(out=MK[:, 3, :], in_=MK[:, 3, :], compare_op=ALU.is_ge, fill=0.0, base=0, pattern=[[-1, 128]], channel_multiplier=1)
        g.tensor_scalar(out=TR[:], in0=MK, scalar1=C0, scalar2=None, op0=ALU.mult)
        g.memset(identf[:], 0.0)
        g.affine_select(out=identf[:], in_=identf[:], compare_op=ALU.not_equal, fill=1.0, base=0, pattern=[[-1, 128]], channel_multiplier=1)
        g.memset(BD[:], 0.0)
        g.memset(BD[0:64, 0:64], 1.0)
        g.memset(BD[64:128, 64:128], 1.0)
        g.memset(onesr[:], 1.0)
        for d in range(2):
            st, inc, lm = (0, 1, 2) if d == 0 else (2, 3, 0)
            g.tensor_copy(out=mA[d][:, 0:128], in_=MK[:, st, :])
            g.tensor_copy(out=mA[d][:, 128:256], in_=MK[:, inc, :])
            for q in range(2):
                g.tensor_copy(out=mL[d][:, q * 128:(q + 1) * 128], in_=MK[:, lm, :])
            g.tensor_copy(out=trc[d][:, 0:128], in_=TR[:, inc, :])
            g.tensor_copy(out=trc[d][:, 128:256], in_=TR[:, st, :])
            i = g.tensor_copy(out=trc[d][:, 256:384], in_=TR[:, lm, :])
        return i

    add("pool", f_masks, wr=[b_cst])
    par = sb("par", [128, 8, 8], F32)
    b_par = Buf()
    for i, ap_ in enumerate((T["k_k"], T["k_a"], T["k_a"], T["r_k"], T["ln_x_w"], T["ln_x_b"], T["a0"][0:1024], T["a0"][1024:2048])):
        ph.dma(par[:, i, :], ap_.rearrange("(c p) -> p c", p=128), wr=[b_par], allow_slow_non_contiguous=True)
    add("pool", lambda g: g.tensor_scalar(out=par[:, 2, :], in0=par[:, 2, :], scalar1=-1.0, scalar2=1.0, op0=ALU.mult, op1=ALU.add), wr=[b_par])
    b_wst = Buf()
    wup = sb("wup", [128, 1024], BF16)
    aup = sb("aup", [128, 1024], BF16)
    gup = sb("gup", [128, 1024], BF16)
    b_w = Buf()
    for dst, src in ((wup, T["w_lora_up"]), (aup, T["a_lora_up"]), (gup, T["g_lora_up"])):
        ph.dma(wst[:], src[:, :], wr=[b_wst])
        add("pool", lambda g, dst=dst: g.tensor_copy(out=dst[:], in_=wst[:]), rd=[b_wst], wr=[b_w])
    w0h = sb("w0h", [1, 2, 1024], BF16)
    w0l = sb("w0l", [1, 2, 1024], BF16)
    w0t = Y
    for d_ in range(2):
        ph.dma(wst[0:1, :], T["w0"][d_ * 1024:(d_ + 1) * 1024].rearrange("(o c) -> o c", o=1), wr=[b_wst])

        def f_w0(g, d_=d_):
            g.tensor_copy(out=w0h[0:1, d_, :], in_=wst[0:1, :])
            g.tensor_copy(out=w0t[0:1, :], in_=w0h[0:1, d_, :])
            g.tensor_tensor(out=w0t[0:1, :], in0=wst[0:1, :], in1=w0t[0:1, :], op=ALU.subtract)
            return g.tensor_copy(out=w0l[0:1, d_, :], in_=w0t[0:1, :])

        add("pool", f_w0, rd=[], wr=[b_wst, b_w, b_cst])
    eps = sb("eps", [128, 2], F32)
    add("pool", lambda g: (g.memset(eps[:, 0:1], 64e-5), g.memset(eps[:, 1:2], 1e-24))[1], wr=[b_cst])

    bk = [ph.ps(f"bk{i}", [128, 512], F32) for i in range(6)]
    b_bk = [Buf() for _ in range(6)]
    bt = [ph.ps(f"bt{i}", [128, 1024], BF16) for i in range(2)]
    b_bt = [Buf(), Buf()]
    cnt = dict(b=0, t=0)

    def nb():
        pool_ = cnt.get("pool")
        if pool_ is None:
            i = cnt["b"] % 6
            cnt["b"] += 1
            return i
        k_ = "b" + str(pool_[0])
        i = pool_[cnt.get(k_, 0) % len(pool_)]
        cnt[k_] = cnt.get(k_, 0) + 1
        return i

    def ntb():
        i = cnt["t"] % 2
        cnt["t"] += 1
        return i

    SH = sb("SH", [128, 27, 128], F32)
    bS_ = {k_: Buf("SH" + k_) for k_ in ("r", "k", "v", "lw", "la", "lg")}
    f4 = lambda nm: sb(nm, [128, 8, 128], F32)
    h4 = lambda nm: sb(nm, [128, 8, 128], BF16)
    av, kk, kd, Ei, tmp1 = [f4(n) for n in ("av", "kk", "kd", "Ei", "tmp1")]
    bb_ = av
    En, Ee, Es = [h4(n) for n in ("En", "Ee", "Es")]
    tmp2 = tmp1
    kkr = tmp1
    rn = kd
    names = ("av", "kk", "bb", "kd", "E", "tmp1", "tl", "lab", "sig", "sq", "AR", "bt", "kt", "Bg", "Kg", "vb",
             "Atok", "Bgtok", "Kgtok", "Vtok", "AM", "KM", "LL", "PL0", "PL1", "PA0", "PA1", "X0", "X1", "Zb", "AhT", "What", "Ub", "S", "Sbf",
             "Y", "Yp", "sg", "bon", "bonp", "st", "yo", "Stmp", "Gc")
    dbl = ("AR", "Atok", "Bgtok", "Kgtok", "Vtok", "AM", "KM", "LL", "sg", "bon", "Gc")
    B0 = {n: Buf(n) for n in names}
    Bp = [dict(B0), dict(B0)]
    for n_ in dbl:
        Bp[1][n_] = Buf(n_ + "1")
    grp = ("AM", "KM", "LL", "PL0", "PL1", "PA0", "PA1", "X0", "X1")
    for n_ in grp:
        l0 = [Buf(n_ + str(g_)) for g_ in range(4)]
        Bp[0][n_] = l0
        Bp[1][n_] = [Buf(n_ + "b" + str(g_)) for g_ in range(4)] if n_ in dbl else l0
    for Bx in Bp:
        Bx["bb"] = Bx["av"]
        Bx["kkr"] = Bx["tmp1"]
        Bx["tmp2"] = Bx["tmp1"]
        Bx["rn"] = Bx["kd"]
        Bx["fin"] = Bx["What"]
        Bx["yn"] = Bx["Yp"]
    tl = sb("tl", [128, 128], BF16)
    lab = sb("lab", [128, 128], BF16)
    sig = sb("sig", [128, 1024], BF16)
    sq = h4("sq")
    btT, ktT, BgT, KgT, vb = [h4(n) for n in ("btT", "ktT", "BgT", "KgT", "vbb")]
    Zb = sb("Zb", [128, 1024], BF16)
    two = lambda nm, shp, dt: [sb(nm + "0", shp, dt), sb(nm + "1", shp, dt)]
    sgbs = two("sgb", [128, 128], BF16)
    ARs = two("AR", [128, 8, 256], BF16)
    Atoks, Bgtoks, Kgtoks, Vtoks = [two(n, [128, 1024], BF16) for n in ("Atok", "Bgtok", "Kgtok", "Vtok")]
    AMs = two("AM", [128, 16, 256], BF16)
    KMs = two("KM", [128, 16, 256], BF16)
    LLs = two("LL", [128, 16, 128], BF16)
    bons = two("bon", [128, 8, 128], F32)
    Gcs = two("Gc", [128, 8, 1], F32)
    PL12 = [sb("PL1", [128, 16, 128], BF16), sb("PL2", [128, 16, 128], BF16)]
    PA = [sb("PA1", [128, 16, 128], BF16), sb("PA2", [128, 16, 128], BF16)]
    X = [sb("X0", [128, 16, 128], BF16), sb("X1", [128, 16, 128], BF16)]
    AhT = sb("AhT", [128, 8, 128], BF16)
    What = sb("What", [128, 1024], F32)
    Ub = sb("Ub", [128, 1024], BF16)
    Sf = sb("Sf", [128, 8, 64], F32)
    Stmp = sb("Stmp", [128, 8, 64], F32)
    Sbf = sb("Sbf", [128, 8, 64], BF16)
    bonp = f4("bonp")
    st = sb("st", [128, 16, 4], F32)
    yn2 = Yp
    yn = yn2[:].rearrange("p (h v) -> p h v", v=64)
    fin = What[:].rearrange("p (c t) -> p c t", c=8)
    yo = Zb[:].rearrange("p (c t) -> p c t", c=8)
    for Bx in Bp:
        Bx["yo"] = Bx["Zb"]
    dramB = {}

    def dB(kind, t0):
        return dramB.setdefault((kind, t0), Buf())

    def stageA(si, d, ci, p):
        S = cfg.seqs[si]
        base = cfg.off[si]
        nch = S // 128
        t0 = base + ci * 128
        first = (ci == 0) if d == 0 else (ci == nch - 1)
        final = (d == 1)
        B = Bp[p]
        AR, Atok, Bgtok, Kgtok, Vtok, AM, KM, LL, sgb, bon, Gc = ARs[p], Atoks[p], Bgtoks[p], Kgtoks[p], Vtoks[p], AMs[p], KMs[p], LLs[p], sgbs[p], bons[p], Gcs[p]
        PL = [LL, PL12[0], PL12[1]]
        R, Kx, Vx = SH[:, 0:8, :], SH[:, 8:16, :], SH[:, 16:24, :]
        dsl = slice(d * 64, (d + 1) * 64)
        bc = lambda i: par[:, i, :].unsqueeze(2).to_broadcast([128, 8, 128])
        ec = 127 if d == 0 else 0
        for k_, c0_, c1_ in (("lw", 24, 25), ("la", 25, 26), ("k", 8, 16), ("r", 0, 8), ("v", 16, 24), ("lg", 26, 27)):
            ph.dma(SH[:, c0_:c1_, :], T["prs"][c0_ * 128:c1_ * 128, t0:t0 + 128].rearrange("(c p) t -> p c t", p=128), wr=[bS_[k_]])
        add("act", lambda a: a.activation(out=tl[:], in_=SH[:, 24, :], func=AF.Tanh), rd=[bS_["lw"]], wr=[B["tl"]])
        add("act", lambda a: a.activation(out=lab[:], in_=SH[:, 25, :], func=AF.Copy), rd=[bS_["la"]], wr=[B["lab"]])
        w1_, w2_ = nb(), nb()

        def f_lw(pe):
            for hf, b_ in ((0, w1_), (1, w2_)):
                cs = slice(hf * 512, (hf + 1) * 512)
                pe.matmul(bk[b_][:, :], lhsT=tl[dsl, :], rhs=wup[dsl, cs], start=True, stop=False)
                pe.matmul(bk[b_][:, :], lhsT=onesr[0:1, :], rhs=w0h[0:1, d, cs], start=False, stop=False)
                i = pe.matmul(bk[b_][:, :], lhsT=onesr[0:1, :], rhs=w0l[0:1, d, cs], start=False, stop=True)
            return i

        add("pe", f_lw, rd=[B["tl"], b_w, b_cst], wr=[b_bk[w1_], b_bk[w2_]])
        add("act", lambda a: a.activation(out=sig[:, 0:512], in_=bk[w1_][:, :], func=AF.Sigmoid), wr=[b_bk[w1_], B["sig"]])
        add("act", lambda a: a.activation(out=sig[:, 512:1024], in_=bk[w2_][:, :], func=AF.Sigmoid), wr=[b_bk[w2_], B["sig"]])
        a1_, a2_ = nb(), nb()

        def f_la(pe):
            for c in range(8):
                b_ = a1_ if c < 4 else a2_
                i = pe.matmul(bk[b_][:, (c % 4) * 128:(c % 4 + 1) * 128], lhsT=aup[dsl, c * 128:(c + 1) * 128], rhs=lab[dsl, :], start=True, stop=True)
            return i

        add("pe", f_la, rd=[B["lab"], b_w], wr=[b_bk[a1_], b_bk[a2_]])
        for c in range(8):
            b_ = a1_ if c < 4 else a2_
            add("act", lambda a, c=c, b_=b_: a.activation(out=av[:, c, :], in_=bk[b_][:, (c % 4) * 128:(c % 4 + 1) * 128], func=AF.Sigmoid, bias=par[:, 6 + d, c:c + 1]), rd=[b_par], wr=[b_bk[b_], B["av"]])
        for hf in range(2):
            cb = [nb(), nb(), nb()]

            def f_cum(pe, hf=hf, cb=cb):
                for cc_ in range(4):
                    c = hf * 4 + cc_
                    for x in range(3):
                        i = pe.matmul(bk[cb[x]][:, cc_ * 128:(cc_ + 1) * 128], lhsT=sig[:, c * 128:(c + 1) * 128], rhs=trc[d][:, x * 128:(x + 1) * 128], start=True, stop=True)
                return i

            add("pe", f_cum, rd=[B["sig"], b_cst], wr=[b_bk[i] for i in cb])
            hs_ = slice(hf * 4, hf * 4 + 4)
            v3 = lambda b_: bk[b_][:, :].rearrange("p (c t) -> p c t", c=4)
            add("act", lambda a, cb=cb, hs_=hs_: (a.activation(out=Ei[:, hs_, :], in_=v3(cb[0]), func=AF.Exp), a.activation(out=En[:, hs_, :], in_=v3(cb[0]), func=AF.Exp, scale=-1.0))[1], wr=[b_bk[cb[0]], B["E"]])
            add("act", lambda a, cb=cb, hs_=hs_: a.activation(out=Ee[:, hs_, :], in_=v3(cb[1]), func=AF.Exp), wr=[b_bk[cb[1]], B["E"]])
            add("act", lambda a, cb=cb, hs_=hs_: a.activation(out=Es[:, hs_, :], in_=v3(cb[2]), func=AF.Exp), wr=[b_bk[cb[2]], B["E"]])
        add("dve", lambda v: v.tensor_tensor(out=kkr[:], in0=Kx, in1=bc(0), op=ALU.mult), rd=[bS_["k"], b_par], wr=[B["kkr"]])
        add("act", lambda a: a.activation(out=sq[:], in_=kkr[:], func=AF.Square), rd=[B["kkr"]], wr=[B["sq"]])
        n1_, n2_ = nb(), nb()

        def f_nrm(pe):
            sqf = sq[:].rearrange("p c t -> p (c t)")
            pe.matmul(bk[n1_][:, :], lhsT=BD[:], rhs=sqf[:, 0:512], start=True, stop=True)
            return pe.matmul(bk[n2_][:, :], lhsT=BD[:], rhs=sqf[:, 512:1024], start=True, stop=True)

        add("pe", f_nrm, rd=[B["sq"], b_cst], wr=[b_bk[n1_], b_bk[n2_]])
        add("act", lambda a: a.activation(out=rn[:, 0:4, :], in_=bk[n1_][:, :].rearrange("p (c t) -> p c t", c=4), func=AF.Ln, bias=eps[:, 1:2]), rd=[b_cst], wr=[b_bk[n1_], B["rn"]])
        add("act", lambda a: a.activation(out=rn[:, 4:8, :], in_=bk[n2_][:, :].rearrange("p (c t) -> p c t", c=4), func=AF.Ln, bias=eps[:, 1:2]), rd=[b_cst], wr=[b_bk[n2_], B["rn"]])
        add("act", lambda a: a.activation(out=rn[:], in_=rn[:], func=AF.Exp, scale=-0.5), wr=[B["rn"]])
        add("dve", lambda v: v.tensor_tensor(out=kk[:], in0=kkr[:], in1=rn[:], op=ALU.mult), rd=[B["kkr"], B["rn"]], wr=[B["kk"]])

        def f_kd(g):
            g.tensor_tensor(out=tmp1[:], in0=av[:], in1=bc(1), op=ALU.mult)
            g.tensor_tensor(out=tmp1[:], in0=tmp1[:], in1=bc(2), op=ALU.add)
            g.tensor_tensor(out=kd[:], in0=tmp1[:], in1=Kx, op=ALU.mult)
            return g.tensor_tensor(out=bb_[:], in0=kk[:], in1=av[:], op=ALU.mult)

        add("dve", f_kd, rd=[B["av"], b_par, bS_["k"], B["kk"]], wr=[B["tmp1"], B["kd"], B["bb"]])

        def f_sc1(v):
            v.scalar_tensor_tensor(out=AR[:, :, 0:128], in0=kk[:], scalar=-1.0, in1=Ee[:], op0=ALU.mult, op1=ALU.mult)
            v.tensor_tensor(out=AR[:, :, 128:256], in0=R, in1=Ei[:], op=ALU.mult)
            return v.tensor_tensor(out=btT[:], in0=bb_[:], in1=En[:], op=ALU.mult)

        add("dve", f_sc1, rd=[B["kk"], B["E"], bS_["r"], B["bb"]], wr=[B["AR"], B["bt"]])

        add("dve", lambda g: g.tensor_tensor(out=ktT[:], in0=kd[:], in1=En[:], op=ALU.mult), rd=[B["kd"], B["E"]], wr=[B["kt"]])
        add("pool", lambda g: g.tensor_tensor(out=BgT[:], in0=bb_[:], in1=Es[:], op=ALU.mult), rd=[B["E"], B["bb"]], wr=[B["Bg"]])
        add("pool", lambda g: g.tensor_tensor(out=KgT[:], in0=kd[:], in1=Es[:], op=ALU.mult), rd=[B["kd"], B["E"]], wr=[B["Kg"]])
        add("act", lambda a: a.activation(out=vb[:], in_=Vx, func=AF.Copy), rd=[bS_["v"]], wr=[B["vb"]])
        add("dve", lambda g: (g.tensor_tensor(out=tmp2[:], in0=R, in1=bc(3), op=ALU.mult), g.tensor_tensor(out=sq[:], in0=tmp2[:], in1=kd[:], op=ALU.mult))[1], rd=[bS_["r"], b_par, B["kd"]], wr=[B["tmp2"], B["sq"]])
        o1_, o2_ = nb(), nb()

        def f_bon(pe):
            sqf = sq[:].rearrange("p c t -> p (c t)")
            pe.matmul(bk[o1_][:, :], lhsT=BD[:], rhs=sqf[:, 0:512], start=True, stop=True)
            return pe.matmul(bk[o2_][:, :], lhsT=BD[:], rhs=sqf[:, 512:1024], start=True, stop=True)

        add("pe", f_bon, rd=[B["sq"], b_cst], wr=[b_bk[o1_], b_bk[o2_]])
        add("dve", lambda v: v.tensor_tensor(out=bon[:, 0:4, :], in0=bk[o1_][:, :].rearrange("p (c t) -> p c t", c=4), in1=SH[:, 16:20, :], op=ALU.mult), rd=[bS_["v"]], wr=[b_bk[o1_], B["bon"]])
        add("dve", lambda v: v.tensor_tensor(out=bon[:, 4:8, :], in0=bk[o2_][:, :].rearrange("p (c t) -> p c t", c=4), in1=SH[:, 20:24, :], op=ALU.mult), rd=[bS_["v"]], wr=[b_bk[o2_], B["bon"]])
        for src, srcb, dst, dstb in ((AR, "AR", Atok, "Atok"), (BgT, "Bg", Bgtok, "Bgtok"), (KgT, "Kg", Kgtok, "Kgtok"), (vb, "vb", Vtok, "Vtok")):
            tb = ntb()

            def f_tr(pe, src=src, tb=tb):
                for c in range(8):
                    i = pe.transpose(out=bt[tb][:, c * 128:(c + 1) * 128], in_=src[:, c, 0:128], identity=identb[:])
                return i

            add("pe", f_tr, rd=[B[srcb], b_identb], wr=[b_bt[tb]])
            add("act", lambda a, dst=dst, tb=tb: a.activation(out=dst[:], in_=bt[tb][:, :], func=AF.Copy), wr=[b_bt[tb], B[dstb]])
        for g4 in range(8):
            hp0 = (g4 // 2) * 2
            par_ = g4 % 2
            hs2 = [2 * hp0 + par_, 2 * (hp0 + 1) + par_]
            ba, bk_, bl = nb(), nb(), nb()
            ps_ = slice(par_ * 64, par_ * 64 + 64)

            def f_sc(pe, hs2=hs2, ba=ba, bk_=bk_, bl=bl, ps_=ps_):
                for q, h in enumerate(hs2):
                    c = h // 2
                    pe.matmul(bk[ba][:, q * 256:(q + 1) * 256], lhsT=btT[ps_, c, :], rhs=AR[ps_, c, :], start=True, stop=True)
                    pe.matmul(bk[bk_][:, q * 256:(q + 1) * 256], lhsT=ktT[ps_, c, :], rhs=AR[ps_, c, :], start=True, stop=True)
                    i = pe.matmul(bk[bl][:, q * 128:(q + 1) * 128], lhsT=AR[ps_, c, 0:128], rhs=btT[ps_, c, :], start=True, stop=True)
                return i

            add("pe", f_sc, rd=[B["AR"], B["bt"], B["kt"]], wr=[b_bk[ba], b_bk[bk_], b_bk[bl]])

            def f_ev(v, ba=ba, bk_=bk_, bl=bl, hs2=hs2):
                for q, h in enumerate(hs2):
                    v.tensor_tensor(out=AM[:, h, :], in0=bk[ba][:, q * 256:(q + 1) * 256], in1=mA[d][:, 0:256], op=ALU.mult)
                    v.tensor_tensor(out=KM[:, h, :], in0=bk[bk_][:, q * 256:(q + 1) * 256], in1=mA[d][:, 0:256], op=ALU.mult)
                    i = v.tensor_tensor(out=LL[:, h, :], in0=bk[bl][:, q * 128:(q + 1) * 128], in1=mL[d][:, 0:128], op=ALU.mult)
                return i

            gq = hs2[0] // 4
            add("dve", f_ev, rd=[b_cst], wr=[b_bk[ba], b_bk[bk_], b_bk[bl], B["AM"][gq], B["KM"][gq], B["LL"][gq]])
        add("act", lambda a: a.activation(out=Gc[:], in_=Ei[:, :, ec:ec + 1], func=AF.Copy), rd=[B["E"]], wr=[B["Gc"]])
        add("act", lambda a: a.activation(out=sgb[:], in_=SH[:, 26, :], func=AF.Sigmoid), rd=[bS_["lg"]], wr=[B["sg"]])

    def stageBC(si, d, ci, p):
        S = cfg.seqs[si]
        base = cfg.off[si]
        nch = S // 128
        t0 = base + ci * 128
        first = (ci == 0) if d == 0 else (ci == nch - 1)
        final = (d == 1)
        B = Bp[p]
        AR, Atok, Bgtok, Kgtok, Vtok, AM, KM, LL, sgb, bon, Gc = ARs[p], Atoks[p], Bgtoks[p], Kgtoks[p], Vtoks[p], AMs[p], KMs[p], LLs[p], sgbs[p], bons[p], Gcs[p]
        PL = [LL, PL12[0], PL12[1]]
        R, Kx, Vx = SH[:, 0:8, :], SH[:, 8:16, :], SH[:, 16:24, :]
        dsl = slice(d * 64, (d + 1) * 64)
        bc = lambda i: par[:, i, :].unsqueeze(2).to_broadcast([128, 8, 128])
        ec = 127 if d == 0 else 0
        z1, z2 = nb(), nb()

        def f_z(pe):
            for h in range(16):
                b_ = z1 if h < 8 else z2
                i = pe.matmul(bk[b_][:, (h % 8) * 64:(h % 8 + 1) * 64], lhsT=KM[:, h, 0:128], rhs=Vtok[:, h * 64:(h + 1) * 64], start=True, stop=True)
            return i

        add("pe", f_z, rd=B["KM"] + [B["Vtok"]], wr=[b_bk[z1], b_bk[z2]])
        add("act", lambda a: a.activation(out=Zb[:, 0:512], in_=bk[z1][:, :], func=AF.Copy), wr=[b_bk[z1], B["Zb"]])
        add("act", lambda a: a.activation(out=Zb[:, 512:1024], in_=bk[z2][:, :], func=AF.Copy), wr=[b_bk[z2], B["Zb"]])
        for gq in range(4):
            add("dve", lambda g, gq=gq: g.tensor_tensor(out=X[0][:, gq * 4:gq * 4 + 4, :], in0=AM[:, gq * 4:gq * 4 + 4, 0:128], in1=identb[:].unsqueeze(1).to_broadcast([128, 4, 128]), op=ALU.add), rd=[B["AM"][gq], b_identb], wr=[B["X0"][gq]])
        Lcur, Acur, Xc = 0, 0, 0
        Lb = ["LL", "PL0", "PL1"]
        Ab = ["AM", "PA0", "PA1"]
        PAv = [AM[:, :, 0:128], PA[0][:], PA[1][:]]
        for lvl in range(6):
            Ln = 1 + (lvl % 2)
            An = 1 + (lvl % 2)
            Xn = 1 - Xc
            for g4 in range(4):
                hsl = slice(g4 * 4, g4 * 4 + 4)
                b1, b2 = nb(), nb()

                def f_sq(pe, g4=g4, b1=b1, b2=b2, Lc=Lcur, Ac=Acur, lvl=lvl):
                    for q in range(4):
                        h = g4 * 4 + q
                        i = pe.matmul(bk[b1][:, q * 128:(q + 1) * 128], lhsT=PAv[Ac][:, h, :], rhs=PL[Lc][:, h, :], start=True, stop=True)
                    if lvl < 5:
                        for q in range(4):
                            h = g4 * 4 + q
                            i = pe.matmul(bk[b2][:, q * 128:(q + 1) * 128], lhsT=PL[Lc][:, h, :], rhs=PAv[Ac][:, h, :], start=True, stop=True)
                    return i

                add("pe", f_sq, rd=[B[Lb[Lcur]][g4], B[Ab[Acur]][g4]], wr=[b_bk[b1], b_bk[b2]])
                add("act", lambda a, b1=b1, hsl=hsl, Ln=Ln: a.activation(out=PL[Ln][:, hsl, :], in_=bk[b1][:, :].rearrange("p (h t) -> p h t", h=4), func=AF.Copy), wr=[b_bk[b1], B[Lb[Ln]][g4]])
                if lvl < 5:
                    if g4 < 2:
                        add("act", lambda a, b2=b2, hsl=hsl, An=An: a.activation(out=PAv[An][:, hsl, :], in_=bk[b2][:, :].rearrange("p (h t) -> p h t", h=4), func=AF.Copy), wr=[b_bk[b2], B[Ab[An]][g4]])
                    else:
                        add("dve", lambda v, b2=b2, hsl=hsl, An=An: v.tensor_copy(out=PAv[An][:, hsl, :], in_=bk[b2][:, :].rearrange("p (h t) -> p h t", h=4)), wr=[b_bk[b2], B[Ab[An]][g4]])

            for g4 in range(4):
                hsl = slice(g4 * 4, g4 * 4 + 4)
                b3 = nb()

                def f_x(pe, g4=g4, b3=b3, Ln=Ln, Xc=Xc):
                    for q in range(4):
                        h = g4 * 4 + q
                        i = pe.matmul(bk[b3][:, q * 128:(q + 1) * 128], lhsT=PL[Ln][:, h, :], rhs=X[Xc][:, h, :], start=True, stop=True)
                    return i

                add("pe", f_x, rd=[B[Lb[Ln]][g4], B["X%d" % Xc][g4], b_identb], wr=[b_bk[b3]])
                add("dve", lambda v, b3=b3, hsl=hsl, Xc=Xc, Xn=Xn: v.tensor_tensor(out=X[Xn][:, hsl, :], in0=bk[b3][:, :].rearrange("p (h t) -> p h t", h=4), in1=X[Xc][:, hsl, :], op=ALU.add), rd=[B["X%d" % Xc][g4]], wr=[b_bk[b3], B["X%d" % Xn][g4]])
            Lcur, Acur, Xc = Ln, An, Xn
        XT = X[Xc]
        bX = B["X%d" % Xc]
        for pg in range(4):
            b_ = nb()

            def f_ah(pe, pg=pg, b_=b_):
                for q in range(2):
                    hp = pg * 2 + q
                    i = pe.matmul(bk[b_][:, q * 256:(q + 1) * 256], lhsT=Atok[:, hp * 128:(hp + 1) * 128], rhs=XT[:].rearrange("p h t -> p (h t)")[:, hp * 256:(hp + 1) * 256], start=True, stop=True)
                return i

            add("pe", f_ah, rd=[B["Atok"], bX[pg]], wr=[b_bk[b_]])

            def f_ahe(v, pg=pg, b_=b_):
                v4 = bk[b_][:, :].rearrange("p (q s t) -> p q s t", q=2, s=2)
                v.tensor_copy(out=AhT[0:64, pg * 2:pg * 2 + 2, :], in_=v4[0:64, :, 0, :])
                return v.tensor_copy(out=AhT[64:128, pg * 2:pg * 2 + 2, :], in_=v4[64:128, :, 1, :])

            add("dve", f_ahe, wr=[b_bk[b_], B["AhT"]])
        q1, q2 = nb(), nb()

        def f_w(pe):
            for h in range(16):
                b_ = q1 if h < 8 else q2
                i = pe.matmul(bk[b_][:, (h % 8) * 64:(h % 8 + 1) * 64], lhsT=XT[:, h, :], rhs=Zb[:, h * 64:(h + 1) * 64], start=True, stop=True)
            return i

        add("pe", f_w, rd=bX + [B["Zb"]], wr=[b_bk[q1], b_bk[q2]])
        add("act", lambda a: a.activation(out=What[:, 0:512], in_=bk[q1][:, :], func=AF.Copy), wr=[b_bk[q1], B["What"]])
        add("act", lambda a: a.activation(out=What[:, 512:1024], in_=bk[q2][:, :], func=AF.Copy), wr=[b_bk[q2], B["What"]])
        if first:
            add("pool", lambda g: (g.memset(Sf[:], 0.0), g.memset(Sbf[:], 0.0))[1], wr=[B["S"], B["Sbf"]])
        ue, uo = nb(), nb()

        def f_u(pe):
            for h in range(16):
                hp, p_ = h // 2, h % 2
                i = pe.matmul(bk[(ue, uo)[p_]][:, hp * 64:(hp + 1) * 64], lhsT=AhT[p_ * 64:(p_ + 1) * 64, hp, :], rhs=Sbf[p_ * 64:(p_ + 1) * 64, hp, :], start=True, stop=True)
            return i

        add("pe", f_u, rd=[B["AhT"], B["Sbf"]], wr=[b_bk[ue], b_bk[uo]])
        Ub4 = Ub[:, :].rearrange("p (hp s v) -> p s hp v", s=2, v=64)
        Wh4 = What[:, :].rearrange("p (hp s v) -> p s hp v", s=2, v=64)
        add("dve", lambda v: (v.tensor_tensor(out=Ub4[:, 0], in0=bk[ue][:, :].rearrange("p (hp v) -> p hp v", v=64), in1=Wh4[:, 0], op=ALU.add),
                              v.tensor_tensor(out=Ub4[:, 1], in0=bk[uo][:, :].rearrange("p (hp v) -> p hp v", v=64), in1=Wh4[:, 1], op=ALU.add))[1], rd=[B["What"]], wr=[b_bk[ue], b_bk[uo], B["Ub"]])
        ye, yo_ = nb(), nb()

        def f_y(pe):
            for h in range(16):
                hp, p_ = h // 2, h % 2
                o = bk[(ye, yo_)[p_]][:, hp * 64:(hp + 1) * 64]
                pe.matmul(o, lhsT=AR[p_ * 64:(p_ + 1) * 64, hp, 128:256], rhs=Sbf[p_ * 64:(p_ + 1) * 64, hp, :], start=True, stop=False)
                pe.matmul(o, lhsT=AM[:, h, 128:256], rhs=Ub[:, h * 64:(h + 1) * 64], start=False, stop=False)
                i = pe.matmul(o, lhsT=KM[:, h, 128:256], rhs=Vtok[:, h * 64:(h + 1) * 64], start=False, stop=True)
            return i

        add("pe", f_y, rd=[B["AR"], B["Sbf"], B["Ub"], B["Vtok"]] + B["AM"] + B["KM"], wr=[b_bk[ye], b_bk[yo_]])
        Y4 = Y[:, :].rearrange("p (hp s v) -> p s hp v", s=2, v=64)
        if final:
            ph.dma(Yp[:], T["ytmp"][t0:t0 + 128, :], rd=[dB("y", t0)], wr=[B["Yp"]])
            Yp4 = Yp[:, :].rearrange("p (hp s v) -> p s hp v", s=2, v=64)
            add("dve", lambda v: (v.tensor_tensor(out=Y4[:, 0], in0=bk[ye][:, :].rearrange("p (hp v) -> p hp v", v=64), in1=Yp4[:, 0], op=ALU.add),
                                  v.tensor_tensor(out=Y4[:, 1], in0=bk[yo_][:, :].rearrange("p (hp v) -> p hp v", v=64), in1=Yp4[:, 1], op=ALU.add))[1], rd=[B["Yp"]], wr=[b_bk[ye], b_bk[yo_], B["Y"]])
        else:
            add("act", lambda a: (a.activation(out=Y4[:, 0], in_=bk[ye][:, :].rearrange("p (hp v) -> p hp v", v=64), func=AF.Copy),
                                  a.activation(out=Y4[:, 1], in_=bk[yo_][:, :].rearrange("p (hp v) -> p hp v", v=64), func=AF.Copy))[1], wr=[b_bk[ye], b_bk[yo_], B["Y"]])
            ph.dma(T["ytmp"][t0:t0 + 128, :], Y[:], rd=[B["Y"]], wr=[dB("y", t0)])
            ph.dma(T["bon"][:, t0:t0 + 128].rearrange("(c p) t -> p c t", p=128), bon[:], rd=[B["bon"]], wr=[dB("b", t0)])
        s1, s2 = nb(), nb()

        def f_s(pe):
            for hp in range(8):
                o = bk[s1 if hp < 4 else s2][:, (hp % 4) * 128:(hp % 4 + 1) * 128]
                pe.matmul(o, lhsT=Bgtok[:, hp * 128:(hp + 1) * 128], rhs=Ub[:, hp * 128:(hp + 1) * 128], start=True, stop=False)
                i = pe.matmul(o, lhsT=Kgtok[:, hp * 128:(hp + 1) * 128], rhs=Vtok[:, hp * 128:(hp + 1) * 128], start=False, stop=True)
            return i

        add("pe", f_s, rd=[B["Bgtok"], B["Ub"], B["Kgtok"], B["Vtok"]], wr=[b_bk[s1], b_bk[s2]])
        add("dve", lambda g: g.tensor_tensor(out=Stmp[:], in0=Sf[:], in1=Gc[:].to_broadcast([128, 8, 64]), op=ALU.mult), rd=[B["Gc"]], wr=[B["S"], B["Stmp"]])

        def f_su(v):
            for q, b_ in ((0, s1), (1, s2)):
                v4 = bk[b_][:, :].rearrange("p (hp s v) -> p hp s v", hp=4, s=2)
                v.tensor_tensor(out=Sf[0:64, q * 4:q * 4 + 4, :], in0=v4[0:64, :, 0, :], in1=Stmp[0:64, q * 4:q * 4 + 4, :], op=ALU.add)
                i = v.tensor_tensor(out=Sf[64:128, q * 4:q * 4 + 4, :], in0=v4[64:128, :, 1, :], in1=Stmp[64:128, q * 4:q * 4 + 4, :], op=ALU.add)
            return i

        add("dve", f_su, rd=[B["Stmp"]], wr=[b_bk[s1], b_bk[s2], B["S"]])
        add("act", lambda a: a.activation(out=Sbf[:], in_=Sf[:], func=AF.Copy), rd=[B["S"]], wr=[B["Sbf"]])
        if not final:
            return
        ph.dma(bonp[:], T["bon"][:, t0:t0 + 128].rearrange("(c p) t -> p c t", p=128), rd=[dB("b", t0)], wr=[B["bonp"]])
        Y3 = Y[:, :].rearrange("p (h v) -> p h v", v=64)

        def f_gn(v):
            v.reduce_sum(out=st[:, :, 0], in_=Y3, axis=AX.X)
            v.tensor_tensor(out=yn, in0=Y3, in1=Y3, op=ALU.mult)
            v.reduce_sum(out=st[:, :, 1], in_=yn, axis=AX.X)
            v.tensor_scalar(out=st[:, :, 0], in0=st[:, :, 0], scalar1=1.0 / 64, scalar2=None, op0=ALU.mult)
            v.tensor_tensor(out=st[:, :, 2], in0=st[:, :, 0], in1=st[:, :, 0], op=ALU.mult)
            return v.scalar_tensor_tensor(out=st[:, :, 1], in0=st[:, :, 1], scalar=1.0 / 64, in1=st[:, :, 2], op0=ALU.mult, op1=ALU.subtract)

        add("dve", f_gn, rd=[B["Y"]], wr=[B["st"], B["yn"]])
        add("act", lambda a: (a.activation(out=st[:, :, 3], in_=st[:, :, 1], func=AF.Ln, bias=eps[:, 0:1]),
                              a.activation(out=st[:, :, 3], in_=st[:, :, 3], func=AF.Exp, scale=-0.5))[1], rd=[b_cst], wr=[B["st"]])

        def f_gn2(v):
            v.tensor_tensor(out=yn, in0=Y3, in1=st[:, :, 0:1].to_broadcast([128, 16, 64]), op=ALU.subtract)
            return v.tensor_tensor(out=yn, in0=yn, in1=st[:, :, 3:4].to_broadcast([128, 16, 64]), op=ALU.mult)

        add("dve", f_gn2, rd=[B["Y"]], wr=[B["st"], B["yn"]])
        f1, f2 = nb(), nb()

        def f_trf(pe):
            for c in range(8):
                i = pe.transpose(out=bk[f1 if c < 4 else f2][:, (c % 4) * 128:(c % 4 + 1) * 128], in_=yn2[:, c * 128:(c + 1) * 128], identity=identf[:])
            return i

        add("pe", f_trf, rd=[B["yn"], b_cst], wr=[b_bk[f1], b_bk[f2]])
        for c in range(8):
            b_ = f1 if c < 4 else f2
            add("act", lambda a, c=c, b_=b_: a.activation(out=fin[:, c, :], in_=bk[b_][:, (c % 4) * 128:(c % 4 + 1) * 128], func=AF.Identity, scale=par[:, 4, c:c + 1], bias=par[:, 5, c:c + 1]), rd=[b_par], wr=[b_bk[b_], B["fin"]])
        g1, g2 = nb(), nb()

        def f_g(pe):
            for c in range(8):
                i = pe.matmul(bk[g1 if c < 4 else g2][:, (c % 4) * 128:(c % 4 + 1) * 128], lhsT=gup[:, c * 128:(c + 1) * 128], rhs=sgb[:], start=True, stop=True)
            return i

        add("pe", f_g, rd=[B["sg"], b_w], wr=[b_bk[g1], b_bk[g2]])

        def f_fin(g):
            g.tensor_tensor(out=fin, in0=fin, in1=bon[:], op=ALU.add)
            return g.tensor_tensor(out=fin, in0=fin, in1=bonp[:], op=ALU.add)

        add("dve", f_fin, rd=[B["bon"], B["bonp"]], wr=[B["fin"]])
        add("dve", lambda v: (v.tensor_tensor(out=yo[:, 0:4, :], in0=bk[g1][:, :].rearrange("p (c t) -> p c t", c=4), in1=fin[:, 0:4, :], op=ALU.mult),
                              v.tensor_tensor(out=yo[:, 4:8, :], in0=bk[g2][:, :].rearrange("p (c t) -> p c t", c=4), in1=fin[:, 4:8, :], op=ALU.mult))[1], rd=[B["fin"]], wr=[b_bk[g1], b_bk[g2], B["yo"]])
        ph.dma(T["yaT"][:, t0:t0 + 128].rearrange("(c p) t -> p c t", p=128), yo, rd=[B["yo"]])

    def capture(fn, *args):
        ph.cap = []
        cnt["pool"] = ((0, 1, 2) if fn is stageA else (3, 4, 5)) if SPLITBANKS else None
        fn(*args)
        ops_, ph.cap = ph.cap, None
        return ops_

    ph.sched = True
    steps = []
    for si, S in enumerate(cfg.seqs):
        nch = S // 128
        steps += [(si, 0, ci) for ci in range(nch)]
        steps += [(si, 1, ci) for ci in reversed(range(nch))]
    curA = capture(stageA, *steps[0], 0)
    for op_ in curA:
        ph._add_cap(*op_)
    for i, stp in enumerate(steps):
        bc_ops = capture(stageBC, *stp, i % 2)
        a_ops = capture(stageA, *steps[i + 1], (i + 1) % 2) if i + 1 < len(steps) else []
        def groups(ops_):
            gs = []
            for o_ in ops_:
                if o_[0] == "pe" or not gs:
                    gs.append([])
                gs[-1].append(o_)
            return gs

        ga, gb = groups(a_ops), groups(bc_ops)
        na, nbc = len(a_ops), len(bc_ops)
        ia = ib = 0
        ja = jb = 0
        while ja < len(ga) or jb < len(gb):
            if ja < len(ga) and jb < len(gb) and not NOINTER:
                ta = ph.peek_ready(ga[ja][0][0], ga[ja][0][2], ga[ja][0][3])
                tb_ = ph.peek_ready(gb[jb][0][0], gb[jb][0][2], gb[jb][0][3])
                if abs(ta - tb_) < 300.0:
                    pick_a = ia * nbc <= ib * na
                else:
                    pick_a = ta < tb_
                if DBG_PICK is not None:
                    DBG_PICK.append(("A" if pick_a else "B", round(ta), round(tb_)))
            else:
                pick_a = jb >= len(gb) or (ja < len(ga) and NOINTER)
            if pick_a:
                for o_ in ga[ja]:
                    ph._add_cap(*o_)
                ia += len(ga[ja])
                ja += 1
            else:
                for o_ in gb[jb]:
                    ph._add_cap(*o_)
                ib += len(gb[jb])
                jb += 1
    return ph.run()

def build(cfg, debug=False, phases="1234"):
    nc = bass.Bass("TRN2", target_bir_lowering=False)
    NT = cfg.ntok
    T = {}

    def inp(name, shape):
        T[name] = nc.dram_tensor(name, list(shape), F32, kind="ExternalInput").ap()

    inp("x", [NT, D])
    inp("w_in", [D, NIN])
    for nm, n in (("norm_mix", D), ("norm_mlp", D), ("norm_final", D), ("mu_shift", NRW), ("w0", 2048), ("a0", 2048), ("k_k", D), ("k_a", D),
                  ("r_k", D), ("ln_x_w", D), ("ln_x_b", D), ("lambda_q1", 64), ("lambda_k1", 64), ("lambda_q2", 64), ("lambda_k2", 64), ("subln_w", 128)):
        inp(nm, [n])
    for nm in ("w_lora_up", "a_lora_up", "g_lora_up"):
        inp(nm, [128, 1024])
    for nm in ("proj_a", "proj_b", "w_out"):
        inp(nm, [D, D])
    inp("w_mlp_in", [D, DFF])
    inp("w_mlp_out", [DFF, D])
    inp("cst", [128, 8])
    inp("pm", [128, 128])
    kind = "ExternalOutput" if debug else "Internal"
    for nm, shp, dt in (("pr", [NRW, NT], F32), ("prs", [NRW, NT], F32), ("qk", [2048, NT], BF16), ("gt", [2048, NT], BF16), ("vv", [NT, 1024], BF16),
                        ("ytmp", [NT, D], F32), ("bon", [D, NT], F32), ("yaT", [D, NT], BF16), ("ybT", [D, NT], BF16),
                        ("hs", [NT, D], F32), ("hnT", [D, NT], BF16)):
        T[nm] = nc.dram_tensor(nm, shp, dt, kind=kind).ap()
    T["y"] = nc.dram_tensor("y", [NT, D], F32, kind="ExternalOutput").ap()
    with ExitStack() as gst, nc.semaphore("bar") as bar_sem:
        GST[0] = ({e: gst.enter_context(nc.semaphore("s_" + e)) for e in ENGS}, [gst.enter_context(nc.semaphore(f"d{i}")) for i in range(NDS)])
        bar = (bar_sem, 0)
        if "1" in phases:
            bar = phase1(nc, cfg, T, bar)
        if "2" in phases or "s" in phases:
            bar = phase15(nc, cfg, T, bar)
        if "2" in phases:
            bar = phase2(nc, cfg, T, bar)
        if "3" in phases:
            bar = phase3(nc, cfg, T, bar)
        if "4" in phases or "a" in phases:
            bar = phase4a(nc, cfg, T, bar)
        if "4" in phases or "b" in phases:
            bar = phase4b(nc, cfg, T, bar)
    return nc


def core_inputs(inputs, xs):
    cst, pm = host_consts()
    f = lambda k: np.ascontiguousarray(np.asarray(inputs[k], np.float32))
    m = {"x": np.ascontiguousarray(np.concatenate(xs, axis=0)), "w_in": f("w_in")[0], "cst": cst, "pm": pm}
    for nm in ("norm_mix", "norm_mlp", "mu_shift", "w0", "a0", "k_k", "k_a", "r_k", "ln_x_w", "ln_x_b", "lambda_q1", "lambda_k1", "lambda_q2", "lambda_k2", "subln_w"):
        m[nm] = f(nm)[0].reshape(-1)
    m["norm_final"] = f("norm_final").reshape(-1)
    m["w_lora_up"] = f("w_lora_up")[0].reshape(128, 1024)
    m["a_lora_up"] = f("a_lora_up")[0].reshape(128, 1024)
    m["g_lora_up"] = f("g_lora_up")[0].reshape(128, 1024)
    for nm in ("proj_a", "proj_b", "w_out", "w_mlp_in", "w_mlp_out"):
        m[nm] = f(nm)[0]
    return m


_NC_CACHE = {}


def kernel(**inputs):
    xp = np.asarray(inputs["x_prompt"], np.float32)
    xs = np.asarray(inputs["x_sample"], np.float32)
    n = 8
    cfg = Cfg([xp.shape[1], xp.shape[1], xs.shape[1]])
    key = tuple(cfg.seqs)
    if key not in _NC_CACHE:
        _NC_CACHE[key] = build(cfg)
    nc = _NC_CACHE[key]
    in_maps = [core_inputs(inputs, [xp[2 * c], xp[2 * c + 1], xs[c]]) for c in range(n)]
    res = run_bass_kernel_spmd(nc, in_maps, core_ids=list(range(n)))
    yp = np.empty_like(xp)
    ys = np.empty_like(xs)
    S1 = xp.shape[1]
    for c in range(n):
        y = res.results[c]["y"]
        yp[2 * c] = y[0:S1]
        yp[2 * c + 1] = y[S1:2 * S1]
        ys[c] = y[2 * S1:]
    return (yp, ys)


def host_consts():
    p = np.arange(128)
    cst = np.zeros((128, 8), np.float32)
    cst[:, 0] = p % 8
    m = ((p % 64) < 16).astype(np.float32)
    cst[:, 1] = m
    cst[:, 2] = np.where((p % 16) >= 8, 1.0, -1.0) * m
    cst[:, 3] = -math.pi
    cst[:, 4] = 1e-6
    pm = np.zeros((128, 128), np.float32)
    for mm in range(128):
        if (mm % 64) < 16:
            k = mm + 8 if (mm % 16) < 8 else mm - 8
            pm[k, mm] = 1.0
    return cst, pm
```

```python
import math
from contextlib import ExitStack
import numpy as np
import concourse.bass as bass
import concourse.mybir as mybir
from concourse.bass_utils import run_bass_kernel_spmd

F32 = mybir.dt.float32
BF16 = mybir.dt.bfloat16
I32 = mybir.dt.int32
AF = mybir.ActivationFunctionType
ALU = mybir.AluOpType
AX = mybir.AxisListType

D = 1024
NRW = 3456
NIN = 8576
DFF = 4096
ENGS = ("pe", "act", "dve", "pool", "sp")
NDS = 24


GST = [None]
NOINTER = False
SPLITBANKS = False
DBG_PICK = None
LAT = 180.0


class Buf:
    __slots__ = ("w", "r", "name")

    def __init__(self, name=""):
        self.w = None
        self.r = []
        self.name = name


class Op:
    __slots__ = ("fn", "deps", "dma", "sig", "val", "sem")

    def __init__(self, fn, deps, dma):
        self.fn = fn
        self.deps = deps
        self.dma = dma
        self.sig = False
        self.val = 0
        self.sem = None


class _Rec:
    def __init__(self):
        self.calls = []

    def __getattr__(self, name):
        def f(*a, **k):
            self.calls.append((name, a, k))
            return None

        return f


class Phase:
    def __init__(self, nc, name, bar):
        self.nc = nc
        self.name = name
        self.bar = bar
        self.ops = {e: [] for e in ENGS}
        self.stack = ExitStack()
        self.ndma = 0
        self.dma_ids = []
        self.nps = 0
        self.sched = False
        self.est_end = {}
        self.eng_free = {}

    def sb(self, name, shape, dt):
        return self.stack.enter_context(self.nc.sbuf_tensor(self.name + "_" + name, list(shape), dt))

    def ps(self, name, shape, dt=F32):
        return self.stack.enter_context(self.nc.psum_tensor(self.name + "_" + name, list(shape), dt))

    cap = None

    def _add_cap(self, eng, fn, rd, wr, dma):
        return self.add(eng, fn, rd, wr, dma)

    def add(self, eng, fn, rd=(), wr=(), dma=False):
        if self.cap is not None:
            self.cap.append((eng, fn, tuple(rd), tuple(wr), dma))
            return None
        if eng != "pe" and not dma:
            rec = _Rec()
            fn(rec)
            me = None
            for name, a, k in rec.calls:
                me = self._add1(eng, (lambda e, name=name, a=a, k=k: getattr(e, name)(*a, **k)), rd, wr, False)
            return me
        return self._add1(eng, fn, rd, wr, dma)

    def est_dur(self, eng, fn, dma):
        rec = _Rec()
        try:
            fn(rec)
        except Exception:
            return 500.0
        tot = 0.0
        for name, a, k in rec.calls:
            o = k.get("out", a[0] if a else None)
            try:
                n = 1
                for d_ in o.shape[1:]:
                    n *= d_
            except Exception:
                n = 128
            if dma:
                tot += 2000.0 + n * 0.5
            elif eng == "pe":
                tot += (100.0 if name == "transpose" else max(64, n) / 2.4 + 8.0)
            elif eng == "act":
                tot += 230.0 + 0.83 * n
            elif eng == "dve":
                tot += 130.0 + 0.95 * n
            else:
                tot += 250.0 + 6.5 * n
        return tot

    def peek_ready(self, eng, rd, wr):
        t = self.eng_free.get(eng, 0.0)
        for b in list(rd) + list(wr):
            if b.w is not None:
                t = max(t, self.est_end.get(b.w, 0.0) + (0.0 if b.w[0] == eng == "pe" else LAT))
        for b in wr:
            for r_ in b.r:
                t = max(t, self.est_end.get(r_, 0.0) + LAT)
        return t

    def _add1(self, eng, fn, rd=(), wr=(), dma=False):
        ops = self.ops[eng]
        me = (eng, len(ops))
        if self.sched:
            st_ = self.peek_ready(eng, rd, wr)
            en_ = st_ + self.est_dur(eng, fn, dma)
            self.est_end[me] = en_
            if not dma:
                self.eng_free[eng] = en_
            else:
                self.eng_free[eng] = st_ + 60.0
        deps = set()
        for b in rd:
            if b.w is not None:
                deps.add(b.w)
        for b in wr:
            if b.w is not None:
                deps.add(b.w)
            deps.update(b.r)
        deps.discard(me)
        op = Op(fn, deps, dma)
        if dma:
            k = self.ndma
            self.ndma += 1
            op.sem = k % NDS
            op.val = 16 * (k // NDS + 1)
            if k >= NDS:
                deps.add(self.dma_ids[k - NDS])
            self.dma_ids.append(me)
        ops.append(op)
        for b in rd:
            b.r.append(me)
        for b in wr:
            b.w = me
            b.r = []
        return me

    def dma(self, out, in_, rd=(), wr=(), q="sp", **kw):
        return self.add(q, lambda e: e.dma_start(out=out, in_=in_, **kw), rd, wr, dma=True)

    def run(self):
        nc = self.nc
        ops = self.ops
        bar_sem, bar_val = self.bar
        fin = set(self.dma_ids)
        for e in ENGS:
            if e != "sp" and ops[e]:
                fin.add((e, len(ops[e]) - 1))
        sems, dsems = GST[0]

        def f_bar(e):
            for sm in list(sems.values()) + list(dsems):
                e.sem_clear(sm)
            return e.sem_inc(bar_sem, 1)

        ops["sp"].append(Op(f_bar, fin, False))
        for e in ENGS:
            for op in ops[e]:
                for d in op.deps:
                    ops[d[0]][d[1]].sig = True
        if True:
            for e in ENGS:
                c = 0
                for op in ops[e]:
                    if op.dma:
                        op.sem = dsems[op.sem]
                    elif op.sig:
                        c += 1
                        op.val = c
                        op.sem = sems[e]

            def emit(ename, eng):
                known = {}
                if bar_val > 0:
                    eng.wait_ge(bar_sem, bar_val)
                for op in ops[ename]:
                    need = {}
                    for d in op.deps:
                        dop = ops[d[0]][d[1]]
                        if ename == "pe" and d[0] == "pe" and not dop.dma:
                            continue
                        key = id(dop.sem)
                        if known.get(key, 0) < dop.val and need.get(key, (None, 0))[1] < dop.val:
                            need[key] = (dop.sem, dop.val)
                    for key, (sem, val) in need.items():
                        eng.wait_ge(sem, val)
                        known[key] = val
                    inst = op.fn(eng)
                    if op.dma:
                        inst.then_inc(op.sem, 16)
                    elif op.sig:
                        inst.then_inc(op.sem, 1)

            with nc.Block() as block:
                @block.sync
                def _(e):
                    emit("sp", e)

                @block.scalar
                def _(e):
                    emit("act", e)

                @block.vector
                def _(e):
                    emit("dve", e)

                @block.gpsimd
                def _(e):
                    emit("pool", e)

                @block.tensor
                def _(e):
                    emit("pe", e)
        self.stack.close()
        return (bar_sem, bar_val + 1)


def make_ident(ph, dt=BF16, n=128, name="ident"):
    f = ph.sb(name + "_f", [128, n], F32)
    o = ph.sb(name, [128, n], dt)
    b = Buf(name)

    def fn(g):
        g.memset(f[:], 0.0)
        g.affine_select(out=f[:], in_=f[:], compare_op=ALU.not_equal, fill=1.0, base=0, pattern=[[-1, n]], channel_multiplier=1)
        return g.tensor_copy(out=o[:], in_=f[:])

    ph.add("pool", fn, wr=[b])
    return o, b


class Cfg:
    def __init__(self, seqs):
        self.seqs = list(seqs)
        self.off = [0]
        for s in self.seqs:
            self.off.append(self.off[-1] + s)
        self.ntok = self.off[-1]
        self.smax = max(self.seqs)


def phase1(nc, cfg, T, bar):
    ph = Phase(nc, "p1", bar)
    NT = cfg.ntok
    NB = NT // 512
    ident, b_ident = make_ident(ph)
    xnT = ph.sb("xnT", [128, 8, NT], BF16)
    b_xnT = [Buf() for _ in range(NT // 128)]
    gcol = ph.sb("gcol", [128, 8], F32)
    b_g = Buf()
    ph.dma(gcol[:], T["norm_mix"].rearrange("(kc p) -> p kc", p=128), wr=[b_g], allow_slow_non_contiguous=True)
    banks = [ph.ps(f"bk{i}", [128, 512], F32) for i in range(6)]
    b_bank = [Buf() for _ in range(6)]
    pst = [ph.ps(f"pt{i}", [128, 1024], BF16) for i in range(2)]
    b_pst = [Buf(), Buf()]

    SM = cfg.smax
    Ct = ph.sb("Ct", [128, SM], F32)
    St = ph.sb("St", [128, SM], F32)
    b_rope = Buf()
    Pm = ph.sb("Pm", [128, 128], BF16)
    SEG = 512
    with_tmp = ph.sb("rtmp", [128, SEG], F32)
    posf = ph.sb("posf", [128, SEG], F32)
    cols = ph.sb("rcols", [128, 8], F32)
    pmf = ph.sb("pmf", [128, 128], F32)
    posi = ph.sb("posi", [128, SEG], I32)
    ki = ph.sb("rki", [128, SEG], I32)
    b_r0 = Buf()
    b_seg = Buf()
    ph.dma(cols[:], T["cst"][:, :], wr=[b_r0])
    ph.dma(pmf[:], T["pm"][:, :], wr=[b_r0])
    ph.add("pool", lambda g: g.tensor_copy(out=Pm[:], in_=pmf[:]), rd=[b_r0], wr=[b_r0])
    ph.add("act", lambda a: a.activation(out=cols[:, 7:8], in_=cols[:, 0:1], func=AF.Exp, scale=-math.log(500000.0) / 8.0), rd=[b_r0], wr=[b_r0])
    TWO_PI = 2.0 * math.pi
    for sg in range(SM // SEG):
        ss_ = slice(sg * SEG, (sg + 1) * SEG)

        def f_pos(g, sg=sg):
            g.iota(posi[:], pattern=[[1, SEG]], base=sg * SEG, channel_multiplier=0)
            return g.tensor_copy(out=posf[:], in_=posi[:])

        ph.add("pool", f_pos, wr=[b_seg])

        def reduce_fn(v, dst, shift):
            v.tensor_scalar(out=dst, in0=posf[:], scalar1=cols[:, 7:8], scalar2=shift, op0=ALU.mult, op1=ALU.add)
            v.tensor_scalar(out=ki[:], in0=dst, scalar1=1.0 / TWO_PI, scalar2=None, op0=ALU.mult)
            v.tensor_copy(out=with_tmp[:], in_=ki[:])
            v.scalar_tensor_tensor(out=dst, in0=with_tmp[:], scalar=-TWO_PI, in1=dst, op0=ALU.mult, op1=ALU.add)
            v.tensor_scalar(out=with_tmp[:], in0=dst, scalar1=math.pi, scalar2=TWO_PI, op0=ALU.is_gt, op1=ALU.mult)
            v.tensor_tensor(out=dst, in0=dst, in1=with_tmp[:], op=ALU.subtract)
            v.tensor_scalar(out=with_tmp[:], in0=dst, scalar1=-math.pi, scalar2=TWO_PI, op0=ALU.is_lt, op1=ALU.mult)
            return v.tensor_tensor(out=dst, in0=dst, in1=with_tmp[:], op=ALU.add)

        def rope_fn2(v, ss_=ss_):
            reduce_fn(v, St[:, ss_], 0.0)
            return reduce_fn(v, Ct[:, ss_], 0.5 * math.pi)

        ph.add("dve", rope_fn2, rd=[b_r0], wr=[b_seg, b_rope])

        def rope_fn3(a, ss_=ss_):
            a.activation(out=St[:, ss_], in_=St[:, ss_], func=AF.Sin)
            return a.activation(out=Ct[:, ss_], in_=Ct[:, ss_], func=AF.Sin)

        ph.add("act", rope_fn3, rd=[], wr=[b_rope])

        def rope_fn4(v, ss_=ss_):
            v.tensor_scalar(out=St[:, ss_], in0=St[:, ss_], scalar1=cols[:, 2:3], scalar2=None, op0=ALU.mult)
            v.tensor_scalar(out=Ct[:, ss_], in0=Ct[:, ss_], scalar1=-1.0, scalar2=None, op0=ALU.add)
            return v.tensor_scalar(out=Ct[:, ss_], in0=Ct[:, ss_], scalar1=cols[:, 1:2], scalar2=1.0, op0=ALU.mult, op1=ALU.add)

        ph.add("dve", rope_fn4, rd=[b_r0], wr=[b_rope])

    xt = [ph.sb(f"xt{i}", [128, D], F32) for i in range(2)]
    b_xt = [Buf(), Buf()]
    xnb = [ph.sb(f"xnb{i}", [128, D], BF16) for i in range(2)]
    b_xnb = [Buf(), Buf()]
    ss = ph.sb("ss", [128, 2, 2], F32)
    b_ss = [Buf(), Buf()]
    xin = T["x"]
    for t in range(NT // 128):
        s = t % 2
        ph.dma(xt[s][:], xin[t * 128:(t + 1) * 128, :], wr=[b_xt[s]])
        ph.add("act", lambda a, s=s: a.activation(out=xnb[s][:], in_=xt[s][:], func=AF.Square, accum_out=ss[:, s, 0:1]), rd=[b_xt[s]], wr=[b_xnb[s], b_ss[s]])

        ph.add("act", lambda a, s=s: a.activation(out=ss[:, s, 1:2], in_=ss[:, s, 0:1], func=AF.Sqrt, scale=1.0 / D, bias=cols[:, 4:5]), rd=[b_r0], wr=[b_ss[s]])
        ph.add("dve", lambda v, s=s: v.reciprocal(out=ss[:, s, 1:2], in_=ss[:, s, 1:2]), rd=[], wr=[b_ss[s]])
        ph.add("act", lambda a, s=s: a.activation(out=xnb[s][:], in_=xt[s][:], func=AF.Copy, scale=ss[:, s, 1:2]), rd=[b_xt[s], b_ss[s]], wr=[b_xnb[s]])

        def f_tr(pe, s=s):
            for kc in range(8):
                i = pe.transpose(out=pst[s][:, kc * 128:(kc + 1) * 128], in_=xnb[s][:, kc * 128:(kc + 1) * 128], identity=ident[:])
            return i

        ph.add("pe", f_tr, rd=[b_xnb[s], b_ident], wr=[b_pst[s]])
        ph.add("dve", lambda v, s=s, t=t: v.tensor_copy(out=xnT[:, :, t * 128:(t + 1) * 128], in_=pst[s][:, :].rearrange("p (k t) -> p k t", k=8)), rd=[], wr=[b_pst[s], b_xnT[t]])

    wf = [ph.sb(f"wf{i}", [128, 8, 128], F32) for i in range(2)]
    b_wf = [Buf(), Buf()]
    wb = [ph.sb(f"wb{i}", [128, 8, 128], BF16) for i in range(2)]
    b_wb = [Buf(), Buf()]
    stg = [ph.sb(f"stg{i}", [128, 512], F32) for i in range(3)]
    b_stg = [Buf() for _ in range(3)]
    stgb = [ph.sb(f"stgb{i}", [128, 512], BF16) for i in range(3)]
    b_stgb = [Buf() for _ in range(3)]
    qraw = [ph.sb("qraw0", [128, 512], BF16)] * 2
    b_qraw = [Buf()] * 2
    t1 = [ph.sb("t1_0", [128, 512], F32)] * 2
    b_t1 = [Buf()] * 2
    t2 = [ph.sb("t2_0", [128, 512], F32)] * 2
    b_t2 = [Buf()] * 2
    w_in = T["w_in"]
    chunks = []
    for c in range(27):
        chunks.append(("rw", c * 128, c * 128))
    for c in range(16):
        chunks.append(("qk", NRW + c * 128, c * 128))
    for c in range(16):
        chunks.append(("gt", NRW + 3072 + c * 128, c * 128))
    for c in range(8):
        chunks.append(("vv", NRW + 2048 + c * 128, c * 128))
    cnt = dict(bank=0, stg=0, stgb=0, q=0)
    blk_pos = []
    for si, S in enumerate(cfg.seqs):
        for j in range(S // 512):
            blk_pos.append(j * 512)
    for ci, (kind, c0, r0) in enumerate(chunks):
        s = ci % 2
        ph.dma(wf[s][:], w_in[:, c0:c0 + 128].rearrange("(kc p) c -> p kc c", p=128), wr=[b_wf[s]])
        ph.add("pool", lambda g, s=s: g.tensor_tensor(out=wb[s][:], in0=wf[s][:], in1=gcol[:].unsqueeze(2).to_broadcast([128, 8, 128]), op=ALU.mult), rd=[b_wf[s], b_g], wr=[b_wb[s]])
        for b in range(NB):
            bk = cnt["bank"] % 4
            cnt["bank"] += 1

            def f_mm(pe, s=s, b=b, bk=bk, kind=kind):
                if kind == "vv":
                    for j in range(4):
                        for kc in range(8):
                            i = pe.matmul(banks[bk][:, j * 128:(j + 1) * 128], lhsT=xnT[:, kc, b * 512 + j * 128:b * 512 + (j + 1) * 128], rhs=wb[s][:, kc, :], start=(kc == 0), stop=(kc == 7))
                    return i
                for kc in range(8):
                    i = pe.matmul(banks[bk][:, :], lhsT=wb[s][:, kc, :], rhs=xnT[:, kc, b * 512:(b + 1) * 512], start=(kc == 0), stop=(kc == 7))
                return i

            ph.add("pe", f_mm, rd=[b_wb[s]] + b_xnT[b * 4:(b + 1) * 4], wr=[b_bank[bk]])
            tok = slice(b * 512, (b + 1) * 512)
            if kind == "rw":
                g_ = cnt["stg"] % 3
                cnt["stg"] += 1
                ph.add("act", lambda a, bk=bk, g_=g_: a.activation(out=stg[g_][:], in_=banks[bk][:, :], func=AF.Copy), wr=[b_bank[bk], b_stg[g_]])
                ph.dma(T["pr"][r0:r0 + 128, tok], stg[g_][:], rd=[b_stg[g_]])
            elif kind == "vv":
                g_ = cnt["stgb"] % 3
                cnt["stgb"] += 1
                ph.add("act", lambda a, bk=bk, g_=g_: a.activation(out=stgb[g_][:], in_=banks[bk][:, :], func=AF.Copy), wr=[b_bank[bk], b_stgb[g_]])
                ph.dma(T["vv"][tok, r0:r0 + 128].rearrange("(j p) c -> p j c", p=128), stgb[g_][:].rearrange("p (j c) -> p j c", j=4), rd=[b_stgb[g_]])
            elif kind == "gt":
                g_ = cnt["stgb"] % 3
                cnt["stgb"] += 1
                ph.add("act", lambda a, bk=bk, g_=g_: a.activation(out=stgb[g_][:], in_=banks[bk][:, :], func=AF.Sigmoid), wr=[b_bank[bk], b_stgb[g_]])
                ph.dma(T["gt"][r0:r0 + 128, tok], stgb[g_][:], rd=[b_stgb[g_]])
            else:
                q_ = cnt["q"] % 2
                cnt["q"] += 1
                g_ = cnt["stgb"] % 3
                cnt["stgb"] += 1
                p0 = blk_pos[b]
                ph.add("act", lambda a, bk=bk, q_=q_: a.activation(out=qraw[q_][:], in_=banks[bk][:, :], func=AF.Copy), wr=[b_bank[bk], b_qraw[q_]])
                ph.add("dve", lambda v, bk=bk, q_=q_, p0=p0: v.tensor_tensor(out=t1[q_][:], in0=banks[bk][:, :], in1=Ct[:, p0:p0 + 512], op=ALU.mult), rd=[b_rope], wr=[b_bank[bk], b_t1[q_]])
                pb = 4 + (cnt["q"] % 2)
                ph.add("pe", lambda pe, q_=q_, pb=pb: pe.matmul(banks[pb][:, :], lhsT=Pm[:], rhs=qraw[q_][:], start=True, stop=True), rd=[b_qraw[q_], b_r0], wr=[b_bank[pb]])
                ph.add("dve", lambda v, pb=pb, q_=q_, p0=p0: v.tensor_tensor(out=t2[q_][:], in0=banks[pb][:, :], in1=St[:, p0:p0 + 512], op=ALU.mult), rd=[b_rope], wr=[b_bank[pb], b_t2[q_]])
                ph.add("dve", lambda g, q_=q_, g_=g_: g.tensor_tensor(out=stgb[g_][:], in0=t1[q_][:], in1=t2[q_][:], op=ALU.add), rd=[b_t1[q_], b_t2[q_]], wr=[b_stgb[g_]])
                ph.dma(T["qk"][r0:r0 + 128, tok], stgb[g_][:], rd=[b_stgb[g_]])

    return ph.run()


def bcast_rows(ap_1d, n):
    return ap_1d.partition_broadcast(128)


def phase3(nc, cfg, T, bar):
    ph = Phase(nc, "p3", bar)
    SM = cfg.smax
    lam_init = 0.8 - 0.6 * math.exp(-0.3 * 0)
    lv = ph.sb("lv", [128, 4, 64], F32)
    b_lv = Buf()
    for i, nm in enumerate(("lambda_q1", "lambda_k1", "lambda_q2", "lambda_k2")):
        ph.dma(lv[:, i, :], T[nm].partition_broadcast(128), wr=[b_lv])
    cc = ph.sb("cc", [128, 8], F32)
    b_cc = Buf()
    ph.dma(cc[:, 3:4], T["subln_w"].rearrange("(p o) -> p o", o=1), wr=[b_cc])
    lt = ph.sb("ltmp", [128, 2, 64], F32)
    ones = ph.sb("ones", [128, 128], BF16)

    def f_c(v):
        v.memset(ones[:], 1.0)
        v.memset(cc[:, 4:5], 1e-5)
        v.tensor_tensor(out=lt[:, 0, :], in0=lv[:, 0, :], in1=lv[:, 1, :], op=ALU.mult)
        v.tensor_tensor(out=lt[:, 1, :], in0=lv[:, 2, :], in1=lv[:, 3, :], op=ALU.mult)
        v.reduce_sum(out=cc[:, 0:2], in_=lt[:], axis=AX.X)
        return v.tensor_scalar(out=cc[:, 3:4], in0=cc[:, 3:4], scalar1=1.0 - lam_init, scalar2=None, op0=ALU.mult)

    ph.add("dve", f_c, rd=[b_lv], wr=[b_cc])
    ph.add("act", lambda a: a.activation(out=cc[:, 0:2], in_=cc[:, 0:2], func=AF.Exp), wr=[b_cc])

    def f_c2(v):
        v.tensor_tensor(out=cc[:, 2:3], in0=cc[:, 1:2], in1=cc[:, 0:1], op=ALU.subtract)
        return v.tensor_scalar(out=cc[:, 2:3], in0=cc[:, 2:3], scalar1=-lam_init, scalar2=None, op0=ALU.add)

    ph.add("dve", f_c2, wr=[b_cc])

    qT = [ph.sb(f"qT{i}", [128, SM], BF16) for i in range(2)]
    kT = [ph.sb(f"kT{i}", [128, SM], BF16) for i in range(2)]
    Vt = [ph.sb(f"Vt{i}", [128, SM // 128, 128], BF16) for i in range(2)]
    b_in = [Buf(), Buf()]
    bS = [ph.ps(f"bS{i}", [128, 512], F32) for i in range(4)]
    b_bS = [Buf() for _ in range(4)]
    bO = [ph.ps(f"bO{i}", [128, 512], F32) for i in range(4)]
    b_bO = [Buf() for _ in range(4)]
    Pt = [ph.sb(f"Pt{i}", [128, 512], BF16) for i in range(4)]
    b_Pt = [Buf() for _ in range(4)]
    acc = ph.sb("acc", [128, 512], F32)
    accb = ph.sb("accb", [128, 512], BF16)
    b_acc = Buf()
    rz = [ph.sb(f"rz{i}", [128, 512], F32) for i in range(2)]
    oo = [ph.sb(f"oo{i}", [128, 512], F32) for i in range(2)]
    sq = ph.sb("sq", [128, 512], BF16)
    rs = ph.sb("rs", [128, 512], F32)
    yb = [ph.sb(f"yb{i}", [128, 512], BF16) for i in range(2)]
    b_ep = Buf()
    b_yb = [Buf(), Buf()]
    blocks = []
    it = 0
    for si, S in enumerate(cfg.seqs):
        for h in range(8):
            u = it % 2
            it += 1
            for qb in range(S // 512):
                for kt in range(S // 128):
                    blocks.append((si, h, u, qb, kt))
    state = dict(ne=0)

    def emit_load(si, h, u):
        S = cfg.seqs[si]
        t0 = cfg.off[si]
        ph.dma(qT[u][:, 0:S], T["qk"][h * 128:(h + 1) * 128, t0:t0 + S], wr=[b_in[u]])
        ph.dma(kT[u][:, 0:S], T["qk"][1024 + h * 128:1024 + (h + 1) * 128, t0:t0 + S], wr=[b_in[u]])
        ph.dma(Vt[u][:, 0:S // 128, :], T["vv"][t0:t0 + S, h * 128:(h + 1) * 128].rearrange("(kt p) v -> p kt v", p=128), wr=[b_in[u]])

    def emit_scores(n):
        si, h, u, qb, kt = blocks[n]
        w = (n % 2) * 2
        ks = slice(kt * 128, (kt + 1) * 128)
        qs = slice(qb * 512, (qb + 1) * 512)

        def f_s(pe):
            pe.matmul(bS[w][:, :], lhsT=kT[u][0:64, ks], rhs=qT[u][0:64, qs], start=True, stop=True)
            return pe.matmul(bS[w + 1][:, :], lhsT=kT[u][64:128, ks], rhs=qT[u][64:128, qs], start=True, stop=True)

        ph.add("pe", f_s, rd=[b_in[u]], wr=[b_bS[w], b_bS[w + 1]])
        ph.add("act", lambda a: a.activation(out=Pt[w][:], in_=bS[w][:, :], func=AF.Exp, scale=0.125), wr=[b_bS[w], b_Pt[w]])
        ph.add("act", lambda a: a.activation(out=Pt[w + 1][:], in_=bS[w + 1][:, :], func=AF.Exp, scale=0.125), wr=[b_bS[w + 1], b_Pt[w + 1]])

    def emit_pv(n):
        si, h, u, qb, kt = blocks[n]
        S = cfg.seqs[si]
        t0 = cfg.off[si]
        nkt = S // 128
        w = (n % 2) * 2

        def f_pv(pe):
            st, sp_ = (kt == 0), (kt == nkt - 1)
            pe.matmul(bO[0][:, :], lhsT=Vt[u][:, kt, :], rhs=Pt[w][:], start=st, stop=sp_)
            pe.matmul(bO[1][:, :], lhsT=Vt[u][:, kt, :], rhs=Pt[w + 1][:], start=st, stop=sp_)
            return pe.matmul(bO[3][:, :], lhsT=ones[:], rhs=Pt[w + 1][:], start=st, stop=sp_)

        ph.add("pe", f_pv, rd=[b_in[u], b_Pt[w], b_Pt[w + 1], b_cc], wr=[b_bO[0], b_bO[1], b_bO[3]])
        if kt == 0:
            ph.add("dve", lambda v: v.tensor_copy(out=acc[:], in_=Pt[w][:]), rd=[b_Pt[w]], wr=[b_acc])
        else:
            ph.add("dve", lambda v: v.tensor_tensor(out=acc[:], in0=acc[:], in1=Pt[w][:], op=ALU.add), rd=[b_Pt[w]], wr=[b_acc])
        if kt != nkt - 1:
            return
        ph.add("dve", lambda v: v.tensor_copy(out=accb[:], in_=acc[:]), wr=[b_acc])
        ph.add("pe", lambda pe: pe.matmul(bO[2][:, :], lhsT=ones[:], rhs=accb[:], start=True, stop=True), rd=[b_acc, b_cc], wr=[b_bO[2]])
        e = state["ne"] % 2
        state["ne"] += 1

        def f_e0(a):
            a.activation(out=rz[0][:], in_=bO[2][:, :], func=AF.Ln)
            a.activation(out=rz[1][:], in_=bO[3][:, :], func=AF.Ln)
            a.activation(out=rz[0][:], in_=rz[0][:], func=AF.Exp, scale=-1.0)
            return a.activation(out=rz[1][:], in_=rz[1][:], func=AF.Exp, scale=-1.0)

        ph.add("act", f_e0, wr=[b_bO[2], b_bO[3], b_ep])

        def f_e1(v):
            v.tensor_tensor(out=oo[0][:], in0=bO[0][:, :], in1=rz[0][:], op=ALU.mult)
            v.tensor_tensor(out=oo[1][:], in0=bO[1][:, :], in1=rz[1][:], op=ALU.mult)
            return v.scalar_tensor_tensor(out=oo[0][:], in0=oo[1][:], scalar=cc[:, 2:3], in1=oo[0][:], op0=ALU.mult, op1=ALU.add)

        ph.add("dve", f_e1, rd=[b_cc], wr=b_bO + [b_ep])
        ph.add("act", lambda a: a.activation(out=sq[:], in_=oo[0][:], func=AF.Square), wr=[b_ep])
        ph.add("pe", lambda pe: pe.matmul(bS[w][:, :], lhsT=ones[:], rhs=sq[:], start=True, stop=True), rd=[b_ep], wr=[b_bS[w]])
        ph.add("act", lambda a: (a.activation(out=rs[:], in_=bS[w][:, :], func=AF.Ln, scale=1.0 / 128.0, bias=cc[:, 4:5]),
                                 a.activation(out=rs[:], in_=rs[:], func=AF.Exp, scale=-0.5))[1], rd=[b_cc], wr=[b_bS[w], b_ep])
        ph.add("dve", lambda g: g.scalar_tensor_tensor(out=yb[e][:], in0=oo[0][:], scalar=cc[:, 3:4], in1=rs[:], op0=ALU.mult, op1=ALU.mult), rd=[b_ep, b_cc], wr=[b_yb[e]])
        ph.dma(T["ybT"][h * 128:(h + 1) * 128, t0 + qb * 512:t0 + (qb + 1) * 512], yb[e][:], rd=[b_yb[e]])

    heads = []
    for bl in blocks:
        if not heads or heads[-1] != bl[0:3]:
            heads.append(bl[0:3])
    for hd in heads[0:2]:
        emit_load(*hd)
    emit_scores(0)
    hj = 0
    for n in range(len(blocks)):
        if n + 1 < len(blocks):
            emit_scores(n + 1)
        emit_pv(n)
        if n + 1 == len(blocks) or blocks[n + 1][0:3] != blocks[n][0:3]:
            if hj + 2 < len(heads):
                emit_load(*heads[hj + 2])
            hj += 1
    return ph.run()


def load_w_bf16(ph, dst, b_dst, w_ap, nk, ncols, stage, b_stage, scale_col=None, b_scale=None, eng="dve"):
    step = 512
    kst = stage.shape[1]
    for c0 in range(0, ncols, step):
        for k0 in range(0, nk, kst):
            k1 = min(nk, k0 + kst)
            ph.dma(stage[:, 0:k1 - k0, :], w_ap[k0 * 128:k1 * 128, c0:c0 + step].rearrange("(kc p) c -> p kc c", p=128), wr=[b_stage])
            if scale_col is None:
                ph.add(eng, lambda g, k0=k0, k1=k1, c0=c0: g.tensor_copy(out=dst[:, k0:k1, c0:c0 + step], in_=stage[:, 0:k1 - k0, :]), rd=[b_stage], wr=[b_dst])
            else:
                ph.add(eng, lambda g, k0=k0, k1=k1, c0=c0: g.tensor_tensor(out=dst[:, k0:k1, c0:c0 + step], in0=stage[:, 0:k1 - k0, :], in1=scale_col[:, k0:k1].unsqueeze(2).to_broadcast([128, k1 - k0, step]), op=ALU.mult), rd=[b_stage, b_scale], wr=[b_dst])


def phase4a(nc, cfg, T, bar):
    ph = Phase(nc, "p4a", bar)
    NT = cfg.ntok
    ident, b_ident = make_ident(ph)
    stage = ph.sb("stage", [128, 8, 512], F32)
    b_stage = Buf()
    W = {}
    bW = {}
    for nm in ("proj_a", "proj_b", "w_out"):
        W[nm] = ph.sb("w_" + nm, [128, 8, 1024], BF16)
        bW[nm] = Buf()
        load_w_bf16(ph, W[nm], bW[nm], T[nm], 8, 1024, stage, b_stage)
    cst = ph.sb("cst", [128, 1], F32)
    b_cst = Buf()
    ph.add("pool", lambda g: g.memset(cst[:], 1e-6), wr=[b_cst])
    banks = [ph.ps(f"bk{i}", [128, 512], F32) for i in range(6)]
    b_bank = [Buf() for _ in range(6)]
    pst = ph.ps("pt", [128, 1024], BF16)
    b_pst = Buf()
    yas = [ph.sb(f"ya{i}", [128, 8, 512], BF16) for i in range(2)]
    ybbs = [ph.sb(f"ybb{i}", [128, 8, 512], BF16) for i in range(2)]
    gas = [ph.sb(f"ga{i}", [128, 8, 512], BF16) for i in range(2)]
    gbs = [ph.sb(f"gb{i}", [128, 8, 512], BF16) for i in range(2)]
    b_lds = [Buf(), Buf()]
    mg = ph.sb("mg", [128, 8, 512], BF16)
    b_mg = Buf()
    m1 = [ph.sb(f"m1_{i}", [128, 512], F32) for i in range(2)]
    m2 = [ph.sb(f"m2_{i}", [128, 512], F32) for i in range(2)]
    b_m = [Buf(), Buf()]
    xt = [ph.sb(f"xt{i}", [128, D], F32) for i in range(4)]
    b_xt = [Buf() for _ in range(4)]
    hh = [ph.sb(f"hh{i}", [128, D], F32) for i in range(4)]
    b_hh = [Buf() for _ in range(4)]
    hb = [ph.sb(f"hb{i}", [128, D], BF16) for i in range(4)]
    b_hb = [Buf() for _ in range(4)]
    hT = [ph.sb(f"hT{i}", [128, 8, 128], BF16) for i in range(4)]
    b_hT = [Buf() for _ in range(4)]
    junk = ph.sb("junk", [128, D], BF16)
    b_junk = Buf()
    ss = ph.sb("ss", [128, 4, 2], F32)
    b_ss = [Buf() for _ in range(4)]
    nb = 0
    nm_ = 0
    nt_ = 0
    pending = []
    for b in range(NT // 512):
        tok = slice(b * 512, (b + 1) * 512)
        ya, ybb, ga, gb, b_ld = yas[b % 2], ybbs[b % 2], gas[b % 2], gbs[b % 2], b_lds[b % 2]
        for dst, src, r0 in ((ya, "yaT", 0), (ybb, "ybT", 0), (ga, "gt", 0), (gb, "gt", 1024)):
            ph.dma(dst[:], T[src][r0:r0 + 1024, tok].rearrange("(kc p) t -> p kc t", p=128), wr=[b_ld])
        for oc in range(8):
            ba, bb = nb % 6, (nb + 1) % 6
            nb += 2

            def f_ab(pe, oc=oc, ba=ba, bb=bb, ya=ya, ybb=ybb):
                for kc in range(8):
                    pe.matmul(banks[ba][:, :], lhsT=W["proj_a"][:, kc, oc * 128:(oc + 1) * 128], rhs=ya[:, kc, :], start=(kc == 0), stop=(kc == 7))
                for kc in range(8):
                    i = pe.matmul(banks[bb][:, :], lhsT=W["proj_b"][:, kc, oc * 128:(oc + 1) * 128], rhs=ybb[:, kc, :], start=(kc == 0), stop=(kc == 7))
                return i

            ph.add("pe", f_ab, rd=[b_ld, bW["proj_a"], bW["proj_b"]], wr=[b_bank[ba], b_bank[bb]])
            m = nm_ % 2
            nm_ += 1

            def f_m(v, oc=oc, ba=ba, bb=bb, m=m, ga=ga, gb=gb):
                v.tensor_tensor(out=m1[m][:], in0=banks[ba][:, :], in1=ga[:, oc, :], op=ALU.mult)
                return v.tensor_tensor(out=m2[m][:], in0=banks[bb][:, :], in1=gb[:, oc, :], op=ALU.mult)

            ph.add("dve", f_m, rd=[b_ld], wr=[b_bank[ba], b_bank[bb], b_m[m]])
            if oc % 2 == 1 and pending:
                for o_ in pending.pop(0):
                    ph._add_cap(*o_)
            ph.add("dve", lambda g, oc=oc, m=m: g.tensor_tensor(out=mg[:, oc, :], in0=m1[m][:], in1=m2[m][:], op=ALU.add), rd=[b_m[m]], wr=[b_mg])
        for ti in range(4):
            t = b * 4 + ti
            s_ = nt_ % 4
            nt_ += 1
            ph.dma(xt[s_][:], T["x"][t * 128:(t + 1) * 128, :], wr=[b_xt[s_]])
            ba, bb = nb % 6, (nb + 1) % 6
            nb += 2

            def f_o(pe, ti=ti, ba=ba, bb=bb):
                for hf, bk in ((0, ba), (1, bb)):
                    for kc in range(8):
                        i = pe.matmul(banks[bk][:, :], lhsT=mg[:, kc, ti * 128:(ti + 1) * 128], rhs=W["w_out"][:, kc, hf * 512:(hf + 1) * 512], start=(kc == 0), stop=(kc == 7))
                return i

            ph.add("pe", f_o, rd=[b_mg, bW["w_out"]], wr=[b_bank[ba], b_bank[bb]])

            def f_h(v, s_=s_, ba=ba, bb=bb):
                v.tensor_tensor(out=hh[s_][:, 0:512], in0=banks[ba][:, :], in1=xt[s_][:, 0:512], op=ALU.add)
                return v.tensor_tensor(out=hh[s_][:, 512:1024], in0=banks[bb][:, :], in1=xt[s_][:, 512:1024], op=ALU.add)

            ph.add("dve", f_h, rd=[b_xt[s_]], wr=[b_bank[ba], b_bank[bb], b_hh[s_]])
            ph.dma(T["hs"][t * 128:(t + 1) * 128, :], hh[s_][:], rd=[b_hh[s_]])
            ph.cap = []
            ph.add("act", lambda a, s_=s_: a.activation(out=junk[:], in_=hh[s_][:], func=AF.Square, accum_out=ss[:, s_, 0:1]), rd=[b_hh[s_]], wr=[b_junk, b_ss[s_]])
            ph.add("act", lambda a, s_=s_: a.activation(out=ss[:, s_, 1:2], in_=ss[:, s_, 0:1], func=AF.Sqrt, scale=1.0 / D, bias=cst[:, 0:1]), rd=[b_cst], wr=[b_ss[s_]])
            ph.add("dve", lambda v, s_=s_: v.reciprocal(out=ss[:, s_, 1:2], in_=ss[:, s_, 1:2]), wr=[b_ss[s_]])
            ph.add("act", lambda a, s_=s_: a.activation(out=hb[s_][:], in_=hh[s_][:], func=AF.Copy, scale=ss[:, s_, 1:2]), rd=[b_hh[s_], b_ss[s_]], wr=[b_hb[s_]])

            def f_tr(pe, s_=s_):
                for kc in range(8):
                    i = pe.transpose(out=pst[:, kc * 128:(kc + 1) * 128], in_=hb[s_][:, kc * 128:(kc + 1) * 128], identity=ident[:])
                return i

            ph.add("pe", f_tr, rd=[b_hb[s_], b_ident], wr=[b_pst])
            ph.add("dve", lambda v, s_=s_: v.tensor_copy(out=hT[s_][:], in_=pst[:, :].rearrange("p (k t) -> p k t", k=8)), wr=[b_pst, b_hT[s_]])
            ph.dma(T["hnT"][:, t * 128:(t + 1) * 128].rearrange("(kc p) t -> p kc t", p=128), hT[s_][:], rd=[b_hT[s_]])
            pending.append(ph.cap)
            ph.cap = None
    while pending:
        for o_ in pending.pop(0):
            ph._add_cap(*o_)
    return ph.run()


def phase4b(nc, cfg, T, bar):
    ph = Phase(nc, "p4b", bar)
    NT = cfg.ntok
    stage = ph.sb("stage", [128, 4, 512], F32)
    b_stage = Buf()
    gcol = ph.sb("gcol", [128, 8], F32)
    b_g = Buf()
    ph.dma(gcol[:], T["norm_mlp"].rearrange("(kc p) -> p kc", p=128), wr=[b_g], allow_slow_non_contiguous=True)
    gfin = ph.sb("gfin", [128, D], F32)
    b_gf = Buf()
    ph.dma(gfin[:], T["norm_final"].partition_broadcast(128), wr=[b_gf])
    w1 = ph.sb("w1", [128, 8, DFF], BF16)
    b_w1 = Buf()
    w2 = ph.sb("w2", [128, 32, D], BF16)
    b_w2 = Buf()
    load_w_bf16(ph, w1, b_w1, T["w_mlp_in"], 8, DFF, stage, b_stage, scale_col=gcol, b_scale=b_g)
    load_w_bf16(ph, w2, b_w2, T["w_mlp_out"], 32, D, stage, b_stage)
    cst = ph.sb("cst", [128, 1], F32)
    b_cst = Buf()
    ph.add("pool", lambda g: g.memset(cst[:], 1e-6), wr=[b_cst])
    banks = [ph.ps(f"bk{i}", [128, 512], F32) for i in range(8)]
    b_bank = [Buf() for _ in range(8)]
    hn = [ph.sb("hn0", [128, 8, 512], BF16)] * 2
    b_hn = [Buf()] * 2
    hid = ph.sb("hid", [128, 32, 512], BF16)
    b_hid = [Buf() for _ in range(32)]
    rl = [ph.sb(f"rl{i}", [128, 512], F32) for i in range(2)]
    b_rl = [Buf(), Buf()]
    ht = [ph.sb(f"ht{i}", [128, D], F32) for i in range(2)]
    b_ht = [Buf(), Buf()]
    oo = [ph.sb(f"oo{i}", [128, D], F32) for i in range(2)]
    b_oo = [Buf(), Buf()]
    junk = ph.sb("junk", [128, D], BF16)
    b_junk = Buf()
    ss = ph.sb("ss", [128, 2, 2], F32)
    b_ss = [Buf(), Buf()]
    nb = 0
    nr = 0
    nt_ = 0
    for b in range(NT // 512):
        u = b % 2
        tok = slice(b * 512, (b + 1) * 512)
        ph.dma(hn[u][:], T["hnT"][:, tok].rearrange("(kc p) t -> p kc t", p=128), wr=[b_hn[u]])
        for fc in range(32):
            bk = nb % 8
            nb += 1

            def f_h(pe, fc=fc, bk=bk, u=u):
                for kc in range(8):
                    i = pe.matmul(banks[bk][:, :], lhsT=w1[:, kc, fc * 128:(fc + 1) * 128], rhs=hn[u][:, kc, :], start=(kc == 0), stop=(kc == 7))
                return i

            ph.add("pe", f_h, rd=[b_w1, b_hn[u]], wr=[b_bank[bk]])
            r_ = nr % 2
            nr += 1
            ph.add("act", lambda a, bk=bk, r_=r_: a.activation(out=rl[r_][:], in_=banks[bk][:, :], func=AF.Relu), wr=[b_bank[bk], b_rl[r_]])
            ph.add("dve", lambda g, fc=fc, r_=r_: g.tensor_tensor(out=hid[:, fc, :], in0=rl[r_][:], in1=rl[r_][:], op=ALU.mult), rd=[b_rl[r_]], wr=[b_hid[fc]])
        for ti in range(4):
            t = b * 4 + ti
            s_ = nt_ % 2
            nt_ += 1
            ph.dma(ht[s_][:], T["hs"][t * 128:(t + 1) * 128, :], wr=[b_ht[s_]])
            ba, bb = nb % 8, (nb + 1) % 8
            nb += 2

            def f_o(pe, ti=ti, ba=ba, bb=bb):
                for hf, bk in ((0, ba), (1, bb)):
                    for fc in range(32):
                        i = pe.matmul(banks[bk][:, :], lhsT=hid[:, fc, ti * 128:(ti + 1) * 128], rhs=w2[:, fc, hf * 512:(hf + 1) * 512], start=(fc == 0), stop=(fc == 31))
                return i

            ph.add("pe", f_o, rd=[b_w2] + b_hid, wr=[b_bank[ba], b_bank[bb]])

            def f_r(v, s_=s_, ba=ba, bb=bb):
                v.tensor_tensor(out=oo[s_][:, 0:512], in0=banks[ba][:, :], in1=ht[s_][:, 0:512], op=ALU.add)
                return v.tensor_tensor(out=oo[s_][:, 512:1024], in0=banks[bb][:, :], in1=ht[s_][:, 512:1024], op=ALU.add)

            ph.add("dve", f_r, rd=[b_ht[s_]], wr=[b_bank[ba], b_bank[bb], b_oo[s_]])
            ph.add("act", lambda a, s_=s_: a.activation(out=junk[:], in_=oo[s_][:], func=AF.Square, accum_out=ss[:, s_, 0:1]), rd=[b_oo[s_]], wr=[b_junk, b_ss[s_]])
            ph.add("act", lambda a, s_=s_: a.activation(out=ss[:, s_, 1:2], in_=ss[:, s_, 0:1], func=AF.Sqrt, scale=1.0 / D, bias=cst[:, 0:1]), rd=[b_cst], wr=[b_ss[s_]])
            ph.add("dve", lambda v, s_=s_: v.reciprocal(out=ss[:, s_, 1:2], in_=ss[:, s_, 1:2]), wr=[b_ss[s_]])
            ph.add("dve", lambda g, s_=s_: g.scalar_tensor_tensor(out=oo[s_][:], in0=oo[s_][:], scalar=ss[:, s_, 1:2], in1=gfin[:], op0=ALU.mult, op1=ALU.mult), rd=[b_ss[s_], b_gf], wr=[b_oo[s_]])
            ph.dma(T["y"][t * 128:(t + 1) * 128, :], oo[s_][:], rd=[b_oo[s_]])
    return ph.run()


def phase15(nc, cfg, T, bar):
    ph = Phase(nc, "p15", bar)
    TB = 256
    mu = ph.sb("mu", [128, 27], F32)
    omu = ph.sb("omu", [128, 27], F32)
    hmu = ph.sb("hmu", [128, 27], F32)
    b_par = Buf()
    ph.dma(mu[:], T["mu_shift"].rearrange("(c p) -> p c", p=128), wr=[b_par], allow_slow_non_contiguous=True)
    ph.add("dve", lambda v: v.tensor_scalar(out=omu[:], in0=mu[:], scalar1=-1.0, scalar2=1.0, op0=ALU.mult, op1=ALU.add), wr=[b_par])
    ph.add("dve", lambda v: v.tensor_scalar(out=hmu[:], in0=mu[:], scalar1=0.5, scalar2=None, op0=ALU.mult), wr=[b_par])
    P = [ph.sb(f"P{i}", [128, 27, TB + 2], F32) for i in range(2)]
    b_P = [Buf(), Buf()]
    t1 = [ph.sb(f"t1_{i}", [128, 27, TB], F32) for i in range(2)]
    b_t1 = [Buf(), Buf()]
    t2 = [ph.sb(f"t2_{i}", [128, 27, TB], F32) for i in range(2)]
    b_t2 = [Buf(), Buf()]
    n = 0
    for si, S in enumerate(cfg.seqs):
        base = cfg.off[si]
        nb_ = S // TB
        for bi in range(nb_):
            u = n % 2
            n += 1
            t0 = base + bi * TB
            lo = 1 if bi == 0 else 0
            hi = TB + 1 if bi == nb_ - 1 else TB + 2
            if lo == 1:
                ph.add("pool", lambda g, u=u: g.memset(P[u][:, :, 0:1], 0.0), wr=[b_P[u]])
            if hi == TB + 1:
                ph.add("pool", lambda g, u=u: g.memset(P[u][:, :, TB + 1:TB + 2], 0.0), wr=[b_P[u]])
            for c0_, c1_ in ((0, 9), (9, 18), (18, 27)):
                ph.dma(P[u][:, c0_:c1_, lo:hi], T["pr"][c0_ * 128:c1_ * 128, t0 - 1 + lo:t0 - 1 + hi].rearrange("(c p) t -> p c t", p=128), wr=[b_P[u]])
            for c in range(27):
                ph.add("act", lambda a, u=u, c=c: a.activation(out=t1[u][:, c, :], in_=P[u][:, c, 1:TB + 1], func=AF.Copy, scale=omu[:, c:c + 1]), rd=[b_P[u], b_par], wr=[b_t1[u]])

            def f_sh(v, u=u):
                v.tensor_tensor(out=t2[u][:], in0=P[u][:, :, 0:TB], in1=P[u][:, :, 2:TB + 2], op=ALU.add)
                v.tensor_tensor(out=t2[u][:], in0=t2[u][:], in1=hmu[:].unsqueeze(2).to_broadcast([128, 27, TB]), op=ALU.mult)
                return v.tensor_tensor(out=t2[u][:], in0=t2[u][:], in1=t1[u][:], op=ALU.add)

            ph.add("dve", f_sh, rd=[b_P[u], b_par, b_t1[u]], wr=[b_t2[u]])
            for c0_, c1_ in ((0, 9), (9, 18), (18, 27)):
                ph.dma(T["prs"][c0_ * 128:c1_ * 128, t0:t0 + TB].rearrange("(c p) t -> p c t", p=128), t2[u][:, c0_:c1_, :], rd=[b_t2[u]])
    return ph.run()


def phase2(nc, cfg, T, bar):
    ph = Phase(nc, "p2", bar)
    NT = cfg.ntok
    C0 = -math.exp(-0.5)
    sb, add = ph.sb, ph.add
    identb, b_identb = make_ident(ph, BF16, name="idb")
    Y = sb("Y", [128, 1024], F32)
    Yp = sb("Yp", [128, 1024], F32)
    wst = Yp
    MK = Y[:, 0:512].rearrange("p (m c) -> p m c", m=4)
    TRt = sb("TRt", [128, 4, 128], BF16)
    TR = TRt
    identf = ph.sb("idf", [128, 128], F32)
    BD = sb("BD", [128, 128], BF16)
    onesr = sb("onesr", [1, 128], BF16)
    mA = [sb(f"mA{d}", [128, 256], F32) for d in range(2)]
    mL = [sb(f"mL{d}", [128, 256], F32) for d in range(2)]
    trc = [sb(f"trc{d}", [128, 384], BF16) for d in range(2)]
    b_cst = Buf()

    def f_masks(g):
        g.memset(MK, 1.0)
        g.affine_select(out=MK[:, 0, :], in_=MK[:, 0, :], compare_op=ALU.is_gt, fill=0.0, base=0, pattern=[[1, 128]], channel_multiplier=-1)
        g.affine_select(out=MK[:, 1, :], in_=MK[:, 1, :], compare_op=ALU.is_ge, fill=0.0, base=0, pattern=[[1, 128]], channel_multiplier=-1)
        g.affine_select(out=MK[:, 2, :], in_=MK[:, 2, :], compare_op=ALU.is_gt, fill=0.0, base=0, pattern=[[-1, 128]], channel_multiplier=1)
        g.affine_select(out=MK[:, 3, :], in_=MK[:, 3, :], compare_op=ALU.is_ge, fill=0.0, base=0, pattern=[[-1, 128]], channel_multiplier=1)
        g.tensor_scalar(out=TR[:], in0=MK, scalar1=C0, scalar2=None, op0=ALU.mult)
        g.memset(identf[:], 0.0)
        g.affine_select(out=identf[:], in_=identf[:], compare_op=ALU.not_equal, fill=1.0, base=0, pattern=[[-1, 128]], channel_multiplier=1)
        g.memset(BD[:], 0.0)
        g.memset(BD[0:64, 0:64], 1.0)
        g.memset(BD[64:128, 64:128], 1.0)
        g.memset(onesr[:], 1.0)
        for d in range(2):
            st, inc, lm = (0, 1, 2) if d == 0 else (2, 3, 0)
            g.tensor_copy(out=mA[d][:, 0:128], in_=MK[:, st, :])
            g.tensor_copy(out=mA[d][:, 128:256], in_=MK[:, inc, :])
            for q in range(2):
                g.tensor_copy(out=mL[d][:, q * 128:(q + 1) * 128], in_=MK[:, lm, :])
            g.tensor_copy(out=trc[d][:, 0:128], in_=TR[:, inc, :])
            g.tensor_copy(out=trc[d][:, 128:256], in_=TR[:, st, :])
            i = g.tensor_copy(out=trc[d][:, 256:384], in_=TR[:, lm, :])
        return i

    add("pool", f_masks, wr=[b_cst])
    par = sb("par", [128, 8, 8], F32)
    b_par = Buf()
    for i, ap_ in enumerate((T["k_k"], T["k_a"], T["k_a"], T["r_k"], T["ln_x_w"], T["ln_x_b"], T["a0"][0:1024], T["a0"][1024:2048])):
        ph.dma(par[:, i, :], ap_.rearrange("(c p) -> p c", p=128), wr=[b_par], allow_slow_non_contiguous=True)
    add("pool", lambda g: g.tensor_scalar(out=par[:, 2, :], in0=par[:, 2, :], scalar1=-1.0, scalar2=1.0, op0=ALU.mult, op1=ALU.add), wr=[b_par])
    b_wst = Buf()
    wup = sb("wup", [128, 1024], BF16)
    aup = sb("aup", [128, 1024], BF16)
    gup = sb("gup", [128, 1024], BF16)
    b_w = Buf()
    for dst, src in ((wup, T["w_lora_up"]), (aup, T["a_lora_up"]), (gup, T["g_lora_up"])):
        ph.dma(wst[:], src[:, :], wr=[b_wst])
        add("pool", lambda g, dst=dst: g.tensor_copy(out=dst[:], in_=wst[:]), rd=[b_wst], wr=[b_w])
    w0h = sb("w0h", [1, 2, 1024], BF16)
    w0l = sb("w0l", [1, 2, 1024], BF16)
    w0t = Y
    for d_ in range(2):
        ph.dma(wst[0:1, :], T["w0"][d_ * 1024:(d_ + 1) * 1024].rearrange("(o c) -> o c", o=1), wr=[b_wst])

        def f_w0(g, d_=d_):
            g.tensor_copy(out=w0h[0:1, d_, :], in_=wst[0:1, :])
            g.tensor_copy(out=w0t[0:1, :], in_=w0h[0:1, d_, :])
            g.tensor_tensor(out=w0t[0:1, :], in0=wst[0:1, :], in1=w0t[0:1, :], op=ALU.subtract)
            return g.tensor_copy(out=w0l[0:1, d_, :], in_=w0t[0:1, :])

        add("pool", f_w0, rd=[], wr=[b_wst, b_w, b_cst])
    eps = sb("eps", [128, 2], F32)
    add("pool", lambda g: (g.memset(eps[:, 0:1], 64e-5), g.memset(eps[:, 1:2], 1e-24))[1], wr=[b_cst])

    bk = [ph.ps(f"bk{i}", [128, 512], F32) for i in range(6)]
    b_bk = [Buf() for _ in range(6)]
    bt = [ph.ps(f"bt{i}", [128, 1024], BF16) for i in range(2)]
    b_bt = [Buf(), Buf()]
    cnt = dict(b=0, t=0)

    def nb():
        pool_ = cnt.get("pool")
        if pool_ is None:
            i = cnt["b"] % 6
            cnt["b"] += 1
            return i
        k_ = "b" + str(pool_[0])
        i = pool_[cnt.get(k_, 0) % len(pool_)]
        cnt[k_] = cnt.get(k_, 0) + 1
        return i

    def ntb():
        i = cnt["t"] % 2
        cnt["t"] += 1
        return i

    SH = sb("SH", [128, 27, 128], F32)
    bS_ = {k_: Buf("SH" + k_) for k_ in ("r", "k", "v", "lw", "la", "lg")}
    f4 = lambda nm: sb(nm, [128, 8, 128], F32)
    h4 = lambda nm: sb(nm, [128, 8, 128], BF16)
    av, kk, kd, Ei, tmp1 = [f4(n) for n in ("av", "kk", "kd", "Ei", "tmp1")]
    bb_ = av
    En, Ee, Es = [h4(n) for n in ("En", "Ee", "Es")]
    tmp2 = tmp1
    kkr = tmp1
    rn = kd
    names = ("av", "kk", "bb", "kd", "E", "tmp1", "tl", "lab", "sig", "sq", "AR", "bt", "kt", "Bg", "Kg", "vb",
             "Atok", "Bgtok", "Kgtok", "Vtok", "AM", "KM", "LL", "PL0", "PL1", "PA0", "PA1", "X0", "X1", "Zb", "AhT", "What", "Ub", "S", "Sbf",
             "Y", "Yp", "sg", "bon", "bonp", "st", "yo", "Stmp", "Gc")
    dbl = ("AR", "Atok", "Bgtok", "Kgtok", "Vtok", "AM", "KM", "LL", "sg", "bon", "Gc")
    B0 = {n: Buf(n) for n in names}
    Bp = [dict(B0), dict(B0)]
    for n_ in dbl:
        Bp[1][n_] = Buf(n_ + "1")
    grp = ("AM", "KM", "LL", "PL0", "PL1", "PA0", "PA1", "X0", "X1")
    for n_ in grp:
        l0 = [Buf(n_ + str(g_)) for g_ in range(4)]
        Bp[0][n_] = l0
        Bp[1][n_] = [Buf(n_ + "b" + str(g_)) for g_ in range(4)] if n_ in dbl else l0
    for Bx in Bp:
        Bx["bb"] = Bx["av"]
        Bx["kkr"] = Bx["tmp1"]
        Bx["tmp2"] = Bx["tmp1"]
        Bx["rn"] = Bx["kd"]
        Bx["fin"] = Bx["What"]
        Bx["yn"] = Bx["Yp"]
    tl = sb("tl", [128, 128], BF16)
    lab = sb("lab", [128, 128], BF16)
    sig = sb("sig", [128, 1024], BF16)
    sq = h4("sq")
    btT, ktT, BgT, KgT, vb = [h4(n) for n in ("btT", "ktT", "BgT", "KgT", "vbb")]
    Zb = sb("Zb", [128, 1024], BF16)
    two = lambda nm, shp, dt: [sb(nm + "0", shp, dt), sb(nm + "1", shp, dt)]
    sgbs = two("sgb", [128, 128], BF16)
    ARs = two("AR", [128, 8, 256], BF16)
    Atoks, Bgtoks, Kgtoks, Vtoks = [two(n, [128, 1024], BF16) for n in ("Atok", "Bgtok", "Kgtok", "Vtok")]
    AMs = two("AM", [128, 16, 256], BF16)
    KMs = two("KM", [128, 16, 256], BF16)
    LLs = two("LL", [128, 16, 128], BF16)
    bons = two("bon", [128, 8, 128], F32)
    Gcs = two("Gc", [128, 8, 1], F32)
    PL12 = [sb("PL1", [128, 16, 128], BF16), sb("PL2", [128, 16, 128], BF16)]
    PA = [sb("PA1", [128, 16, 128], BF16), sb("PA2", [128, 16, 128], BF16)]
    X = [sb("X0", [128, 16, 128], BF16), sb("X1", [128, 16, 128], BF16)]
    AhT = sb("AhT", [128, 8, 128], BF16)
    What = sb("What", [128, 1024], F32)
    Ub = sb("Ub", [128, 1024], BF16)
    Sf = sb("Sf", [128, 8, 64], F32)
    Stmp = sb("Stmp", [128, 8, 64], F32)
    Sbf = sb("Sbf", [128, 8, 64], BF16)
    bonp = f4("bonp")
    st = sb("st", [128, 16, 4], F32)
    yn2 = Yp
    yn = yn2[:].rearrange("p (h v) -> p h v", v=64)
    fin = What[:].rearrange("p (c t) -> p c t", c=8)
    yo = Zb[:].rearrange("p (c t) -> p c t", c=8)
    for Bx in Bp:
        Bx["yo"] = Bx["Zb"]
    dramB = {}

    def dB(kind, t0):
        return dramB.setdefault((kind, t0), Buf())

    def stageA(si, d, ci, p):
        S = cfg.seqs[si]
        base = cfg.off[si]
        nch = S // 128
        t0 = base + ci * 128
        first = (ci == 0) if d == 0 else (ci == nch - 1)
        final = (d == 1)
        B = Bp[p]
        AR, Atok, Bgtok, Kgtok, Vtok, AM, KM, LL, sgb, bon, Gc = ARs[p], Atoks[p], Bgtoks[p], Kgtoks[p], Vtoks[p], AMs[p], KMs[p], LLs[p], sgbs[p], bons[p], Gcs[p]
        PL = [LL, PL12[0], PL12[1]]
        R, Kx, Vx = SH[:, 0:8, :], SH[:, 8:16, :], SH[:, 16:24, :]
        dsl = slice(d * 64, (d + 1) * 64)
        bc = lambda i: par[:, i, :].unsqueeze(2).to_broadcast([128, 8, 128])
        ec = 127 if d == 0 else 0
        for k_, c0_, c1_ in (("lw", 24, 25), ("la", 25, 26), ("k", 8, 16), ("r", 0, 8), ("v", 16, 24), ("lg", 26, 27)):
            ph.dma(SH[:, c0_:c1_, :], T["prs"][c0_ * 128:c1_ * 128, t0:t0 + 128].rearrange("(c p) t -> p c t", p=128), wr=[bS_[k_]])
        add("act", lambda a: a.activation(out=tl[:], in_=SH[:, 24, :], func=AF.Tanh), rd=[bS_["lw"]], wr=[B["tl"]])
        add("act", lambda a: a.activation(out=lab[:], in_=SH[:, 25, :], func=AF.Copy), rd=[bS_["la"]], wr=[B["lab"]])
        w1_, w2_ = nb(), nb()

        def f_lw(pe):
            for hf, b_ in ((0, w1_), (1, w2_)):
                cs = slice(hf * 512, (hf + 1) * 512)
                pe.matmul(bk[b_][:, :], lhsT=tl[dsl, :], rhs=wup[dsl, cs], start=True, stop=False)
                pe.matmul(bk[b_][:, :], lhsT=onesr[0:1, :], rhs=w0h[0:1, d, cs], start=False, stop=False)
                i = pe.matmul(bk[b_][:, :], lhsT=onesr[0:1, :], rhs=w0l[0:1, d, cs], start=False, stop=True)
            return i

        add("pe", f_lw, rd=[B["tl"], b_w, b_cst], wr=[b_bk[w1_], b_bk[w2_]])
        add("act", lambda a: a.activation(out=sig[:, 0:512], in_=bk[w1_][:, :], func=AF.Sigmoid), wr=[b_bk[w1_], B["sig"]])
        add("act", lambda a: a.activation(out=sig[:, 512:1024], in_=bk[w2_][:, :], func=AF.Sigmoid), wr=[b_bk[w2_], B["sig"]])
        a1_, a2_ = nb(), nb()

        def f_la(pe):
            for c in range(8):
                b_ = a1_ if c < 4 else a2_
                i = pe.matmul(bk[b_][:, (c % 4) * 128:(c % 4 + 1) * 128], lhsT=aup[dsl, c * 128:(c + 1) * 128], rhs=lab[dsl, :], start=True, stop=True)
            return i

        add("pe", f_la, rd=[B["lab"], b_w], wr=[b_bk[a1_], b_bk[a2_]])
        for c in range(8):
            b_ = a1_ if c < 4 else a2_
            add("act", lambda a, c=c, b_=b_: a.activation(out=av[:, c, :], in_=bk[b_][:, (c % 4) * 128:(c % 4 + 1) * 128], func=AF.Sigmoid, bias=par[:, 6 + d, c:c + 1]), rd=[b_par], wr=[b_bk[b_], B["av"]])
        for hf in range(2):
            cb = [nb(), nb(), nb()]

            def f_cum(pe, hf=hf, cb=cb):
                for cc_ in range(4):
                    c = hf * 4 + cc_
                    for x in range(3):
                        i = pe.matmul(bk[cb[x]][:, cc_ * 128:(cc_ + 1) * 128], lhsT=sig[:, c * 128:(c + 1) * 128], rhs=trc[d][:, x * 128:(x + 1) * 128], start=True, stop=True)
                return i

            add("pe", f_cum, rd=[B["sig"], b_cst], wr=[b_bk[i] for i in cb])
            hs_ = slice(hf * 4, hf * 4 + 4)
            v3 = lambda b_: bk[b_][:, :].rearrange("p (c t) -> p c t", c=4)
            add("act", lambda a, cb=cb, hs_=hs_: (a.activation(out=Ei[:, hs_, :], in_=v3(cb[0]), func=AF.Exp), a.activation(out=En[:, hs_, :], in_=v3(cb[0]), func=AF.Exp, scale=-1.0))[1], wr=[b_bk[cb[0]], B["E"]])
            add("act", lambda a, cb=cb, hs_=hs_: a.activation(out=Ee[:, hs_, :], in_=v3(cb[1]), func=AF.Exp), wr=[b_bk[cb[1]], B["E"]])
            add("act", lambda a, cb=cb, hs_=hs_: a.activation(out=Es[:, hs_, :], in_=v3(cb[2]), func=AF.Exp), wr=[b_bk[cb[2]], B["E"]])
        add("dve", lambda v: v.tensor_tensor(out=kkr[:], in0=Kx, in1=bc(0), op=ALU.mult), rd=[bS_["k"], b_par], wr=[B["kkr"]])
        add("act", lambda a: a.activation(out=sq[:], in_=kkr[:], func=AF.Square), rd=[B["kkr"]], wr=[B["sq"]])
        n1_, n2_ = nb(), nb()

        def f_nrm(pe):
            sqf = sq[:].rearrange("p c t -> p (c t)")
            pe.matmul(bk[n1_][:, :], lhsT=BD[:], rhs=sqf[:, 0:512], start=True, stop=True)
            return pe.matmul(bk[n2_][:, :], lhsT=BD[:], rhs=sqf[:, 512:1024], start=True, stop=True)

        add("pe", f_nrm, rd=[B["sq"], b_cst], wr=[b_bk[n1_], b_bk[n2_]])
        add("act", lambda a: a.activation(out=rn[:, 0:4, :], in_=bk[n1_][:, :].rearrange("p (c t) -> p c t", c=4), func=AF.Ln, bias=eps[:, 1:2]), rd=[b_cst], wr=[b_bk[n1_], B["rn"]])
        add("act", lambda a: a.activation(out=rn[:, 4:8, :], in_=bk[n2_][:, :].rearrange("p (c t) -> p c t", c=4), func=AF.Ln, bias=eps[:, 1:2]), rd=[b_cst], wr=[b_bk[n2_], B["rn"]])
        add("act", lambda a: a.activation(out=rn[:], in_=rn[:], func=AF.Exp, scale=-0.5), wr=[B["rn"]])
        add("dve", lambda v: v.tensor_tensor(out=kk[:], in0=kkr[:], in1=rn[:], op=ALU.mult), rd=[B["kkr"], B["rn"]], wr=[B["kk"]])

        def f_kd(g):
            g.tensor_tensor(out=tmp1[:], in0=av[:], in1=bc(1), op=ALU.mult)
            g.tensor_tensor(out=tmp1[:], in0=tmp1[:], in1=bc(2), op=ALU.add)
            g.tensor_tensor(out=kd[:], in0=tmp1[:], in1=Kx, op=ALU.mult)
            return g.tensor_tensor(out=bb_[:], in0=kk[:], in1=av[:], op=ALU.mult)

        add("dve", f_kd, rd=[B["av"], b_par, bS_["k"], B["kk"]], wr=[B["tmp1"], B["kd"], B["bb"]])

        def f_sc1(v):
            v.scalar_tensor_tensor(out=AR[:, :, 0:128], in0=kk[:], scalar=-1.0, in1=Ee[:], op0=ALU.mult, op1=ALU.mult)
            v.tensor_tensor(out=AR[:, :, 128:256], in0=R, in1=Ei[:], op=ALU.mult)
            return v.tensor_tensor(out=btT[:], in0=bb_[:], in1=En[:], op=ALU.mult)

        add("dve", f_sc1, rd=[B["kk"], B["E"], bS_["r"], B["bb"]], wr=[B["AR"], B["bt"]])

        add("dve", lambda g: g.tensor_tensor(out=ktT[:], in0=kd[:], in1=En[:], op=ALU.mult), rd=[B["kd"], B["E"]], wr=[B["kt"]])
        add("pool", lambda g: g.tensor_tensor(out=BgT[:], in0=bb_[:], in1=Es[:], op=ALU.mult), rd=[B["E"], B["bb"]], wr=[B["Bg"]])
        add("pool", lambda g: g.tensor_tensor(out=KgT[:], in0=kd[:], in1=Es[:], op=ALU.mult), rd=[B["kd"], B["E"]], wr=[B["Kg"]])
        add("act", lambda a: a.activation(out=vb[:], in_=Vx, func=AF.Copy), rd=[bS_["v"]], wr=[B["vb"]])
        add("dve", lambda g: (g.tensor_tensor(out=tmp2[:], in0=R, in1=bc(3), op=ALU.mult), g.tensor_tensor(out=sq[:], in0=tmp2[:], in1=kd[:], op=ALU.mult))[1], rd=[bS_["r"], b_par, B["kd"]], wr=[B["tmp2"], B["sq"]])
        o1_, o2_ = nb(), nb()

        def f_bon(pe):
            sqf = sq[:].rearrange("p c t -> p (c t)")
            pe.matmul(bk[o1_][:, :], lhsT=BD[:], rhs=sqf[:, 0:512], start=True, stop=True)
            return pe.matmul(bk[o2_][:, :], lhsT=BD[:], rhs=sqf[:, 512:1024], start=True, stop=True)

        add("pe", f_bon, rd=[B["sq"], b_cst], wr=[b_bk[o1_], b_bk[o2_]])
        add("dve", lambda v: v.tensor_tensor(out=bon[:, 0:4, :], in0=bk[o1_][:, :].rearrange("p (c t) -> p c t", c=4), in1=SH[:, 16:20, :], op=ALU.mult), rd=[bS_["v"]], wr=[b_bk[o1_], B["bon"]])
        add("dve", lambda v: v.tensor_tensor(out=bon[:, 4:8, :], in0=bk[o2_][:, :].rearrange("p (c t) -> p c t", c=4), in1=SH[:, 20:24, :], op=ALU.mult), rd=[bS_["v"]], wr=[b_bk[o2_], B["bon"]])
        for src, srcb, dst, dstb in ((AR, "AR", Atok, "Atok"), (BgT, "Bg", Bgtok, "Bgtok"), (KgT, "Kg", Kgtok, "Kgtok"), (vb, "vb", Vtok, "Vtok")):
            tb = ntb()

            def f_tr(pe, src=src, tb=tb):
                for c in range(8):
                    i = pe.transpose(out=bt[tb][:, c * 128:(c + 1) * 128], in_=src[:, c, 0:128], identity=identb[:])
                return i

            add("pe", f_tr, rd=[B[srcb], b_identb], wr=[b_bt[tb]])
            add("act", lambda a, dst=dst, tb=tb: a.activation(out=dst[:], in_=bt[tb][:, :], func=AF.Copy), wr=[b_bt[tb], B[dstb]])
        for g4 in range(8):
            hp0 = (g4 // 2) * 2
            par_ = g4 % 2
            hs2 = [2 * hp0 + par_, 2 * (hp0 + 1) + par_]
            ba, bk_, bl = nb(), nb(), nb()
            ps_ = slice(par_ * 64, par_ * 64 + 64)

            def f_sc(pe, hs2=hs2, ba=ba, bk_=bk_, bl=bl, ps_=ps_):
                for q, h in enumerate(hs2):
                    c = h // 2
                    pe.matmul(bk[ba][:, q * 256:(q + 1) * 256], lhsT=btT[ps_, c, :], rhs=AR[ps_, c, :], start=True, stop=True)
                    pe.matmul(bk[bk_][:, q * 256:(q + 1) * 256], lhsT=ktT[ps_, c, :], rhs=AR[ps_, c, :], start=True, stop=True)
                    i = pe.matmul(bk[bl][:, q * 128:(q + 1) * 128], lhsT=AR[ps_, c, 0:128], rhs=btT[ps_, c, :], start=True, stop=True)
                return i

            add("pe", f_sc, rd=[B["AR"], B["bt"], B["kt"]], wr=[b_bk[ba], b_bk[bk_], b_bk[bl]])

            def f_ev(v, ba=ba, bk_=bk_, bl=bl, hs2=hs2):
                for q, h in enumerate(hs2):
                    v.tensor_tensor(out=AM[:, h, :], in0=bk[ba][:, q * 256:(q + 1) * 256], in1=mA[d][:, 0:256], op=ALU.mult)
                    v.tensor_tensor(out=KM[:, h, :], in0=bk[bk_][:, q * 256:(q + 1) * 256], in1=mA[d][:, 0:256], op=ALU.mult)
                    i = v.tensor_tensor(out=LL[:, h, :], in0=bk[bl][:, q * 128:(q + 1) * 128], in1=mL[d][:, 0:128], op=ALU.mult)
                return i

            gq = hs2[0] // 4
            add("dve", f_ev, rd=[b_cst], wr=[b_bk[ba], b_bk[bk_], b_bk[bl], B["AM"][gq], B["KM"][gq], B["LL"][gq]])
        add("act", lambda a: a.activation(out=Gc[:], in_=Ei[:, :, ec:ec + 1], func=AF.Copy), rd=[B["E"]], wr=[B["Gc"]])
        add("act", lambda a: a.activation(out=sgb[:], in_=SH[:, 26, :], func=AF.Sigmoid), rd=[bS_["lg"]], wr=[B["sg"]])

    def stageBC(si, d, ci, p):
        S = cfg.seqs[si]
        base = cfg.off[si]
        nch = S // 128
        t0 = base + ci * 128
        first = (ci == 0) if d == 0 else (ci == nch - 1)
        final = (d == 1)
        B = Bp[p]
        AR, Atok, Bgtok, Kgtok, Vtok, AM, KM, LL, sgb, bon, Gc = ARs[p], Atoks[p], Bgtoks[p], Kgtoks[p], Vtoks[p], AMs[p], KMs[p], LLs[p], sgbs[p], bons[p], Gcs[p]
        PL = [LL, PL12[0], PL12[1]]
        R, Kx, Vx = SH[:, 0:8, :], SH[:, 8:16, :], SH[:, 16:24, :]
        dsl = slice(d * 64, (d + 1) * 64)
        bc = lambda i: par[:, i, :].unsqueeze(2).to_broadcast([128, 8, 128])
        ec = 127 if d == 0 else 0
        z1, z2 = nb(), nb()

        def f_z(pe):
            for h in range(16):
                b_ = z1 if h < 8 else z2
                i = pe.matmul(bk[b_][:, (h % 8) * 64:(h % 8 + 1) * 64], lhsT=KM[:, h, 0:128], rhs=Vtok[:, h * 64:(h + 1) * 64], start=True, stop=True)
            return i

        add("pe", f_z, rd=B["KM"] + [B["Vtok"]], wr=[b_bk[z1], b_bk[z2]])
        add("act", lambda a: a.activation(out=Zb[:, 0:512], in_=bk[z1][:, :], func=AF.Copy), wr=[b_bk[z1], B["Zb"]])
        add("act", lambda a: a.activation(out=Zb[:, 512:1024], in_=bk[z2][:, :], func=AF.Copy), wr=[b_bk[z2], B["Zb"]])
        for gq in range(4):
            add("dve", lambda g, gq=gq: g.tensor_tensor(out=X[0][:, gq * 4:gq * 4 + 4, :], in0=AM[:, gq * 4:gq * 4 + 4, 0:128], in1=identb[:].unsqueeze(1).to_broadcast([128, 4, 128]), op=ALU.add), rd=[B["AM"][gq], b_identb], wr=[B["X0"][gq]])
        Lcur, Acur, Xc = 0, 0, 0
        Lb = ["LL", "PL0", "PL1"]
        Ab = ["AM", "PA0", "PA1"]
        PAv = [AM[:, :, 0:128], PA[0][:], PA[1][:]]
        for lvl in range(6):
            Ln = 1 + (lvl % 2)
            An = 1 + (lvl % 2)
            Xn = 1 - Xc
            for g4 in range(4):
                hsl = slice(g4 * 4, g4 * 4 + 4)
                b1, b2 = nb(), nb()

                def f_sq(pe, g4=g4, b1=b1, b2=b2, Lc=Lcur, Ac=Acur, lvl=lvl):
                    for q in range(4):
                        h = g4 * 4 + q
                        i = pe.matmul(bk[b1][:, q * 128:(q + 1) * 128], lhsT=PAv[Ac][:, h, :], rhs=PL[Lc][:, h, :], start=True, stop=True)
                    if lvl < 5:
                        for q in range(4):
                            h = g4 * 4 + q
                            i = pe.matmul(bk[b2][:, q * 128:(q + 1) * 128], lhsT=PL[Lc][:, h, :], rhs=PAv[Ac][:, h, :], start=True, stop=True)
                    return i

                add("pe", f_sq, rd=[B[Lb[Lcur]][g4], B[Ab[Acur]][g4]], wr=[b_bk[b1], b_bk[b2]])
                add("act", lambda a, b1=b1, hsl=hsl, Ln=Ln: a.activation(out=PL[Ln][:, hsl, :], in_=bk[b1][:, :].rearrange("p (h t) -> p h t", h=4), func=AF.Copy), wr=[b_bk[b1], B[Lb[Ln]][g4]])
                if lvl < 5:
                    if g4 < 2:
                        add("act", lambda a, b2=b2, hsl=hsl, An=An: a.activation(out=PAv[An][:, hsl, :], in_=bk[b2][:, :].rearrange("p (h t) -> p h t", h=4), func=AF.Copy), wr=[b_bk[b2], B[Ab[An]][g4]])
                    else:
                        add("dve", lambda v, b2=b2, hsl=hsl, An=An: v.tensor_copy(out=PAv[An][:, hsl, :], in_=bk[b2][:, :].rearrange("p (h t) -> p h t", h=4)), wr=[b_bk[b2], B[Ab[An]][g4]])

            for g4 in range(4):
                hsl = slice(g4 * 4, g4 * 4 + 4)
                b3 = nb()

                def f_x(pe, g4=g4, b3=b3, Ln=Ln, Xc=Xc):
                    for q in range(4):
                        h = g4 * 4 + q
                        i = pe.matmul(bk[b3][:, q * 128:(q + 1) * 128], lhsT=PL[Ln][:, h, :], rhs=X[Xc][:, h, :], start=True, stop=True)
                    return i

                add("pe", f_x, rd=[B[Lb[Ln]][g4], B["X%d" % Xc][g4], b_identb], wr=[b_bk[b3]])
                add("dve", lambda v, b3=b3, hsl=hsl, Xc=Xc, Xn=Xn: v.tensor_tensor(out=X[Xn][:, hsl, :], in0=bk[b3][:, :].rearrange("p (h t) -> p h t", h=4), in1=X[Xc][:, hsl, :], op=ALU.add), rd=[B["X%d" % Xc][g4]], wr=[b_bk[b3], B["X%d" % Xn][g4]])
            Lcur, Acur, Xc = Ln, An, Xn
        XT = X[Xc]
        bX = B["X%d" % Xc]
        for pg in range(4):
            b_ = nb()

            def f_ah(pe, pg=pg, b_=b_):
                for q in range(2):
                    hp = pg * 2 + q
                    i = pe.matmul(bk[b_][:, q * 256:(q + 1) * 256], lhsT=Atok[:, hp * 128:(hp + 1) * 128], rhs=XT[:].rearrange("p h t -> p (h t)")[:, hp * 256:(hp + 1) * 256], start=True, stop=True)
                return i

            add("pe", f_ah, rd=[B["Atok"], bX[pg]], wr=[b_bk[b_]])

            def f_ahe(v, pg=pg, b_=b_):
                v4 = bk[b_][:, :].rearrange("p (q s t) -> p q s t", q=2, s=2)
                v.tensor_copy(out=AhT[0:64, pg * 2:pg * 2 + 2, :], in_=v4[0:64, :, 0, :])
                return v.tensor_copy(out=AhT[64:128, pg * 2:pg * 2 + 2, :], in_=v4[64:128, :, 1, :])

            add("dve", f_ahe, wr=[b_bk[b_], B["AhT"]])
        q1, q2 = nb(), nb()

        def f_w(pe):
            for h in range(16):
                b_ = q1 if h < 8 else q2
                i = pe.matmul(bk[b_][:, (h % 8) * 64:(h % 8 + 1) * 64], lhsT=XT[:, h, :], rhs=Zb[:, h * 64:(h + 1) * 64], start=True, stop=True)
            return i

        add("pe", f_w, rd=bX + [B["Zb"]], wr=[b_bk[q1], b_bk[q2]])
        add("act", lambda a: a.activation(out=What[:, 0:512], in_=bk[q1][:, :], func=AF.Copy), wr=[b_bk[q1], B["What"]])
        add("act", lambda a: a.activation(out=What[:, 512:1024], in_=bk[q2][:, :], func=AF.Copy), wr=[b_bk[q2], B["What"]])
        if first:
            add("pool", lambda g: (g.memset(Sf[:], 0.0), g.memset(Sbf[:], 0.0))[1], wr=[B["S"], B["Sbf"]])
        ue, uo = nb(), nb()

        def f_u(pe):
            for h in range(16):
                hp, p_ = h // 2, h % 2
                i = pe.matmul(bk[(ue, uo)[p_]][:, hp * 64:(hp + 1) * 64], lhsT=AhT[p_ * 64:(p_ + 1) * 64, hp, :], rhs=Sbf[p_ * 64:(p_ + 1) * 64, hp, :], start=True, stop=True)
            return i

        add("pe", f_u, rd=[B["AhT"], B["Sbf"]], wr=[b_bk[ue], b_bk[uo]])
        Ub4 = Ub[:, :].rearrange("p (hp s v) -> p s hp v", s=2, v=64)
        Wh4 = What[:, :].rearrange("p (hp s v) -> p s hp v", s=2, v=64)
        add("dve", lambda v: (v.tensor_tensor(out=Ub4[:, 0], in0=bk[ue][:, :].rearrange("p (hp v) -> p hp v", v=64), in1=Wh4[:, 0], op=ALU.add),
                              v.tensor_tensor(out=Ub4[:, 1], in0=bk[uo][:, :].rearrange("p (hp v) -> p hp v", v=64), in1=Wh4[:, 1], op=ALU.add))[1], rd=[B["What"]], wr=[b_bk[ue], b_bk[uo], B["Ub"]])
        ye, yo_ = nb(), nb()

        def f_y(pe):
            for h in range(16):
                hp, p_ = h // 2, h % 2
                o = bk[(ye, yo_)[p_]][:, hp * 64:(hp + 1) * 64]
                pe.matmul(o, lhsT=AR[p_ * 64:(p_ + 1) * 64, hp, 128:256], rhs=Sbf[p_ * 64:(p_ + 1) * 64, hp, :], start=True, stop=False)
                pe.matmul(o, lhsT=AM[:, h, 128:256], rhs=Ub[:, h * 64:(h + 1) * 64], start=False, stop=False)
                i = pe.matmul(o, lhsT=KM[:, h, 128:256], rhs=Vtok[:, h * 64:(h + 1) * 64], start=False, stop=True)
            return i

        add("pe", f_y, rd=[B["AR"], B["Sbf"], B["Ub"], B["Vtok"]] + B["AM"] + B["KM"], wr=[b_bk[ye], b_bk[yo_]])
        Y4 = Y[:, :].rearrange("p (hp s v) -> p s hp v", s=2, v=64)
        if final:
            ph.dma(Yp[:], T["ytmp"][t0:t0 + 128, :], rd=[dB("y", t0)], wr=[B["Yp"]])
            Yp4 = Yp[:, :].rearrange("p (hp s v) -> p s hp v", s=2, v=64)
            add("dve", lambda v: (v.tensor_tensor(out=Y4[:, 0], in0=bk[ye][:, :].rearrange("p (hp v) -> p hp v", v=64), in1=Yp4[:, 0], op=ALU.add),
                                  v.tensor_tensor(out=Y4[:, 1], in0=bk[yo_][:, :].rearrange("p (hp v) -> p hp v", v=64), in1=Yp4[:, 1], op=ALU.add))[1], rd=[B["Yp"]], wr=[b_bk[ye], b_bk[yo_], B["Y"]])
        else:
            add("act", lambda a: (a.activation(out=Y4[:, 0], in_=bk[ye][:, :].rearrange("p (hp v) -> p hp v", v=64), func=AF.Copy),
                                  a.activation(out=Y4[:, 1], in_=bk[yo_][:, :].rearrange("p (hp v) -> p hp v", v=64), func=AF.Copy))[1], wr=[b_bk[ye], b_bk[yo_], B["Y"]])
            ph.dma(T["ytmp"][t0:t0 + 128, :], Y[:], rd=[B["Y"]], wr=[dB("y", t0)])
            ph.dma(T["bon"][:, t0:t0 + 128].rearrange("(c p) t -> p c t", p=128), bon[:], rd=[B["bon"]], wr=[dB("b", t0)])
        s1, s2 = nb(), nb()

        def f_s(pe):
            for hp in range(8):
                o = bk[s1 if hp < 4 else s2][:, (hp % 4) * 128:(hp % 4 + 1) * 128]
                pe.matmul(o, lhsT=Bgtok[:, hp * 128:(hp + 1) * 128], rhs=Ub[:, hp * 128:(hp + 1) * 128], start=True, stop=False)
                i = pe.matmul(o, lhsT=Kgtok[:, hp * 128:(hp + 1) * 128], rhs=Vtok[:, hp * 128:(hp + 1) * 128], start=False, stop=True)
            return i

        add("pe", f_s, rd=[B["Bgtok"], B["Ub"], B["Kgtok"], B["Vtok"]], wr=[b_bk[s1], b_bk[s2]])
        add("dve", lambda g: g.tensor_tensor(out=Stmp[:], in0=Sf[:], in1=Gc[:].to_broadcast([128, 8, 64]), op=ALU.mult), rd=[B["Gc"]], wr=[B["S"], B["Stmp"]])

        def f_su(v):
            for q, b_ in ((0, s1), (1, s2)):
                v4 = bk[b_][:, :].rearrange("p (hp s v) -> p hp s v", hp=4, s=2)
                v.tensor_tensor(out=Sf[0:64, q * 4:q * 4 + 4, :], in0=v4[0:64, :, 0, :], in1=Stmp[0:64, q * 4:q * 4 + 4, :], op=ALU.add)
                i = v.tensor_tensor(out=Sf[64:128, q * 4:q * 4 + 4, :], in0=v4[64:128, :, 1, :], in1=Stmp[64:128, q * 4:q * 4 + 4, :], op=ALU.add)
            return i

        add("dve", f_su, rd=[B["Stmp"]], wr=[b_bk[s1], b_bk[s2], B["S"]])
        add("act", lambda a: a.activation(out=Sbf[:], in_=Sf[:], func=AF.Copy), rd=[B["S"]], wr=[B["Sbf"]])
        if not final:
            return
        ph.dma(bonp[:], T["bon"][:, t0:t0 + 128].rearrange("(c p) t -> p c t", p=128), rd=[dB("b", t0)], wr=[B["bonp"]])
        Y3 = Y[:, :].rearrange("p (h v) -> p h v", v=64)

        def f_gn(v):
            v.reduce_sum(out=st[:, :, 0], in_=Y3, axis=AX.X)
            v.tensor_tensor(out=yn, in0=Y3, in1=Y3, op=ALU.mult)
            v.reduce_sum(out=st[:, :, 1], in_=yn, axis=AX.X)
            v.tensor_scalar(out=st[:, :, 0], in0=st[:, :, 0], scalar1=1.0 / 64, scalar2=None, op0=ALU.mult)
            v.tensor_tensor(out=st[:, :, 2], in0=st[:, :, 0], in1=st[:, :, 0], op=ALU.mult)
            return v.scalar_tensor_tensor(out=st[:, :, 1], in0=st[:, :, 1], scalar=1.0 / 64, in1=st[:, :, 2], op0=ALU.mult, op1=ALU.subtract)

        add("dve", f_gn, rd=[B["Y"]], wr=[B["st"], B["yn"]])
        add("act", lambda a: (a.activation(out=st[:, :, 3], in_=st[:, :, 1], func=AF.Ln, bias=eps[:, 0:1]),
                              a.activation(out=st[:, :, 3], in_=st[:, :, 3], func=AF.Exp, scale=-0.5))[1], rd=[b_cst], wr=[B["st"]])

        def f_gn2(v):
            v.tensor_tensor(out=yn, in0=Y3, in1=st[:, :, 0:1].to_broadcast([128, 16, 64]), op=ALU.subtract)
            return v.tensor_tensor(out=yn, in0=yn, in1=st[:, :, 3:4].to_broadcast([128, 16, 64]), op=ALU.mult)

        add("dve", f_gn2, rd=[B["Y"]], wr=[B["st"], B["yn"]])
        f1, f2 = nb(), nb()

        def f_trf(pe):
            for c in range(8):
                i = pe.transpose(out=bk[f1 if c < 4 else f2][:, (c % 4) * 128:(c % 4 + 1) * 128], in_=yn2[:, c * 128:(c + 1) * 128], identity=identf[:])
            return i

        add("pe", f_trf, rd=[B["yn"], b_cst], wr=[b_bk[f1], b_bk[f2]])
        for c in range(8):
            b_ = f1 if c < 4 else f2
            add("act", lambda a, c=c, b_=b_: a.activation(out=fin[:, c, :], in_=bk[b_][:, (c % 4) * 128:(c % 4 + 1) * 128], func=AF.Identity, scale=par[:, 4, c:c + 1], bias=par[:, 5, c:c + 1]), rd=[b_par], wr=[b_bk[b_], B["fin"]])
        g1, g2 = nb(), nb()

        def f_g(pe):
            for c in range(8):
                i = pe.matmul(bk[g1 if c < 4 else g2][:, (c % 4) * 128:(c % 4 + 1) * 128], lhsT=gup[:, c * 128:(c + 1) * 128], rhs=sgb[:], start=True, stop=True)
            return i

        add("pe", f_g, rd=[B["sg"], b_w], wr=[b_bk[g1], b_bk[g2]])

        def f_fin(g):
            g.tensor_tensor(out=fin, in0=fin, in1=bon[:], op=ALU.add)
            return g.tensor_tensor(out=fin, in0=fin, in1=bonp[:], op=ALU.add)

        add("dve", f_fin, rd=[B["bon"], B["bonp"]], wr=[B["fin"]])
        add("dve", lambda v: (v.tensor_tensor(out=yo[:, 0:4, :], in0=bk[g1][:, :].rearrange("p (c t) -> p c t", c=4), in1=fin[:, 0:4, :], op=ALU.mult),
                              v.tensor_tensor(out=yo[:, 4:8, :], in0=bk[g2][:, :].rearrange("p (c t) -> p c t", c=4), in1=fin[:, 4:8, :], op=ALU.mult))[1], rd=[B["fin"]], wr=[b_bk[g1], b_bk[g2], B["yo"]])
        ph.dma(T["yaT"][:, t0:t0 + 128].rearrange("(c p) t -> p c t", p=128), yo, rd=[B["yo"]])

    def capture(fn, *args):
        ph.cap = []
        cnt["pool"] = ((0, 1, 2) if fn is stageA else (3, 4, 5)) if SPLITBANKS else None
        fn(*args)
        ops_, ph.cap = ph.cap, None
        return ops_

    ph.sched = True
    steps = []
    for si, S in enumerate(cfg.seqs):
        nch = S // 128
        steps += [(si, 0, ci) for ci in range(nch)]
        steps += [(si, 1, ci) for ci in reversed(range(nch))]
    curA = capture(stageA, *steps[0], 0)
    for op_ in curA:
        ph._add_cap(*op_)
    for i, stp in enumerate(steps):
        bc_ops = capture(stageBC, *stp, i % 2)
        a_ops = capture(stageA, *steps[i + 1], (i + 1) % 2) if i + 1 < len(steps) else []
        def groups(ops_):
            gs = []
            for o_ in ops_:
                if o_[0] == "pe" or not gs:
                    gs.append([])
                gs[-1].append(o_)
            return gs

        ga, gb = groups(a_ops), groups(bc_ops)
        na, nbc = len(a_ops), len(bc_ops)
        ia = ib = 0
        ja = jb = 0
        while ja < len(ga) or jb < len(gb):
            if ja < len(ga) and jb < len(gb) and not NOINTER:
                ta = ph.peek_ready(ga[ja][0][0], ga[ja][0][2], ga[ja][0][3])
                tb_ = ph.peek_ready(gb[jb][0][0], gb[jb][0][2], gb[jb][0][3])
                if abs(ta - tb_) < 300.0:
                    pick_a = ia * nbc <= ib * na
                else:
                    pick_a = ta < tb_
                if DBG_PICK is not None:
                    DBG_PICK.append(("A" if pick_a else "B", round(ta), round(tb_)))
            else:
                pick_a = jb >= len(gb) or (ja < len(ga) and NOINTER)
            if pick_a:
                for o_ in ga[ja]:
                    ph._add_cap(*o_)
                ia += len(ga[ja])
                ja += 1
            else:
                for o_ in gb[jb]:
                    ph._add_cap(*o_)
                ib += len(gb[jb])
                jb += 1
    return ph.run()

def build(cfg, debug=False, phases="1234"):
    nc = bass.Bass("TRN2", target_bir_lowering=False)
    NT = cfg.ntok
    T = {}

    def inp(name, shape):
        T[name] = nc.dram_tensor(name, list(shape), F32, kind="ExternalInput").ap()

    inp("x", [NT, D])
    inp("w_in", [D, NIN])
    for nm, n in (("norm_mix", D), ("norm_mlp", D), ("norm_final", D), ("mu_shift", NRW), ("w0", 2048), ("a0", 2048), ("k_k", D), ("k_a", D),
                  ("r_k", D), ("ln_x_w", D), ("ln_x_b", D), ("lambda_q1", 64), ("lambda_k1", 64), ("lambda_q2", 64), ("lambda_k2", 64), ("subln_w", 128)):
        inp(nm, [n])
    for nm in ("w_lora_up", "a_lora_up", "g_lora_up"):
        inp(nm, [128, 1024])
    for nm in ("proj_a", "proj_b", "w_out"):
        inp(nm, [D, D])
    inp("w_mlp_in", [D, DFF])
    inp("w_mlp_out", [DFF, D])
    inp("cst", [128, 8])
    inp("pm", [128, 128])
    kind = "ExternalOutput" if debug else "Internal"
    for nm, shp, dt in (("pr", [NRW, NT], F32), ("prs", [NRW, NT], F32), ("qk", [2048, NT], BF16), ("gt", [2048, NT], BF16), ("vv", [NT, 1024], BF16),
                        ("ytmp", [NT, D], F32), ("bon", [D, NT], F32), ("yaT", [D, NT], BF16), ("ybT", [D, NT], BF16),
                        ("hs", [NT, D], F32), ("hnT", [D, NT], BF16)):
        T[nm] = nc.dram_tensor(nm, shp, dt, kind=kind).ap()
    T["y"] = nc.dram_tensor("y", [NT, D], F32, kind="ExternalOutput").ap()
    with ExitStack() as gst, nc.semaphore("bar") as bar_sem:
        GST[0] = ({e: gst.enter_context(nc.semaphore("s_" + e)) for e in ENGS}, [gst.enter_context(nc.semaphore(f"d{i}")) for i in range(NDS)])
        bar = (bar_sem, 0)
        if "1" in phases:
            bar = phase1(nc, cfg, T, bar)
        if "2" in phases or "s" in phases:
            bar = phase15(nc, cfg, T, bar)
        if "2" in phases:
            bar = phase2(nc, cfg, T, bar)
        if "3" in phases:
            bar = phase3(nc, cfg, T, bar)
        if "4" in phases or "a" in phases:
            bar = phase4a(nc, cfg, T, bar)
        if "4" in phases or "b" in phases:
            bar = phase4b(nc, cfg, T, bar)
    return nc


def core_inputs(inputs, xs):
    cst, pm = host_consts()
    f = lambda k: np.ascontiguousarray(np.asarray(inputs[k], np.float32))
    m = {"x": np.ascontiguousarray(np.concatenate(xs, axis=0)), "w_in": f("w_in")[0], "cst": cst, "pm": pm}
    for nm in ("norm_mix", "norm_mlp", "mu_shift", "w0", "a0", "k_k", "k_a", "r_k", "ln_x_w", "ln_x_b", "lambda_q1", "lambda_k1", "lambda_q2", "lambda_k2", "subln_w"):
        m[nm] = f(nm)[0].reshape(-1)
    m["norm_final"] = f("norm_final").reshape(-1)
    m["w_lora_up"] = f("w_lora_up")[0].reshape(128, 1024)
    m["a_lora_up"] = f("a_lora_up")[0].reshape(128, 1024)
    m["g_lora_up"] = f("g_lora_up")[0].reshape(128, 1024)
    for nm in ("proj_a", "proj_b", "w_out", "w_mlp_in", "w_mlp_out"):
        m[nm] = f(nm)[0]
    return m


_NC_CACHE = {}


def kernel(**inputs):
    xp = np.asarray(inputs["x_prompt"], np.float32)
    xs = np.asarray(inputs["x_sample"], np.float32)
    n = 8
    cfg = Cfg([xp.shape[1], xp.shape[1], xs.shape[1]])
    key = tuple(cfg.seqs)
    if key not in _NC_CACHE:
        _NC_CACHE[key] = build(cfg)
    nc = _NC_CACHE[key]
    in_maps = [core_inputs(inputs, [xp[2 * c], xp[2 * c + 1], xs[c]]) for c in range(n)]
    res = run_bass_kernel_spmd(nc, in_maps, core_ids=list(range(n)))
    yp = np.empty_like(xp)
    ys = np.empty_like(xs)
    S1 = xp.shape[1]
    for c in range(n):
        y = res.results[c]["y"]
        yp[2 * c] = y[0:S1]
        yp[2 * c + 1] = y[S1:2 * S1]
        ys[c] = y[2 * S1:]
    return (yp, ys)


def host_consts():
    p = np.arange(128)
    cst = np.zeros((128, 8), np.float32)
    cst[:, 0] = p % 8
    m = ((p % 64) < 16).astype(np.float32)
    cst[:, 1] = m
    cst[:, 2] = np.where((p % 16) >= 8, 1.0, -1.0) * m
    cst[:, 3] = -math.pi
    cst[:, 4] = 1e-6
    pm = np.zeros((128, 128), np.float32)
    for mm in range(128):
        if (mm % 64) < 16:
            k = mm + 8 if (mm % 16) < 8 else mm - 8
            pm[k, mm] = 1.0
    return cst, pm
```

```python
import math
from contextlib import ExitStack
import numpy as np
import concourse.bass as bass
import concourse.mybir as mybir
from concourse.bass_utils import run_bass_kernel_spmd

F32 = mybir.dt.float32
BF16 = mybir.dt.bfloat16
I32 = mybir.dt.int32
AF = mybir.ActivationFunctionType
ALU = mybir.AluOpType
AX = mybir.AxisListType

D = 1024
NRW = 3456
NIN = 8576
DFF = 4096
ENGS = ("pe", "act", "dve", "pool", "sp")
NDS = 24


GST = [None]
NOINTER = False
SPLITBANKS = False
DBG_PICK = None
LAT = 180.0


class Buf:
    __slots__ = ("w", "r", "name")

    def __init__(self, name=""):
        self.w = None
        self.r = []
        self.name = name


class Op:
    __slots__ = ("fn", "deps", "dma", "sig", "val", "sem")

    def __init__(self, fn, deps, dma):
        self.fn = fn
        self.deps = deps
        self.dma = dma
        self.sig = False
        self.val = 0
        self.sem = None


class _Rec:
    def __init__(self):
        self.calls = []

    def __getattr__(self, name):
        def f(*a, **k):
            self.calls.append((name, a, k))
            return None

        return f


class Phase:
    def __init__(self, nc, name, bar):
        self.nc = nc
        self.name = name
        self.bar = bar
        self.ops = {e: [] for e in ENGS}
        self.stack = ExitStack()
        self.ndma = 0
        self.dma_ids = []
        self.nps = 0
        self.sched = False
        self.est_end = {}
        self.eng_free = {}

    def sb(self, name, shape, dt):
        return self.stack.enter_context(self.nc.sbuf_tensor(self.name + "_" + name, list(shape), dt))

    def ps(self, name, shape, dt=F32):
        return self.stack.enter_context(self.nc.psum_tensor(self.name + "_" + name, list(shape), dt))

    cap = None

    def _add_cap(self, eng, fn, rd, wr, dma):
        return self.add(eng, fn, rd, wr, dma)

    def add(self, eng, fn, rd=(), wr=(), dma=False):
        if self.cap is not None:
            self.cap.append((eng, fn, tuple(rd), tuple(wr), dma))
            return None
        if eng != "pe" and not dma:
            rec = _Rec()
            fn(rec)
            me = None
            for name, a, k in rec.calls:
                me = self._add1(eng, (lambda e, name=name, a=a, k=k: getattr(e, name)(*a, **k)), rd, wr, False)
            return me
        return self._add1(eng, fn, rd, wr, dma)

    def est_dur(self, eng, fn, dma):
        rec = _Rec()
        try:
            fn(rec)
        except Exception:
            return 500.0
        tot = 0.0
        for name, a, k in rec.calls:
            o = k.get("out", a[0] if a else None)
            try:
                n = 1
                for d_ in o.shape[1:]:
                    n *= d_
            except Exception:
                n = 128
            if dma:
                tot += 2000.0 + n * 0.5
            elif eng == "pe":
                tot += (100.0 if name == "transpose" else max(64, n) / 2.4 + 8.0)
            elif eng == "act":
                tot += 230.0 + 0.83 * n
            elif eng == "dve":
                tot += 130.0 + 0.95 * n
            else:
                tot += 250.0 + 6.5 * n
        return tot

    def peek_ready(self, eng, rd, wr):
        t = self.eng_free.get(eng, 0.0)
        for b in list(rd) + list(wr):
            if b.w is not None:
                t = max(t, self.est_end.get(b.w, 0.0) + (0.0 if b.w[0] == eng == "pe" else LAT))
        for b in wr:
            for r_ in b.r:
                t = max(t, self.est_end.get(r_, 0.0) + LAT)
        return t

    def _add1(self, eng, fn, rd=(), wr=(), dma=False):
        ops = self.ops[eng]
        me = (eng, len(ops))
        if self.sched:
            st_ = self.peek_ready(eng, rd, wr)
            en_ = st_ + self.est_dur(eng, fn, dma)
            self.est_end[me] = en_
            if not dma:
                self.eng_free[eng] = en_
            else:
                self.eng_free[eng] = st_ + 60.0
        deps = set()
        for b in rd:
            if b.w is not None:
                deps.add(b.w)
        for b in wr:
            if b.w is not None:
                deps.add(b.w)
            deps.update(b.r)
        deps.discard(me)
        op = Op(fn, deps, dma)
        if dma:
            k = self.ndma
            self.ndma += 1
            op.sem = k % NDS
            op.val = 16 * (k // NDS + 1)
            if k >= NDS:
                deps.add(self.dma_ids[k - NDS])
            self.dma_ids.append(me)
        ops.append(op)
        for b in rd:
            b.r.append(me)
        for b in wr:
            b.w = me
            b.r = []
        return me

    def dma(self, out, in_, rd=(), wr=(), q="sp", **kw):
        return self.add(q, lambda e: e.dma_start(out=out, in_=in_, **kw), rd, wr, dma=True)

    def run(self):
        nc = self.nc
        ops = self.ops
        bar_sem, bar_val = self.bar
        fin = set(self.dma_ids)
        for e in ENGS:
            if e != "sp" and ops[e]:
                fin.add((e, len(ops[e]) - 1))
        sems, dsems = GST[0]

        def f_bar(e):
            for sm in list(sems.values()) + list(dsems):
                e.sem_clear(sm)
            return e.sem_inc(bar_sem, 1)

        ops["sp"].append(Op(f_bar, fin, False))
        for e in ENGS:
            for op in ops[e]:
                for d in op.deps:
                    ops[d[0]][d[1]].sig = True
        if True:
            for e in ENGS:
                c = 0
                for op in ops[e]:
                    if op.dma:
                        op.sem = dsems[op.sem]
                    elif op.sig:
                        c += 1
                        op.val = c
                        op.sem = sems[e]

            def emit(ename, eng):
                known = {}
                if bar_val > 0:
                    eng.wait_ge(bar_sem, bar_val)
                for op in ops[ename]:
                    need = {}
                    for d in op.deps:
                        dop = ops[d[0]][d[1]]
                        if ename == "pe" and d[0] == "pe" and not dop.dma:
                            continue
                        key = id(dop.sem)
                        if known.get(key, 0) < dop.val and need.get(key, (None, 0))[1] < dop.val:
                            need[key] = (dop.sem, dop.val)
                    for key, (sem, val) in need.items():
                        eng.wait_ge(sem, val)
                        known[key] = val
                    inst = op.fn(eng)
                    if op.dma:
                        inst.then_inc(op.sem, 16)
                    elif op.sig:
                        inst.then_inc(op.sem, 1)

            with nc.Block() as block:
                @block.sync
                def _(e):
                    emit("sp", e)

                @block.scalar
                def _(e):
                    emit("act", e)

                @block.vector
                def _(e):
                    emit("dve", e)

                @block.gpsimd
                def _(e):
                    emit("pool", e)

                @block.tensor
                def _(e):
                    emit("pe", e)
        self.stack.close()
        return (bar_sem, bar_val + 1)


def make_ident(ph, dt=BF16, n=128, name="ident"):
    f = ph.sb(name + "_f", [128, n], F32)
    o = ph.sb(name, [128, n], dt)
    b = Buf(name)

    def fn(g):
        g.memset(f[:], 0.0)
        g.affine_select(out=f[:], in_=f[:], compare_op=ALU.not_equal, fill=1.0, base=0, pattern=[[-1, n]], channel_multiplier=1)
        return g.tensor_copy(out=o[:], in_=f[:])

    ph.add("pool", fn, wr=[b])
    return o, b


class Cfg:
    def __init__(self, seqs):
        self.seqs = list(seqs)
        self.off = [0]
        for s in self.seqs:
            self.off.append(self.off[-1] + s)
        self.ntok = self.off[-1]
        self.smax = max(self.seqs)


def phase1(nc, cfg, T, bar):
    ph = Phase(nc, "p1", bar)
    NT = cfg.ntok
    NB = NT // 512
    ident, b_ident = make_ident(ph)
    xnT = ph.sb("xnT", [128, 8, NT], BF16)
    b_xnT = [Buf() for _ in range(NT // 128)]
    gcol = ph.sb("gcol", [128, 8], F32)
    b_g = Buf()
    ph.dma(gcol[:], T["norm_mix"].rearrange("(kc p) -> p kc", p=128), wr=[b_g], allow_slow_non_contiguous=True)
    banks = [ph.ps(f"bk{i}", [128, 512], F32) for i in range(6)]
    b_bank = [Buf() for _ in range(6)]
    pst = [ph.ps(f"pt{i}", [128, 1024], BF16) for i in range(2)]
    b_pst = [Buf(), Buf()]

    SM = cfg.smax
    Ct = ph.sb("Ct", [128, SM], F32)
    St = ph.sb("St", [128, SM], F32)
    b_rope = Buf()
    Pm = ph.sb("Pm", [128, 128], BF16)
    SEG = 512
    with_tmp = ph.sb("rtmp", [128, SEG], F32)
    posf = ph.sb("posf", [128, SEG], F32)
    cols = ph.sb("rcols", [128, 8], F32)
    pmf = ph.sb("pmf", [128, 128], F32)
    posi = ph.sb("posi", [128, SEG], I32)
    ki = ph.sb("rki", [128, SEG], I32)
    b_r0 = Buf()
    b_seg = Buf()
    ph.dma(cols[:], T["cst"][:, :], wr=[b_r0])
    ph.dma(pmf[:], T["pm"][:, :], wr=[b_r0])
    ph.add("pool", lambda g: g.tensor_copy(out=Pm[:], in_=pmf[:]), rd=[b_r0], wr=[b_r0])
    ph.add("act", lambda a: a.activation(out=cols[:, 7:8], in_=cols[:, 0:1], func=AF.Exp, scale=-math.log(500000.0) / 8.0), rd=[b_r0], wr=[b_r0])
    TWO_PI = 2.0 * math.pi
    for sg in range(SM // SEG):
        ss_ = slice(sg * SEG, (sg + 1) * SEG)

        def f_pos(g, sg=sg):
            g.iota(posi[:], pattern=[[1, SEG]], base=sg * SEG, channel_multiplier=0)
            return g.tensor_copy(out=posf[:], in_=posi[:])

        ph.add("pool", f_pos, wr=[b_seg])

        def reduce_fn(v, dst, shift):
            v.tensor_scalar(out=dst, in0=posf[:], scalar1=cols[:, 7:8], scalar2=shift, op0=ALU.mult, op1=ALU.add)
            v.tensor_scalar(out=ki[:], in0=dst, scalar1=1.0 / TWO_PI, scalar2=None, op0=ALU.mult)
            v.tensor_copy(out=with_tmp[:], in_=ki[:])
            v.scalar_tensor_tensor(out=dst, in0=with_tmp[:], scalar=-TWO_PI, in1=dst, op0=ALU.mult, op1=ALU.add)
            v.tensor_scalar(out=with_tmp[:], in0=dst, scalar1=math.pi, scalar2=TWO_PI, op0=ALU.is_gt, op1=ALU.mult)
            v.tensor_tensor(out=dst, in0=dst, in1=with_tmp[:], op=ALU.subtract)
            v.tensor_scalar(out=with_tmp[:], in0=dst, scalar1=-math.pi, scalar2=TWO_PI, op0=ALU.is_lt, op1=ALU.mult)
            return v.tensor_tensor(out=dst, in0=dst, in1=with_tmp[:], op=ALU.add)

        def rope_fn2(v, ss_=ss_):
            reduce_fn(v, St[:, ss_], 0.0)
            return reduce_fn(v, Ct[:, ss_], 0.5 * math.pi)

        ph.add("dve", rope_fn2, rd=[b_r0], wr=[b_seg, b_rope])

        def rope_fn3(a, ss_=ss_):
            a.activation(out=St[:, ss_], in_=St[:, ss_], func=AF.Sin)
            return a.activation(out=Ct[:, ss_], in_=Ct[:, ss_], func=AF.Sin)

        ph.add("act", rope_fn3, rd=[], wr=[b_rope])

        def rope_fn4(v, ss_=ss_):
            v.tensor_scalar(out=St[:, ss_], in0=St[:, ss_], scalar1=cols[:, 2:3], scalar2=None, op0=ALU.mult)
            v.tensor_scalar(out=Ct[:, ss_], in0=Ct[:, ss_], scalar1=-1.0, scalar2=None, op0=ALU.add)
            return v.tensor_scalar(out=Ct[:, ss_], in0=Ct[:, ss_], scalar1=cols[:, 1:2], scalar2=1.0, op0=ALU.mult, op1=ALU.add)

        ph.add("dve", rope_fn4, rd=[b_r0], wr=[b_rope])

    xt = [ph.sb(f"xt{i}", [128, D], F32) for i in range(2)]
    b_xt = [Buf(), Buf()]
    xnb = [ph.sb(f"xnb{i}", [128, D], BF16) for i in range(2)]
    b_xnb = [Buf(), Buf()]
    ss = ph.sb("ss", [128, 2, 2], F32)
    b_ss = [Buf(), Buf()]
    xin = T["x"]
    for t in range(NT // 128):
        s = t % 2
        ph.dma(xt[s][:], xin[t * 128:(t + 1) * 128, :], wr=[b_xt[s]])
        ph.add("act", lambda a, s=s: a.activation(out=xnb[s][:], in_=xt[s][:], func=AF.Square, accum_out=ss[:, s, 0:1]), rd=[b_xt[s]], wr=[b_xnb[s], b_ss[s]])

        ph.add("act", lambda a, s=s: a.activation(out=ss[:, s, 1:2], in_=ss[:, s, 0:1], func=AF.Sqrt, scale=1.0 / D, bias=cols[:, 4:5]), rd=[b_r0], wr=[b_ss[s]])
        ph.add("dve", lambda v, s=s: v.reciprocal(out=ss[:, s, 1:2], in_=ss[:, s, 1:2]), rd=[], wr=[b_ss[s]])
        ph.add("act", lambda a, s=s: a.activation(out=xnb[s][:], in_=xt[s][:], func=AF.Copy, scale=ss[:, s, 1:2]), rd=[b_xt[s], b_ss[s]], wr=[b_xnb[s]])

        def f_tr(pe, s=s):
            for kc in range(8):
                i = pe.transpose(out=pst[s][:, kc * 128:(kc + 1) * 128], in_=xnb[s][:, kc * 128:(kc + 1) * 128], identity=ident[:])
            return i

        ph.add("pe", f_tr, rd=[b_xnb[s], b_ident], wr=[b_pst[s]])
        ph.add("dve", lambda v, s=s, t=t: v.tensor_copy(out=xnT[:, :, t * 128:(t + 1) * 128], in_=pst[s][:, :].rearrange("p (k t) -> p k t", k=8)), rd=[], wr=[b_pst[s], b_xnT[t]])

    wf = [ph.sb(f"wf{i}", [128, 8, 128], F32) for i in range(2)]
    b_wf = [Buf(), Buf()]
    wb = [ph.sb(f"wb{i}", [128, 8, 128], BF16) for i in range(2)]
    b_wb = [Buf(), Buf()]
    stg = [ph.sb(f"stg{i}", [128, 512], F32) for i in range(3)]
    b_stg = [Buf() for _ in range(3)]
    stgb = [ph.sb(f"stgb{i}", [128, 512], BF16) for i in range(3)]
    b_stgb = [Buf() for _ in range(3)]
    qraw = [ph.sb("qraw0", [128, 512], BF16)] * 2
    b_qraw = [Buf()] * 2
    t1 = [ph.sb("t1_0", [128, 512], F32)] * 2
    b_t1 = [Buf()] * 2
    t2 = [ph.sb("t2_0", [128, 512], F32)] * 2
    b_t2 = [Buf()] * 2
    w_in = T["w_in"]
    chunks = []
    for c in range(27):
        chunks.append(("rw", c * 128, c * 128))
    for c in range(16):
        chunks.append(("qk", NRW + c * 128, c * 128))
    for c in range(16):
        chunks.append(("gt", NRW + 3072 + c * 128, c * 128))
    for c in range(8):
        chunks.append(("vv", NRW + 2048 + c * 128, c * 128))
    cnt = dict(bank=0, stg=0, stgb=0, q=0)
    blk_pos = []
    for si, S in enumerate(cfg.seqs):
        for j in range(S // 512):
            blk_pos.append(j * 512)
    for ci, (kind, c0, r0) in enumerate(chunks):
        s = ci % 2
        ph.dma(wf[s][:], w_in[:, c0:c0 + 128].rearrange("(kc p) c -> p kc c", p=128), wr=[b_wf[s]])
        ph.add("pool", lambda g, s=s: g.tensor_tensor(out=wb[s][:], in0=wf[s][:], in1=gcol[:].unsqueeze(2).to_broadcast([128, 8, 128]), op=ALU.mult), rd=[b_wf[s], b_g], wr=[b_wb[s]])
        for b in range(NB):
            bk = cnt["bank"] % 4
            cnt["bank"] += 1

            def f_mm(pe, s=s, b=b, bk=bk, kind=kind):
                if kind == "vv":
                    for j in range(4):
                        for kc in range(8):
                            i = pe.matmul(banks[bk][:, j * 128:(j + 1) * 128], lhsT=xnT[:, kc, b * 512 + j * 128:b * 512 + (j + 1) * 128], rhs=wb[s][:, kc, :], start=(kc == 0), stop=(kc == 7))
                    return i
                for kc in range(8):
                    i = pe.matmul(banks[bk][:, :], lhsT=wb[s][:, kc, :], rhs=xnT[:, kc, b * 512:(b + 1) * 512], start=(kc == 0), stop=(kc == 7))
                return i

            ph.add("pe", f_mm, rd=[b_wb[s]] + b_xnT[b * 4:(b + 1) * 4], wr=[b_bank[bk]])
            tok = slice(b * 512, (b + 1) * 512)
            if kind == "rw":
                g_ = cnt["stg"] % 3
                cnt["stg"] += 1
                ph.add("act", lambda a, bk=bk, g_=g_: a.activation(out=stg[g_][:], in_=banks[bk][:, :], func=AF.Copy), wr=[b_bank[bk], b_stg[g_]])
                ph.dma(T["pr"][r0:r0 + 128, tok], stg[g_][:], rd=[b_stg[g_]])
            elif kind == "vv":
                g_ = cnt["stgb"] % 3
                cnt["stgb"] += 1
                ph.add("act", lambda a, bk=bk, g_=g_: a.activation(out=stgb[g_][:], in_=banks[bk][:, :], func=AF.Copy), wr=[b_bank[bk], b_stgb[g_]])
                ph.dma(T["vv"][tok, r0:r0 + 128].rearrange("(j p) c -> p j c", p=128), stgb[g_][:].rearrange("p (j c) -> p j c", j=4), rd=[b_stgb[g_]])
            elif kind == "gt":
                g_ = cnt["stgb"] % 3
                cnt["stgb"] += 1
                ph.add("act", lambda a, bk=bk, g_=g_: a.activation(out=stgb[g_][:], in_=banks[bk][:, :], func=AF.Sigmoid), wr=[b_bank[bk], b_stgb[g_]])
                ph.dma(T["gt"][r0:r0 + 128, tok], stgb[g_][:], rd=[b_stgb[g_]])
            else:
                q_ = cnt["q"] % 2
                cnt["q"] += 1
                g_ = cnt["stgb"] % 3
                cnt["stgb"] += 1
                p0 = blk_pos[b]
                ph.add("act", lambda a, bk=bk, q_=q_: a.activation(out=qraw[q_][:], in_=banks[bk][:, :], func=AF.Copy), wr=[b_bank[bk], b_qraw[q_]])
                ph.add("dve", lambda v, bk=bk, q_=q_, p0=p0: v.tensor_tensor(out=t1[q_][:], in0=banks[bk][:, :], in1=Ct[:, p0:p0 + 512], op=ALU.mult), rd=[b_rope], wr=[b_bank[bk], b_t1[q_]])
                pb = 4 + (cnt["q"] % 2)
                ph.add("pe", lambda pe, q_=q_, pb=pb: pe.matmul(banks[pb][:, :], lhsT=Pm[:], rhs=qraw[q_][:], start=True, stop=True), rd=[b_qraw[q_], b_r0], wr=[b_bank[pb]])
                ph.add("dve", lambda v, pb=pb, q_=q_, p0=p0: v.tensor_tensor(out=t2[q_][:], in0=banks[pb][:, :], in1=St[:, p0:p0 + 512], op=ALU.mult), rd=[b_rope], wr=[b_bank[pb], b_t2[q_]])
                ph.add("dve", lambda g, q_=q_, g_=g_: g.tensor_tensor(out=stgb[g_][:], in0=t1[q_][:], in1=t2[q_][:], op=ALU.add), rd=[b_t1[q_], b_t2[q_]], wr=[b_stgb[g_]])
                ph.dma(T["qk"][r0:r0 + 128, tok], stgb[g_][:], rd=[b_stgb[g_]])

    return ph.run()


def bcast_rows(ap_1d, n):
    return ap_1d.partition_broadcast(128)


def phase3(nc, cfg, T, bar):
    ph = Phase(nc, "p3", bar)
    SM = cfg.smax
    lam_init = 0.8 - 0.6 * math.exp(-0.3 * 0)
    lv = ph.sb("lv", [128, 4, 64], F32)
    b_lv = Buf()
    for i, nm in enumerate(("lambda_q1", "lambda_k1", "lambda_q2", "lambda_k2")):
        ph.dma(lv[:, i, :], T[nm].partition_broadcast(128), wr=[b_lv])
    cc = ph.sb("cc", [128, 8], F32)
    b_cc = Buf()
    ph.dma(cc[:, 3:4], T["subln_w"].rearrange("(p o) -> p o", o=1), wr=[b_cc])
    lt = ph.sb("ltmp", [128, 2, 64], F32)
    ones = ph.sb("ones", [128, 128], BF16)

    def f_c(v):
        v.memset(ones[:], 1.0)
        v.memset(cc[:, 4:5], 1e-5)
        v.tensor_tensor(out=lt[:, 0, :], in0=lv[:, 0, :], in1=lv[:, 1, :], op=ALU.mult)
        v.tensor_tensor(out=lt[:, 1, :], in0=lv[:, 2, :], in1=lv[:, 3, :], op=ALU.mult)
        v.reduce_sum(out=cc[:, 0:2], in_=lt[:], axis=AX.X)
        return v.tensor_scalar(out=cc[:, 3:4], in0=cc[:, 3:4], scalar1=1.0 - lam_init, scalar2=None, op0=ALU.mult)

    ph.add("dve", f_c, rd=[b_lv], wr=[b_cc])
    ph.add("act", lambda a: a.activation(out=cc[:, 0:2], in_=cc[:, 0:2], func=AF.Exp), wr=[b_cc])

    def f_c2(v):
        v.tensor_tensor(out=cc[:, 2:3], in0=cc[:, 1:2], in1=cc[:, 0:1], op=ALU.subtract)
        return v.tensor_scalar(out=cc[:, 2:3], in0=cc[:, 2:3], scalar1=-lam_init, scalar2=None, op0=ALU.add)

    ph.add("dve", f_c2, wr=[b_cc])

    qT = [ph.sb(f"qT{i}", [128, SM], BF16) for i in range(2)]
    kT = [ph.sb(f"kT{i}", [128, SM], BF16) for i in range(2)]
    Vt = [ph.sb(f"Vt{i}", [128, SM // 128, 128], BF16) for i in range(2)]
    b_in = [Buf(), Buf()]
    bS = [ph.ps(f"bS{i}", [128, 512], F32) for i in range(4)]
    b_bS = [Buf() for _ in range(4)]
    bO = [ph.ps(f"bO{i}", [128, 512], F32) for i in range(4)]
    b_bO = [Buf() for _ in range(4)]
    Pt = [ph.sb(f"Pt{i}", [128, 512], BF16) for i in range(4)]
    b_Pt = [Buf() for _ in range(4)]
    acc = ph.sb("acc", [128, 512], F32)
    accb = ph.sb("accb", [128, 512], BF16)
    b_acc = Buf()
    rz = [ph.sb(f"rz{i}", [128, 512], F32) for i in range(2)]
    oo = [ph.sb(f"oo{i}", [128, 512], F32) for i in range(2)]
    sq = ph.sb("sq", [128, 512], BF16)
    rs = ph.sb("rs", [128, 512], F32)
    yb = [ph.sb(f"yb{i}", [128, 512], BF16) for i in range(2)]
    b_ep = Buf()
    b_yb = [Buf(), Buf()]
    blocks = []
    it = 0
    for si, S in enumerate(cfg.seqs):
        for h in range(8):
            u = it % 2
            it += 1
            for qb in range(S // 512):
                for kt in range(S // 128):
                    blocks.append((si, h, u, qb, kt))
    state = dict(ne=0)

    def emit_load(si, h, u):
        S = cfg.seqs[si]
        t0 = cfg.off[si]
        ph.dma(qT[u][:, 0:S], T["qk"][h * 128:(h + 1) * 128, t0:t0 + S], wr=[b_in[u]])
        ph.dma(kT[u][:, 0:S], T["qk"][1024 + h * 128:1024 + (h + 1) * 128, t0:t0 + S], wr=[b_in[u]])
        ph.dma(Vt[u][:, 0:S // 128, :], T["vv"][t0:t0 + S, h * 128:(h + 1) * 128].rearrange("(kt p) v -> p kt v", p=128), wr=[b_in[u]])

    def emit_scores(n):
        si, h, u, qb, kt = blocks[n]
        w = (n % 2) * 2
        ks = slice(kt * 128, (kt + 1) * 128)
        qs = slice(qb * 512, (qb + 1) * 512)

        def f_s(pe):
            pe.matmul(bS[w][:, :], lhsT=kT[u][0:64, ks], rhs=qT[u][0:64, qs], start=True, stop=True)
            return pe.matmul(bS[w + 1][:, :], lhsT=kT[u][64:128, ks], rhs=qT[u][64:128, qs], start=True, stop=True)

        ph.add("pe", f_s, rd=[b_in[u]], wr=[b_bS[w], b_bS[w + 1]])
        ph.add("act", lambda a: a.activation(out=Pt[w][:], in_=bS[w][:, :], func=AF.Exp, scale=0.125), wr=[b_bS[w], b_Pt[w]])
        ph.add("act", lambda a: a.activation(out=Pt[w + 1][:], in_=bS[w + 1][:, :], func=AF.Exp, scale=0.125), wr=[b_bS[w + 1], b_Pt[w + 1]])

    def emit_pv(n):
        si, h, u, qb, kt = blocks[n]
        S = cfg.seqs[si]
        t0 = cfg.off[si]
        nkt = S // 128
        w = (n % 2) * 2

        def f_pv(pe):
            st, sp_ = (kt == 0), (kt == nkt - 1)
            pe.matmul(bO[0][:, :], lhsT=Vt[u][:, kt, :], rhs=Pt[w][:], start=st, stop=sp_)
            pe.matmul(bO[1][:, :], lhsT=Vt[u][:, kt, :], rhs=Pt[w + 1][:], start=st, stop=sp_)
            return pe.matmul(bO[3][:, :], lhsT=ones[:], rhs=Pt[w + 1][:], start=st, stop=sp_)

        ph.add("pe", f_pv, rd=[b_in[u], b_Pt[w], b_Pt[w + 1], b_cc], wr=[b_bO[0], b_bO[1], b_bO[3]])
        if kt == 0:
            ph.add("dve", lambda v: v.tensor_copy(out=acc[:], in_=Pt[w][:]), rd=[b_Pt[w]], wr=[b_acc])
        else:
            ph.add("dve", lambda v: v.tensor_tensor(out=acc[:], in0=acc[:], in1=Pt[w][:], op=ALU.add), rd=[b_Pt[w]], wr=[b_acc])
        if kt != nkt - 1:
            return
        ph.add("dve", lambda v: v.tensor_copy(out=accb[:], in_=acc[:]), wr=[b_acc])
        ph.add("pe", lambda pe: pe.matmul(bO[2][:, :], lhsT=ones[:], rhs=accb[:], start=True, stop=True), rd=[b_acc, b_cc], wr=[b_bO[2]])
        e = state["ne"] % 2
        state["ne"] += 1

        def f_e0(a):
            a.activation(out=rz[0][:], in_=bO[2][:, :], func=AF.Ln)
            a.activation(out=rz[1][:], in_=bO[3][:, :], func=AF.Ln)
            a.activation(out=rz[0][:], in_=rz[0][:], func=AF.Exp, scale=-1.0)
            return a.activation(out=rz[1][:], in_=rz[1][:], func=AF.Exp, scale=-1.0)

        ph.add("act", f_e0, wr=[b_bO[2], b_bO[3], b_ep])

        def f_e1(v):
            v.tensor_tensor(out=oo[0][:], in0=bO[0][:, :], in1=rz[0][:], op=ALU.mult)
            v.tensor_tensor(out=oo[1][:], in0=bO[1][:, :], in1=rz[1][:], op=ALU.mult)
            return v.scalar_tensor_tensor(out=oo[0][:], in0=oo[1][:], scalar=cc[:, 2:3], in1=oo[0][:], op0=ALU.mult, op1=ALU.add)

        ph.add("dve", f_e1, rd=[b_cc], wr=b_bO + [b_ep])
        ph.add("act", lambda a: a.activation(out=sq[:], in_=oo[0][:], func=AF.Square), wr=[b_ep])
        ph.add("pe", lambda pe: pe.matmul(bS[w][:, :], lhsT=ones[:], rhs=sq[:], start=True, stop=True), rd=[b_ep], wr=[b_bS[w]])
        ph.add("act", lambda a: (a.activation(out=rs[:], in_=bS[w][:, :], func=AF.Ln, scale=1.0 / 128.0, bias=cc[:, 4:5]),
                                 a.activation(out=rs[:], in_=rs[:], func=AF.Exp, scale=-0.5))[1], rd=[b_cc], wr=[b_bS[w], b_ep])
        ph.add("dve", lambda g: g.scalar_tensor_tensor(out=yb[e][:], in0=oo[0][:], scalar=cc[:, 3:4], in1=rs[:], op0=ALU.mult, op1=ALU.mult), rd=[b_ep, b_cc], wr=[b_yb[e]])
        ph.dma(T["ybT"][h * 128:(h + 1) * 128, t0 + qb * 512:t0 + (qb + 1) * 512], yb[e][:], rd=[b_yb[e]])

    heads = []
    for bl in blocks:
        if not heads or heads[-1] != bl[0:3]:
            heads.append(bl[0:3])
    for hd in heads[0:2]:
        emit_load(*hd)
    emit_scores(0)
    hj = 0
    for n in range(len(blocks)):
        if n + 1 < len(blocks):
            emit_scores(n + 1)
        emit_pv(n)
        if n + 1 == len(blocks) or blocks[n + 1][0:3] != blocks[n][0:3]:
            if hj + 2 < len(heads):
                emit_load(*heads[hj + 2])
            hj += 1
    return ph.run()


def load_w_bf16(ph, dst, b_dst, w_ap, nk, ncols, stage, b_stage, scale_col=None, b_scale=None, eng="dve"):
    step = 512
    kst = stage.shape[1]
    for c0 in range(0, ncols, step):
        for k0 in range(0, nk, kst):
            k1 = min(nk, k0 + kst)
            ph.dma(stage[:, 0:k1 - k0, :], w_ap[k0 * 128:k1 * 128, c0:c0 + step].rearrange("(kc p) c -> p kc c", p=128), wr=[b_stage])
            if scale_col is None:
                ph.add(eng, lambda g, k0=k0, k1=k1, c0=c0: g.tensor_copy(out=dst[:, k0:k1, c0:c0 + step], in_=stage[:, 0:k1 - k0, :]), rd=[b_stage], wr=[b_dst])
            else:
                ph.add(eng, lambda g, k0=k0, k1=k1, c0=c0: g.tensor_tensor(out=dst[:, k0:k1, c0:c0 + step], in0=stage[:, 0:k1 - k0, :], in1=scale_col[:, k0:k1].unsqueeze(2).to_broadcast([128, k1 - k0, step]), op=ALU.mult), rd=[b_stage, b_scale], wr=[b_dst])


def phase4a(nc, cfg, T, bar):
    ph = Phase(nc, "p4a", bar)
    NT = cfg.ntok
    ident, b_ident = make_ident(ph)
    stage = ph.sb("stage", [128, 8, 512], F32)
    b_stage = Buf()
    W = {}
    bW = {}
    for nm in ("proj_a", "proj_b", "w_out"):
        W[nm] = ph.sb("w_" + nm, [128, 8, 1024], BF16)
        bW[nm] = Buf()
        load_w_bf16(ph, W[nm], bW[nm], T[nm], 8, 1024, stage, b_stage)
    cst = ph.sb("cst", [128, 1], F32)
    b_cst = Buf()
    ph.add("pool", lambda g: g.memset(cst[:], 1e-6), wr=[b_cst])
    banks = [ph.ps(f"bk{i}", [128, 512], F32) for i in range(6)]
    b_bank = [Buf() for _ in range(6)]
    pst = ph.ps("pt", [128, 1024], BF16)
    b_pst = Buf()
    yas = [ph.sb(f"ya{i}", [128, 8, 512], BF16) for i in range(2)]
    ybbs = [ph.sb(f"ybb{i}", [128, 8, 512], BF16) for i in range(2)]
    gas = [ph.sb(f"ga{i}", [128, 8, 512], BF16) for i in range(2)]
    gbs = [ph.sb(f"gb{i}", [128, 8, 512], BF16) for i in range(2)]
    b_lds = [Buf(), Buf()]
    mg = ph.sb("mg", [128, 8, 512], BF16)
    b_mg = Buf()
    m1 = [ph.sb(f"m1_{i}", [128, 512], F32) for i in range(2)]
    m2 = [ph.sb(f"m2_{i}", [128, 512], F32) for i in range(2)]
    b_m = [Buf(), Buf()]
    xt = [ph.sb(f"xt{i}", [128, D], F32) for i in range(4)]
    b_xt = [Buf() for _ in range(4)]
    hh = [ph.sb(f"hh{i}", [128, D], F32) for i in range(4)]
    b_hh = [Buf() for _ in range(4)]
    hb = [ph.sb(f"hb{i}", [128, D], BF16) for i in range(4)]
    b_hb = [Buf() for _ in range(4)]
    hT = [ph.sb(f"hT{i}", [128, 8, 128], BF16) for i in range(4)]
    b_hT = [Buf() for _ in range(4)]
    junk = ph.sb("junk", [128, D], BF16)
    b_junk = Buf()
    ss = ph.sb("ss", [128, 4, 2], F32)
    b_ss = [Buf() for _ in range(4)]
    nb = 0
    nm_ = 0
    nt_ = 0
    pending = []
    for b in range(NT // 512):
        tok = slice(b * 512, (b + 1) * 512)
        ya, ybb, ga, gb, b_ld = yas[b % 2], ybbs[b % 2], gas[b % 2], gbs[b % 2], b_lds[b % 2]
        for dst, src, r0 in ((ya, "yaT", 0), (ybb, "ybT", 0), (ga, "gt", 0), (gb, "gt", 1024)):
            ph.dma(dst[:], T[src][r0:r0 + 1024, tok].rearrange("(kc p) t -> p kc t", p=128), wr=[b_ld])
        for oc in range(8):
            ba, bb = nb % 6, (nb + 1) % 6
            nb += 2

            def f_ab(pe, oc=oc, ba=ba, bb=bb, ya=ya, ybb=ybb):
                for kc in range(8):
                    pe.matmul(banks[ba][:, :], lhsT=W["proj_a"][:, kc, oc * 128:(oc + 1) * 128], rhs=ya[:, kc, :], start=(kc == 0), stop=(kc == 7))
                for kc in range(8):
                    i = pe.matmul(banks[bb][:, :], lhsT=W["proj_b"][:, kc, oc * 128:(oc + 1) * 128], rhs=ybb[:, kc, :], start=(kc == 0), stop=(kc == 7))
                return i

            ph.add("pe", f_ab, rd=[b_ld, bW["proj_a"], bW["proj_b"]], wr=[b_bank[ba], b_bank[bb]])
            m = nm_ % 2
            nm_ += 1

            def f_m(v, oc=oc, ba=ba, bb=bb, m=m, ga=ga, gb=gb):
                v.tensor_tensor(out=m1[m][:], in0=banks[ba][:, :], in1=ga[:, oc, :], op=ALU.mult)
                return v.tensor_tensor(out=m2[m][:], in0=banks[bb][:, :], in1=gb[:, oc, :], op=ALU.mult)

            ph.add("dve", f_m, rd=[b_ld], wr=[b_bank[ba], b_bank[bb], b_m[m]])
            if oc % 2 == 1 and pending:
                for o_ in pending.pop(0):
                    ph._add_cap(*o_)
            ph.add("dve", lambda g, oc=oc, m=m: g.tensor_tensor(out=mg[:, oc, :], in0=m1[m][:], in1=m2[m][:], op=ALU.add), rd=[b_m[m]], wr=[b_mg])
        for ti in range(4):
            t = b * 4 + ti
            s_ = nt_ % 4
            nt_ += 1
            ph.dma(xt[s_][:], T["x"][t * 128:(t + 1) * 128, :], wr=[b_xt[s_]])
            ba, bb = nb % 6, (nb + 1) % 6
            nb += 2

            def f_o(pe, ti=ti, ba=ba, bb=bb):
                for hf, bk in ((0, ba), (1, bb)):
                    for kc in range(8):
                        i = pe.matmul(banks[bk][:, :], lhsT=mg[:, kc, ti * 128:(ti + 1) * 128], rhs=W["w_out"][:, kc, hf * 512:(hf + 1) * 512], start=(kc == 0), stop=(kc == 7))
                return i

            ph.add("pe", f_o, rd=[b_mg, bW["w_out"]], wr=[b_bank[ba], b_bank[bb]])

            def f_h(v, s_=s_, ba=ba, bb=bb):
                v.tensor_tensor(out=hh[s_][:, 0:512], in0=banks[ba][:, :], in1=xt[s_][:, 0:512], op=ALU.add)
                return v.tensor_tensor(out=hh[s_][:, 512:1024], in0=banks[bb][:, :], in1=xt[s_][:, 512:1024], op=ALU.add)

            ph.add("dve", f_h, rd=[b_xt[s_]], wr=[b_bank[ba], b_bank[bb], b_hh[s_]])
            ph.dma(T["hs"][t * 128:(t + 1) * 128, :], hh[s_][:], rd=[b_hh[s_]])
            ph.cap = []
            ph.add("act", lambda a, s_=s_: a.activation(out=junk[:], in_=hh[s_][:], func=AF.Square, accum_out=ss[:, s_, 0:1]), rd=[b_hh[s_]], wr=[b_junk, b_ss[s_]])
            ph.add("act", lambda a, s_=s_: a.activation(out=ss[:, s_, 1:2], in_=ss[:, s_, 0:1], func=AF.Sqrt, scale=1.0 / D, bias=cst[:, 0:1]), rd=[b_cst], wr=[b_ss[s_]])
            ph.add("dve", lambda v, s_=s_: v.reciprocal(out=ss[:, s_, 1:2], in_=ss[:, s_, 1:2]), wr=[b_ss[s_]])
            ph.add("act", lambda a, s_=s_: a.activation(out=hb[s_][:], in_=hh[s_][:], func=AF.Copy, scale=ss[:, s_, 1:2]), rd=[b_hh[s_], b_ss[s_]], wr=[b_hb[s_]])

            def f_tr(pe, s_=s_):
                for kc in range(8):
                    i = pe.transpose(out=pst[:, kc * 128:(kc + 1) * 128], in_=hb[s_][:, kc * 128:(kc + 1) * 128], identity=ident[:])
                return i

            ph.add("pe", f_tr, rd=[b_hb[s_], b_ident], wr=[b_pst])
            ph.add("dve", lambda v, s_=s_: v.tensor_copy(out=hT[s_][:], in_=pst[:, :].rearrange("p (k t) -> p k t", k=8)), wr=[b_pst, b_hT[s_]])
            ph.dma(T["hnT"][:, t * 128:(t + 1) * 128].rearrange("(kc p) t -> p kc t", p=128), hT[s_][:], rd=[b_hT[s_]])
            pending.append(ph.cap)
            ph.cap = None
    while pending:
        for o_ in pending.pop(0):
            ph._add_cap(*o_)
    return ph.run()


def phase4b(nc, cfg, T, bar):
    ph = Phase(nc, "p4b", bar)
    NT = cfg.ntok
    stage = ph.sb("stage", [128, 4, 512], F32)
    b_stage = Buf()
    gcol = ph.sb("gcol", [128, 8], F32)
    b_g = Buf()
    ph.dma(gcol[:], T["norm_mlp"].rearrange("(kc p) -> p kc", p=128), wr=[b_g], allow_slow_non_contiguous=True)
    gfin = ph.sb("gfin", [128, D], F32)
    b_gf = Buf()
    ph.dma(gfin[:], T["norm_final"].partition_broadcast(128), wr=[b_gf])
    w1 = ph.sb("w1", [128, 8, DFF], BF16)
    b_w1 = Buf()
    w2 = ph.sb("w2", [128, 32, D], BF16)
    b_w2 = Buf()
    load_w_bf16(ph, w1, b_w1, T["w_mlp_in"], 8, DFF, stage, b_stage, scale_col=gcol, b_scale=b_g)
    load_w_bf16(ph, w2, b_w2, T["w_mlp_out"], 32, D, stage, b_stage)
    cst = ph.sb("cst", [128, 1], F32)
    b_cst = Buf()
    ph.add("pool", lambda g: g.memset(cst[:], 1e-6), wr=[b_cst])
    banks = [ph.ps(f"bk{i}", [128, 512], F32) for i in range(8)]
    b_bank = [Buf() for _ in range(8)]
    hn = [ph.sb("hn0", [128, 8, 512], BF16)] * 2
    b_hn = [Buf()] * 2
    hid = ph.sb("hid", [128, 32, 512], BF16)
    b_hid = [Buf() for _ in range(32)]
    rl = [ph.sb(f"rl{i}", [128, 512], F32) for i in range(2)]
    b_rl = [Buf(), Buf()]
    ht = [ph.sb(f"ht{i}", [128, D], F32) for i in range(2)]
    b_ht = [Buf(), Buf()]
    oo = [ph.sb(f"oo{i}", [128, D], F32) for i in range(2)]
    b_oo = [Buf(), Buf()]
    junk = ph.sb("junk", [128, D], BF16)
    b_junk = Buf()
    ss = ph.sb("ss", [128, 2, 2], F32)
    b_ss = [Buf(), Buf()]
    nb = 0
    nr = 0
    nt_ = 0
    for b in range(NT // 512):
        u = b % 2
        tok = slice(b * 512, (b + 1) * 512)
        ph.dma(hn[u][:], T["hnT"][:, tok].rearrange("(kc p) t -> p kc t", p=128), wr=[b_hn[u]])
        for fc in range(32):
            bk = nb % 8
            nb += 1

            def f_h(pe, fc=fc, bk=bk, u=u):
                for kc in range(8):
                    i = pe.matmul(banks[bk][:, :], lhsT=w1[:, kc, fc * 128:(fc + 1) * 128], rhs=hn[u][:, kc, :], start=(kc == 0), stop=(kc == 7))
                return i

            ph.add("pe", f_h, rd=[b_w1, b_hn[u]], wr=[b_bank[bk]])
            r_ = nr % 2
            nr += 1
            ph.add("act", lambda a, bk=bk, r_=r_: a.activation(out=rl[r_][:], in_=banks[bk][:, :], func=AF.Relu), wr=[b_bank[bk], b_rl[r_]])
            ph.add("dve", lambda g, fc=fc, r_=r_: g.tensor_tensor(out=hid[:, fc, :], in0=rl[r_][:], in1=rl[r_][:], op=ALU.mult), rd=[b_rl[r_]], wr=[b_hid[fc]])
        for ti in range(4):
            t = b * 4 + ti
            s_ = nt_ % 2
            nt_ += 1
            ph.dma(ht[s_][:], T["hs"][t * 128:(t + 1) * 128, :], wr=[b_ht[s_]])
            ba, bb = nb % 8, (nb + 1) % 8
            nb += 2

            def f_o(pe, ti=ti, ba=ba, bb=bb):
                for hf, bk in ((0, ba), (1, bb)):
                    for fc in range(32):
                        i = pe.matmul(banks[bk][:, :], lhsT=hid[:, fc, ti * 128:(ti + 1) * 128], rhs=w2[:, fc, hf * 512:(hf + 1) * 512], start=(fc == 0), stop=(fc == 31))
                return i

            ph.add("pe", f_o, rd=[b_w2] + b_hid, wr=[b_bank[ba], b_bank[bb]])

            def f_r(v, s_=s_, ba=ba, bb=bb):
                v.tensor_tensor(out=oo[s_][:, 0:512], in0=banks[ba][:, :], in1=ht[s_][:, 0:512], op=ALU.add)
                return v.tensor_tensor(out=oo[s_][:, 512:1024], in0=banks[bb][:, :], in1=ht[s_][:, 512:1024], op=ALU.add)

            ph.add("dve", f_r, rd=[b_ht[s_]], wr=[b_bank[ba], b_bank[bb], b_oo[s_]])
            ph.add("act", lambda a, s_=s_: a.activation(out=junk[:], in_=oo[s_][:], func=AF.Square, accum_out=ss[:, s_, 0:1]), rd=[b_oo[s_]], wr=[b_junk, b_ss[s_]])
            ph.add("act", lambda a, s_=s_: a.activation(out=ss[:, s_, 1:2], in_=ss[:, s_, 0:1], func=AF.Sqrt, scale=1.0 / D, bias=cst[:, 0:1]), rd=[b_cst], wr=[b_ss[s_]])
            ph.add("dve", lambda v, s_=s_: v.reciprocal(out=ss[:, s_, 1:2], in_=ss[:, s_, 1:2]), wr=[b_ss[s_]])
            ph.add("dve", lambda g, s_=s_: g.scalar_tensor_tensor(out=oo[s_][:], in0=oo[s_][:], scalar=ss[:, s_, 1:2], in1=gfin[:], op0=ALU.mult, op1=ALU.mult), rd=[b_ss[s_], b_gf], wr=[b_oo[s_]])
            ph.dma(T["y"][t * 128:(t + 1) * 128, :], oo[s_][:], rd=[b_oo[s_]])
    return ph.run()


def phase15(nc, cfg, T, bar):
    ph = Phase(nc, "p15", bar)
    TB = 256
    mu = ph.sb("mu", [128, 27], F32)
    omu = ph.sb("omu", [128, 27], F32)
    hmu = ph.sb("hmu", [128, 27], F32)
    b_par = Buf()
    ph.dma(mu[:], T["mu_shift"].rearrange("(c p) -> p c", p=128), wr=[b_par], allow_slow_non_contiguous=True)
    ph.add("dve", lambda v: v.tensor_scalar(out=omu[:], in0=mu[:], scalar1=-1.0, scalar2=1.0, op0=ALU.mult, op1=ALU.add), wr=[b_par])
    ph.add("dve", lambda v: v.tensor_scalar(out=hmu[:], in0=mu[:], scalar1=0.5, scalar2=None, op0=ALU.mult), wr=[b_par])
    P = [ph.sb(f"P{i}", [128, 27, TB + 2], F32) for i in range(2)]
    b_P = [Buf(), Buf()]
    t1 = [ph.sb(f"t1_{i}", [128, 27, TB], F32) for i in range(2)]
    b_t1 = [Buf(), Buf()]
    t2 = [ph.sb(f"t2_{i}", [128, 27, TB], F32) for i in range(2)]
    b_t2 = [Buf(), Buf()]
    n = 0
    for si, S in enumerate(cfg.seqs):
        base = cfg.off[si]
        nb_ = S // TB
        for bi in range(nb_):
            u = n % 2
            n += 1
            t0 = base + bi * TB
            lo = 1 if bi == 0 else 0
            hi = TB + 1 if bi == nb_ - 1 else TB + 2
            if lo == 1:
                ph.add("pool", lambda g, u=u: g.memset(P[u][:, :, 0:1], 0.0), wr=[b_P[u]])
            if hi == TB + 1:
                ph.add("pool", lambda g, u=u: g.memset(P[u][:, :, TB + 1:TB + 2], 0.0), wr=[b_P[u]])
            for c0_, c1_ in ((0, 9), (9, 18), (18, 27)):
                ph.dma(P[u][:, c0_:c1_, lo:hi], T["pr"][c0_ * 128:c1_ * 128, t0 - 1 + lo:t0 - 1 + hi].rearrange("(c p) t -> p c t", p=128), wr=[b_P[u]])
            for c in range(27):
                ph.add("act", lambda a, u=u, c=c: a.activation(out=t1[u][:, c, :], in_=P[u][:, c, 1:TB + 1], func=AF.Copy, scale=omu[:, c:c + 1]), rd=[b_P[u], b_par], wr=[b_t1[u]])

            def f_sh(v, u=u):
                v.tensor_tensor(out=t2[u][:], in0=P[u][:, :, 0:TB], in1=P[u][:, :, 2:TB + 2], op=ALU.add)
                v.tensor_tensor(out=t2[u][:], in0=t2[u][:], in1=hmu[:].unsqueeze(2).to_broadcast([128, 27, TB]), op=ALU.mult)
                return v.tensor_tensor(out=t2[u][:], in0=t2[u][:], in1=t1[u][:], op=ALU.add)

            ph.add("dve", f_sh, rd=[b_P[u], b_par, b_t1[u]], wr=[b_t2[u]])
            for c0_, c1_ in ((0, 9), (9, 18), (18, 27)):
                ph.dma(T["prs"][c0_ * 128:c1_ * 128, t0:t0 + TB].rearrange("(c p) t -> p c t", p=128), t2[u][:, c0_:c1_, :], rd=[b_t2[u]])
    return ph.run()


def phase2(nc, cfg, T, bar):
    ph = Phase(nc, "p2", bar)
    NT = cfg.ntok
    C0 = -math.exp(-0.5)
    sb, add = ph.sb, ph.add
    identb, b_identb = make_ident(ph, BF16, name="idb")
    Y = sb("Y", [128, 1024], F32)
    Yp = sb("Yp", [128, 1024], F32)
    wst = Yp
    MK = Y[:, 0:512].rearrange("p (m c) -> p m c", m=4)
    TRt = sb("TRt", [128, 4, 128], BF16)
    TR = TRt
    identf = ph.sb("idf", [128, 128], F32)
    BD = sb("BD", [128, 128], BF16)
    onesr = sb("onesr", [1, 128], BF16)
    mA = [sb(f"mA{d}", [128, 256], F32) for d in range(2)]
    mL = [sb(f"mL{d}", [128, 256], F32) for d in range(2)]
    trc = [sb(f"trc{d}", [128, 384], BF16) for d in range(2)]
    b_cst = Buf()

    def f_masks(g):
        g.memset(MK, 1.0)
        g.affine_select(out=MK[:, 0, :], in_=MK[:, 0, :], compare_op=ALU.is_gt, fill=0.0, base=0, pattern=[[1, 128]], channel_multiplier=-1)
        g.affine_select(out=MK[:, 1, :], in_=MK[:, 1, :], compare_op=ALU.is_ge, fill=0.0, base=0, pattern=[[1, 128]], channel_multiplier=-1)
        g.affine_select(out=MK[:, 2, :], in_=MK[:, 2, :], compare_op=ALU.is_gt, fill=0.0, base=0, pattern=[[-1, 128]], channel_multiplier=1)
        g.affine_select(out=MK[:, 3, :], in_=MK[:, 3, :], compare_op=ALU.is_ge, fill=0.0, base=0, pattern=[[-1, 128]], channel_multiplier=1)
        g.tensor_scalar(out=TR[:], in0=MK, scalar1=C0, scalar2=None, op0=ALU.mult)
        g.memset(identf[:], 0.0)
        g.affine_select(out=identf[:], in_=identf[:], compare_op=ALU.not_equal, fill=1.0, base=0, pattern=[[-1, 128]], channel_multiplier=1)
        g.memset(BD[:], 0.0)
        g.memset(BD[0:64, 0:64], 1.0)
        g.memset(BD[64:128, 64:128], 1.0)
        g.memset(onesr[:], 1.0)
        for d in range(2):
            st, inc, lm = (0, 1, 2) if d == 0 else (2, 3, 0)
            g.tensor_copy(out=mA[d][:, 0:128], in_=MK[:, st, :])
            g.tensor_copy(out=mA[d][:, 128:256], in_=MK[:, inc, :])
            for q in range(2):
                g.tensor_copy(out=mL[d][:, q * 128:(q + 1) * 128], in_=MK[:, lm, :])
            g.tensor_copy(out=trc[d][:, 0:128], in_=TR[:, inc, :])
            g.tensor_copy(out=trc[d][:, 128:256], in_=TR[:, st, :])
            i = g.tensor_copy(out=trc[d][:, 256:384], in_=TR[:, lm, :])
        return i

    add("pool", f_masks, wr=[b_cst])
    par = sb("par", [128, 8, 8], F32)
    b_par = Buf()
    for i, ap_ in enumerate((T["k_k"], T["k_a"], T["k_a"], T["r_k"], T["ln_x_w"], T["ln_x_b"], T["a0"][0:1024], T["a0"][1024:2048])):
        ph.dma(par[:, i, :], ap_.rearrange("(c p) -> p c", p=128), wr=[b_par], allow_slow_non_contiguous=True)
    add("pool", lambda g: g.tensor_scalar(out=par[:, 2, :], in0=par[:, 2, :], scalar1=-1.0, scalar2=1.0, op0=ALU.mult, op1=ALU.add), wr=[b_par])
    b_wst = Buf()
    wup = sb("wup", [128, 1024], BF16)
    aup = sb("aup", [128, 1024], BF16)
    gup = sb("gup", [128, 1024], BF16)
    b_w = Buf()
    for dst, src in ((wup, T["w_lora_up"]), (aup, T["a_lora_up"]), (gup, T["g_lora_up"])):
        ph.dma(wst[:], src[:, :], wr=[b_wst])
        add("pool", lambda g, dst=dst: g.tensor_copy(out=dst[:], in_=wst[:]), rd=[b_wst], wr=[b_w])
    w0h = sb("w0h", [1, 2, 1024], BF16)
    w0l = sb("w0l", [1, 2, 1024], BF16)
    w0t = Y
    for d_ in range(2):
        ph.dma(wst[0:1, :], T["w0"][d_ * 1024:(d_ + 1) * 1024].rearrange("(o c) -> o c", o=1), wr=[b_wst])

        def f_w0(g, d_=d_):
            g.tensor_copy(out=w0h[0:1, d_, :], in_=wst[0:1, :])
            g.tensor_copy(out=w0t[0:1, :], in_=w0h[0:1, d_, :])
            g.tensor_tensor(out=w0t[0:1, :], in0=wst[0:1, :], in1=w0t[0:1, :], op=ALU.subtract)
            return g.tensor_copy(out=w0l[0:1, d_, :], in_=w0t[0:1, :])

        add("pool", f_w0, rd=[], wr=[b_wst, b_w, b_cst])
    eps = sb("eps", [128, 2], F32)
    add("pool", lambda g: (g.memset(eps[:, 0:1], 64e-5), g.memset(eps[:, 1:2], 1e-24))[1], wr=[b_cst])

    bk = [ph.ps(f"bk{i}", [128, 512], F32) for i in range(6)]
    b_bk = [Buf() for _ in range(6)]
    bt = [ph.ps(f"bt{i}", [128, 1024], BF16) for i in range(2)]
    b_bt = [Buf(), Buf()]
    cnt = dict(b=0, t=0)

    def nb():
        pool_ = cnt.get("pool")
        if pool_ is None:
            i = cnt["b"] % 6
            cnt["b"] += 1
            return i
        k_ = "b" + str(pool_[0])
        i = pool_[cnt.get(k_, 0) % len(pool_)]
        cnt[k_] = cnt.get(k_, 0) + 1
        return i

    def ntb():
        i = cnt["t"] % 2
        cnt["t"] += 1
        return i

    SH = sb("SH", [128, 27, 128], F32)
    bS_ = {k_: Buf("SH" + k_) for k_ in ("r", "k", "v", "lw", "la", "lg")}
    f4 = lambda nm: sb(nm, [128, 8, 128], F32)
    h4 = lambda nm: sb(nm, [128, 8, 128], BF16)
    av, kk, kd, Ei, tmp1 = [f4(n) for n in ("av", "kk", "kd", "Ei", "tmp1")]
    bb_ = av
    En, Ee, Es = [h4(n) for n in ("En", "Ee", "Es")]
    tmp2 = tmp1
    kkr = tmp1
    rn = kd
    names = ("av", "kk", "bb", "kd", "E", "tmp1", "tl", "lab", "sig", "sq", "AR", "bt", "kt", "Bg", "Kg", "vb",
             "Atok", "Bgtok", "Kgtok", "Vtok", "AM", "KM", "LL", "PL0", "PL1", "PA0", "PA1", "X0", "X1", "Zb", "AhT", "What", "Ub", "S", "Sbf",
             "Y", "Yp", "sg", "bon", "bonp", "st", "yo", "Stmp", "Gc")
    dbl = ("AR", "Atok", "Bgtok", "Kgtok", "Vtok", "AM", "KM", "LL", "sg", "bon", "Gc")
    B0 = {n: Buf(n) for n in names}
    Bp = [dict(B0), dict(B0)]
    for n_ in dbl:
        Bp[1][n_] = Buf(n_ + "1")
    grp = ("AM", "KM", "LL", "PL0", "PL1", "PA0", "PA1", "X0", "X1")
    for n_ in grp:
        l0 = [Buf(n_ + str(g_)) for g_ in range(4)]
        Bp[0][n_] = l0
        Bp[1][n_] = [Buf(n_ + "b" + str(g_)) for g_ in range(4)] if n_ in dbl else l0
    for Bx in Bp:
        Bx["bb"] = Bx["av"]
        Bx["kkr"] = Bx["tmp1"]
        Bx["tmp2"] = Bx["tmp1"]
        Bx["rn"] = Bx["kd"]
        Bx["fin"] = Bx["What"]
        Bx["yn"] = Bx["Yp"]
    tl = sb("tl", [128, 128], BF16)
    lab = sb("lab", [128, 128], BF16)
    sig = sb("sig", [128, 1024], BF16)
    sq = h4("sq")
    btT, ktT, BgT, KgT, vb = [h4(n) for n in ("btT", "ktT", "BgT", "KgT", "vbb")]
    Zb = sb("Zb", [128, 1024], BF16)
    two = lambda nm, shp, dt: [sb(nm + "0", shp, dt), sb(nm + "1", shp, dt)]
    sgbs = two("sgb", [128, 128], BF16)
    ARs = two("AR", [128, 8, 256], BF16)
    Atoks, Bgtoks, Kgtoks, Vtoks = [two(n, [128, 1024], BF16) for n in ("Atok", "Bgtok", "Kgtok", "Vtok")]
    AMs = two("AM", [128, 16, 256], BF16)
    KMs = two("KM", [128, 16, 256], BF16)
    LLs = two("LL", [128, 16, 128], BF16)
    bons = two("bon", [128, 8, 128], F32)
    Gcs = two("Gc", [128, 8, 1], F32)
    PL12 = [sb("PL1", [128, 16, 128], BF16), sb("PL2", [128, 16, 128], BF16)]
    PA = [sb("PA1", [128, 16, 128], BF16), sb("PA2", [128, 16, 128], BF16)]
    X = [sb("X0", [128, 16, 128], BF16), sb("X1", [128, 16, 128], BF16)]
    AhT = sb("AhT", [128, 8, 128], BF16)
    What = sb("What", [128, 1024], F32)
    Ub = sb("Ub", [128, 1024], BF16)
    Sf = sb("Sf", [128, 8, 64], F32)
    Stmp = sb("Stmp", [128, 8, 64], F32)
    Sbf = sb("Sbf", [128, 8, 64], BF16)
    bonp = f4("bonp")
    st = sb("st", [128, 16, 4], F32)
    yn2 = Yp
    yn = yn2[:].rearrange("p (h v) -> p h v", v=64)
    fin = What[:].rearrange("p (c t) -> p c t", c=8)
    yo = Zb[:].rearrange("p (c t) -> p c t", c=8)
    for Bx in Bp:
        Bx["yo"] = Bx["Zb"]
    dramB = {}

    def dB(kind, t0):
        return dramB.setdefault((kind, t0), Buf())

    def stageA(si, d, ci, p):
        S = cfg.seqs[si]
        base = cfg.off[si]
        nch = S // 128
        t0 = base + ci * 128
        first = (ci == 0) if d == 0 else (ci == nch - 1)
        final = (d == 1)
        B = Bp[p]
        AR, Atok, Bgtok, Kgtok, Vtok, AM, KM, LL, sgb, bon, Gc = ARs[p], Atoks[p], Bgtoks[p], Kgtoks[p], Vtoks[p], AMs[p], KMs[p], LLs[p], sgbs[p], bons[p], Gcs[p]
        PL = [LL, PL12[0], PL12[1]]
        R, Kx, Vx = SH[:, 0:8, :], SH[:, 8:16, :], SH[:, 16:24, :]
        dsl = slice(d * 64, (d + 1) * 64)
        bc = lambda i: par[:, i, :].unsqueeze(2).to_broadcast([128, 8, 128])
        ec = 127 if d == 0 else 0
        for k_, c0_, c1_ in (("lw", 24, 25), ("la", 25, 26), ("k", 8, 16), ("r", 0, 8), ("v", 16, 24), ("lg", 26, 27)):
            ph.dma(SH[:, c0_:c1_, :], T["prs"][c0_ * 128:c1_ * 128, t0:t0 + 128].rearrange("(c p) t -> p c t", p=128), wr=[bS_[k_]])
        add("act", lambda a: a.activation(out=tl[:], in_=SH[:, 24, :], func=AF.Tanh), rd=[bS_["lw"]], wr=[B["tl"]])
        add("act", lambda a: a.activation(out=lab[:], in_=SH[:, 25, :], func=AF.Copy), rd=[bS_["la"]], wr=[B["lab"]])
        w1_, w2_ = nb(), nb()

        def f_lw(pe):
            for hf, b_ in ((0, w1_), (1, w2_)):
                cs = slice(hf * 512, (hf + 1) * 512)
                pe.matmul(bk[b_][:, :], lhsT=tl[dsl, :], rhs=wup[dsl, cs], start=True, stop=False)
                pe.matmul(bk[b_][:, :], lhsT=onesr[0:1, :], rhs=w0h[0:1, d, cs], start=False, stop=False)
                i = pe.matmul(bk[b_][:, :], lhsT=onesr[0:1, :], rhs=w0l[0:1, d, cs], start=False, stop=True)
            return i

        add("pe", f_lw, rd=[B["tl"], b_w, b_cst], wr=[b_bk[w1_], b_bk[w2_]])
        add("act", lambda a: a.activation(out=sig[:, 0:512], in_=bk[w1_][:, :], func=AF.Sigmoid), wr=[b_bk[w1_], B["sig"]])
        add("act", lambda a: a.activation(out=sig[:, 512:1024], in_=bk[w2_][:, :], func=AF.Sigmoid), wr=[b_bk[w2_], B["sig"]])
        a1_, a2_ = nb(), nb()

        def f_la(pe):
            for c in range(8):
                b_ = a1_ if c < 4 else a2_
                i = pe.matmul(bk[b_][:, (c % 4) * 128:(c % 4 + 1) * 128], lhsT=aup[dsl, c * 128:(c + 1) * 128], rhs=lab[dsl, :], start=True, stop=True)
            return i

        add("pe", f_la, rd=[B["lab"], b_w], wr=[b_bk[a1_], b_bk[a2_]])
        for c in range(8):
            b_ = a1_ if c < 4 else a2_
            add("act", lambda a, c=c, b_=b_: a.activation(out=av[:, c, :], in_=bk[b_][:, (c % 4) * 128:(c % 4 + 1) * 128], func=AF.Sigmoid, bias=par[:, 6 + d, c:c + 1]), rd=[b_par], wr=[b_bk[b_], B["av"]])
        for hf in range(2):
            cb = [nb(), nb(), nb()]

            def f_cum(pe, hf=hf, cb=cb):
                for cc_ in range(4):
                    c = hf * 4 + cc_
                    for x in range(3):
                        i = pe.matmul(bk[cb[x]][:, cc_ * 128:(cc_ + 1) * 128], lhsT=sig[:, c * 128:(c + 1) * 128], rhs=trc[d][:, x * 128:(x + 1) * 128], start=True, stop=True)
                return i

            add("pe", f_cum, rd=[B["sig"], b_cst], wr=[b_bk[i] for i in cb])
            hs_ = slice(hf * 4, hf * 4 + 4)
            v3 = lambda b_: bk[b_][:, :].rearrange("p (c t) -> p c t", c=4)
            add("act", lambda a, cb=cb, hs_=hs_: (a.activation(out=Ei[:, hs_, :], in_=v3(cb[0]), func=AF.Exp), a.activation(out=En[:, hs_, :], in_=v3(cb[0]), func=AF.Exp, scale=-1.0))[1], wr=[b_bk[cb[0]], B["E"]])
            add("act", lambda a, cb=cb, hs_=hs_: a.activation(out=Ee[:, hs_, :], in_=v3(cb[1]), func=AF.Exp), wr=[b_bk[cb[1]], B["E"]])
            add("act", lambda a, cb=cb, hs_=hs_: a.activation(out=Es[:, hs_, :], in_=v3(cb[2]), func=AF.Exp), wr=[b_bk[cb[2]], B["E"]])
        add("dve", lambda v: v.tensor_tensor(out=kkr[:], in0=Kx, in1=bc(0), op=ALU.mult), rd=[bS_["k"], b_par], wr=[B["kkr"]])
        add("act", lambda a: a.activation(out=sq[:], in_=kkr[:], func=AF.Square), rd=[B["kkr"]], wr=[B["sq"]])
        n1_, n2_ = nb(), nb()

        def f_nrm(pe):
            sqf = sq[:].rearrange("p c t -> p (c t)")
            pe.matmul(bk[n1_][:, :], lhsT=BD[:], rhs=sqf[:, 0:512], start=True, stop=True)
            return pe.matmul(bk[n2_][:, :], lhsT=BD[:], rhs=sqf[:, 512:1024], start=True, stop=True)

        add("pe", f_nrm, rd=[B["sq"], b_cst], wr=[b_bk[n1_], b_bk[n2_]])
        add("act", lambda a: a.activation(out=rn[:, 0:4, :], in_=bk[n1_][:, :].rearrange("p (c t) -> p c t", c=4), func=AF.Ln, bias=eps[:, 1:2]), rd=[b_cst], wr=[b_bk[n1_], B["rn"]])
        add("act", lambda a: a.activation(out=rn[:, 4:8, :], in_=bk[n2_][:, :].rearrange("p (c t) -> p c t", c=4), func=AF.Ln, bias=eps[:, 1:2]), rd=[b_cst], wr=[b_bk[n2_], B["rn"]])
        add("act", lambda a: a.activation(out=rn[:], in_=rn[:], func=AF.Exp, scale=-0.5), wr=[B["rn"]])
        add("dve", lambda v: v.tensor_tensor(out=kk[:], in0=kkr[:], in1=rn[:], op=ALU.mult), rd=[B["kkr"], B["rn"]], wr=[B["kk"]])

        def f_kd(g):
            g.tensor_tensor(out=tmp1[:], in0=av[:], in1=bc(1), op=ALU.mult)
            g.tensor_tensor(out=tmp1[:], in0=tmp1[:], in1=bc(2), op=ALU.add)
            g.tensor_tensor(out=kd[:], in0=tmp1[:], in1=Kx, op=ALU.mult)
            return g.tensor_tensor(out=bb_[:], in0=kk[:], in1=av[:], op=ALU.mult)

        add("dve", f_kd, rd=[B["av"], b_par, bS_["k"], B["kk"]], wr=[B["tmp1"], B["kd"], B["bb"]])

        def f_sc1(v):
            v.scalar_tensor_tensor(out=AR[:, :, 0:128], in0=kk[:], scalar=-1.0, in1=Ee[:], op0=ALU.mult, op1=ALU.mult)
            v.tensor_tensor(out=AR[:, :, 128:256], in0=R, in1=Ei[:], op=ALU.mult)
            return v.tensor_tensor(out=btT[:], in0=bb_[:], in1=En[:], op=ALU.mult)

        add("dve", f_sc1, rd=[B["kk"], B["E"], bS_["r"], B["bb"]], wr=[B["AR"], B["bt"]])

        add("dve", lambda g: g.tensor_tensor(out=ktT[:], in0=kd[:], in1=En[:], op=ALU.mult), rd=[B["kd"], B["E"]], wr=[B["kt"]])
        add("pool", lambda g: g.tensor_tensor(out=BgT[:], in0=bb_[:], in1=Es[:], op=ALU.mult), rd=[B["E"], B["bb"]], wr=[B["Bg"]])
        add("pool", lambda g: g.tensor_tensor(out=KgT[:], in0=kd[:], in1=Es[:], op=ALU.mult), rd=[B["kd"], B["E"]], wr=[B["Kg"]])
        add("act", lambda a: a.activation(out=vb[:], in_=Vx, func=AF.Copy), rd=[bS_["v"]], wr=[B["vb"]])
        add("dve", lambda g: (g.tensor_tensor(out=tmp2[:], in0=R, in1=bc(3), op=ALU.mult), g.tensor_tensor(out=sq[:], in0=tmp2[:], in1=kd[:], op=ALU.mult))[1], rd=[bS_["r"], b_par, B["kd"]], wr=[B["tmp2"], B["sq"]])
        o1_, o2_ = nb(), nb()

        def f_bon(pe):
            sqf = sq[:].rearrange("p c t -> p (c t)")
            pe.matmul(bk[o1_][:, :], lhsT=BD[:], rhs=sqf[:, 0:512], start=True, stop=True)
            return pe.matmul(bk[o2_][:, :], lhsT=BD[:], rhs=sqf[:, 512:1024], start=True, stop=True)

        add("pe", f_bon, rd=[B["sq"], b_cst], wr=[b_bk[o1_], b_bk[o2_]])
        add("dve", lambda v: v.tensor_tensor(out=bon[:, 0:4, :], in0=bk[o1_][:, :].rearrange("p (c t) -> p c t", c=4), in1=SH[:, 16:20, :], op=ALU.mult), rd=[bS_["v"]], wr=[b_bk[o1_], B["bon"]])
        add("dve", lambda v: v.tensor_tensor(out=bon[:, 4:8, :], in0=bk[o2_][:, :].rearrange("p (c t) -> p c t", c=4), in1=SH[:, 20:24, :], op=ALU.mult), rd=[bS_["v"]], wr=[b_bk[o2_], B["bon"]])
        for src, srcb, dst, dstb in ((AR, "AR", Atok, "Atok"), (BgT, "Bg", Bgtok, "Bgtok"), (KgT, "Kg", Kgtok, "Kgtok"), (vb, "vb", Vtok, "Vtok")):
            tb = ntb()

            def f_tr(pe, src=src, tb=tb):
                for c in range(8):
                    i = pe.transpose(out=bt[tb][:, c * 128:(c + 1) * 128], in_=src[:, c, 0:128], identity=identb[:])
                return i

            add("pe", f_tr, rd=[B[srcb], b_identb], wr=[b_bt[tb]])
            add("act", lambda a, dst=dst, tb=tb: a.activation(out=dst[:], in_=bt[tb][:, :], func=AF.Copy), wr=[b_bt[tb], B[dstb]])
        for g4 in range(8):
            hp0 = (g4 // 2) * 2
            par_ = g4 % 2
            hs2 = [2 * hp0 + par_, 2 * (hp0 + 1) + par_]
            ba, bk_, bl = nb(), nb(), nb()
            ps_ = slice(par_ * 64, par_ * 64 + 64)

            def f_sc(pe, hs2=hs2, ba=ba, bk_=bk_, bl=bl, ps_=ps_):
                for q, h in enumerate(hs2):
                    c = h // 2
                    pe.matmul(bk[ba][:, q * 256:(q + 1) * 256], lhsT=btT[ps_, c, :], rhs=AR[ps_, c, :], start=True, stop=True)
                    pe.matmul(bk[bk_][:, q * 256:(q + 1) * 256], lhsT=ktT[ps_, c, :], rhs=AR[ps_, c, :], start=True, stop=True)
                    i = pe.matmul(bk[bl][:, q * 128:(q + 1) * 128], lhsT=AR[ps_, c, 0:128], rhs=btT[ps_, c, :], start=True, stop=True)
                return i

            add("pe", f_sc, rd=[B["AR"], B["bt"], B["kt"]], wr=[b_bk[ba], b_bk[bk_], b_bk[bl]])

            def f_ev(v, ba=ba, bk_=bk_, bl=bl, hs2=hs2):
                for q, h in enumerate(hs2):
                    v.tensor_tensor(out=AM[:, h, :], in0=bk[ba][:, q * 256:(q + 1) * 256], in1=mA[d][:, 0:256], op=ALU.mult)
                    v.tensor_tensor(out=KM[:, h, :], in0=bk[bk_][:, q * 256:(q + 1) * 256], in1=mA[d][:, 0:256], op=ALU.mult)
                    i = v.tensor_tensor(out=LL[:, h, :], in0=bk[bl][:, q * 128:(q + 1) * 128], in1=mL[d][:, 0:128], op=ALU.mult)
                return i

            gq = hs2[0] // 4
            add("dve", f_ev, rd=[b_cst], wr=[b_bk[ba], b_bk[bk_], b_bk[bl], B["AM"][gq], B["KM"][gq], B["LL"][gq]])
        add("act", lambda a: a.activation(out=Gc[:], in_=Ei[:, :, ec:ec + 1], func=AF.Copy), rd=[B["E"]], wr=[B["Gc"]])
        add("act", lambda a: a.activation(out=sgb[:], in_=SH[:, 26, :], func=AF.Sigmoid), rd=[bS_["lg"]], wr=[B["sg"]])

    def stageBC(si, d, ci, p):
        S = cfg.seqs[si]
        base = cfg.off[si]
        nch = S // 128
        t0 = base + ci * 128
        first = (ci == 0) if d == 0 else (ci == nch - 1)
        final = (d == 1)
        B = Bp[p]
        AR, Atok, Bgtok, Kgtok, Vtok, AM, KM, LL, sgb, bon, Gc = ARs[p], Atoks[p], Bgtoks[p], Kgtoks[p], Vtoks[p], AMs[p], KMs[p], LLs[p], sgbs[p], bons[p], Gcs[p]
        PL = [LL, PL12[0], PL12[1]]
        R, Kx, Vx = SH[:, 0:8, :], SH[:, 8:16, :], SH[:, 16:24, :]
        dsl = slice(d * 64, (d + 1) * 64)
        bc = lambda i: par[:, i, :].unsqueeze(2).to_broadcast([128, 8, 128])
        ec = 127 if d == 0 else 0
        z1, z2 = nb(), nb()

        def f_z(pe):
            for h in range(16):
                b_ = z1 if h < 8 else z2
                i = pe.matmul(bk[b_][:, (h % 8) * 64:(h % 8 + 1) * 64], lhsT=KM[:, h, 0:128], rhs=Vtok[:, h * 64:(h + 1) * 64], start=True, stop=True)
            return i

        add("pe", f_z, rd=B["KM"] + [B["Vtok"]], wr=[b_bk[z1], b_bk[z2]])
        add("act", lambda a: a.activation(out=Zb[:, 0:512], in_=bk[z1][:, :], func=AF.Copy), wr=[b_bk[z1], B["Zb"]])
        add("act", lambda a: a.activation(out=Zb[:, 512:1024], in_=bk[z2][:, :], func=AF.Copy), wr=[b_bk[z2], B["Zb"]])
        for gq in range(4):
            add("dve", lambda g, gq=gq: g.tensor_tensor(out=X[0][:, gq * 4:gq * 4 + 4, :], in0=AM[:, gq * 4:gq * 4 + 4, 0:128], in1=identb[:].unsqueeze(1).to_broadcast([128, 4, 128]), op=ALU.add), rd=[B["AM"][gq], b_identb], wr=[B["X0"][gq]])
        Lcur, Acur, Xc = 0, 0, 0
        Lb = ["LL", "PL0", "PL1"]
        Ab = ["AM", "PA0", "PA1"]
        PAv = [AM[:, :, 0:128], PA[0][:], PA[1][:]]
        for lvl in range(6):
            Ln = 1 + (lvl % 2)
            An = 1 + (lvl % 2)
            Xn = 1 - Xc
            for g4 in range(4):
                hsl = slice(g4 * 4, g4 * 4 + 4)
                b1, b2 = nb(), nb()

                def f_sq(pe, g4=g4, b1=b1, b2=b2, Lc=Lcur, Ac=Acur, lvl=lvl):
                    for q in range(4):
                        h = g4 * 4 + q
                        i = pe.matmul(bk[b1][:, q * 128:(q + 1) * 128], lhsT=PAv[Ac][:, h, :], rhs=PL[Lc][:, h, :], start=True, stop=True)
                    if lvl < 5:
                        for q in range(4):
                            h = g4 * 4 + q
                            i = pe.matmul(bk[b2][:, q * 128:(q + 1) * 128], lhsT=PL[Lc][:, h, :], rhs=PAv[Ac][:, h, :], start=True, stop=True)
                    return i

                add("pe", f_sq, rd=[B[Lb[Lcur]][g4], B[Ab[Acur]][g4]], wr=[b_bk[b1], b_bk[b2]])
                add("act", lambda a, b1=b1, hsl=hsl, Ln=Ln: a.activation(out=PL[Ln][:, hsl, :], in_=bk[b1][:, :].rearrange("p (h t) -> p h t", h=4), func=AF.Copy), wr=[b_bk[b1], B[Lb[Ln]][g4]])
                if lvl < 5:
                    if g4 < 2:
                        add("act", lambda a, b2=b2, hsl=hsl, An=An: a.activation(out=PAv[An][:, hsl, :], in_=bk[b2][:, :].rearrange("p (h t) -> p h t", h=4), func=AF.Copy), wr=[b_bk[b2], B[Ab[An]][g4]])
                    else:
                        add("dve", lambda v, b2=b2, hsl=hsl, An=An: v.tensor_copy(out=PAv[An][:, hsl, :], in_=bk[b2][:, :].rearrange("p (h t) -> p h t", h=4)), wr=[b_bk[b2], B[Ab[An]][g4]])

            for g4 in range(4):
                hsl = slice(g4 * 4, g4 * 4 + 4)
                b3 = nb()

                def f_x(pe, g4=g4, b3=b3, Ln=Ln, Xc=Xc):
                    for q in range(4):
                        h = g4 * 4 + q
                        i = pe.matmul(bk[b3][:, q * 128:(q + 1) * 128], lhsT=PL[Ln][:, h, :], rhs=X[Xc][:, h, :], start=True, stop=True)
                    return i

                add("pe", f_x, rd=[B[Lb[Ln]][g4], B["X%d" % Xc][g4], b_identb], wr=[b_bk[b3]])
                add("dve", lambda v, b3=b3, hsl=hsl, Xc=Xc, Xn=Xn: v.tensor_tensor(out=X[Xn][:, hsl, :], in0=bk[b3][:, :].rearrange("p (h t) -> p h t", h=4), in1=X[Xc][:, hsl, :], op=ALU.add), rd=[B["X%d" % Xc][g4]], wr=[b_bk[b3], B["X%d" % Xn][g4]])
            Lcur, Acur, Xc = Ln, An, Xn
        XT = X[Xc]
        bX = B["X%d" % Xc]
        for pg in range(4):
            b_ = nb()

            def f_ah(pe, pg=pg, b_=b_):
                for q in range(2):
                    hp = pg * 2 + q
                    i = pe.matmul(bk[b_][:, q * 256:(q + 1) * 256], lhsT=Atok[:, hp * 128:(hp + 1) * 128], rhs=XT[:].rearrange("p h t -> p (h t)")[:, hp * 256:(hp + 1) * 256], start=True, stop=True)
                return i

            add("pe", f_ah, rd=[B["Atok"], bX[pg]], wr=[b_bk[b_]])

            def f_ahe(v, pg=pg, b_=b_):
                v4 = bk[b_][:, :].rearrange("p (q s t) -> p q s t", q=2, s=2)
                v.tensor_copy(out=AhT[0:64, pg * 2:pg * 2 + 2, :], in_=v4[0:64, :, 0, :])
                return v.tensor_copy(out=AhT[64:128, pg * 2:pg * 2 + 2, :], in_=v4[64:128, :, 1, :])

            add("dve", f_ahe, wr=[b_bk[b_], B["AhT"]])
        q1, q2 = nb(), nb()

        def f_w(pe):
            for h in range(16):
                b_ = q1 if h < 8 else q2
                i = pe.matmul(bk[b_][:, (h % 8) * 64:(h % 8 + 1) * 64], lhsT=XT[:, h, :], rhs=Zb[:, h * 64:(h + 1) * 64], start=True, stop=True)
            return i

        add("pe", f_w, rd=bX + [B["Zb"]], wr=[b_bk[q1], b_bk[q2]])
        add("act", lambda a: a.activation(out=What[:, 0:512], in_=bk[q1][:, :], func=AF.Copy), wr=[b_bk[q1], B["What"]])
        add("act", lambda a: a.activation(out=What[:, 512:1024], in_=bk[q2][:, :], func=AF.Copy), wr=[b_bk[q2], B["What"]])
        if first:
            add("pool", lambda g: (g.memset(Sf[:], 0.0), g.memset(Sbf[:], 0.0))[1], wr=[B["S"], B["Sbf"]])
        ue, uo = nb(), nb()

        def f_u(pe):
            for h in range(16):
                hp, p_ = h // 2, h % 2
                i = pe.matmul(bk[(ue, uo)[p_]][:, hp * 64:(hp + 1) * 64], lhsT=AhT[p_ * 64:(p_ + 1) * 64, hp, :], rhs=Sbf[p_ * 64:(p_ + 1) * 64, hp, :], start=True, stop=True)
            return i

        add("pe", f_u, rd=[B["AhT"], B["Sbf"]], wr=[b_bk[ue], b_bk[uo]])
        Ub4 = Ub[:, :].rearrange("p (hp s v) -> p s hp v", s=2, v=64)
        Wh4 = What[:, :].rearrange("p (hp s v) -> p s hp v", s=2, v=64)
        add("dve", lambda v: (v.tensor_tensor(out=Ub4[:, 0], in0=bk[ue][:, :].rearrange("p (hp v) -> p hp v", v=64), in1=Wh4[:, 0], op=ALU.add),
                              v.tensor_tensor(out=Ub4[:, 1], in0=bk[uo][:, :].rearrange("p (hp v) -> p hp v", v=64), in1=Wh4[:, 1], op=ALU.add))[1], rd=[B["What"]], wr=[b_bk[ue], b_bk[uo], B["Ub"]])
        ye, yo_ = nb(), nb()

        def f_y(pe):
            for h in range(16):
                hp, p_ = h // 2, h % 2
                o = bk[(ye, yo_)[p_]][:, hp * 64:(hp + 1) * 64]
                pe.matmul(o, lhsT=AR[p_ * 64:(p_ + 1) * 64, hp, 128:256], rhs=Sbf[p_ * 64:(p_ + 1) * 64, hp, :], start=True, stop=False)
                pe.matmul(o, lhsT=AM[:, h, 128:256], rhs=Ub[:, h * 64:(h + 1) * 64], start=False, stop=False)
                i = pe.matmul(o, lhsT=KM[:, h, 128:256], rhs=Vtok[:, h * 64:(h + 1) * 64], start=False, stop=True)
            return i

        add("pe", f_y, rd=[B["AR"], B["Sbf"], B["Ub"], B["Vtok"]] + B["AM"] + B["KM"], wr=[b_bk[ye], b_bk[yo_]])
        Y4 = Y[:, :].rearrange("p (hp s v) -> p s hp v", s=2, v=64)
        if final:
            ph.dma(Yp[:], T["ytmp"][t0:t0 + 128, :], rd=[dB("y", t0)], wr=[B["Yp"]])
            Yp4 = Yp[:, :].rearrange("p (hp s v) -> p s hp v", s=2, v=64)
            add("dve", lambda v: (v.tensor_tensor(out=Y4[:, 0], in0=bk[ye][:, :].rearrange("p (hp v) -> p hp v", v=64), in1=Yp4[:, 0], op=ALU.add),
                                  v.tensor_tensor(out=Y4[:, 1], in0=bk[yo_][:, :].rearrange("p (hp v) -> p hp v", v=64), in1=Yp4[:, 1], op=ALU.add))[1], rd=[B["Yp"]], wr=[b_bk[ye], b_bk[yo_], B["Y"]])
        else:
            add("act", lambda a: (a.activation(out=Y4[:, 0], in_=bk[ye][:, :].rearrange("p (hp v) -> p hp v", v=64), func=AF.Copy),
                                  a.activation(out=Y4[:, 1], in_=bk[yo_][:, :].rearrange("p (hp v) -> p hp v", v=64), func=AF.Copy))[1], wr=[b_bk[ye], b_bk[yo_], B["Y"]])
            ph.dma(T["ytmp"][t0:t0 + 128, :], Y[:], rd=[B["Y"]], wr=[dB("y", t0)])
            ph.dma(T["bon"][:, t0:t0 + 128].rearrange("(c p) t -> p c t", p=128), bon[:], rd=[B["bon"]], wr=[dB("b", t0)])
        s1, s2 = nb(), nb()

        def f_s(pe):
            for hp in range(8):
                o = bk[s1 if hp < 4 else s2][:, (hp % 4) * 128:(hp % 4 + 1) * 128]
                pe.matmul(o, lhsT=Bgtok[:, hp * 128:(hp + 1) * 128], rhs=Ub[:, hp * 128:(hp + 1) * 128], start=True, stop=False)
                i = pe.matmul(o, lhsT=Kgtok[:, hp * 128:(hp + 1) * 128], rhs=Vtok[:, hp * 128:(hp + 1) * 128], start=False, stop=True)
            return i

        add("pe", f_s, rd=[B["Bgtok"], B["Ub"], B["Kgtok"], B["Vtok"]], wr=[b_bk[s1], b_bk[s2]])
        add("dve", lambda g: g.tensor_tensor(out=Stmp[:], in0=Sf[:], in1=Gc[:].to_broadcast([128, 8, 64]), op=ALU.mult), rd=[B["Gc"]], wr=[B["S"], B["Stmp"]])

        def f_su(v):
            for q, b_ in ((0, s1), (1, s2)):
                v4 = bk[b_][:, :].rearrange("p (hp s v) -> p hp s v", hp=4, s=2)
                v.tensor_tensor(out=Sf[0:64, q * 4:q * 4 + 4, :], in0=v4[0:64, :, 0, :], in1=Stmp[0:64, q * 4:q * 4 + 4, :], op=ALU.add)
                i = v.tensor_tensor(out=Sf[64:128, q * 4:q * 4 + 4, :], in0=v4[64:128, :, 1, :], in1=Stmp[64:128, q * 4:q * 4 + 4, :], op=ALU.add)
            return i

        add("dve", f_su, rd=[B["Stmp"]], wr=[b_bk[s1], b_bk[s2], B["S"]])
        add("act", lambda a: a.activation(out=Sbf[:], in_=Sf[:], func=AF.Copy), rd=[B["S"]], wr=[B["Sbf"]])
        if not final:
            return
        ph.dma(bonp[:], T["bon"][:, t0:t0 + 128].rearrange("(c p) t -> p c t", p=128), rd=[dB("b", t0)], wr=[B["bonp"]])
        Y3 = Y[:, :].rearrange("p (h v) -> p h v", v=64)

        def f_gn(v):
            v.reduce_sum(out=st[:, :, 0], in_=Y3, axis=AX.X)
            v.tensor_tensor(out=yn, in0=Y3, in1=Y3, op=ALU.mult)
            v.reduce_sum(out=st[:, :, 1], in_=yn, axis=AX.X)
            v.tensor_scalar(out=st[:, :, 0], in0=st[:, :, 0], scalar1=1.0 / 64, scalar2=None, op0=ALU.mult)
            v.tensor_tensor(out=st[:, :, 2], in0=st[:, :, 0], in1=st[:, :, 0], op=ALU.mult)
            return v.scalar_tensor_tensor(out=st[:, :, 1], in0=st[:, :, 1], scalar=1.0 / 64, in1=st[:, :, 2], op0=ALU.mult, op1=ALU.subtract)

        add("dve", f_gn, rd=[B["Y"]], wr=[B["st"], B["yn"]])
        add("act", lambda a: (a.activation(out=st[:, :, 3], in_=st[:, :, 1], func=AF.Ln, bias=eps[:, 0:1]),
                              a.activation(out=st[:, :, 3], in_=st[:, :, 3], func=AF.Exp, scale=-0.5))[1], rd=[b_cst], wr=[B["st"]])

        def f_gn2(v):
            v.tensor_tensor(out=yn, in0=Y3, in1=st[:, :, 0:1].to_broadcast([128, 16, 64]), op=ALU.subtract)
            return v.tensor_tensor(out=yn, in0=yn, in1=st[:, :, 3:4].to_broadcast([128, 16, 64]), op=ALU.mult)

        add("dve", f_gn2, rd=[B["Y"]], wr=[B["st"], B["yn"]])
        f1, f2 = nb(), nb()

        def f_trf(pe):
            for c in range(8):
                i = pe.transpose(out=bk[f1 if c < 4 else f2][:, (c % 4) * 128:(c % 4 + 1) * 128], in_=yn2[:, c * 128:(c + 1) * 128], identity=identf[:])
            return i

        add("pe", f_trf, rd=[B["yn"], b_cst], wr=[b_bk[f1], b_bk[f2]])
        for c in range(8):
            b_ = f1 if c < 4 else f2
            add("act", lambda a, c=c, b_=b_: a.activation(out=fin[:, c, :], in_=bk[b_][:, (c % 4) * 128:(c % 4 + 1) * 128], func=AF.Identity, scale=par[:, 4, c:c + 1], bias=par[:, 5, c:c + 1]), rd=[b_par], wr=[b_bk[b_], B["fin"]])
        g1, g2 = nb(), nb()

        def f_g(pe):
            for c in range(8):
                i = pe.matmul(bk[g1 if c < 4 else g2][:, (c % 4) * 128:(c % 4 + 1) * 128], lhsT=gup[:, c * 128:(c + 1) * 128], rhs=sgb[:], start=True, stop=True)
            return i

        add("pe", f_g, rd=[B["sg"], b_w], wr=[b_bk[g1], b_bk[g2]])

        def f_fin(g):
            g.tensor_tensor(out=fin, in0=fin, in1=bon[:], op=ALU.add)
            return g.tensor_tensor(out=fin, in0=fin, in1=bonp[:], op=ALU.add)

        add("dve", f_fin, rd=[B["bon"], B["bonp"]], wr=[B["fin"]])
        add("dve", lambda v: (v.tensor_tensor(out=yo[:, 0:4, :], in0=bk[g1][:, :].rearrange("p (c t) -> p c t", c=4), in1=fin[:, 0:4, :], op=ALU.mult),
                              v.tensor_tensor(out=yo[:, 4:8, :], in0=bk[g2][:, :].rearrange("p (c t) -> p c t", c=4), in1=fin[:, 4:8, :], op=ALU.mult))[1], rd=[B["fin"]], wr=[b_bk[g1], b_bk[g2], B["yo"]])
        ph.dma(T["yaT"][:, t0:t0 + 128].rearrange("(c p) t -> p c t", p=128), yo, rd=[B["yo"]])

    def capture(fn, *args):
        ph.cap = []
        cnt["pool"] = ((0, 1, 2) if fn is stageA else (3, 4, 5)) if SPLITBANKS else None
        fn(*args)
        ops_, ph.cap = ph.cap, None
        return ops_

    ph.sched = True
    steps = []
    for si, S in enumerate(cfg.seqs):
        nch = S // 128
        steps += [(si, 0, ci) for ci in range(nch)]
        steps += [(si, 1, ci) for ci in reversed(range(nch))]
    curA = capture(stageA, *steps[0], 0)
    for op_ in curA:
        ph._add_cap(*op_)
    for i, stp in enumerate(steps):
        bc_ops = capture(stageBC, *stp, i % 2)
        a_ops = capture(stageA, *steps[i + 1], (i + 1) % 2) if i + 1 < len(steps) else []
        def groups(ops_):
            gs = []
            for o_ in ops_:
                if o_[0] == "pe" or not gs:
                    gs.append([])
                gs[-1].append(o_)
            return gs

        ga, gb = groups(a_ops), groups(bc_ops)
        na, nbc = len(a_ops), len(bc_ops)
        ia = ib = 0
        ja = jb = 0
        while ja < len(ga) or jb < len(gb):
            if ja < len(ga) and jb < len(gb) and not NOINTER:
                ta = ph.peek_ready(ga[ja][0][0], ga[ja][0][2], ga[ja][0][3])
                tb_ = ph.peek_ready(gb[jb][0][0], gb[jb][0][2], gb[jb][0][3])
                if abs(ta - tb_) < 300.0:
                    pick_a = False
                else:
                    pick_a = ta < tb_
                if DBG_PICK is not None:
                    DBG_PICK.append(("A" if pick_a else "B", round(ta), round(tb_)))
            else:
                pick_a = jb >= len(gb) or (ja < len(ga) and NOINTER)
            if pick_a:
                for o_ in ga[ja]:
                    ph._add_cap(*o_)
                ia += len(ga[ja])
                ja += 1
            else:
                for o_ in gb[jb]:
                    ph._add_cap(*o_)
                ib += len(gb[jb])
                jb += 1
    return ph.run()

def build(cfg, debug=False, phases="1234"):
    nc = bass.Bass("TRN2", target_bir_lowering=False)
    NT = cfg.ntok
    T = {}

    def inp(name, shape):
        T[name] = nc.dram_tensor(name, list(shape), F32, kind="ExternalInput").ap()

    inp("x", [NT, D])
    inp("w_in", [D, NIN])
    for nm, n in (("norm_mix", D), ("norm_mlp", D), ("norm_final", D), ("mu_shift", NRW), ("w0", 2048), ("a0", 2048), ("k_k", D), ("k_a", D),
                  ("r_k", D), ("ln_x_w", D), ("ln_x_b", D), ("lambda_q1", 64), ("lambda_k1", 64), ("lambda_q2", 64), ("lambda_k2", 64), ("subln_w", 128)):
        inp(nm, [n])
    for nm in ("w_lora_up", "a_lora_up", "g_lora_up"):
        inp(nm, [128, 1024])
    for nm in ("proj_a", "proj_b", "w_out"):
        inp(nm, [D, D])
    inp("w_mlp_in", [D, DFF])
    inp("w_mlp_out", [DFF, D])
    inp("cst", [128, 8])
    inp("pm", [128, 128])
    kind = "ExternalOutput" if debug else "Internal"
    for nm, shp, dt in (("pr", [NRW, NT], F32), ("prs", [NRW, NT], F32), ("qk", [2048, NT], BF16), ("gt", [2048, NT], BF16), ("vv", [NT, 1024], BF16),
                        ("ytmp", [NT, D], F32), ("bon", [D, NT], F32), ("yaT", [D, NT], BF16), ("ybT", [D, NT], BF16),
                        ("hs", [NT, D], F32), ("hnT", [D, NT], BF16)):
        T[nm] = nc.dram_tensor(nm, shp, dt, kind=kind).ap()
    T["y"] = nc.dram_tensor("y", [NT, D], F32, kind="ExternalOutput").ap()
    with ExitStack() as gst, nc.semaphore("bar") as bar_sem:
        GST[0] = ({e: gst.enter_context(nc.semaphore("s_" + e)) for e in ENGS}, [gst.enter_context(nc.semaphore(f"d{i}")) for i in range(NDS)])
        bar = (bar_sem, 0)
        if "1" in phases:
            bar = phase1(nc, cfg, T, bar)
        if "2" in phases or "s" in phases:
            bar = phase15(nc, cfg, T, bar)
        if "2" in phases:
            bar = phase2(nc, cfg, T, bar)
        if "3" in phases:
            bar = phase3(nc, cfg, T, bar)
        if "4" in phases or "a" in phases:
            bar = phase4a(nc, cfg, T, bar)
        if "4" in phases or "b" in phases:
            bar = phase4b(nc, cfg, T, bar)
    return nc


def core_inputs(inputs, xs):
    cst, pm = host_consts()
    f = lambda k: np.ascontiguousarray(np.asarray(inputs[k], np.float32))
    m = {"x": np.ascontiguousarray(np.concatenate(xs, axis=0)), "w_in": f("w_in")[0], "cst": cst, "pm": pm}
    for nm in ("norm_mix", "norm_mlp", "mu_shift", "w0", "a0", "k_k", "k_a", "r_k", "ln_x_w", "ln_x_b", "lambda_q1", "lambda_k1", "lambda_q2", "lambda_k2", "subln_w"):
        m[nm] = f(nm)[0].reshape(-1)
    m["norm_final"] = f("norm_final").reshape(-1)
    m["w_lora_up"] = f("w_lora_up")[0].reshape(128, 1024)
    m["a_lora_up"] = f("a_lora_up")[0].reshape(128, 1024)
    m["g_lora_up"] = f("g_lora_up")[0].reshape(128, 1024)
    for nm in ("proj_a", "proj_b", "w_out", "w_mlp_in", "w_mlp_out"):
        m[nm] = f(nm)[0]
    return m


_NC_CACHE = {}


def kernel(**inputs):
    xp = np.asarray(inputs["x_prompt"], np.float32)
    xs = np.asarray(inputs["x_sample"], np.float32)
    n = 8
    cfg = Cfg([xp.shape[1], xp.shape[1], xs.shape[1]])
    key = tuple(cfg.seqs)
    if key not in _NC_CACHE:
        _NC_CACHE[key] = build(cfg)
    nc = _NC_CACHE[key]
    in_maps = [core_inputs(inputs, [xp[2 * c], xp[2 * c + 1], xs[c]]) for c in range(n)]
    res = run_bass_kernel_spmd(nc, in_maps, core_ids=list(range(n)))
    yp = np.empty_like(xp)
    ys = np.empty_like(xs)
    S1 = xp.shape[1]
    for c in range(n):
        y = res.results[c]["y"]
        yp[2 * c] = y[0:S1]
        yp[2 * c + 1] = y[S1:2 * S1]
        ys[c] = y[2 * S1:]
    return (yp, ys)


def host_consts():
    p = np.arange(128)
    cst = np.zeros((128, 8), np.float32)
    cst[:, 0] = p % 8
    m = ((p % 64) < 16).astype(np.float32)
    cst[:, 1] = m
    cst[:, 2] = np.where((p % 16) >= 8, 1.0, -1.0) * m
    cst[:, 3] = -math.pi
    cst[:, 4] = 1e-6
    pm = np.zeros((128, 128), np.float32)
    for mm in range(128):
        if (mm % 64) < 16:
            k = mm + 8 if (mm % 16) < 8 else mm - 8
            pm[k, mm] = 1.0
    return cst, pm
```
